# Optimizing a Trainium2 kernel written in Bass

```python
import math
import jax, jax.numpy as jnp
from jax import lax
import numpy as np

D_MODEL = 1024
BATCH = 8
SEQ = 4096
DEPTH = 2
DEC_BATCH = 16
DEC_SEQ = 2048
PAST_LEN = 128

PLE_DIM = 256
N_MIXERS = 2
N_S5_LAYERS = (DEPTH + 1) // 2
N_RG_LAYERS = DEPTH // 2
S5_WIDTH = D_MODEL
S5_GROUP = 16
S5_GROUPS = S5_WIDTH // S5_GROUP
S5_STATE = 64
S5_DT_MIN = 1e-3
S5_DT_MAX = 1e-1
RG_WIDTH = D_MODEL
RG_BLOCKS = 4
RG_BLOCK = RG_WIDTH // RG_BLOCKS
RG_CONV = 4
RG_CONV_LEFT = 1
RG_C = 8.0
PEER_HEADS = 8
PEER_NKEYS = 128
PEER_EXPERTS = PEER_NKEYS * PEER_NKEYS
PEER_QDIM = 256
PEER_HALF = PEER_QDIM // 2
PEER_TOPK = 16
PEER_CHUNK = 512
ALPHA = (2 * DEPTH) ** 0.25
BETA = (8 * DEPTH) ** -0.25
LN_EPS = 1e-5

kernel_name = 'hybrid_s5_rglru_peer_encoder'

F32 = jnp.float32


def _layernorm(x, g, b):
    xf = x.astype(F32)
    mu = jnp.mean(xf, axis=-1, keepdims=True)
    var = jnp.mean(jnp.square(xf - mu), axis=-1, keepdims=True)
    return ((xf - mu) * lax.rsqrt(var + LN_EPS) * g.astype(F32) + b.astype(F32)).astype(x.dtype)


def _cplx_combine(left, right):
    a1r, a1i, b1r, b1i = left
    a2r, a2i, b2r, b2i = right
    return (a1r * a2r - a1i * a2i,
            a1r * a2i + a1i * a2r,
            a2r * b1r - a2i * b1i + b2r,
            a2r * b1i + a2i * b1r + b2i)


def _real_combine(left, right):
    a1, b1 = left
    a2, b2 = right
    return (a1 * a2, a2 * b1 + b2)


def _s5_direction(ug, lam_re, lam_im, log_step, b_re, b_im, c_re, c_im, reverse):
    L = ug.shape[1]
    step = jnp.exp(log_step.astype(F32))[:, None]
    lr = lam_re.astype(F32)
    li = lam_im.astype(F32)
    mag = jnp.exp(lr * step)
    ang = li * step
    ar = mag * jnp.cos(ang)
    ai = mag * jnp.sin(ang)
    den = lr * lr + li * li
    zr = ar - 1.0
    qr = (zr * lr + ai * li) / den
    qi = (ai * lr - zr * li) / den
    br = b_re.astype(F32)
    bi = b_im.astype(F32)
    bbr = qr[..., None] * br - qi[..., None] * bi
    bbi = qr[..., None] * bi + qi[..., None] * br
    bu_r = jnp.einsum('blgc,gpc->blgp', ug, bbr)
    bu_i = jnp.einsum('blgc,gpc->blgp', ug, bbi)
    a_r = jnp.broadcast_to(ar, (1, L) + ar.shape)
    a_i = jnp.broadcast_to(ai, (1, L) + ai.shape)
    _, _, h_r, h_i = lax.associative_scan(_cplx_combine, (a_r, a_i, bu_r, bu_i), reverse=reverse, axis=1)
    return (jnp.einsum('blgp,gcp->blgc', h_r, c_re.astype(F32))
            - jnp.einsum('blgp,gcp->blgc', h_i, c_im.astype(F32)))


def _s5_mixer(x, w_in, lam_re, lam_im, log_step, b_re, b_im, c_re, c_im, d_skip, w_glu):
    bsz, L, _ = x.shape
    u = (x @ w_in).astype(F32)
    ug = u.reshape(bsz, L, S5_GROUPS, S5_GROUP)
    y = (_s5_direction(ug, lam_re[0], lam_im[0], log_step[0], b_re[0], b_im[0], c_re[0], c_im[0], False)
         + _s5_direction(ug, lam_re[1], lam_im[1], log_step[1], b_re[1], b_im[1], c_re[1], c_im[1], True))
    y = y.reshape(bsz, L, S5_WIDTH) + d_skip.astype(F32) * u
    h = jax.nn.gelu(y).astype(x.dtype)
    val, gate = jnp.split(h @ w_glu, 2, axis=-1)
    return val * jax.nn.sigmoid(gate)


def _rg_direction(cf, cb, w_a, b_a, w_x, b_x, lam, reverse):
    shape = cf.shape
    r_gate = jax.nn.sigmoid(jnp.einsum('blhi,hij->blhj', cb, w_a.astype(F32)).reshape(shape) + b_a.astype(F32))
    i_gate = jax.nn.sigmoid(jnp.einsum('blhi,hij->blhj', cb, w_x.astype(F32)).reshape(shape) + b_x.astype(F32))
    log_a = -RG_C * r_gate * jax.nn.softplus(-lam.astype(F32))
    a = jnp.exp(log_a)
    b = jnp.sqrt(-jnp.expm1(2.0 * log_a)) * (i_gate * cf)
    _, h = lax.associative_scan(_real_combine, (a, b), reverse=reverse, axis=1)
    return h


def _rg_mixer(x, w_in, conv_w, conv_b, w_ga, b_ga, w_gx, b_gx, lam, w_out):
    bsz, L, _ = x.shape
    g, r = jnp.split(x @ w_in, 2, axis=-1)
    rp = jnp.pad(r, ((0, 0), (RG_CONV_LEFT, RG_CONV - 1 - RG_CONV_LEFT), (0, 0)))
    c = conv_b + sum(rp[:, k:k + L] * conv_w[k] for k in range(RG_CONV))
    cf = c.astype(F32)
    cb = cf.reshape(bsz, L, RG_BLOCKS, RG_BLOCK)
    h = (_rg_direction(cf, cb, w_ga[0], b_ga[0], w_gx[0], b_gx[0], lam[0], False)
         + _rg_direction(cf, cb, w_ga[1], b_ga[1], w_gx[1], b_gx[1], lam[1], True))
    y = h.astype(x.dtype) * jax.nn.gelu(g)
    return y @ w_out


def _peer(x, w_q, subkeys, u_tab, v_tab):
    bsz, L, D = x.shape
    T = bsz * L
    xt = x.reshape(T, D)
    q = (xt @ w_q).astype(F32).reshape(T, PEER_HEADS, 2, PEER_HALF)
    s = jnp.einsum('thcd,cnd->thcn', q, subkeys.astype(F32))
    sv, si = lax.top_k(s, PEER_TOPK)
    cand_s = (sv[:, :, 0, :, None] + sv[:, :, 1, None, :]).reshape(T, PEER_HEADS, PEER_TOPK * PEER_TOPK)
    cand_e = (si[:, :, 0, :, None] * PEER_NKEYS + si[:, :, 1, None, :]).reshape(T, PEER_HEADS, PEER_TOPK * PEER_TOPK)
    top_s, top_p = lax.top_k(cand_s, PEER_TOPK)
    experts = jnp.take_along_axis(cand_e, top_p, axis=-1).reshape(T, PEER_HEADS * PEER_TOPK)
    gates = jax.nn.softmax(top_s, axis=-1).reshape(T, PEER_HEADS * PEER_TOPK)
    chunk = math.gcd(T, PEER_CHUNK)
    n_blocks = T // chunk

    def expert_block(args):
        xc, ec, gc = args
        act = jax.nn.gelu(jnp.einsum('ckd,cd->ck', u_tab[ec], xc).astype(F32))
        return jnp.einsum('ck,ckd->cd', (gc * act).astype(xc.dtype), v_tab[ec])

    out = lax.map(expert_block, (xt.reshape(n_blocks, chunk, D),
                                 experts.reshape(n_blocks, chunk, PEER_HEADS * PEER_TOPK),
                                 gates.reshape(n_blocks, chunk, PEER_HEADS * PEER_TOPK)))
    return out.reshape(bsz, L, D)


def _trunk(x, p, s5, rg, ln, peer, ple):
    ln1_g, ln1_b, ln2_g, ln2_b = ln
    peer_w_q, peer_subkeys, peer_u, peer_v = peer
    ple_w_proj, ple_w_gate = ple
    for i in range(DEPTH):
        j = i // N_MIXERS
        if i % N_MIXERS == 0:
            mix = _s5_mixer(x, *(w[j] for w in s5))
        else:
            mix = _rg_mixer(x, *(w[j] for w in rg))
        x = _layernorm(ALPHA * x + mix, ln1_g[i], ln1_b[i])
        x = _layernorm(ALPHA * x + _peer(x, peer_w_q[i], peer_subkeys[i], peer_u[i], peer_v[i]), ln2_g[i], ln2_b[i])
        x = x + (p[i] @ ple_w_proj[i]) * jax.nn.sigmoid(x @ ple_w_gate[i])
    return x


def setup_inputs(seed: int = 0) -> dict:
    key = jax.random.key(seed)
    ks = jax.random.split(key, 40)

    def nrm(k, shape, scale):
        return jax.random.normal(k, shape, F32) * scale

    NA, NB, D = N_S5_LAYERS, N_RG_LAYERS, D_MODEL
    G, P, GC = S5_GROUPS, S5_STATE, S5_GROUP
    lam_im_base = math.pi * jnp.arange(P, dtype=F32)
    u_rg = jax.random.uniform(ks[10], (NB, 2, RG_WIDTH), F32, 0.9, 0.999)
    a_rg = u_rg ** (1.0 / RG_C)
    inputs = {
        'x_prompt': nrm(ks[0], (BATCH, SEQ, D), 1.0),
        'x_sample': nrm(ks[1], (DEC_BATCH, DEC_SEQ, D), 1.0),
        'p_prompt': nrm(ks[2], (DEPTH, BATCH, SEQ, PLE_DIM), 1.0),
        'p_sample': nrm(ks[3], (DEPTH, DEC_BATCH, DEC_SEQ, PLE_DIM), 1.0),
        's5_w_in': nrm(ks[4], (NA, D, S5_WIDTH), D ** -0.5),
        's5_lam_re': -0.5 + nrm(ks[5], (NA, 2, G, P), 0.01),
        's5_lam_im': lam_im_base + nrm(ks[6], (NA, 2, G, P), 0.01),
        's5_log_step': jax.random.uniform(ks[7], (NA, 2, G), F32, math.log(S5_DT_MIN), math.log(S5_DT_MAX)),
        's5_b_re': nrm(ks[8], (NA, 2, G, P, GC), (2 * GC) ** -0.5),
        's5_b_im': nrm(ks[9], (NA, 2, G, P, GC), (2 * GC) ** -0.5),
        's5_c_re': nrm(ks[11], (NA, 2, G, GC, P), (2 * P) ** -0.5),
        's5_c_im': nrm(ks[12], (NA, 2, G, GC, P), (2 * P) ** -0.5),
        's5_d': nrm(ks[13], (NA, S5_WIDTH), 1.0),
        's5_w_glu': jnp.concatenate([nrm(ks[14], (NA, S5_WIDTH, D), BETA * S5_WIDTH ** -0.5),
                                     nrm(ks[15], (NA, S5_WIDTH, D), S5_WIDTH ** -0.5)], axis=-1),
        'rg_w_in': nrm(ks[16], (NB, D, 2 * RG_WIDTH), D ** -0.5),
        'rg_conv_w': nrm(ks[17], (NB, RG_CONV, RG_WIDTH), RG_CONV ** -0.5),
        'rg_conv_b': nrm(ks[18], (NB, RG_WIDTH), 0.01),
        'rg_w_gate_a': nrm(ks[19], (NB, 2, RG_BLOCKS, RG_BLOCK, RG_BLOCK), RG_BLOCK ** -0.5),
        'rg_b_gate_a': nrm(ks[20], (NB, 2, RG_WIDTH), 0.01),
        'rg_w_gate_x': nrm(ks[21], (NB, 2, RG_BLOCKS, RG_BLOCK, RG_BLOCK), RG_BLOCK ** -0.5),
        'rg_b_gate_x': nrm(ks[22], (NB, 2, RG_WIDTH), 0.01),
        'rg_lambda': jnp.log(a_rg) - jnp.log1p(-a_rg),
        'rg_w_out': nrm(ks[23], (NB, RG_WIDTH, D), BETA * RG_WIDTH ** -0.5),
        'ln1_g': 1.0 + nrm(ks[24], (DEPTH, D), 0.01),
        'ln1_b': nrm(ks[25], (DEPTH, D), 0.01),
        'ln2_g': 1.0 + nrm(ks[26], (DEPTH, D), 0.01),
        'ln2_b': nrm(ks[27], (DEPTH, D), 0.01),
        'peer_w_q': nrm(ks[28], (DEPTH, D, PEER_HEADS * PEER_QDIM), D ** -0.5),
        'peer_subkeys': nrm(ks[29], (DEPTH, 2, PEER_NKEYS, PEER_HALF), PEER_HALF ** -0.5),
        'peer_u': nrm(ks[30], (DEPTH, PEER_EXPERTS, D), D ** -0.5),
        'peer_v': nrm(ks[31], (DEPTH, PEER_EXPERTS, D), BETA * PEER_HEADS ** -0.5),
        'ple_w_proj': nrm(ks[32], (DEPTH, PLE_DIM, D), PLE_DIM ** -0.5),
        'ple_w_gate': nrm(ks[33], (DEPTH, D, D), D ** -0.5),
    }
    return inputs


def reference(x_prompt, x_sample, p_prompt, p_sample,
              s5_w_in, s5_lam_re, s5_lam_im, s5_log_step, s5_b_re, s5_b_im, s5_c_re, s5_c_im, s5_d, s5_w_glu,
              rg_w_in, rg_conv_w, rg_conv_b, rg_w_gate_a, rg_b_gate_a, rg_w_gate_x, rg_b_gate_x, rg_lambda, rg_w_out,
              ln1_g, ln1_b, ln2_g, ln2_b,
              peer_w_q, peer_subkeys, peer_u, peer_v,
              ple_w_proj, ple_w_gate):
    s5 = (s5_w_in, s5_lam_re, s5_lam_im, s5_log_step, s5_b_re, s5_b_im, s5_c_re, s5_c_im, s5_d, s5_w_glu)
    rg = (rg_w_in, rg_conv_w, rg_conv_b, rg_w_gate_a, rg_b_gate_a, rg_w_gate_x, rg_b_gate_x, rg_lambda, rg_w_out)
    ln = (ln1_g, ln1_b, ln2_g, ln2_b)
    peer = (peer_w_q, peer_subkeys, peer_u, peer_v)
    ple = (ple_w_proj, ple_w_gate)
    y_prompt = _trunk(x_prompt, p_prompt, s5, rg, ln, peer, ple)
    y_sample = _trunk(x_sample, p_sample, s5, rg, ln, peer, ple)
    return (y_prompt, y_sample)
```

```python
from contextlib import ExitStack
import math
import numpy as np
import concourse.bass as bass
import concourse.mybir as mybir
from concourse.bass_utils import run_bass_kernel_spmd

F32 = mybir.dt.float32
BF16 = mybir.dt.bfloat16
I32 = mybir.dt.int32
AF = mybir.ActivationFunctionType
ALU = mybir.AluOpType

NTOK = 8192
D = 1024
ALPHA = 4.0 ** 0.25
LN_EPS = 1e-5
SEGS = [(0, 4096), (4096, 2048), (6144, 2048)]
TWO_PI = 2.0 * math.pi


class Trk:
    ENGS = ['pe', 'act', 'dve', 'pool', 'sp']

    def __init__(self, nc, stack, nslots=12):
        self.nc = nc
        self.e = {'pe': nc.tensor, 'act': nc.scalar, 'dve': nc.vector, 'pool': nc.gpsimd, 'sp': nc.sync}
        self.sem = {}
        self.cnt = {}
        for n in self.ENGS:
            self.sem[n] = stack.enter_context(nc.semaphore('s_' + n))
            self.cnt[n] = 0
        self.slots = ['d%d' % i for i in range(nslots)]
        for s in self.slots:
            self.sem[s] = stack.enter_context(nc.semaphore('s_' + s))
            self.cnt[s] = 0
        self.rr = 0
        self.seen = {n: {} for n in self.ENGS}
        self.lw = {}
        self.lr = {}
        self.ninst = 0

    def _deps(self, reads, writes):
        need = {}
        for k in reads:
            for e, c in self.lw.get(k, {}).items():
                if c > need.get(e, 0):
                    need[e] = c
        for k in writes:
            for e, c in self.lw.get(k, {}).items():
                if c > need.get(e, 0):
                    need[e] = c
            for e, c in self.lr.get(k, {}).items():
                if c > need.get(e, 0):
                    need[e] = c
        return need

    def _wait(self, eng, need, skip_self=False):
        for e, c in need.items():
            if skip_self and e == eng:
                continue
            if self.seen[eng].get(e, 0) >= c:
                continue
            self.e[eng].wait_ge(self.sem[e], c)
            self.seen[eng][e] = c

    def _record(self, who, c, reads, writes):
        for k in writes:
            self.lw[k] = {who: c}
            self.lr[k] = {}
        for k in reads:
            self.lr.setdefault(k, {})[who] = c

    def op(self, eng, fn, reads=(), writes=(), skip_self=False):
        need = self._deps(reads, writes)
        self._wait(eng, need, skip_self)
        ins = fn(self.e[eng])
        self.cnt[eng] += 1
        ins.then_inc(self.sem[eng], 1)
        self._record(eng, self.cnt[eng], reads, writes)
        self.ninst += 1

    def dma(self, out, in_, reads=(), writes=(), eng='sp'):
        need = self._deps(reads, writes)
        slot = self.slots[self.rr]
        self.rr = (self.rr + 1) % len(self.slots)
        if self.cnt[slot] > 0:
            need[slot] = max(need.get(slot, 0), self.cnt[slot])
        self._wait(eng, need)
        ins = self.e[eng].dma_start(out=out, in_=in_)
        self.cnt[slot] += 16
        ins.then_inc(self.sem[slot], 16)
        self._record(slot, self.cnt[slot], reads, writes)
        self.ninst += 1

    def barrier(self):
        allc = {k: v for k, v in self.cnt.items() if v > 0}
        for eng in self.ENGS:
            self._wait(eng, dict(allc), skip_self=True)

    def finish(self):
        allc = {k: v for k, v in self.cnt.items() if v > 0}
        self._wait('sp', dict(allc), skip_self=True)


def bc(ap, shape):
    return ap.to_broadcast(list(shape))


class Ker:
    def __init__(self, dbg_out=(), dbg_in=(), phases=None):
        self.dbg_out = set(dbg_out)
        self.dbg_in = set(dbg_in)
        self.phases = phases
        self.nc = bass.Bass("TRN2", target_bir_lowering=False)
        self.in_names = []
        self.out_names = []
        self.tmp_id = 0

    def sbuf(self, name, shape, dt):
        self.tmp_id += 1
        return self.nc.sbuf_tensor("%s_u%d" % (name, self.tmp_id), list(shape), dt)

    def din(self, name, shape, dt=F32):
        self.in_names.append(name)
        return self.nc.dram_tensor(name, list(shape), dt, kind="ExternalInput").ap()

    def dout(self, name, shape, dt=F32):
        self.out_names.append(name)
        return self.nc.dram_tensor(name, list(shape), dt, kind="ExternalOutput").ap()

    def scratch(self, name, shape, dt=F32):
        if name in self.dbg_in:
            return self.din(name, shape, dt)
        if name in self.dbg_out:
            return self.dout(name, shape, dt)
        return self.nc.dram_tensor(name, list(shape), dt, kind="Internal").ap()

    def __getattr__(self, attr):
        specs = self.__dict__.get("specs", {})
        if attr in specs:
            kind, name, shape = specs[attr]
            ap = self.din(name, shape)
            self.__dict__[attr] = ap
            return ap
        raise AttributeError(attr)

    def on(self, ph):
        return self.phases is None or ph in self.phases

    def build(self):
        nc = self.nc
        with ExitStack() as st:
            self.st = st
            self.t = Trk(nc, st)
            t = self.t
            self.specs = {
                "xT": ("din", "xT", [D, NTOK]),
                "pT": ("din", "pT", [2, 256, NTOK]),
                "yT": ("dout", "yT", [D, NTOK]),
                "c_ident": ("din", "c_ident", [128, 128]),
                "c_onesm": ("din", "c_onesm", [128, 128]),
                "c_maskf": ("din", "c_maskf", [128, 128]),
                "c_maskb": ("din", "c_maskb", [128, 128]),
                "c_segmask": ("din", "c_segmask", [128, 1024]),
                "w_s5_in": ("din", "s5_w_in", [D, D]),
                "w_s5_glu": ("din", "s5_w_glu", [D, 2 * D]),
                "s5_lamre": ("din", "s5_lamre", [128, 64]),
                "s5_lamim": ("din", "s5_lamim", [128, 64]),
                "s5_lstep": ("din", "s5_lstep", [128, 64]),
                "s5_bre": ("din", "s5_bre", [128, 64, 16]),
                "s5_bim": ("din", "s5_bim", [128, 64, 16]),
                "s5_ctre": ("din", "s5_ctre", [128, 64, 16]),
                "s5_ctim": ("din", "s5_ctim", [128, 64, 16]),
                "s5_dpk": ("din", "s5_dpk", [128, 64]),
                "w_rg_in": ("din", "rg_w_in", [D, 2 * D]),
                "rg_convw": ("din", "rg_convw", [128, 8, 4]),
                "rg_convb": ("din", "rg_convb", [128, 8]),
                "rg_wga": ("din", "rg_wga", [2, 4, 256, 256]),
                "rg_wgx": ("din", "rg_wgx", [2, 4, 256, 256]),
                "rg_bga": ("din", "rg_bga", [128, 2, 8]),
                "rg_bgx": ("din", "rg_bgx", [128, 2, 8]),
                "rg_lam": ("din", "rg_lam", [128, 2, 8]),
                "w_rg_out": ("din", "rg_w_out", [D, D]),
                "ln_par": ("din", "ln_par", [128, 4, 2, 8]),
                "w_peer_q": ("din", "peer_w_q", [2, D, 2 * D]),
                "peer_skT": ("din", "peer_skT", [2, 2, 128, 128]),
                "peer_uT": ("din", "peer_uT", [2, D, 16384]),
                "peer_v": ("din", "peer_v", [2, 16384, D]),
                "w_ple_proj": ("din", "ple_w_proj", [2, 256, D]),
                "w_ple_gate": ("din", "ple_w_gate", [2, D, D]),
            }
            self.yT = self.dout("yT", [D, NTOK]) if self.on("peer1") else None
            self.UpD = self.scratch("UpD", [64, 128, 1024], BF16)
            self.HpD = self.scratch("HpD", [64, 128, 1024], BF16)
            self.X1 = self.scratch("X1", [D, NTOK])
            self.XL1 = self.scratch("XL1", [D, NTOK])
            self.RGr = self.scratch("RGr", [D, NTOK])
            self.RGg = self.scratch("RGg", [D, NTOK], BF16)
            self.RGy = self.scratch("RGy", [D, NTOK], BF16)
            self.XB = self.scratch("XB", [D, NTOK], BF16)
            self.QT = self.scratch("QT", [2 * D, NTOK], BF16)
            self.PEo = self.scratch("PEo", [D, NTOK])
            self.Ub = self.scratch("Ub", [2, D, 16384], BF16)
            self.Vb = self.scratch("Vb", [2, 16384, D], BF16)
            self.ident = st.enter_context(self.sbuf("ident", [128, 128], F32))
            self.identb = st.enter_context(self.sbuf("identb", [128, 128], BF16))
            self.onesm = st.enter_context(self.sbuf("onesm", [128, 128], F32))
            self.lnp = st.enter_context(self.sbuf("lnp", [128, 4, 2, 8], F32))
            self.ps = [st.enter_context(nc.psum_tensor("ps%d" % i, [128, 512], F32)) for i in range(8)]
            t.dma(self.ident[:], self.c_ident, writes=["ident"])
            t.dma(self.onesm[:], self.c_onesm, writes=["onesm"])
            t.dma(self.lnp[:], self.ln_par, writes=["lnp"])
            t.op('dve', lambda e: e.tensor_copy(out=self.identb[:], in_=self.ident[:]), reads=["ident"], writes=["identb"])

            if self.on("tabcast"):
                self.phase_tabcast()
            if self.on("s5a"):
                self.phase_s5a()
            if self.on("s5b"):
                self.phase_s5b()
            if self.on("s5c"):
                self.phase_s5c()
            if self.on("peer0"):
                self.phase_peer(0, self.X1, self.XL1)
            if self.on("rg"):
                self.phase_rg()
            if self.on("peer1"):
                self.phase_peer(1, self.X1, self.yT)
            t.barrier()
            t.finish()
        return nc

    def load_w_bf16(self, ph, name, dram_ap, kt, ncols, eng_cast='pool'):
        nc, t = self.nc, self.t
        wb = ph.enter_context(self.sbuf(name, [128, kt, ncols], BF16))
        stg = ph.enter_context(self.sbuf(name + "_stg", [128, 2, 2048], F32))
        i = 0
        for k in range(kt):
            for c0 in range(0, ncols, 2048):
                cw = min(2048, ncols - c0)
                b = i % 2
                t.dma(stg[:, b, 0:cw], dram_ap[k * 128:(k + 1) * 128, c0:c0 + cw],
                      writes=[name + "_stg%d" % b])
                eng = ['pool', 'act'][i % 2] if eng_cast == 'mix' else eng_cast
                if eng == 'act':
                    t.op('act', lambda e, b=b, k=k, c0=c0, cw=cw: e.copy(out=wb[:, k, c0:c0 + cw], in_=stg[:, b, 0:cw]),
                         reads=[name + "_stg%d" % b], writes=[name])
                else:
                    t.op(eng, lambda e, b=b, k=k, c0=c0, cw=cw: e.tensor_copy(out=wb[:, k, c0:c0 + cw], in_=stg[:, b, 0:cw]),
                         reads=[name + "_stg%d" % b], writes=[name])
                i += 1
        return wb

    def ln_block(self, z, zkey, layer, which, out, outkey, N, tmp, pbank_a, pbank_b):
        t = self.t
        pm = self.ps[pbank_a]
        pv = self.ps[pbank_b]
        ka, kb = "ps%d" % pbank_a, "ps%d" % pbank_b
        for k in range(8):
            t.op('pe', lambda e, k=k: e.matmul(pm[:, 0:N], self.onesm[:], z[:, k, :], start=(k == 0), stop=(k == 7)),
                 reads=[zkey, "onesm"], writes=[ka], skip_self=True)
        zc, sq, sd = tmp['zc'], tmp['sq'], tmp['sd']
        t.op('dve', lambda e: e.tensor_tensor(out=zc[:], in0=z[:], in1=bc(pm[:, 0:N].unsqueeze(1), [128, 8, N]), op=ALU.subtract),
             reads=[zkey, ka], writes=[zc.name])
        t.op('act', lambda e: e.activation(out=sq[:], in_=zc[:], func=AF.Square), reads=[zc.name], writes=[sq.name])
        for k in range(8):
            t.op('pe', lambda e, k=k: e.matmul(pv[:, 0:N], self.onesm[:], sq[:, k, :], start=(k == 0), stop=(k == 7)),
                 reads=[sq.name, "onesm"], writes=[kb], skip_self=True)
        t.op('act', lambda e: e.activation(out=sd[:], in_=pv[:, 0:N], func=AF.Sqrt, bias=self.epsc[:, 0:1], scale=1.0),
             reads=[kb, "epsc"], writes=[sd.name])
        t.op('dve', lambda e: e.reciprocal(out=sd[:], in_=sd[:]), reads=[sd.name], writes=[sd.name])
        t.op('dve', lambda e: e.tensor_tensor(out=zc[:], in0=zc[:], in1=bc(sd[:].unsqueeze(1), [128, 8, N]), op=ALU.mult),
             reads=[zc.name, sd.name], writes=[zc.name])
        for k in range(8):
            t.op('act', lambda e, k=k: e.activation(out=out[:, k, :], in_=zc[:, k, :], func=AF.Identity,
                                                    bias=self.lnp[:, 2 * which + 1, layer, k:k + 1],
                                                    scale=self.lnp[:, 2 * which, layer, k:k + 1]),
                 reads=[zc.name, "lnp"], writes=[outkey])

    def mk_eps(self, ph):
        nc, t = self.nc, self.t
        self.epsc = ph.enter_context(self.sbuf("epsc", [128, 1], F32))
        t.op('dve', lambda e: e.memset(self.epsc[:], LN_EPS), writes=["epsc"])

    def phase_tabcast(self):
        t = self.t
        for l in range(2):
            for r0 in range(0, D, 128):
                for c0 in range(0, 16384, 2048):
                    t.dma(self.Ub[l, r0:r0 + 128, c0:c0 + 2048], self.peer_uT[l, r0:r0 + 128, c0:c0 + 2048],
                          writes=["Ub%d_%d_%d" % (l, r0 // 128, c0 // 2048)], eng='pool')
            for r0 in range(0, 16384, 256):
                t.dma(self.Vb[l, r0:r0 + 256, :], self.peer_v[l, r0:r0 + 256, :], writes=["Vb%d_%d" % (l, r0 // 256)], eng='pool')

    def phase_s5a(self):
        nc, t = self.nc, self.t
        with ExitStack() as ph:
            wb = self.load_w_bf16(ph, "s5win", self.w_s5_in, 8, D, eng_cast='mix')
            xf = ph.enter_context(self.sbuf("a_xf", [128, 2, 8, 512], F32))
            xb = ph.enter_context(self.sbuf("a_xb", [128, 8, 1024], BF16))
            U8 = ph.enter_context(self.sbuf("a_U8", [128, 64, 8, 16], BF16))
            Upt = ph.enter_context(self.sbuf("a_Upt", [128, 64, 128], BF16))
            xTv = self.xT.rearrange("(k p) n -> p k n", p=128)
            for tile in range(8):
                t0 = tile * 1024
                for h in range(2):
                    t.dma(xf[:, h, :, :], xTv[:, :, t0 + h * 512:t0 + (h + 1) * 512], writes=["a_xf%d" % h])
                    t.op('act' if h == 0 else 'pool',
                         (lambda e, h=h: e.copy(out=xb[:, :, h * 512:(h + 1) * 512], in_=xf[:, h, :, :])) if h == 0 else
                         (lambda e, h=h: e.tensor_copy(out=xb[:, :, h * 512:(h + 1) * 512], in_=xf[:, h, :, :])),
                         reads=["a_xf%d" % h], writes=["a_xb"])
                for s in range(8):
                    for fh in range(2):
                        bank = (s * 2 + fh) % 4
                        pk = "ps%d" % bank
                        for k in range(8):
                            t.op('pe', lambda e, k=k, s=s, fh=fh, bank=bank: e.matmul(
                                self.ps[bank][:], xb[:, k, s::8], wb[:, k, fh * 512:(fh + 1) * 512],
                                start=(k == 0), stop=(k == 7)),
                                reads=["a_xb", "s5win"], writes=[pk], skip_self=True)
                        if (s * 2 + fh) % 2 == 0:
                            t.op('act', lambda e, s=s, fh=fh, bank=bank: e.copy(out=U8[:, fh * 32:(fh + 1) * 32, s, :], in_=self.ps[bank][:].rearrange("p (g c) -> p g c", c=16)),
                                 reads=[pk], writes=["a_U8"])
                        else:
                            t.op('dve', lambda e, s=s, fh=fh, bank=bank: e.tensor_copy(out=U8[:, fh * 32:(fh + 1) * 32, s, :], in_=self.ps[bank][:].rearrange("p (g c) -> p g c", c=16)),
                                 reads=[pk], writes=["a_U8"])
                for gb in range(8):
                    bank = 4 + gb % 2
                    pk = "ps%d" % bank
                    pb = self.ps[bank][:].bitcast(BF16)
                    for gi in range(8):
                        g = gb * 8 + gi
                        t.op('pe', lambda e, g=g, gi=gi, pb=pb: e.transpose(pb[:, gi * 128:(gi + 1) * 128],
                                                                          U8[:, g, :, :].rearrange("p s c -> p (s c)"), self.identb[:]),
                             reads=["a_U8", "identb"], writes=[pk], skip_self=True)
                    if gb % 2 == 0:
                        t.op('dve', lambda e, gb=gb, pb=pb: e.tensor_copy(out=Upt[:, gb * 8:(gb + 1) * 8, :],
                                                                        in_=pb.rearrange("p (g c) -> p g c", g=8)),
                             reads=[pk], writes=["a_Upt"])
                    else:
                        t.op('act', lambda e, gb=gb, pb=pb: e.copy(out=Upt[:, gb * 8:(gb + 1) * 8, :],
                                                                 in_=pb.rearrange("p (g c) -> p g c", g=8)),
                             reads=[pk], writes=["a_Upt"])
                t.dma(self.UpD[:, :, tile * 128:(tile + 1) * 128].rearrange("g p c -> p g c"), Upt[:],
                      reads=["a_Upt"], writes=["UpD"])
            t.barrier()

    def cmul(self, eng, outr, outi, ar, ai, br, bi, tmp1, tmp2, rk, wk):
        t = self.t
        t.op(eng, lambda e: e.tensor_tensor(out=tmp1, in0=ar, in1=br, op=ALU.mult), reads=rk, writes=["cm_t1"])
        t.op(eng, lambda e: e.tensor_tensor(out=tmp2, in0=ai, in1=bi, op=ALU.mult), reads=rk, writes=["cm_t2"])
        t.op(eng, lambda e: e.tensor_tensor(out=outr, in0=tmp1, in1=tmp2, op=ALU.subtract), reads=["cm_t1", "cm_t2"] + rk, writes=wk)
        t.op(eng, lambda e: e.tensor_tensor(out=tmp1, in0=ar, in1=bi, op=ALU.mult), reads=rk + wk, writes=["cm_t1"])
        t.op(eng, lambda e: e.tensor_tensor(out=tmp2, in0=ai, in1=br, op=ALU.mult), reads=rk + wk, writes=["cm_t2"])
        t.op(eng, lambda e: e.tensor_tensor(out=outi, in0=tmp1, in1=tmp2, op=ALU.add), reads=["cm_t1", "cm_t2"] + rk, writes=wk)

    def phase_s5b(self):
        nc, t = self.nc, self.t
        with ExitStack() as ph:
            sb = lambda name, shape, dt=F32: ph.enter_context(self.sbuf(name, list(shape), dt))
            MATS = sb("b_mats", [128, 64, 5, 128], BF16)
            POW = sb("b_pow", [128, 2, 16, 64])
            PH = sb("b_ph", [128, 2, 10, 64])
            RHO = sb("b_rho", [128, 64])
            dpk = sb("b_dpk", [128, 64])
            segm = sb("b_segm", [128, 1024])
            t.dma(dpk[:], self.s5_dpk, writes=["b_dpk"])
            t.dma(segm[:], self.c_segmask, writes=["b_segm"])
            with ExitStack() as pg:
                sg = lambda name, shape, dt=F32: pg.enter_context(self.sbuf(name, list(shape), dt))
                lre = sg("g_lre", [128, 64]); lim = sg("g_lim", [128, 64]); lst = sg("g_lst", [128, 64])
                Bre = sg("g_bre", [128, 64, 16]); Bim = sg("g_bim", [128, 64, 16])
                Cre = sg("g_cre", [128, 64, 16]); Cim = sg("g_cim", [128, 64, 16])
                maskf = sg("g_maskf", [128, 128]); maskb = sg("g_maskb", [128, 128])
                for dst, src in [(lre, self.s5_lamre), (lim, self.s5_lamim), (lst, self.s5_lstep), (Bre, self.s5_bre),
                                 (Bim, self.s5_bim), (Cre, self.s5_ctre), (Cim, self.s5_ctim), (maskf, self.c_maskf),
                                 (maskb, self.c_maskb)]:
                    t.dma(dst[:], src, writes=[dst.name])
                S = {}
                for nm in ["step", "ang", "lrs", "mag", "magi", "kf", "r", "m1", "s1", "c1", "ar", "ai", "ari", "aii",
                           "den", "zr", "qr", "qi", "u1", "u2", "e8"]:
                    S[nm] = sg("g_" + nm, [128, 64])
                ki = sg("g_ki", [128, 64], I32)
                V = 'dve'

                def tt(out, a, b, op, rk, wk):
                    t.op(V, lambda e: e.tensor_tensor(out=out, in0=a, in1=b, op=op), reads=rk, writes=wk)

                def ts(out, a, s1, s2, op0, op1, rk, wk):
                    t.op(V, lambda e: e.tensor_scalar(out=out, in0=a, scalar1=s1, scalar2=s2, op0=op0, op1=op1), reads=rk, writes=wk)

                def act(out, a, func, rk, wk, scale=1.0):
                    t.op('act', lambda e: e.activation(out=out, in_=a, func=func, scale=scale), reads=rk, writes=wk)

                n = lambda k: S[k].name
                act(S["step"][:], lst[:], AF.Exp, [lst.name], [n("step")])
                tt(S["ang"][:], lim[:], S["step"][:], ALU.mult, [lim.name, n("step")], [n("ang")])
                tt(S["lrs"][:], lre[:], S["step"][:], ALU.mult, [lre.name, n("step")], [n("lrs")])
                act(S["mag"][:], S["lrs"][:], AF.Exp, [n("lrs")], [n("mag")])
                act(S["magi"][:], S["lrs"][:], AF.Exp, [n("lrs")], [n("magi")], scale=-1.0)
                act(S["e8"][:], S["lrs"][:], AF.Exp, [n("lrs")], [n("e8")], scale=-8.0)
                act(RHO[:], S["lrs"][:], AF.Exp, [n("lrs")], ["b_rho"], scale=8.0)

                def range_reduce(dst, src, shift):
                    ts(S["kf"][:], src, 1.0 / TWO_PI, shift / TWO_PI + 0.5, ALU.mult, ALU.add, [n("ang")], [n("kf")])
                    t.op(V, lambda e: e.tensor_copy(out=ki[:], in_=S["kf"][:]), reads=[n("kf")], writes=[ki.name])
                    t.op(V, lambda e: e.tensor_copy(out=S["kf"][:], in_=ki[:]), reads=[ki.name], writes=[n("kf")])
                    ts(S["kf"][:], S["kf"][:], -TWO_PI, shift, ALU.mult, ALU.add, [n("kf")], [n("kf")])
                    tt(dst, src, S["kf"][:], ALU.add, [n("ang"), n("kf")], [n("r")])
                    ts(S["m1"][:], dst, math.pi, -TWO_PI, ALU.is_gt, ALU.mult, [n("r")], [n("m1")])
                    tt(dst, dst, S["m1"][:], ALU.add, [n("r"), n("m1")], [n("r")])
                    ts(S["m1"][:], dst, -math.pi, TWO_PI, ALU.is_lt, ALU.mult, [n("r")], [n("m1")])
                    tt(dst, dst, S["m1"][:], ALU.add, [n("r"), n("m1")], [n("r")])
                    ts(dst, dst, math.pi, -math.pi, ALU.min, ALU.max, [n("r")], [n("r")])

                range_reduce(S["r"][:], S["ang"][:], 0.0)
                act(S["s1"][:], S["r"][:], AF.Sin, [n("r")], [n("s1")])
                range_reduce(S["r"][:], S["ang"][:], math.pi / 2)
                act(S["c1"][:], S["r"][:], AF.Sin, [n("r")], [n("c1")])
                tt(S["ar"][:], S["mag"][:], S["c1"][:], ALU.mult, [n("mag"), n("c1")], [n("ar")])
                tt(S["ai"][:], S["mag"][:], S["s1"][:], ALU.mult, [n("mag"), n("s1")], [n("ai")])
                tt(S["ari"][:], S["magi"][:], S["c1"][:], ALU.mult, [n("magi"), n("c1")], [n("ari")])
                tt(S["aii"][:], S["magi"][:], S["s1"][:], ALU.mult, [n("magi"), n("s1")], [n("aii")])
                ts(S["aii"][:], S["aii"][:], -1.0, None, ALU.mult, ALU.bypass, [n("aii")], [n("aii")])
                tt(S["den"][:], lre[:], lre[:], ALU.mult, [lre.name], [n("den")])
                tt(S["u1"][:], lim[:], lim[:], ALU.mult, [lim.name], [n("u1")])
                tt(S["den"][:], S["den"][:], S["u1"][:], ALU.add, [n("den"), n("u1")], [n("den")])
                t.op(V, lambda e: e.reciprocal(out=S["den"][:], in_=S["den"][:]), reads=[n("den")], writes=[n("den")])
                ts(S["zr"][:], S["ar"][:], -1.0, None, ALU.add, ALU.bypass, [n("ar")], [n("zr")])
                tt(S["u1"][:], S["zr"][:], lre[:], ALU.mult, [n("zr"), lre.name], [n("u1")])
                tt(S["u2"][:], S["ai"][:], lim[:], ALU.mult, [n("ai"), lim.name], [n("u2")])
                tt(S["u1"][:], S["u1"][:], S["u2"][:], ALU.add, [n("u1"), n("u2")], [n("u1")])
                tt(S["qr"][:], S["u1"][:], S["den"][:], ALU.mult, [n("u1"), n("den")], [n("qr")])
                tt(S["u1"][:], S["ai"][:], lre[:], ALU.mult, [n("ai"), lre.name], [n("u1")])
                tt(S["u2"][:], S["zr"][:], lim[:], ALU.mult, [n("zr"), lim.name], [n("u2")])
                tt(S["u1"][:], S["u1"][:], S["u2"][:], ALU.subtract, [n("u1"), n("u2")], [n("u1")])
                tt(S["qi"][:], S["u1"][:], S["den"][:], ALU.mult, [n("u1"), n("den")], [n("qi")])
                BBr = sg("g_bbr", [128, 64, 16]); BBi = sg("g_bbi", [128, 64, 16])
                T1 = sg("g_T1", [128, 1024]); T2 = sg("g_T2", [128, 1024])
                T1v = T1[:].rearrange("p (g c) -> p g c", c=16)
                T2v = T2[:].rearrange("p (g c) -> p g c", c=16)
                qrb = bc(S["qr"][:].unsqueeze(2), [128, 64, 16]); qib = bc(S["qi"][:].unsqueeze(2), [128, 64, 16])
                self.cmul(V, BBr[:], BBi[:], qrb, qib, Bre[:], Bim[:], T1v, T2v, [n("qr"), n("qi"), Bre.name, Bim.name], [BBr.name, BBi.name])
                t.op(V, lambda e: e.memset(POW[:, 0, 7, :], 1.0), writes=["b_pow"])
                t.op(V, lambda e: e.memset(POW[:, 1, 7, :], 0.0), reads=["b_pow"], writes=["b_pow"])
                for k in range(0, 8):
                    self.cmul(V, POW[:, 0, 8 + k, :], POW[:, 1, 8 + k, :], POW[:, 0, 7 + k, :], POW[:, 1, 7 + k, :],
                              S["ar"][:], S["ai"][:], T1[:, 0:64], T2[:, 0:64], ["b_pow", n("ar"), n("ai")], ["b_pow"])
                for k in range(0, 7):
                    self.cmul(V, POW[:, 0, 6 - k, :], POW[:, 1, 6 - k, :], POW[:, 0, 7 - k, :], POW[:, 1, 7 - k, :],
                              S["ari"][:], S["aii"][:], T1[:, 0:64], T2[:, 0:64], ["b_pow", n("ari"), n("aii")], ["b_pow"])
                tt(PH[:, 0, 0, :], POW[:, 0, 15, :], S["e8"][:], ALU.mult, ["b_pow", n("e8")], ["b_ph"])
                tt(PH[:, 1, 0, :], POW[:, 1, 15, :], S["e8"][:], ALU.mult, ["b_pow", n("e8"), "b_ph"], ["b_ph"])
                t.op(V, lambda e: e.tensor_scalar(out=PH[64:128, 1, 0, :], in0=PH[64:128, 1, 0, :], scalar1=-1.0, scalar2=None,
                                                  op0=ALU.mult, op1=ALU.bypass), reads=["b_ph"], writes=["b_ph"])
                for L in range(9):
                    self.cmul(V, PH[:, 0, L + 1, :], PH[:, 1, L + 1, :], PH[:, 0, L, :], PH[:, 1, L, :],
                              PH[:, 0, L, :], PH[:, 1, L, :], T1[:, 0:64], T2[:, 0:64], ["b_ph"], ["b_ph"])
                PA = sg("g_pa", [128, 4, 2, 8, 64])
                kmap = {0: (lambda j: 7 - j, lambda j: j), 1: (lambda j: j + 1, lambda j: 8 - j),
                        2: (lambda j: -j, lambda j: j), 3: (lambda j: j, lambda j: -j)}
                ci = 0
                for kind in range(4):
                    for j in range(8):
                        for half, (p0, p1) in enumerate([(0, 64), (64, 128)]):
                            kk = kmap[kind][half](j) + 7
                            eng = ['act', 'pool'][ci % 2]
                            ci += 1
                            if eng == 'act':
                                t.op('act', lambda e, kind=kind, j=j, p0=p0, p1=p1, kk=kk: e.copy(out=PA[p0:p1, kind, :, j, :], in_=POW[p0:p1, :, kk, :]),
                                     reads=["b_pow"], writes=["g_pa%d_%d_%d" % (kind, j, half)])
                            else:
                                t.op('pool', lambda e, kind=kind, j=j, p0=p0, p1=p1, kk=kk: e.tensor_copy(out=PA[p0:p1, kind, :, j, :], in_=POW[p0:p1, :, kk, :]),
                                     reads=["b_pow"], writes=["g_pa%d_%d_%d" % (kind, j, half)])
                pa_keys = ["g_pa%d_%d_%d" % (kind, j, half) for kind in range(4) for j in range(8) for half in range(2)]
                TB = sg("g_tb", [128, 4, 2, 1024])
                Ysn = sg("g_ysn", [128, 1024])
                for gb in range(8):
                    g0 = gb * 8
                    for kind in range(4):
                        src_r, src_i = (BBr, BBi) if kind in (0, 2) else (Cre, Cim)
                        par = bc(PA[:, kind, 0, :, g0:g0 + 8].rearrange("p j g -> p g j").unsqueeze(3), [128, 8, 8, 16])
                        pai = bc(PA[:, kind, 1, :, g0:g0 + 8].rearrange("p j g -> p g j").unsqueeze(3), [128, 8, 8, 16])
                        br = bc(src_r[:, g0:g0 + 8, :].unsqueeze(2), [128, 8, 8, 16])
                        bi = bc(src_i[:, g0:g0 + 8, :].unsqueeze(2), [128, 8, 8, 16])
                        outr = TB[:, kind, 0, :].rearrange("p (g j c) -> p g j c", g=8, j=8)
                        outi = TB[:, kind, 1, :].rearrange("p (g j c) -> p g j c", g=8, j=8)
                        t1 = T1[:].rearrange("p (g j c) -> p g j c", g=8, j=8)
                        t2 = T2[:].rearrange("p (g j c) -> p g j c", g=8, j=8)
                        self.cmul(V, outr, outi, par, pai, br, bi, t1, t2, pa_keys + [src_r.name, src_i.name], ["g_tb%d" % kind])
                    t.op(V, lambda e: e.tensor_scalar(out=Ysn[:], in0=TB[:, 2, 1, :], scalar1=-1.0, scalar2=None, op0=ALU.mult, op1=ALU.bypass),
                         reads=["g_tb2"], writes=["g_ysn"])
                    for gi in range(8):
                        g = g0 + gi
                        sl = slice(gi * 128, (gi + 1) * 128)
                        for ri in range(2):
                            bank = 4 + ri
                            t.op('pe', lambda e, ri=ri, sl=sl, bank=bank: e.transpose(self.ps[bank][:, 0:128], TB[:, 0, ri, sl], self.ident[:]),
                                 reads=["g_tb0", "ident"], writes=["ps%d" % bank], skip_self=True)
                            t.op('act', lambda e, ri=ri, g=g, bank=bank: e.copy(out=MATS[:, g, ri, :], in_=self.ps[bank][:, 0:128]),
                                 reads=["ps%d" % bank], writes=["b_mats"])
                        t.op('pool', lambda e, g=g, sl=sl: e.tensor_copy(out=MATS[:, g, 3, :], in_=TB[:, 1, 0, sl]), reads=["g_tb1"], writes=["b_mats"])
                        t.op('pool', lambda e, g=g, sl=sl: e.tensor_scalar(out=MATS[:, g, 4, :], in0=TB[:, 1, 1, sl], scalar1=-1.0, scalar2=None,
                                                                          op0=ALU.mult, op1=ALU.bypass), reads=["g_tb1"], writes=["b_mats"])
                        for half, (p0, p1) in enumerate([(0, 64), (64, 128)]):
                            bank = 6 + half
                            t.op('pe', lambda e, p0=p0, p1=p1, sl=sl, bank=bank: e.matmul(self.ps[bank][:, 0:128], TB[p0:p1, 2, 0, sl], TB[p0:p1, 3, 0, sl], start=True, stop=False),
                                 reads=["g_tb2", "g_tb3"], writes=["ps%d" % bank], skip_self=True)
                            t.op('pe', lambda e, p0=p0, p1=p1, sl=sl, bank=bank: e.matmul(self.ps[bank][:, 0:128], Ysn[p0:p1, sl], TB[p0:p1, 3, 1, sl], start=False, stop=True),
                                 reads=["g_ysn", "g_tb3"], writes=["ps%d" % bank], skip_self=True)
                        t.op(V, lambda e: e.tensor_tensor(out=T1[:, 0:128], in0=self.ps[6][:, 0:128], in1=maskf[:], op=ALU.mult),
                             reads=["ps6", maskf.name], writes=["cm_t1"])
                        t.op(V, lambda e: e.tensor_tensor(out=T2[:, 0:128], in0=self.ps[7][:, 0:128], in1=maskb[:], op=ALU.mult),
                             reads=["ps7", maskb.name], writes=["cm_t2"])
                        t.op(V, lambda e, g=g: e.tensor_tensor(out=MATS[:, g, 2, :], in0=T1[:, 0:128], in1=T2[:, 0:128], op=ALU.add),
                             reads=["cm_t1", "cm_t2"], writes=["b_mats"])
                t.barrier()
            Up = sb("b_up", [128, 2, 1024], BF16)
            TAB = sb("b_tab", [128, 2, 1024])
            RM = sb("b_rm", [128, 1024])
            W = [sb("b_w%d" % i, [128, 1024]) for i in range(6)]
            Gs = [sb("b_g%d" % i, [128, 1024]) for i in range(2)]
            Hu = [sb("b_h%d" % i, [128, 1024]) for i in range(2)]
            Hs = sb("b_hs", [128, 2, 1024], BF16)
            yd = sb("b_yd", [128, 1024])
            hp = sb("b_hp", [128, 2, 1024], BF16)
            t.op('pool', lambda e: e.memset(Hs[:], 0.0), writes=["b_hs"])
            V = 'dve'
            for g in range(64):
                ub = g % 2
                uk = "b_up%d" % ub
                t.dma(Up[:, ub, :], self.UpD[g, :, :], reads=["UpD"], writes=[uk])
                t.op('pool', lambda e, g=g: e.tensor_scalar(out=RM[:], in0=segm[:], scalar1=RHO[:, g:g + 1], scalar2=None, op0=ALU.mult, op1=ALU.bypass),
                     reads=["b_segm", "b_rho"], writes=["b_rm"])
                t.op(V, lambda e: e.memset(TAB[:, 0, 0:1], 1.0), writes=["b_tab"])
                t.op(V, lambda e: e.memset(TAB[:, 1, 0:1], 0.0), reads=["b_tab"], writes=["b_tab"])
                for L in range(9):
                    n0 = 1 << L
                    cr = PH[:, 0, L, g:g + 1]
                    ci_ = PH[:, 1, L, g:g + 1]
                    src_r = TAB[:, 0, 0:n0]; src_i = TAB[:, 1, 0:n0]
                    dst_r = TAB[:, 0, n0:2 * n0]; dst_i = TAB[:, 1, n0:2 * n0]
                    t.op(V, lambda e, src_i=src_i, ci_=ci_, n0=n0: e.tensor_scalar(out=W[0][:, 0:n0], in0=src_i, scalar1=ci_, scalar2=None, op0=ALU.mult, op1=ALU.bypass),
                         reads=["b_tab", "b_ph"], writes=["b_w0"])
                    t.op(V, lambda e, src_r=src_r, cr=cr, n0=n0, dst_r=dst_r: e.scalar_tensor_tensor(out=dst_r, in0=src_r, scalar=cr, in1=W[0][:, 0:n0], op0=ALU.mult, op1=ALU.subtract),
                         reads=["b_tab", "b_ph", "b_w0"], writes=["b_tab"])
                    t.op(V, lambda e, src_r=src_r, ci_=ci_, n0=n0: e.tensor_scalar(out=W[1][:, 0:n0], in0=src_r, scalar1=ci_, scalar2=None, op0=ALU.mult, op1=ALU.bypass),
                         reads=["b_tab", "b_ph"], writes=["b_w1"])
                    t.op(V, lambda e, src_i=src_i, cr=cr, n0=n0, dst_i=dst_i: e.scalar_tensor_tensor(out=dst_i, in0=src_i, scalar=cr, in1=W[1][:, 0:n0], op0=ALU.mult, op1=ALU.add),
                         reads=["b_tab", "b_ph", "b_w1"], writes=["b_tab"])
                for ri in range(2):
                    t.op('pool', lambda e, ri=ri: e.tensor_copy(out=TAB[:, ri, 512:768], in_=TAB[:, ri, 0:256]), reads=["b_tab"], writes=["b_tab"])
                    t.op('pool', lambda e, ri=ri: e.tensor_copy(out=TAB[:, ri, 768:1024], in_=TAB[:, ri, 0:256]), reads=["b_tab"], writes=["b_tab"])
                for ri in range(2):
                    for h in range(2):
                        bank = ri * 2 + h
                        t.op('pe', lambda e, ri=ri, h=h, bank=bank, g=g, ub=ub: e.matmul(self.ps[bank][:], MATS[:, g, ri, :], Up[:, ub, h * 512:(h + 1) * 512], start=True, stop=True),
                             reads=["b_mats", uk], writes=["ps%d" % bank], skip_self=True)
                cosT = TAB[:, 0, :]; sinT = TAB[:, 1, :]
                for h in range(2):
                    cs = slice(h * 512, (h + 1) * 512)
                    t.op(V, lambda e, h=h, cs=cs: e.tensor_tensor(out=W[0][:, cs], in0=self.ps[h][:], in1=cosT[:, cs], op=ALU.mult), reads=["ps%d" % h, "b_tab"], writes=["b_w0"])
                    t.op(V, lambda e, h=h, cs=cs: e.tensor_tensor(out=W[1][:, cs], in0=self.ps[2 + h][:], in1=sinT[:, cs], op=ALU.mult), reads=["ps%d" % (2 + h), "b_tab"], writes=["b_w1"])
                    t.op(V, lambda e, h=h, cs=cs: e.tensor_tensor(out=W[2][:, cs], in0=self.ps[2 + h][:], in1=cosT[:, cs], op=ALU.mult), reads=["ps%d" % (2 + h), "b_tab"], writes=["b_w2"])
                    t.op(V, lambda e, h=h, cs=cs: e.tensor_tensor(out=W[3][:, cs], in0=self.ps[h][:], in1=sinT[:, cs], op=ALU.mult), reads=["ps%d" % h, "b_tab"], writes=["b_w3"])
                t.op('pool', lambda e: e.tensor_tensor(out=W[4][:], in0=W[0][:], in1=W[1][:], op=ALU.add), reads=["b_w0", "b_w1"], writes=["b_w4"])
                t.op('pool', lambda e: e.tensor_tensor(out=W[5][:], in0=W[2][:], in1=W[3][:], op=ALU.subtract), reads=["b_w2", "b_w3"], writes=["b_w5"])
                for ri in range(2):
                    src = W[4 + ri]
                    t.op(V, lambda e, ri=ri, src=src: e.tensor_tensor_scan(out=Gs[ri][0:64, :], data0=RM[0:64, :], data1=src[0:64, :], initial=0.0, op0=ALU.mult, op1=ALU.add),
                         reads=["b_rm", src.name], writes=["b_g%d_f" % ri])
                    t.op(V, lambda e, ri=ri, src=src: e.tensor_tensor_scan(out=Gs[ri][64:128, ::-1], data0=RM[64:128, ::-1], data1=src[64:128, ::-1], initial=0.0, op0=ALU.mult, op1=ALU.add),
                         reads=["b_rm", src.name], writes=["b_g%d_b" % ri])
                gk = ["b_g0_f", "b_g0_b", "b_g1_f", "b_g1_b"]
                t.op(V, lambda e: e.tensor_tensor(out=W[0][:], in0=Gs[0][:], in1=cosT, op=ALU.mult), reads=gk + ["b_tab"], writes=["b_w0"])
                t.op('pool', lambda e: e.tensor_tensor(out=W[1][:], in0=Gs[1][:], in1=sinT, op=ALU.mult), reads=gk + ["b_tab"], writes=["b_w1"])
                t.op(V, lambda e: e.tensor_tensor(out=W[2][:], in0=Gs[1][:], in1=cosT, op=ALU.mult), reads=gk + ["b_tab"], writes=["b_w2"])
                t.op('pool', lambda e: e.tensor_tensor(out=W[3][:], in0=Gs[0][:], in1=sinT, op=ALU.mult), reads=gk + ["b_tab"], writes=["b_w3"])
                t.op(V, lambda e: e.tensor_tensor(out=Hu[0][:], in0=W[0][:], in1=W[1][:], op=ALU.subtract), reads=["b_w0", "b_w1"], writes=["b_h0"])
                t.op('pool', lambda e: e.tensor_tensor(out=Hu[1][:], in0=W[2][:], in1=W[3][:], op=ALU.add), reads=["b_w2", "b_w3"], writes=["b_h1"])
                for ri in range(2):
                    t.op(V, lambda e, ri=ri: e.tensor_tensor(out=Hs[0:64, ri, 1:1024], in0=Hu[ri][0:64, 0:1023], in1=segm[0:64, 1:1024], op=ALU.mult),
                         reads=["b_h%d" % ri, "b_segm"], writes=["b_hs"])
                    t.op('pool', lambda e, ri=ri: e.tensor_tensor(out=Hs[64:128, ri, 0:1023], in0=Hu[ri][64:128, 1:1024], in1=segm[64:128, 0:1023], op=ALU.mult),
                         reads=["b_h%d" % ri, "b_segm"], writes=["b_hs"])
                for h in range(2):
                    bank = 4 + h
                    cs = slice(h * 512, (h + 1) * 512)
                    t.op('pe', lambda e, g=g, ub=ub, cs=cs, bank=bank: e.matmul(self.ps[bank][:], MATS[:, g, 2, :], Up[:, ub, cs], start=True, stop=False),
                         reads=["b_mats", uk], writes=["ps%d" % bank], skip_self=True)
                    t.op('pe', lambda e, g=g, cs=cs, bank=bank: e.matmul(self.ps[bank][:], MATS[:, g, 3, :], Hs[:, 0, cs], start=False, stop=False),
                         reads=["b_mats", "b_hs"], writes=["ps%d" % bank], skip_self=True)
                    t.op('pe', lambda e, g=g, cs=cs, bank=bank: e.matmul(self.ps[bank][:], MATS[:, g, 4, :], Hs[:, 1, cs], start=False, stop=True),
                         reads=["b_mats", "b_hs"], writes=["ps%d" % bank], skip_self=True)
                    t.op(V, lambda e, g=g, ub=ub, cs=cs, bank=bank: e.scalar_tensor_tensor(out=yd[:, cs], in0=Up[:, ub, cs], scalar=dpk[:, g:g + 1], in1=self.ps[bank][:],
                                                                                         op0=ALU.mult, op1=ALU.add),
                         reads=[uk, "b_dpk", "ps%d" % bank], writes=["b_yd"])
                t.op('act', lambda e, ub=ub: e.activation(out=hp[:, ub, :], in_=yd[:], func=AF.Gelu_apprx_tanh), reads=["b_yd"], writes=["b_hp%d" % ub])
                t.dma(self.HpD[g, :, :], hp[:, ub, :], reads=["b_hp%d" % ub], writes=["HpD"])
            t.barrier()

    def phase_s5c(self):
        nc, t = self.nc, self.t
        with ExitStack() as ph:
            sb = lambda name, shape, dt=F32: ph.enter_context(self.sbuf(name, list(shape), dt))
            self.mk_eps(ph)
            wg = self.load_w_bf16(ph, "s5wglu", self.w_s5_glu, 8, 2 * D, eng_cast='mix')
            hpt = sb("c_hpt", [128, 64, 128], BF16)
            H8 = sb("c_H8", [128, 8, 1024], BF16)
            hT = sb("c_hT", [128, 8, 1024], BF16)
            xf = sb("c_xf", [128, 8, 512])
            z = sb("c_z", [128, 8, 512])
            sg_ = sb("c_sg", [128, 512])
            tmp = {'zc': sb("c_zc", [128, 8, 512]), 'sq': sb("c_sq", [128, 8, 512]), 'sd': sb("c_sd", [128, 512])}
            xo = sb("c_xo", [128, 8, 512])
            xTv = self.xT.rearrange("(k p) n -> p k n", p=128)
            X1v = self.X1.rearrange("(k p) n -> p k n", p=128)
            for tile in range(8):
                t.dma(hpt[:], self.HpD[:, :, tile * 128:(tile + 1) * 128].rearrange("g p c -> p g c"), reads=["HpD"], writes=["c_hpt"])
                for gb in range(8):
                    bank = gb % 2
                    pk = "ps%d" % bank
                    pb = self.ps[bank][:].bitcast(BF16)
                    for gi in range(8):
                        g = gb * 8 + gi
                        t.op('pe', lambda e, g=g, gi=gi, pb=pb: e.transpose(pb[:, gi * 128:(gi + 1) * 128], hpt[:, g, :], self.identb[:]),
                             reads=["c_hpt", "identb"], writes=[pk], skip_self=True)
                    src = pb.rearrange("p (g t c) -> p g t c", g=8, t=8)
                    dst = H8[:, :, gb * 128:(gb + 1) * 128].rearrange("p t (g c) -> p g t c", g=8)
                    if gb % 2 == 0:
                        t.op('dve', lambda e, src=src, dst=dst: e.tensor_copy(out=dst, in_=src), reads=[pk], writes=["c_H8"])
                    else:
                        t.op('act', lambda e, src=src, dst=dst: e.copy(out=dst, in_=src), reads=[pk], writes=["c_H8"])
                for k in range(8):
                    bank = 2 + k % 2
                    pk = "ps%d" % bank
                    pb = self.ps[bank][:].bitcast(BF16)
                    for tt_ in range(8):
                        t.op('pe', lambda e, k=k, tt_=tt_, pb=pb: e.transpose(pb[:, tt_ * 128:(tt_ + 1) * 128], H8[:, tt_, k * 128:(k + 1) * 128], self.identb[:]),
                             reads=["c_H8", "identb"], writes=[pk], skip_self=True)
                    src = pb.rearrange("p (t c) -> p t c", t=8)
                    dst = hT[:, k, :].rearrange("p (c t) -> p t c", t=8)
                    if k % 2 == 0:
                        t.op('dve', lambda e, src=src, dst=dst: e.tensor_copy(out=dst, in_=src), reads=[pk], writes=["c_hT"])
                    else:
                        t.op('act', lambda e, src=src, dst=dst: e.copy(out=dst, in_=src), reads=[pk], writes=["c_hT"])
                for th in range(2):
                    tok0 = tile * 1024 + th * 512
                    t.dma(xf[:], xTv[:, :, tok0:tok0 + 512], writes=["c_xf"])
                    for fo in range(8):
                        bv, bg = 4 + (fo % 2) * 2, 5 + (fo % 2) * 2
                        for k in range(8):
                            t.op('pe', lambda e, k=k, fo=fo, th=th, bv=bv: e.matmul(self.ps[bv][:], wg[:, k, fo * 128:(fo + 1) * 128], hT[:, k, th * 512:(th + 1) * 512],
                                                                              start=(k == 0), stop=(k == 7)), reads=["s5wglu", "c_hT"], writes=["ps%d" % bv], skip_self=True)
                        for k in range(8):
                            t.op('pe', lambda e, k=k, fo=fo, th=th, bg=bg: e.matmul(self.ps[bg][:], wg[:, k, D + fo * 128:D + (fo + 1) * 128], hT[:, k, th * 512:(th + 1) * 512],
                                                                              start=(k == 0), stop=(k == 7)), reads=["s5wglu", "c_hT"], writes=["ps%d" % bg], skip_self=True)
                        t.op('act', lambda e, bg=bg: e.activation(out=sg_[:], in_=self.ps[bg][:], func=AF.Sigmoid), reads=["ps%d" % bg], writes=["c_sg"])
                        t.op('dve', lambda e, bv=bv: e.tensor_tensor(out=sg_[:], in0=self.ps[bv][:], in1=sg_[:], op=ALU.mult), reads=["ps%d" % bv, "c_sg"], writes=["c_sg"])
                        t.op('dve', lambda e, fo=fo: e.scalar_tensor_tensor(out=z[:, fo, :], in0=xf[:, fo, :], scalar=ALPHA, in1=sg_[:], op0=ALU.mult, op1=ALU.add),
                             reads=["c_xf", "c_sg"], writes=["c_z"])
                    self.ln_block(z, "c_z", 0, 0, xo, "c_xo", 512, tmp, 0, 1)
                    t.dma(X1v[:, :, tok0:tok0 + 512], xo[:], reads=["c_xo"], writes=["X1"])
            t.barrier()

    def phase_peer(self, layer, xin, xout):
        self.peer_p1(layer, xin)
        self.peer_p2(layer)
        self.peer_p3(layer, xin, xout)

    def peer_p1(self, layer, xin):
        nc, t = self.nc, self.t
        with ExitStack() as ph:
            sb = lambda name, shape, dt=F32: ph.enter_context(self.sbuf(name, list(shape), dt))
            wq = self.load_w_bf16(ph, "p1_wq", self.w_peer_q[layer], 8, 2 * D, eng_cast='mix')
            xf = sb("p1_xf", [128, 2, 8, 512])
            xb = sb("p1_xb", [128, 2, 8, 512], BF16)
            qb = sb("p1_qb", [128, 2, 16, 512], BF16)
            xv = xin.rearrange("(k p) n -> p k n", p=128)
            XBv = self.XB.rearrange("(k p) n -> p k n", p=128)
            QTv = self.QT.rearrange("(k p) n -> p k n", p=128)
            for blk in range(16):
                b = blk % 2
                tok0 = blk * 512
                t.dma(xf[:, b], xv[:, :, tok0:tok0 + 512], reads=["X1"], writes=["p1_xf%d" % b])
                t.op('pool', lambda e, b=b: e.tensor_copy(out=xb[:, b], in_=xf[:, b]), reads=["p1_xf%d" % b], writes=["p1_xb%d" % b])
                t.dma(XBv[:, :, tok0:tok0 + 512], xb[:, b], reads=["p1_xb%d" % b], writes=["XB_%d" % blk])
                for fo in range(16):
                    bank = fo % 4
                    for k in range(8):
                        t.op('pe', lambda e, k=k, fo=fo, b=b, bank=bank: e.matmul(self.ps[bank][:], wq[:, k, fo * 128:(fo + 1) * 128], xb[:, b, k, :],
                                                                             start=(k == 0), stop=(k == 7)),
                             reads=["p1_wq", "p1_xb%d" % b], writes=["ps%d" % bank], skip_self=True)
                    if fo % 2 == 0:
                        t.op('act', lambda e, fo=fo, b=b, bank=bank: e.copy(out=qb[:, b, fo, :], in_=self.ps[bank][:]), reads=["ps%d" % bank], writes=["p1_qb%d" % b])
                    else:
                        t.op('dve', lambda e, fo=fo, b=b, bank=bank: e.tensor_copy(out=qb[:, b, fo, :], in_=self.ps[bank][:]), reads=["ps%d" % bank], writes=["p1_qb%d" % b])
                t.dma(QTv[:, :, tok0:tok0 + 512], qb[:, b], reads=["p1_qb%d" % b], writes=["QT_%d" % blk])
            t.barrier()

    def peer_p2(self, layer):
        nc, t = self.nc, self.t
        NB = self.peer_nblk if hasattr(self, "peer_nblk") else 32
        with ExitStack() as ph:
            sb = lambda name, shape, dt=F32: ph.enter_context(self.sbuf(name, list(shape), dt))
            skf = sb("p2_skf", [128, 2, 128])
            skb = sb("p2_skb", [128, 2, 128], BF16)
            t.dma(skf[:], self.peer_skT[layer].rearrange("c d n -> d c n"), writes=["p2_skf"])
            t.op('dve', lambda e: e.tensor_copy(out=skb[:], in_=skf[:]), reads=["p2_skf"], writes=["p2_skb"])
            xb = sb("p2_xb", [128, 8, 256], BF16)
            qT = sb("p2_qT", [128, 16, 256], BF16)
            sc = sb("p2_sc", [128, 16, 128])
            sc2 = sb("p2_sc2", [128, 16, 128])
            cs = sb("p2_cs", [128, 8, 256])
            cs2 = sb("p2_cs2", [128, 8, 256])
            sv = sb("p2_sv", [128, 16, 16])
            ts_ = sb("p2_ts", [128, 8, 16])
            ex = sb("p2_ex", [128, 8, 16])
            st8 = sb("p2_st8", [128, 4, 8])
            TM = sb("p2_TM", [128, 3, 128])
            SM = sb("p2_SM", [128, 3, 256])
            QR = sb("p2_qr", [128, 2, 2, 16, 128], BF16)
            Pt = sb("p2_P", [128, 4, 128], BF16)
            Et = sb("p2_E", [128, 4, 128])
            Qt = sb("p2_Q", [128, 4, 128], BF16)
            Gs = sb("p2_Gs", [128, 256, 128], BF16)
            UTs = sb("p2_UT", [128, 2, 8, 512], BF16)
            Vs = sb("p2_V", [128, 2, 4, 1024], BF16)
            ga = sb("p2_ga", [128, 2, 256])
            Hh = sb("p2_H", [128, 2, 256], BF16)
            otok = sb("p2_otok", [128, 2, 1024])
            peT = sb("p2_peT", [128, 8, 256])
            XBv = self.XB.rearrange("(k p) n -> p k n", p=128)
            QTv = self.QT.rearrange("(k p) n -> p k n", p=128)
            PEv = self.PEo.rearrange("(k p) n -> p k n", p=128)
            Ubv = self.Ub[layer].rearrange("(k p) e -> p k e", p=128)
            Vbv = self.Vb[layer].rearrange("(i p) f -> p i f", p=128)
            NEG = -1.0e30
            for blk in range(NB):
                tok0 = blk * 256
                b512 = tok0 // 512
                t.dma(xb[:], XBv[:, :, tok0:tok0 + 256], reads=["XB_%d" % b512], writes=["p2_xb"])
                t.dma(qT[:], QTv[:, :, tok0:tok0 + 256], reads=["QT_%d" % b512], writes=["p2_qT"])
                for st in range(2):
                    tsl = slice(st * 128, (st + 1) * 128)
                    for hc in range(16):
                        bank = hc // 4
                        t.op('pe', lambda e, hc=hc, bank=bank, tsl=tsl: e.matmul(self.ps[bank][:, (hc % 4) * 128:(hc % 4 + 1) * 128], qT[:, hc, tsl], skb[:, hc % 2, :],
                                                                              start=True, stop=True),
                             reads=["p2_qT", "p2_skb"], writes=["ps%d" % bank], skip_self=True)
                    for bank in range(4):
                        t.op('act', lambda e, bank=bank: e.copy(out=sc[:, bank * 4:(bank + 1) * 4, :], in_=self.ps[bank][:].rearrange("p (a n) -> p a n", a=4)),
                             reads=["ps%d" % bank], writes=["p2_sc%d" % bank])
                    for hc in range(16):
                        kk = "p2_sc%d" % (hc // 4)
                        t.op('dve', lambda e, hc=hc: e.max(out=sv[:, hc, 0:8], in_=sc[:, hc, :]), reads=[kk], writes=["p2_sv%d" % hc])
                        t.op('dve', lambda e, hc=hc: e.match_replace(out=sc2[:, hc, :], in_to_replace=sv[:, hc, 0:8], in_values=sc[:, hc, :], imm_value=NEG),
                             reads=[kk, "p2_sv%d" % hc], writes=["p2_sc2_%d" % hc])
                        t.op('dve', lambda e, hc=hc: e.max(out=sv[:, hc, 8:16], in_=sc2[:, hc, :]), reads=["p2_sc2_%d" % hc], writes=["p2_sv%d" % hc])
                    svk = ["p2_sv%d" % hc for hc in range(16)]
                    sv4 = sv[:].rearrange("p (h c) a -> p h c a", c=2)
                    t.op('dve', lambda e: e.tensor_tensor(out=cs[:].rearrange("p h (a b) -> p h a b", a=16),
                                                          in0=bc(sv4[:, :, 0, :].unsqueeze(3), [128, 8, 16, 16]),
                                                          in1=bc(sv4[:, :, 1, :].unsqueeze(2), [128, 8, 16, 16]), op=ALU.add),
                         reads=svk, writes=["p2_cs"])
                    for h in range(8):
                        t.op('dve', lambda e, h=h: e.max(out=ts_[:, h, 0:8], in_=cs[:, h, :]), reads=["p2_cs"], writes=["p2_ts%d" % h])
                        t.op('dve', lambda e, h=h: e.match_replace(out=cs2[:, h, :], in_to_replace=ts_[:, h, 0:8], in_values=cs[:, h, :], imm_value=NEG),
                             reads=["p2_cs", "p2_ts%d" % h], writes=["p2_cs2_%d" % h])
                        t.op('dve', lambda e, h=h: e.max(out=ts_[:, h, 8:16], in_=cs2[:, h, :]), reads=["p2_cs2_%d" % h], writes=["p2_ts%d" % h])
                    tsk = ["p2_ts%d" % h for h in range(8)]
                    t.op('dve', lambda e: e.tensor_tensor(out=ex[:], in0=ts_[:], in1=bc(ts_[:, :, 0:1], [128, 8, 16]), op=ALU.subtract), reads=tsk, writes=["p2_ex"])
                    t.op('act', lambda e: e.activation(out=ex[:], in_=ex[:], func=AF.Exp), reads=["p2_ex"], writes=["p2_ex"])
                    t.op('dve', lambda e: e.tensor_reduce(out=st8[:, 0, :], in_=ex[:], axis=mybir.AxisListType.X, op=ALU.add), reads=["p2_ex"], writes=["p2_st8"])
                    t.op('act', lambda e: e.activation(out=st8[:, 1, :], in_=st8[:, 0, :], func=AF.Ln), reads=["p2_st8"], writes=["p2_st8"])
                    t.op('dve', lambda e: e.tensor_tensor(out=st8[:, 2, :], in0=st8[:, 1, :], in1=ts_[:, :, 0], op=ALU.add), reads=["p2_st8"] + tsk, writes=["p2_st8"])
                    TMv = TM[:].rearrange("p j (h a) -> p j h a", h=8)
                    t.op('pool', lambda e: e.tensor_copy(out=TMv[:, 0], in_=sv4[:, :, 0, :]), reads=svk, writes=["p2_TM0"])
                    t.op('dve', lambda e: e.scalar_tensor_tensor(out=TMv[:, 1], in0=bc(ts_[:, :, 15:16], [128, 8, 16]), scalar=-1.0e-5, in1=sv4[:, :, 0, :], op0=ALU.add, op1=ALU.subtract),
                         reads=svk + tsk, writes=["p2_TM1"])
                    t.op('dve', lambda e: e.tensor_tensor(out=TMv[:, 2], in0=sv4[:, :, 0, :], in1=bc(st8[:, 2, :].unsqueeze(2), [128, 8, 16]), op=ALU.subtract),
                         reads=svk + ["p2_st8"], writes=["p2_TM2"])
                    for j in range(3):
                        t.op('pe', lambda e, j=j: e.transpose(self.ps[4][:, j * 128:(j + 1) * 128], TM[:, j, :], self.ident[:]),
                             reads=["p2_TM%d" % j, "ident"], writes=["ps4"], skip_self=True)
                    t.op('act', lambda e, tsl=tsl: e.copy(out=SM[:, :, tsl], in_=self.ps[4][:, 0:384].rearrange("p (j n) -> p j n", j=3)),
                         reads=["ps4"], writes=["p2_SM%d" % st])
                for t16 in range(16):
                    qb_ = t16 % 2
                    for c in range(2):
                        eng = 'pool' if c == 0 else 'act'
                        src = bc(qT[:, c::2, t16 * 16:(t16 + 1) * 16].rearrange("p h t -> p t h").unsqueeze(3), [128, 16, 8, 16])
                        dst = QR[:, qb_, c].rearrange("p t (h a) -> p t h a", h=8)
                        if eng == 'pool':
                            t.op('pool', lambda e, src=src, dst=dst: e.tensor_copy(out=dst, in_=src), reads=["p2_qT"], writes=["p2_qr%d_%d" % (qb_, c)])
                        else:
                            t.op('act', lambda e, src=src, dst=dst: e.copy(out=dst, in_=src), reads=["p2_qT"], writes=["p2_qr%d_%d" % (qb_, c)])
                    for tl in range(16):
                        tt_ = t16 * 16 + tl
                        s4 = tt_ % 4
                        smk = "p2_SM%d" % (tt_ // 128)
                        t.op('pe', lambda e, tl=tl, s4=s4, qb_=qb_: e.matmul(self.ps[5][:, s4 * 128:(s4 + 1) * 128], QR[:, qb_, 0, tl, :], skb[:, 0, :], start=True, stop=True),
                             reads=["p2_qr%d_0" % qb_, "p2_skb"], writes=["ps5_%d" % s4], skip_self=True)
                        t.op('pe', lambda e, tl=tl, s4=s4, qb_=qb_: e.matmul(self.ps[6][:, s4 * 128:(s4 + 1) * 128], QR[:, qb_, 1, tl, :], skb[:, 1, :], start=True, stop=True),
                             reads=["p2_qr%d_1" % qb_, "p2_skb"], writes=["ps6_%d" % s4], skip_self=True)
                        t.op('dve', lambda e, s4=s4, tt_=tt_: e.tensor_scalar(out=Pt[:, s4, :], in0=self.ps[5][:, s4 * 128:(s4 + 1) * 128], scalar1=SM[:, 0, tt_:tt_ + 1], scalar2=None,
                                                                            op0=ALU.is_equal, op1=ALU.bypass),
                             reads=["ps5_%d" % s4, smk], writes=["p2_P%d" % s4])
                        t.op('act', lambda e, s4=s4, tt_=tt_: e.activation(out=Et[:, s4, :], in_=self.ps[6][:, s4 * 128:(s4 + 1) * 128], func=AF.Exp, bias=SM[:, 2, tt_:tt_ + 1], scale=1.0),
                             reads=["ps6_%d" % s4, smk], writes=["p2_E%d" % s4])
                        t.op('dve', lambda e, s4=s4, tt_=tt_: e.scalar_tensor_tensor(out=Qt[:, s4, :], in0=self.ps[6][:, s4 * 128:(s4 + 1) * 128], scalar=SM[:, 1, tt_:tt_ + 1],
                                                                                   in1=Et[:, s4, :], op0=ALU.is_ge, op1=ALU.mult),
                             reads=["ps6_%d" % s4, smk, "p2_E%d" % s4], writes=["p2_Q%d" % s4])
                        gbank = 7 if (tt_ // 4) % 2 == 0 else 4
                        t.op('pe', lambda e, s4=s4, gbank=gbank: e.matmul(self.ps[gbank][:, s4 * 128:(s4 + 1) * 128], Qt[:, s4, :], Pt[:, s4, :], start=True, stop=True),
                             reads=["p2_Q%d" % s4, "p2_P%d" % s4], writes=["ps%d" % gbank], skip_self=True)
                        if s4 == 3:
                            t4 = tt_ - 3
                            if (tt_ // 4) % 2 == 0:
                                t.op('act', lambda e, t4=t4, gbank=gbank: e.copy(out=Gs[:, t4:t4 + 4, :], in_=self.ps[gbank][:].rearrange("p (t i) -> p t i", t=4)),
                                     reads=["ps%d" % gbank], writes=["p2_Gs"])
                            else:
                                t.op('dve', lambda e, t4=t4, gbank=gbank: e.tensor_copy(out=Gs[:, t4:t4 + 4, :], in_=self.ps[gbank][:].rearrange("p (t i) -> p t i", t=4)),
                                     reads=["ps%d" % gbank], writes=["p2_Gs"])
                for ib in range(32):
                    wbuf = ib % 2
                    e0 = ib * 512
                    ukeys = ["Ub%d_%d_%d" % (layer, r0, e0 // 2048) for r0 in range(8)]
                    vkeys = ["Vb%d_%d" % (layer, (ib * 512) // 256 + x) for x in range(2)]
                    t.dma(UTs[:, wbuf], Ubv[:, :, e0:e0 + 512], reads=ukeys, writes=["p2_UT%d" % wbuf])
                    t.dma(Vs[:, wbuf], Vbv[:, ib * 4:(ib + 1) * 4, :], reads=vkeys, writes=["p2_V%d" % wbuf])
                    for ii in range(4):
                        i = ib * 4 + ii
                        ab = i % 2
                        abank = 5 + ab
                        for k in range(8):
                            t.op('pe', lambda e, k=k, ii=ii, wbuf=wbuf, abank=abank: e.matmul(self.ps[abank][:, 0:256], UTs[:, wbuf, k, ii * 128:(ii + 1) * 128], xb[:, k, :],
                                                                                        start=(k == 0), stop=(k == 7)),
                                 reads=["p2_UT%d" % wbuf, "p2_xb"], writes=["ps%d_0" % abank, "ps%d_1" % abank], skip_self=True)
                        t.op('act', lambda e, ab=ab, abank=abank: e.activation(out=ga[:, ab, :], in_=self.ps[abank][:, 0:256], func=AF.Gelu_apprx_tanh),
                             reads=["ps%d_0" % abank, "ps%d_1" % abank], writes=["p2_ga%d" % ab])
                        t.op('dve', lambda e, ab=ab, i=i: e.tensor_tensor(out=Hh[:, ab, :], in0=ga[:, ab, :], in1=Gs[:, :, i], op=ALU.mult),
                             reads=["p2_ga%d" % ab, "p2_Gs"], writes=["p2_H%d" % ab])
                        for st in range(2):
                            for fh in range(2):
                                ob = st * 2 + fh
                                t.op('pe', lambda e, st=st, fh=fh, ob=ob, ab=ab, ii=ii, wbuf=wbuf, i=i: e.matmul(
                                    self.ps[ob][:], Hh[:, ab, st * 128:(st + 1) * 128], Vs[:, wbuf, ii, fh * 512:(fh + 1) * 512], start=(i == 0), stop=(i == 127)),
                                    reads=["p2_H%d" % ab, "p2_V%d" % wbuf], writes=["ps%d" % ob], skip_self=True)
                for st in range(2):
                    for fh in range(2):
                        ob = st * 2 + fh
                        if ob % 2 == 0:
                            t.op('act', lambda e, st=st, fh=fh, ob=ob: e.copy(out=otok[:, st, fh * 512:(fh + 1) * 512], in_=self.ps[ob][:]), reads=["ps%d" % ob], writes=["p2_otok%d" % st])
                        else:
                            t.op('dve', lambda e, st=st, fh=fh, ob=ob: e.tensor_copy(out=otok[:, st, fh * 512:(fh + 1) * 512], in_=self.ps[ob][:]), reads=["ps%d" % ob], writes=["p2_otok%d" % st])
                for st in range(2):
                    for half in range(2):
                        tb = 5 + half
                        for kk in range(4):
                            fk = half * 4 + kk
                            t.op('pe', lambda e, st=st, fk=fk, kk=kk, tb=tb: e.transpose(self.ps[tb][:, kk * 128:(kk + 1) * 128], otok[:, st, fk * 128:(fk + 1) * 128], self.ident[:]),
                                 reads=["p2_otok%d" % st, "ident"], writes=["ps%d_%d" % (tb, kk)], skip_self=True)
                        if half == 0:
                            t.op('act', lambda e, st=st, half=half, tb=tb: e.copy(out=peT[:, half * 4:(half + 1) * 4, st * 128:(st + 1) * 128],
                                                                                in_=self.ps[tb][:].rearrange("p (k n) -> p k n", k=4)),
                                 reads=["ps%d_%d" % (tb, q) for q in range(4)], writes=["p2_peT"])
                        else:
                            t.op('dve', lambda e, st=st, half=half, tb=tb: e.tensor_copy(out=peT[:, half * 4:(half + 1) * 4, st * 128:(st + 1) * 128],
                                                                                       in_=self.ps[tb][:].rearrange("p (k n) -> p k n", k=4)),
                                 reads=["ps%d_%d" % (tb, q) for q in range(4)], writes=["p2_peT"])
                t.dma(PEv[:, :, tok0:tok0 + 256], peT[:], reads=["p2_peT"], writes=["PEo_%d" % blk])
            t.barrier()

    def peer_p3(self, layer, xin, xout):
        nc, t = self.nc, self.t
        NB = (self.peer_nblk + 1) // 2 if hasattr(self, "peer_nblk") else 16
        with ExitStack() as ph:
            sb = lambda name, shape, dt=F32: ph.enter_context(self.sbuf(name, list(shape), dt))
            self.mk_eps(ph)
            wpg = self.load_w_bf16(ph, "p3_wpg", self.w_ple_gate[layer], 8, D, eng_cast='mix')
            wpp = self.load_w_bf16(ph, "p3_wpp", self.w_ple_proj[layer], 2, D, eng_cast='mix')
            xf = sb("p3_xf", [128, 8, 512])
            pe = sb("p3_pe", [128, 8, 512])
            pf = sb("p3_pf", [128, 2, 512])
            pb = sb("p3_pb", [128, 2, 512], BF16)
            z = sb("p3_z", [128, 8, 512])
            tmp = {'zc': sb("p3_zc", [128, 8, 512]), 'sq': sb("p3_sq", [128, 8, 512]), 'sd': sb("p3_sd", [128, 512])}
            x2 = sb("p3_x2", [128, 8, 512])
            x2b = sb("p3_x2b", [128, 8, 512], BF16)
            sg_ = sb("p3_sg", [128, 2, 512])
            xo = sb("p3_xo", [128, 8, 512])
            xv = xin.rearrange("(k p) n -> p k n", p=128)
            PEv = self.PEo.rearrange("(k p) n -> p k n", p=128)
            pv = self.pT[layer].rearrange("(k p) n -> p k n", p=128)
            ov = xout.rearrange("(k p) n -> p k n", p=128)
            for blk in range(NB):
                tok0 = blk * 512
                t.dma(xf[:], xv[:, :, tok0:tok0 + 512], reads=["X1"], writes=["p3_xf"])
                t.dma(pe[:], PEv[:, :, tok0:tok0 + 512], reads=["PEo_%d" % (2 * blk), "PEo_%d" % (2 * blk + 1)], writes=["p3_pe"])
                t.dma(pf[:], pv[:, :, tok0:tok0 + 512], writes=["p3_pf"])
                t.op('pool', lambda e: e.tensor_copy(out=pb[:], in_=pf[:]), reads=["p3_pf"], writes=["p3_pb"])
                t.op('dve', lambda e: e.scalar_tensor_tensor(out=z[:], in0=xf[:], scalar=ALPHA, in1=pe[:], op0=ALU.mult, op1=ALU.add),
                     reads=["p3_xf", "p3_pe"], writes=["p3_z"])
                self.ln_block(z, "p3_z", layer, 1, x2, "p3_x2", 512, tmp, 0, 1)
                t.op('pool', lambda e: e.tensor_copy(out=x2b[:], in_=x2[:]), reads=["p3_x2"], writes=["p3_x2b"])
                for fo in range(8):
                    bg, bp = 2 + (fo % 2) * 2, 3 + (fo % 2) * 2
                    sgi = fo % 2
                    for k in range(8):
                        t.op('pe', lambda e, k=k, fo=fo, bg=bg: e.matmul(self.ps[bg][:], wpg[:, k, fo * 128:(fo + 1) * 128], x2b[:, k, :], start=(k == 0), stop=(k == 7)),
                             reads=["p3_wpg", "p3_x2b"], writes=["ps%d" % bg], skip_self=True)
                    for k in range(2):
                        t.op('pe', lambda e, k=k, fo=fo, bp=bp: e.matmul(self.ps[bp][:], wpp[:, k, fo * 128:(fo + 1) * 128], pb[:, k, :], start=(k == 0), stop=(k == 1)),
                             reads=["p3_wpp", "p3_pb"], writes=["ps%d" % bp], skip_self=True)
                    t.op('act', lambda e, bg=bg, sgi=sgi: e.activation(out=sg_[:, sgi, :], in_=self.ps[bg][:], func=AF.Sigmoid), reads=["ps%d" % bg], writes=["p3_sg%d" % sgi])
                    t.op('dve', lambda e, bp=bp, sgi=sgi: e.tensor_tensor(out=sg_[:, sgi, :], in0=self.ps[bp][:], in1=sg_[:, sgi, :], op=ALU.mult),
                         reads=["ps%d" % bp, "p3_sg%d" % sgi], writes=["p3_sg%d" % sgi])
                    t.op('pool', lambda e, fo=fo, sgi=sgi: e.tensor_tensor(out=xo[:, fo, :], in0=x2[:, fo, :], in1=sg_[:, sgi, :], op=ALU.add),
                         reads=["p3_x2", "p3_sg%d" % sgi], writes=["p3_xo"])
                t.dma(ov[:, :, tok0:tok0 + 512], xo[:], reads=["p3_xo"], writes=["XOUT%d_%d" % (layer, blk)])
            t.barrier()

    def phase_rg(self):
        self.rg_p1()
        self.rg_p2()
        self.rg_p3()

    def rg_p1(self):
        nc, t = self.nc, self.t
        with ExitStack() as ph:
            sb = lambda name, shape, dt=F32: ph.enter_context(self.sbuf(name, list(shape), dt))
            win = self.load_w_bf16(ph, "r1_win", self.w_rg_in, 8, 2 * D, eng_cast='mix')
            xf = sb("r1_xf", [128, 2, 8, 512])
            xb = sb("r1_xb", [128, 2, 8, 512], BF16)
            gg = sb("r1_gg", [128, 2, 8, 512], BF16)
            rr = sb("r1_rr", [128, 2, 8, 512])
            xv = self.XL1.rearrange("(k p) n -> p k n", p=128)
            ggv = self.RGg.rearrange("(k p) n -> p k n", p=128)
            rrv = self.RGr.rearrange("(k p) n -> p k n", p=128)
            for blk in range(16):
                b = blk % 2
                tok0 = blk * 512
                t.dma(xf[:, b], xv[:, :, tok0:tok0 + 512], writes=["r1_xf%d" % b])
                t.op('pool', lambda e, b=b: e.tensor_copy(out=xb[:, b], in_=xf[:, b]), reads=["r1_xf%d" % b], writes=["r1_xb%d" % b])
                for fo in range(16):
                    bank = fo % 4
                    for k in range(8):
                        t.op('pe', lambda e, k=k, fo=fo, b=b, bank=bank: e.matmul(self.ps[bank][:], win[:, k, fo * 128:(fo + 1) * 128], xb[:, b, k, :],
                                                                             start=(k == 0), stop=(k == 7)),
                             reads=["r1_win", "r1_xb%d" % b], writes=["ps%d" % bank], skip_self=True)
                    if fo < 8:
                        t.op('act', lambda e, fo=fo, b=b, bank=bank: e.activation(out=gg[:, b, fo, :], in_=self.ps[bank][:], func=AF.Gelu_apprx_tanh),
                             reads=["ps%d" % bank], writes=["r1_gg%d" % b])
                    else:
                        t.op('dve', lambda e, fo=fo, b=b, bank=bank: e.tensor_copy(out=rr[:, b, fo - 8, :], in_=self.ps[bank][:]), reads=["ps%d" % bank], writes=["r1_rr%d" % b])
                t.dma(ggv[:, :, tok0:tok0 + 512], gg[:, b], reads=["r1_gg%d" % b], writes=["RGg_%d" % blk])
                t.dma(rrv[:, :, tok0:tok0 + 512], rr[:, b], reads=["r1_rr%d" % b], writes=["RGr_%d" % blk])
            t.barrier()

    def rg_p2(self):
        nc, t = self.nc, self.t
        LM = 4096
        with ExitStack() as ph:
            sb = lambda name, shape, dt=F32: ph.enter_context(self.sbuf(name, list(shape), dt))
            cw = sb("r2_cw", [128, 8, 4]); cbias = sb("r2_cbias", [128, 8])
            bga = sb("r2_bga", [128, 2, 8]); bgx = sb("r2_bgx", [128, 2, 8]); lam = sb("r2_lam", [128, 2, 8])
            sp8 = sb("r2_sp8", [128, 2, 8]); sp16 = sb("r2_sp16", [128, 2, 8])
            for dst, src in [(cw, self.rg_convw), (cbias, self.rg_convb), (bga, self.rg_bga), (bgx, self.rg_bgx), (lam, self.rg_lam)]:
                t.dma(dst[:], src, writes=[dst.name])
            t.op('act', lambda e: e.activation(out=sp8[:], in_=lam[:], func=AF.Exp, scale=-1.0), reads=[lam.name], writes=["r2_sp8"])
            t.op('act', lambda e: e.activation(out=sp8[:], in_=sp8[:], func=AF.Ln, bias=1.0, scale=1.0), reads=["r2_sp8"], writes=["r2_sp8"])
            t.op('dve', lambda e: e.tensor_scalar(out=sp16[:], in0=sp8[:], scalar1=-16.0, scalar2=None, op0=ALU.mult, op1=ALU.bypass), reads=["r2_sp8"], writes=["r2_sp16"])
            t.op('dve', lambda e: e.tensor_scalar(out=sp8[:], in0=sp8[:], scalar1=-8.0, scalar2=None, op0=ALU.mult, op1=ALU.bypass), reads=["r2_sp8", "r2_sp16"], writes=["r2_sp8"])
            wst = sb("r2_wst", [128, 2, 256])
            wgt = sb("r2_wgt", [128, 2, 2, 4, 2, 256], BF16)
            for gi, src in enumerate([self.rg_wga, self.rg_wgx]):
                for d_ in range(2):
                    for h in range(4):
                        t.dma(wst[:], src[d_, h].rearrange("(i p) o -> p i o", p=128), writes=["r2_wst"])
                        t.op('pool', lambda e, gi=gi, d_=d_, h=h: e.tensor_copy(out=wgt[:, gi, d_, h], in_=wst[:]), reads=["r2_wst"], writes=["r2_wgt"])
            rp = sb("r2_rp", [128, 2, LM + 3])
            cc = sb("r2_cc", [128, 2, LM])
            cb = sb("r2_cb", [128, 2, LM], BF16)
            A = sb("r2_A", [128, LM]); B = sb("r2_B", [128, LM])
            HF = sb("r2_HF", [128, LM]); HB = sb("r2_HB", [128, LM])
            gg = sb("r2_gg", [128, LM], BF16); yy = sb("r2_yy", [128, LM], BF16)
            rg_ = sb("r2_rg", [128, 2, 512]); ig_ = sb("r2_ig", [128, 2, 512]); a2_ = sb("r2_a2", [128, 2, 512]); tm_ = sb("r2_tm", [128, 2, 512])
            for si, (s0, L) in enumerate(SEGS):
                for h in range(4):
                    for ct in range(2):
                        ch = 2 * h + ct
                        t.op('pool', lambda e, ct=ct, L=L: e.memset(rp[:, ct, 0:1], 0.0), writes=["r2_rp%d" % ct])
                        t.op('pool', lambda e, ct=ct, L=L: e.memset(rp[:, ct, L + 1:L + 3], 0.0), reads=["r2_rp%d" % ct], writes=["r2_rp%d" % ct])
                        t.dma(rp[:, ct, 1:L + 1], self.RGr[ch * 128:(ch + 1) * 128, s0:s0 + L], reads=["r2_rp%d" % ct], writes=["r2_rp%d" % ct])
                        t.op('dve', lambda e, ct=ct, ch=ch, L=L: e.tensor_scalar(out=cc[:, ct, 0:L], in0=rp[:, ct, 0:L], scalar1=cw[:, ch, 0:1], scalar2=cbias[:, ch:ch + 1],
                                                                            op0=ALU.mult, op1=ALU.add), reads=["r2_rp%d" % ct, cw.name, cbias.name], writes=["r2_cc%d" % ct])
                        for k in range(1, 4):
                            t.op('dve', lambda e, ct=ct, ch=ch, L=L, k=k: e.scalar_tensor_tensor(out=cc[:, ct, 0:L], in0=rp[:, ct, k:k + L], scalar=cw[:, ch, k:k + 1], in1=cc[:, ct, 0:L],
                                                                                           op0=ALU.mult, op1=ALU.add), reads=["r2_rp%d" % ct, cw.name, "r2_cc%d" % ct], writes=["r2_cc%d" % ct])
                        t.op('pool', lambda e, ct=ct, L=L: e.tensor_copy(out=cb[:, ct, 0:L], in_=cc[:, ct, 0:L]), reads=["r2_cc%d" % ct], writes=["r2_cb%d" % ct])
                    for oh in range(2):
                        ch = 2 * h + oh
                        for d_ in range(2):
                            Hd = HF if d_ == 0 else HB
                            for c0 in range(0, L, 512):
                                pi = (c0 // 512) % 2
                                ba, bx = 2 * pi, 2 * pi + 1
                                for ih in range(2):
                                    t.op('pe', lambda e, ih=ih, d_=d_, h=h, oh=oh, c0=c0, ba=ba: e.matmul(self.ps[ba][:], wgt[:, 0, d_, h, ih, oh * 128:(oh + 1) * 128], cb[:, ih, c0:c0 + 512],
                                                                                                    start=(ih == 0), stop=(ih == 1)),
                                         reads=["r2_wgt", "r2_cb0", "r2_cb1"], writes=["ps%d" % ba], skip_self=True)
                                for ih in range(2):
                                    t.op('pe', lambda e, ih=ih, d_=d_, h=h, oh=oh, c0=c0, bx=bx: e.matmul(self.ps[bx][:], wgt[:, 1, d_, h, ih, oh * 128:(oh + 1) * 128], cb[:, ih, c0:c0 + 512],
                                                                                                    start=(ih == 0), stop=(ih == 1)),
                                         reads=["r2_wgt", "r2_cb0", "r2_cb1"], writes=["ps%d" % bx], skip_self=True)
                                t.op('act', lambda e, pi=pi, ba=ba, d_=d_, ch=ch: e.activation(out=rg_[:, pi, :], in_=self.ps[ba][:], func=AF.Sigmoid, bias=bga[:, d_, ch:ch + 1], scale=1.0),
                                     reads=["ps%d" % ba, bga.name], writes=["r2_rg%d" % pi])
                                t.op('act', lambda e, pi=pi, bx=bx, d_=d_, ch=ch: e.activation(out=ig_[:, pi, :], in_=self.ps[bx][:], func=AF.Sigmoid, bias=bgx[:, d_, ch:ch + 1], scale=1.0),
                                     reads=["ps%d" % bx, bgx.name], writes=["r2_ig%d" % pi])
                                t.op('act', lambda e, pi=pi, c0=c0, d_=d_, ch=ch: e.activation(out=A[:, c0:c0 + 512], in_=rg_[:, pi, :], func=AF.Exp, scale=sp8[:, d_, ch:ch + 1]),
                                     reads=["r2_rg%d" % pi, "r2_sp8"], writes=["r2_A"])
                                t.op('act', lambda e, pi=pi, d_=d_, ch=ch: e.activation(out=a2_[:, pi, :], in_=rg_[:, pi, :], func=AF.Exp, scale=sp16[:, d_, ch:ch + 1]),
                                     reads=["r2_rg%d" % pi, "r2_sp16"], writes=["r2_a2%d" % pi])
                                t.op('dve', lambda e, pi=pi: e.tensor_scalar(out=a2_[:, pi, :], in0=a2_[:, pi, :], scalar1=-1.0, scalar2=1.0, op0=ALU.mult, op1=ALU.add),
                                     reads=["r2_a2%d" % pi], writes=["r2_a2%d" % pi])
                                t.op('act', lambda e, pi=pi: e.activation(out=a2_[:, pi, :], in_=a2_[:, pi, :], func=AF.Sqrt), reads=["r2_a2%d" % pi], writes=["r2_a2%d" % pi])
                                t.op('pool', lambda e, pi=pi, oh=oh, c0=c0: e.tensor_tensor(out=tm_[:, pi, :], in0=ig_[:, pi, :], in1=cc[:, oh, c0:c0 + 512], op=ALU.mult),
                                     reads=["r2_ig%d" % pi, "r2_cc%d" % oh], writes=["r2_tm%d" % pi])
                                t.op('dve', lambda e, pi=pi, c0=c0: e.tensor_tensor(out=B[:, c0:c0 + 512], in0=tm_[:, pi, :], in1=a2_[:, pi, :], op=ALU.mult),
                                     reads=["r2_tm%d" % pi, "r2_a2%d" % pi], writes=["r2_B"])
                            if d_ == 0:
                                t.op('dve', lambda e, L=L: e.tensor_tensor_scan(out=HF[:, 0:L], data0=A[:, 0:L], data1=B[:, 0:L], initial=0.0, op0=ALU.mult, op1=ALU.add),
                                     reads=["r2_A", "r2_B"], writes=["r2_HF"])
                            else:
                                t.op('dve', lambda e, L=L: e.tensor_tensor_scan(out=HB[:, 0:L][:, ::-1], data0=A[:, 0:L][:, ::-1], data1=B[:, 0:L][:, ::-1], initial=0.0, op0=ALU.mult, op1=ALU.add),
                                     reads=["r2_A", "r2_B"], writes=["r2_HB"])
                        t.dma(gg[:, 0:L], self.RGg[ch * 128:(ch + 1) * 128, s0:s0 + L], writes=["r2_gg"])
                        t.op('pool', lambda e, L=L: e.tensor_tensor(out=HF[:, 0:L], in0=HF[:, 0:L], in1=HB[:, 0:L], op=ALU.add), reads=["r2_HF", "r2_HB"], writes=["r2_HF"])
                        t.op('dve', lambda e, L=L: e.tensor_tensor(out=yy[:, 0:L], in0=HF[:, 0:L], in1=gg[:, 0:L], op=ALU.mult), reads=["r2_HF", "r2_gg"], writes=["r2_yy"])
                        t.dma(self.RGy[ch * 128:(ch + 1) * 128, s0:s0 + L], yy[:, 0:L], reads=["r2_yy"], writes=["RGy_%d_%d" % (si, ch)])
            t.barrier()

    def rg_p3(self):
        nc, t = self.nc, self.t
        with ExitStack() as ph:
            sb = lambda name, shape, dt=F32: ph.enter_context(self.sbuf(name, list(shape), dt))
            self.mk_eps(ph)
            wo = self.load_w_bf16(ph, "r3_wo", self.w_rg_out, 8, D, eng_cast='mix')
            yb = sb("r3_yb", [128, 8, 512], BF16)
            xf = sb("r3_xf", [128, 8, 512])
            z = sb("r3_z", [128, 8, 512])
            tmp = {'zc': sb("r3_zc", [128, 8, 512]), 'sq': sb("r3_sq", [128, 8, 512]), 'sd': sb("r3_sd", [128, 512])}
            xo = sb("r3_xo", [128, 8, 512])
            xv = self.XL1.rearrange("(k p) n -> p k n", p=128)
            yv = self.RGy.rearrange("(k p) n -> p k n", p=128)
            X1v = self.X1.rearrange("(k p) n -> p k n", p=128)
            for blk in range(16):
                tok0 = blk * 512
                t.dma(yb[:], yv[:, :, tok0:tok0 + 512], writes=["r3_yb"])
                t.dma(xf[:], xv[:, :, tok0:tok0 + 512], writes=["r3_xf"])
                for fo in range(8):
                    bank = 2 + fo % 4
                    for k in range(8):
                        t.op('pe', lambda e, k=k, fo=fo, bank=bank: e.matmul(self.ps[bank][:], wo[:, k, fo * 128:(fo + 1) * 128], yb[:, k, :], start=(k == 0), stop=(k == 7)),
                             reads=["r3_wo", "r3_yb"], writes=["ps%d" % bank], skip_self=True)
                    t.op('dve', lambda e, fo=fo, bank=bank: e.scalar_tensor_tensor(out=z[:, fo, :], in0=xf[:, fo, :], scalar=ALPHA, in1=self.ps[bank][:], op0=ALU.mult, op1=ALU.add),
                         reads=["r3_xf", "ps%d" % bank], writes=["r3_z"])
                self.ln_block(z, "r3_z", 1, 0, xo, "r3_xo", 512, tmp, 0, 1)
                t.dma(X1v[:, :, tok0:tok0 + 512], xo[:], reads=["r3_xo"], writes=["X1"])
            t.barrier()


def _consts():
    c = {}
    c["c_ident"] = np.eye(128, dtype=np.float32)
    c["c_onesm"] = np.full((128, 128), 1.0 / D, dtype=np.float32)
    s = np.arange(128) // 16
    c["c_maskf"] = (s[:, None] <= s[None, :]).astype(np.float32)
    c["c_maskb"] = (s[:, None] >= s[None, :]).astype(np.float32)
    m = np.ones((128, 1024), dtype=np.float32)
    m[0:64, [0, 512, 768]] = 0.0
    m[64:128, [511, 767, 1023]] = 0.0
    c["c_segmask"] = m
    return c


def _shared_weights(inp):
    f = lambda a: np.ascontiguousarray(np.asarray(a, dtype=np.float32))
    w = dict(_consts())
    w["s5_w_in"] = f(inp["s5_w_in"][0])
    w["s5_w_glu"] = f(inp["s5_w_glu"][0])
    w["s5_lamre"] = f(inp["s5_lam_re"][0].transpose(0, 2, 1).reshape(128, 64))
    w["s5_lamim"] = f(inp["s5_lam_im"][0].transpose(0, 2, 1).reshape(128, 64))
    w["s5_lstep"] = f(np.broadcast_to(inp["s5_log_step"][0][:, None, :], (2, 64, 64)).reshape(128, 64))
    w["s5_bre"] = f(inp["s5_b_re"][0].transpose(0, 2, 1, 3).reshape(128, 64, 16))
    w["s5_bim"] = f(inp["s5_b_im"][0].transpose(0, 2, 1, 3).reshape(128, 64, 16))
    w["s5_ctre"] = f(inp["s5_c_re"][0].transpose(0, 3, 1, 2).reshape(128, 64, 16))
    w["s5_ctim"] = f(inp["s5_c_im"][0].transpose(0, 3, 1, 2).reshape(128, 64, 16))
    d = np.asarray(inp["s5_d"][0]).reshape(64, 16)
    w["s5_dpk"] = f(np.broadcast_to(d.T[None, :, :], (8, 16, 64)).reshape(128, 64))
    w["rg_w_in"] = f(inp["rg_w_in"][0])
    w["rg_convw"] = f(inp["rg_conv_w"][0].reshape(4, 8, 128).transpose(2, 1, 0))
    w["rg_convb"] = f(inp["rg_conv_b"][0].reshape(8, 128).T)
    w["rg_wga"] = f(inp["rg_w_gate_a"][0])
    w["rg_wgx"] = f(inp["rg_w_gate_x"][0])
    w["rg_bga"] = f(inp["rg_b_gate_a"][0].reshape(2, 8, 128).transpose(2, 0, 1))
    w["rg_bgx"] = f(inp["rg_b_gate_x"][0].reshape(2, 8, 128).transpose(2, 0, 1))
    w["rg_lam"] = f(inp["rg_lambda"][0].reshape(2, 8, 128).transpose(2, 0, 1))
    w["rg_w_out"] = f(inp["rg_w_out"][0])
    ln = np.stack([np.asarray(inp[k]) for k in ("ln1_g", "ln1_b", "ln2_g", "ln2_b")], 0)
    w["ln_par"] = f(ln.reshape(4, 2, 8, 128).transpose(3, 0, 1, 2))
    w["peer_w_q"] = f(inp["peer_w_q"])
    w["peer_skT"] = f(np.asarray(inp["peer_subkeys"]).transpose(0, 1, 3, 2))
    w["peer_uT"] = f(np.asarray(inp["peer_u"]).transpose(0, 2, 1))
    w["peer_v"] = f(inp["peer_v"])
    w["ple_w_proj"] = f(inp["ple_w_proj"])
    w["ple_w_gate"] = f(inp["ple_w_gate"])
    return w


def _core_acts(inp, core):
    xp = np.asarray(inp["x_prompt"][core])
    xs = np.asarray(inp["x_sample"][2 * core:2 * core + 2]).reshape(4096, D)
    x = np.concatenate([xp, xs], 0)
    pp = np.asarray(inp["p_prompt"][:, core])
    ps = np.asarray(inp["p_sample"][:, 2 * core:2 * core + 2]).reshape(2, 4096, 256)
    p = np.concatenate([pp, ps], 1)
    return {"xT": np.ascontiguousarray(x.T.astype(np.float32)),
            "pT": np.ascontiguousarray(p.transpose(0, 2, 1).astype(np.float32))}


_CACHE = {}


def kernel(**inputs):
    if "nc" not in _CACHE:
        k = Ker()
        _CACHE["nc"] = k.build()
        _CACHE["in_names"] = list(k.in_names)
    nc = _CACHE["nc"]
    names = _CACHE["in_names"]
    w = _shared_weights(inputs)
    in_maps = []
    for core in range(8):
        m = dict(w)
        m.update(_core_acts(inputs, core))
        in_maps.append({n: m[n] for n in names})
    res = run_bass_kernel_spmd(nc, in_maps, core_ids=list(range(8)))
    yp = np.empty((8, 4096, D), dtype=np.float32)
    ys = np.empty((16, 2048, D), dtype=np.float32)
    for core in range(8):
        y = np.asarray(res.results[core]["yT"]).T
        yp[core] = y[0:4096]
        ys[2 * core] = y[4096:6144]
        ys[2 * core + 1] = y[6144:8192]
    return (yp, ys)
```

```python
from contextlib import ExitStack
import math
import numpy as np
import concourse.bass as bass
import concourse.mybir as mybir
from concourse.bass_utils import run_bass_kernel_spmd

F32 = mybir.dt.float32
BF16 = mybir.dt.bfloat16
I32 = mybir.dt.int32
AF = mybir.ActivationFunctionType
ALU = mybir.AluOpType

NTOK = 8192
D = 1024
ALPHA = 4.0 ** 0.25
LN_EPS = 1e-5
SEGS = [(0, 4096), (4096, 2048), (6144, 2048)]
TWO_PI = 2.0 * math.pi


class Trk:
    ENGS = ['pe', 'act', 'dve', 'pool', 'sp']

    def __init__(self, nc, stack, nslots=12):
        self.nc = nc
        self.e = {'pe': nc.tensor, 'act': nc.scalar, 'dve': nc.vector, 'pool': nc.gpsimd, 'sp': nc.sync}
        self.sem = {}
        self.cnt = {}
        for n in self.ENGS:
            self.sem[n] = stack.enter_context(nc.semaphore('s_' + n))
            self.cnt[n] = 0
        self.slots = ['d%d' % i for i in range(nslots)]
        for s in self.slots:
            self.sem[s] = stack.enter_context(nc.semaphore('s_' + s))
            self.cnt[s] = 0
        self.rr = 0
        self.seen = {n: {} for n in self.ENGS}
        self.lw = {}
        self.lr = {}
        self.ninst = 0

    def _deps(self, reads, writes):
        need = {}
        for k in reads:
            for e, c in self.lw.get(k, {}).items():
                if c > need.get(e, 0):
                    need[e] = c
        for k in writes:
            for e, c in self.lw.get(k, {}).items():
                if c > need.get(e, 0):
                    need[e] = c
            for e, c in self.lr.get(k, {}).items():
                if c > need.get(e, 0):
                    need[e] = c
        return need

    def _wait(self, eng, need, skip_self=False):
        for e, c in need.items():
            if skip_self and e == eng:
                continue
            if self.seen[eng].get(e, 0) >= c:
                continue
            self.e[eng].wait_ge(self.sem[e], c)
            self.seen[eng][e] = c

    def _record(self, who, c, reads, writes):
        for k in writes:
            self.lw[k] = {who: c}
            self.lr[k] = {}
        for k in reads:
            self.lr.setdefault(k, {})[who] = c

    def op(self, eng, fn, reads=(), writes=(), skip_self=False):
        need = self._deps(reads, writes)
        self._wait(eng, need, skip_self)
        ins = fn(self.e[eng])
        self.cnt[eng] += 1
        ins.then_inc(self.sem[eng], 1)
        self._record(eng, self.cnt[eng], reads, writes)
        self.ninst += 1

    def dma(self, out, in_, reads=(), writes=(), eng='sp'):
        need = self._deps(reads, writes)
        slot = self.slots[self.rr]
        self.rr = (self.rr + 1) % len(self.slots)
        if self.cnt[slot] > 0:
            need[slot] = max(need.get(slot, 0), self.cnt[slot])
        self._wait(eng, need)
        ins = self.e[eng].dma_start(out=out, in_=in_)
        self.cnt[slot] += 16
        ins.then_inc(self.sem[slot], 16)
        self._record(slot, self.cnt[slot], reads, writes)
        self.ninst += 1

    def barrier(self):
        self.min_rem = min(getattr(self, "min_rem", 1 << 30), self.nc.sbuf_bytes_remaining)
        allc = {k: v for k, v in self.cnt.items() if v > 0}
        for eng in self.ENGS:
            self._wait(eng, dict(allc), skip_self=True)

    def finish(self):
        allc = {k: v for k, v in self.cnt.items() if v > 0}
        self._wait('sp', dict(allc), skip_self=True)


def bc(ap, shape):
    return ap.to_broadcast(list(shape))


class Ker:
    def __init__(self, dbg_out=(), dbg_in=(), phases=None):
        self.dbg_out = set(dbg_out)
        self.dbg_in = set(dbg_in)
        self.phases = phases
        self.nc = bass.Bass("TRN2", target_bir_lowering=False)
        self.in_names = []
        self.out_names = []
        self.tmp_id = 0

    def sbuf(self, name, shape, dt):
        self.tmp_id += 1
        return self.nc.sbuf_tensor("%s_u%d" % (name, self.tmp_id), list(shape), dt)

    def din(self, name, shape, dt=F32):
        self.in_names.append(name)
        return self.nc.dram_tensor(name, list(shape), dt, kind="ExternalInput").ap()

    def dout(self, name, shape, dt=F32):
        self.out_names.append(name)
        return self.nc.dram_tensor(name, list(shape), dt, kind="ExternalOutput").ap()

    def scratch(self, name, shape, dt=F32):
        if name in self.dbg_in:
            return self.din(name, shape, dt)
        if name in self.dbg_out:
            return self.dout(name, shape, dt)
        return self.nc.dram_tensor(name, list(shape), dt, kind="Internal").ap()

    def __getattr__(self, attr):
        specs = self.__dict__.get("specs", {})
        if attr in specs:
            kind, name, shape = specs[attr]
            ap = self.din(name, shape)
            self.__dict__[attr] = ap
            return ap
        raise AttributeError(attr)

    def on(self, ph):
        return self.phases is None or ph in self.phases

    def build(self):
        nc = self.nc
        with ExitStack() as st:
            self.st = st
            self.t = Trk(nc, st)
            t = self.t
            self.specs = {
                "xT": ("din", "xT", [D, NTOK]),
                "pT": ("din", "pT", [2, 256, NTOK]),
                "yT": ("dout", "yT", [D, NTOK]),
                "c_ident": ("din", "c_ident", [128, 128]),
                "c_onesm": ("din", "c_onesm", [128, 128]),
                "c_maskf": ("din", "c_maskf", [128, 128]),
                "c_maskb": ("din", "c_maskb", [128, 128]),
                "c_segmask": ("din", "c_segmask", [128, 1024]),
                "w_s5_in": ("din", "s5_w_in", [D, D]),
                "w_s5_glu": ("din", "s5_w_glu", [D, 2 * D]),
                "s5_lamre": ("din", "s5_lamre", [128, 64]),
                "s5_lamim": ("din", "s5_lamim", [128, 64]),
                "s5_lstep": ("din", "s5_lstep", [128, 64]),
                "s5_bre": ("din", "s5_bre", [128, 64, 16]),
                "s5_bim": ("din", "s5_bim", [128, 64, 16]),
                "s5_ctre": ("din", "s5_ctre", [128, 64, 16]),
                "s5_ctim": ("din", "s5_ctim", [128, 64, 16]),
                "s5_dpk": ("din", "s5_dpk", [128, 64]),
                "w_rg_in": ("din", "rg_w_in", [D, 2 * D]),
                "rg_convw": ("din", "rg_convw", [128, 8, 4]),
                "rg_convb": ("din", "rg_convb", [128, 8]),
                "rg_wga": ("din", "rg_wga", [2, 4, 256, 256]),
                "rg_wgx": ("din", "rg_wgx", [2, 4, 256, 256]),
                "rg_bga": ("din", "rg_bga", [128, 2, 8]),
                "rg_bgx": ("din", "rg_bgx", [128, 2, 8]),
                "rg_lam": ("din", "rg_lam", [128, 2, 8]),
                "w_rg_out": ("din", "rg_w_out", [D, D]),
                "ln_par": ("din", "ln_par", [128, 4, 2, 8]),
                "w_peer_q": ("din", "peer_w_q", [2, D, 2 * D]),
                "peer_skT": ("din", "peer_skT", [2, 2, 128, 128]),
                "peer_uT": ("din", "peer_uT", [2, D, 16384]),
                "peer_v": ("din", "peer_v", [2, 16384, D]),
                "w_ple_proj": ("din", "ple_w_proj", [2, 256, D]),
                "w_ple_gate": ("din", "ple_w_gate", [2, D, D]),
            }
            self.yT = self.dout("yT", [D, NTOK]) if self.on("peer1") else None
            self.UpD = self.scratch("UpD", [64, 128, 1024], BF16)
            self.HpD = self.scratch("HpD", [64, 128, 1024], BF16)
            self.X1 = self.scratch("X1", [D, NTOK])
            self.XL1 = self.scratch("XL1", [D, NTOK])
            self.RGr = self.scratch("RGr", [D, NTOK])
            self.RGg = self.scratch("RGg", [D, NTOK], BF16)
            self.RGy = self.scratch("RGy", [D, NTOK], BF16)
            self.XB = self.scratch("XB", [D, NTOK], BF16)
            self.QT = self.scratch("QT", [2 * D, NTOK], BF16)
            self.PEo = self.scratch("PEo", [D, NTOK])
            self.Ub = self.scratch("Ub", [2, D, 16384], BF16)
            self.Vb = self.scratch("Vb", [2, 16384, D], BF16)
            self.ident = st.enter_context(self.sbuf("ident", [128, 128], F32))
            self.identb = st.enter_context(self.sbuf("identb", [128, 128], BF16))
            self.onesm = st.enter_context(self.sbuf("onesm", [128, 128], F32))
            self.lnp = st.enter_context(self.sbuf("lnp", [128, 4, 2, 8], F32))
            self.ps = [st.enter_context(nc.psum_tensor("ps%d" % i, [128, 512], F32)) for i in range(8)]
            t.dma(self.ident[:], self.c_ident, writes=["ident"])
            t.dma(self.onesm[:], self.c_onesm, writes=["onesm"])
            t.dma(self.lnp[:], self.ln_par, writes=["lnp"])
            t.op('dve', lambda e: e.tensor_copy(out=self.identb[:], in_=self.ident[:]), reads=["ident"], writes=["identb"])

            if self.on("tabcast"):
                self.phase_tabcast()
            if self.on("s5a"):
                self.phase_s5a()
            if self.on("s5b"):
                self.phase_s5b()
            if self.on("s5c"):
                self.phase_s5c()
            if self.on("peer0"):
                self.phase_peer(0, self.X1, self.XL1)
            if self.on("rg"):
                self.phase_rg()
            if self.on("peer1"):
                self.phase_peer(1, self.X1, self.yT)
            t.barrier()
            t.finish()
        return nc

    def load_w_bf16(self, ph, name, dram_ap, kt, ncols, eng_cast='pool'):
        nc, t = self.nc, self.t
        wb = ph.enter_context(self.sbuf(name, [128, kt, ncols], BF16))
        stg = ph.enter_context(self.sbuf(name + "_stg", [128, 2, 2048], F32))
        i = 0
        for k in range(kt):
            for c0 in range(0, ncols, 2048):
                cw = min(2048, ncols - c0)
                b = i % 2
                t.dma(stg[:, b, 0:cw], dram_ap[k * 128:(k + 1) * 128, c0:c0 + cw],
                      writes=[name + "_stg%d" % b])
                eng = ['pool', 'act'][i % 2] if eng_cast == 'mix' else eng_cast
                if eng == 'act':
                    t.op('act', lambda e, b=b, k=k, c0=c0, cw=cw: e.copy(out=wb[:, k, c0:c0 + cw], in_=stg[:, b, 0:cw]),
                         reads=[name + "_stg%d" % b], writes=[name])
                else:
                    t.op(eng, lambda e, b=b, k=k, c0=c0, cw=cw: e.tensor_copy(out=wb[:, k, c0:c0 + cw], in_=stg[:, b, 0:cw]),
                         reads=[name + "_stg%d" % b], writes=[name])
                i += 1
        return wb

    def ln_block(self, z, zkey, layer, which, out, outkey, N, tmp, pbank_a, pbank_b):
        t = self.t
        pm = self.ps[pbank_a]
        pv = self.ps[pbank_b]
        ka, kb = "ps%d" % pbank_a, "ps%d" % pbank_b
        for k in range(8):
            t.op('pe', lambda e, k=k: e.matmul(pm[:, 0:N], self.onesm[:], z[:, k, :], start=(k == 0), stop=(k == 7)),
                 reads=[zkey, "onesm"], writes=[ka], skip_self=True)
        zc, sq, sd = tmp['zc'], tmp['sq'], tmp['sd']
        t.op('dve', lambda e: e.tensor_tensor(out=zc[:], in0=z[:], in1=bc(pm[:, 0:N].unsqueeze(1), [128, 8, N]), op=ALU.subtract),
             reads=[zkey, ka], writes=[zc.name])
        t.op('act', lambda e: e.activation(out=sq[:], in_=zc[:], func=AF.Square), reads=[zc.name], writes=[sq.name])
        for k in range(8):
            t.op('pe', lambda e, k=k: e.matmul(pv[:, 0:N], self.onesm[:], sq[:, k, :], start=(k == 0), stop=(k == 7)),
                 reads=[sq.name, "onesm"], writes=[kb], skip_self=True)
        t.op('act', lambda e: e.activation(out=sd[:], in_=pv[:, 0:N], func=AF.Sqrt, bias=self.epsc[:, 0:1], scale=1.0),
             reads=[kb, "epsc"], writes=[sd.name])
        t.op('dve', lambda e: e.reciprocal(out=sd[:], in_=sd[:]), reads=[sd.name], writes=[sd.name])
        t.op('dve', lambda e: e.tensor_tensor(out=zc[:], in0=zc[:], in1=bc(sd[:].unsqueeze(1), [128, 8, N]), op=ALU.mult),
             reads=[zc.name, sd.name], writes=[zc.name])
        for k in range(8):
            t.op('act', lambda e, k=k: e.activation(out=out[:, k, :], in_=zc[:, k, :], func=AF.Identity,
                                                    bias=self.lnp[:, 2 * which + 1, layer, k:k + 1],
                                                    scale=self.lnp[:, 2 * which, layer, k:k + 1]),
                 reads=[zc.name, "lnp"], writes=[outkey])

    def mk_eps(self, ph):
        nc, t = self.nc, self.t
        self.epsc = ph.enter_context(self.sbuf("epsc", [128, 1], F32))
        t.op('dve', lambda e: e.memset(self.epsc[:], LN_EPS), writes=["epsc"])

    def phase_tabcast(self):
        t = self.t
        for l in range(2):
            for r0 in range(0, D, 128):
                for c0 in range(0, 16384, 2048):
                    t.dma(self.Ub[l, r0:r0 + 128, c0:c0 + 2048], self.peer_uT[l, r0:r0 + 128, c0:c0 + 2048],
                          writes=["Ub%d_%d_%d" % (l, r0 // 128, c0 // 2048)], eng='pool')
            for r0 in range(0, 16384, 256):
                t.dma(self.Vb[l, r0:r0 + 256, :], self.peer_v[l, r0:r0 + 256, :], writes=["Vb%d_%d" % (l, r0 // 256)], eng='pool')

    def phase_s5a(self):
        nc, t = self.nc, self.t
        with ExitStack() as ph:
            wb = self.load_w_bf16(ph, "s5win", self.w_s5_in, 8, D, eng_cast='mix')
            xf = ph.enter_context(self.sbuf("a_xf", [128, 2, 8, 512], F32))
            xb = ph.enter_context(self.sbuf("a_xb", [128, 8, 1024], BF16))
            U8 = ph.enter_context(self.sbuf("a_U8", [128, 64, 8, 16], BF16))
            Upt = ph.enter_context(self.sbuf("a_Upt", [128, 64, 128], BF16))
            xTv = self.xT.rearrange("(k p) n -> p k n", p=128)
            for tile in range(8):
                t0 = tile * 1024
                for h in range(2):
                    t.dma(xf[:, h, :, :], xTv[:, :, t0 + h * 512:t0 + (h + 1) * 512], writes=["a_xf%d" % h])
                    t.op('act' if h == 0 else 'pool',
                         (lambda e, h=h: e.copy(out=xb[:, :, h * 512:(h + 1) * 512], in_=xf[:, h, :, :])) if h == 0 else
                         (lambda e, h=h: e.tensor_copy(out=xb[:, :, h * 512:(h + 1) * 512], in_=xf[:, h, :, :])),
                         reads=["a_xf%d" % h], writes=["a_xb"])
                for s in range(8):
                    for fh in range(2):
                        bank = (s * 2 + fh) % 4
                        pk = "ps%d" % bank
                        for k in range(8):
                            t.op('pe', lambda e, k=k, s=s, fh=fh, bank=bank: e.matmul(
                                self.ps[bank][:], xb[:, k, s::8], wb[:, k, fh * 512:(fh + 1) * 512],
                                start=(k == 0), stop=(k == 7)),
                                reads=["a_xb", "s5win"], writes=[pk], skip_self=True)
                        if (s * 2 + fh) % 2 == 0:
                            t.op('act', lambda e, s=s, fh=fh, bank=bank: e.copy(out=U8[:, fh * 32:(fh + 1) * 32, s, :], in_=self.ps[bank][:].rearrange("p (g c) -> p g c", c=16)),
                                 reads=[pk], writes=["a_U8"])
                        else:
                            t.op('dve', lambda e, s=s, fh=fh, bank=bank: e.tensor_copy(out=U8[:, fh * 32:(fh + 1) * 32, s, :], in_=self.ps[bank][:].rearrange("p (g c) -> p g c", c=16)),
                                 reads=[pk], writes=["a_U8"])
                for gb in range(8):
                    bank = 4 + gb % 2
                    pk = "ps%d" % bank
                    pb = self.ps[bank][:].bitcast(BF16)
                    for gi in range(8):
                        g = gb * 8 + gi
                        t.op('pe', lambda e, g=g, gi=gi, pb=pb: e.transpose(pb[:, gi * 128:(gi + 1) * 128],
                                                                          U8[:, g, :, :].rearrange("p s c -> p (s c)"), self.identb[:]),
                             reads=["a_U8", "identb"], writes=[pk], skip_self=True)
                    if gb % 2 == 0:
                        t.op('dve', lambda e, gb=gb, pb=pb: e.tensor_copy(out=Upt[:, gb * 8:(gb + 1) * 8, :],
                                                                        in_=pb.rearrange("p (g c) -> p g c", g=8)),
                             reads=[pk], writes=["a_Upt"])
                    else:
                        t.op('act', lambda e, gb=gb, pb=pb: e.copy(out=Upt[:, gb * 8:(gb + 1) * 8, :],
                                                                 in_=pb.rearrange("p (g c) -> p g c", g=8)),
                             reads=[pk], writes=["a_Upt"])
                t.dma(self.UpD[:, :, tile * 128:(tile + 1) * 128].rearrange("g p c -> p g c"), Upt[:],
                      reads=["a_Upt"], writes=["UpD"])
            t.barrier()

    def cmul(self, eng, outr, outi, ar, ai, br, bi, tmp1, tmp2, rk, wk):
        t = self.t
        t.op(eng, lambda e: e.tensor_tensor(out=tmp1, in0=ar, in1=br, op=ALU.mult), reads=rk, writes=["cm_t1"])
        t.op(eng, lambda e: e.tensor_tensor(out=tmp2, in0=ai, in1=bi, op=ALU.mult), reads=rk, writes=["cm_t2"])
        t.op(eng, lambda e: e.tensor_tensor(out=outr, in0=tmp1, in1=tmp2, op=ALU.subtract), reads=["cm_t1", "cm_t2"] + rk, writes=wk)
        t.op(eng, lambda e: e.tensor_tensor(out=tmp1, in0=ar, in1=bi, op=ALU.mult), reads=rk + wk, writes=["cm_t1"])
        t.op(eng, lambda e: e.tensor_tensor(out=tmp2, in0=ai, in1=br, op=ALU.mult), reads=rk + wk, writes=["cm_t2"])
        t.op(eng, lambda e: e.tensor_tensor(out=outi, in0=tmp1, in1=tmp2, op=ALU.add), reads=["cm_t1", "cm_t2"] + rk, writes=wk)

    def phase_s5b(self):
        nc, t = self.nc, self.t
        with ExitStack() as ph:
            sb = lambda name, shape, dt=F32: ph.enter_context(self.sbuf(name, list(shape), dt))
            MATS = sb("b_mats", [128, 64, 5, 128], BF16)
            POW = sb("b_pow", [128, 2, 16, 64])
            PH = sb("b_ph", [128, 2, 10, 64])
            RHO = sb("b_rho", [128, 64])
            dpk = sb("b_dpk", [128, 64])
            segm = sb("b_segm", [128, 1024])
            t.dma(dpk[:], self.s5_dpk, writes=["b_dpk"])
            t.dma(segm[:], self.c_segmask, writes=["b_segm"])
            with ExitStack() as pg:
                sg = lambda name, shape, dt=F32: pg.enter_context(self.sbuf(name, list(shape), dt))
                lre = sg("g_lre", [128, 64]); lim = sg("g_lim", [128, 64]); lst = sg("g_lst", [128, 64])
                Bre = sg("g_bre", [128, 64, 16]); Bim = sg("g_bim", [128, 64, 16])
                Cre = sg("g_cre", [128, 64, 16]); Cim = sg("g_cim", [128, 64, 16])
                maskf = sg("g_maskf", [128, 128]); maskb = sg("g_maskb", [128, 128])
                for dst, src in [(lre, self.s5_lamre), (lim, self.s5_lamim), (lst, self.s5_lstep), (Bre, self.s5_bre),
                                 (Bim, self.s5_bim), (Cre, self.s5_ctre), (Cim, self.s5_ctim), (maskf, self.c_maskf),
                                 (maskb, self.c_maskb)]:
                    t.dma(dst[:], src, writes=[dst.name])
                S = {}
                for nm in ["step", "ang", "lrs", "mag", "magi", "kf", "r", "m1", "s1", "c1", "ar", "ai", "ari", "aii",
                           "den", "zr", "qr", "qi", "u1", "u2", "e8"]:
                    S[nm] = sg("g_" + nm, [128, 64])
                ki = sg("g_ki", [128, 64], I32)
                V = 'dve'

                def tt(out, a, b, op, rk, wk):
                    t.op(V, lambda e: e.tensor_tensor(out=out, in0=a, in1=b, op=op), reads=rk, writes=wk)

                def ts(out, a, s1, s2, op0, op1, rk, wk):
                    t.op(V, lambda e: e.tensor_scalar(out=out, in0=a, scalar1=s1, scalar2=s2, op0=op0, op1=op1), reads=rk, writes=wk)

                def act(out, a, func, rk, wk, scale=1.0):
                    t.op('act', lambda e: e.activation(out=out, in_=a, func=func, scale=scale), reads=rk, writes=wk)

                n = lambda k: S[k].name
                act(S["step"][:], lst[:], AF.Exp, [lst.name], [n("step")])
                tt(S["ang"][:], lim[:], S["step"][:], ALU.mult, [lim.name, n("step")], [n("ang")])
                tt(S["lrs"][:], lre[:], S["step"][:], ALU.mult, [lre.name, n("step")], [n("lrs")])
                act(S["mag"][:], S["lrs"][:], AF.Exp, [n("lrs")], [n("mag")])
                act(S["magi"][:], S["lrs"][:], AF.Exp, [n("lrs")], [n("magi")], scale=-1.0)
                act(S["e8"][:], S["lrs"][:], AF.Exp, [n("lrs")], [n("e8")], scale=-8.0)
                act(RHO[:], S["lrs"][:], AF.Exp, [n("lrs")], ["b_rho"], scale=8.0)

                def range_reduce(dst, src, shift):
                    ts(S["kf"][:], src, 1.0 / TWO_PI, shift / TWO_PI + 0.5, ALU.mult, ALU.add, [n("ang")], [n("kf")])
                    t.op(V, lambda e: e.tensor_copy(out=ki[:], in_=S["kf"][:]), reads=[n("kf")], writes=[ki.name])
                    t.op(V, lambda e: e.tensor_copy(out=S["kf"][:], in_=ki[:]), reads=[ki.name], writes=[n("kf")])
                    ts(S["kf"][:], S["kf"][:], -TWO_PI, shift, ALU.mult, ALU.add, [n("kf")], [n("kf")])
                    tt(dst, src, S["kf"][:], ALU.add, [n("ang"), n("kf")], [n("r")])
                    ts(S["m1"][:], dst, math.pi, -TWO_PI, ALU.is_gt, ALU.mult, [n("r")], [n("m1")])
                    tt(dst, dst, S["m1"][:], ALU.add, [n("r"), n("m1")], [n("r")])
                    ts(S["m1"][:], dst, -math.pi, TWO_PI, ALU.is_lt, ALU.mult, [n("r")], [n("m1")])
                    tt(dst, dst, S["m1"][:], ALU.add, [n("r"), n("m1")], [n("r")])
                    ts(dst, dst, math.pi, -math.pi, ALU.min, ALU.max, [n("r")], [n("r")])

                range_reduce(S["r"][:], S["ang"][:], 0.0)
                act(S["s1"][:], S["r"][:], AF.Sin, [n("r")], [n("s1")])
                range_reduce(S["r"][:], S["ang"][:], math.pi / 2)
                act(S["c1"][:], S["r"][:], AF.Sin, [n("r")], [n("c1")])
                tt(S["ar"][:], S["mag"][:], S["c1"][:], ALU.mult, [n("mag"), n("c1")], [n("ar")])
                tt(S["ai"][:], S["mag"][:], S["s1"][:], ALU.mult, [n("mag"), n("s1")], [n("ai")])
                tt(S["ari"][:], S["magi"][:], S["c1"][:], ALU.mult, [n("magi"), n("c1")], [n("ari")])
                tt(S["aii"][:], S["magi"][:], S["s1"][:], ALU.mult, [n("magi"), n("s1")], [n("aii")])
                ts(S["aii"][:], S["aii"][:], -1.0, None, ALU.mult, ALU.bypass, [n("aii")], [n("aii")])
                tt(S["den"][:], lre[:], lre[:], ALU.mult, [lre.name], [n("den")])
                tt(S["u1"][:], lim[:], lim[:], ALU.mult, [lim.name], [n("u1")])
                tt(S["den"][:], S["den"][:], S["u1"][:], ALU.add, [n("den"), n("u1")], [n("den")])
                t.op(V, lambda e: e.reciprocal(out=S["den"][:], in_=S["den"][:]), reads=[n("den")], writes=[n("den")])
                ts(S["zr"][:], S["ar"][:], -1.0, None, ALU.add, ALU.bypass, [n("ar")], [n("zr")])
                tt(S["u1"][:], S["zr"][:], lre[:], ALU.mult, [n("zr"), lre.name], [n("u1")])
                tt(S["u2"][:], S["ai"][:], lim[:], ALU.mult, [n("ai"), lim.name], [n("u2")])
                tt(S["u1"][:], S["u1"][:], S["u2"][:], ALU.add, [n("u1"), n("u2")], [n("u1")])
                tt(S["qr"][:], S["u1"][:], S["den"][:], ALU.mult, [n("u1"), n("den")], [n("qr")])
                tt(S["u1"][:], S["ai"][:], lre[:], ALU.mult, [n("ai"), lre.name], [n("u1")])
                tt(S["u2"][:], S["zr"][:], lim[:], ALU.mult, [n("zr"), lim.name], [n("u2")])
                tt(S["u1"][:], S["u1"][:], S["u2"][:], ALU.subtract, [n("u1"), n("u2")], [n("u1")])
                tt(S["qi"][:], S["u1"][:], S["den"][:], ALU.mult, [n("u1"), n("den")], [n("qi")])
                BBr = sg("g_bbr", [128, 64, 16]); BBi = sg("g_bbi", [128, 64, 16])
                T1 = sg("g_T1", [128, 1024]); T2 = sg("g_T2", [128, 1024])
                T1v = T1[:].rearrange("p (g c) -> p g c", c=16)
                T2v = T2[:].rearrange("p (g c) -> p g c", c=16)
                qrb = bc(S["qr"][:].unsqueeze(2), [128, 64, 16]); qib = bc(S["qi"][:].unsqueeze(2), [128, 64, 16])
                self.cmul(V, BBr[:], BBi[:], qrb, qib, Bre[:], Bim[:], T1v, T2v, [n("qr"), n("qi"), Bre.name, Bim.name], [BBr.name, BBi.name])
                t.op(V, lambda e: e.memset(POW[:, 0, 7, :], 1.0), writes=["b_pow"])
                t.op(V, lambda e: e.memset(POW[:, 1, 7, :], 0.0), reads=["b_pow"], writes=["b_pow"])
                for k in range(0, 8):
                    self.cmul(V, POW[:, 0, 8 + k, :], POW[:, 1, 8 + k, :], POW[:, 0, 7 + k, :], POW[:, 1, 7 + k, :],
                              S["ar"][:], S["ai"][:], T1[:, 0:64], T2[:, 0:64], ["b_pow", n("ar"), n("ai")], ["b_pow"])
                for k in range(0, 7):
                    self.cmul(V, POW[:, 0, 6 - k, :], POW[:, 1, 6 - k, :], POW[:, 0, 7 - k, :], POW[:, 1, 7 - k, :],
                              S["ari"][:], S["aii"][:], T1[:, 0:64], T2[:, 0:64], ["b_pow", n("ari"), n("aii")], ["b_pow"])
                tt(PH[:, 0, 0, :], POW[:, 0, 15, :], S["e8"][:], ALU.mult, ["b_pow", n("e8")], ["b_ph"])
                tt(PH[:, 1, 0, :], POW[:, 1, 15, :], S["e8"][:], ALU.mult, ["b_pow", n("e8"), "b_ph"], ["b_ph"])
                t.op(V, lambda e: e.tensor_scalar(out=PH[64:128, 1, 0, :], in0=PH[64:128, 1, 0, :], scalar1=-1.0, scalar2=None,
                                                  op0=ALU.mult, op1=ALU.bypass), reads=["b_ph"], writes=["b_ph"])
                for L in range(9):
                    self.cmul(V, PH[:, 0, L + 1, :], PH[:, 1, L + 1, :], PH[:, 0, L, :], PH[:, 1, L, :],
                              PH[:, 0, L, :], PH[:, 1, L, :], T1[:, 0:64], T2[:, 0:64], ["b_ph"], ["b_ph"])
                PA = sg("g_pa", [128, 4, 2, 8, 64])
                kmap = {0: (lambda j: 7 - j, lambda j: j), 1: (lambda j: j + 1, lambda j: 8 - j),
                        2: (lambda j: -j, lambda j: j), 3: (lambda j: j, lambda j: -j)}
                ci = 0
                for kind in range(4):
                    for j in range(8):
                        for half, (p0, p1) in enumerate([(0, 64), (64, 128)]):
                            kk = kmap[kind][half](j) + 7
                            eng = ['act', 'pool'][ci % 2]
                            ci += 1
                            if eng == 'act':
                                t.op('act', lambda e, kind=kind, j=j, p0=p0, p1=p1, kk=kk: e.copy(out=PA[p0:p1, kind, :, j, :], in_=POW[p0:p1, :, kk, :]),
                                     reads=["b_pow"], writes=["g_pa%d_%d_%d" % (kind, j, half)])
                            else:
                                t.op('pool', lambda e, kind=kind, j=j, p0=p0, p1=p1, kk=kk: e.tensor_copy(out=PA[p0:p1, kind, :, j, :], in_=POW[p0:p1, :, kk, :]),
                                     reads=["b_pow"], writes=["g_pa%d_%d_%d" % (kind, j, half)])
                pa_keys = ["g_pa%d_%d_%d" % (kind, j, half) for kind in range(4) for j in range(8) for half in range(2)]
                TB = sg("g_tb", [128, 4, 2, 1024])
                Ysn = sg("g_ysn", [128, 1024])
                for gb in range(8):
                    g0 = gb * 8
                    for kind in range(4):
                        src_r, src_i = (BBr, BBi) if kind in (0, 2) else (Cre, Cim)
                        par = bc(PA[:, kind, 0, :, g0:g0 + 8].rearrange("p j g -> p g j").unsqueeze(3), [128, 8, 8, 16])
                        pai = bc(PA[:, kind, 1, :, g0:g0 + 8].rearrange("p j g -> p g j").unsqueeze(3), [128, 8, 8, 16])
                        br = bc(src_r[:, g0:g0 + 8, :].unsqueeze(2), [128, 8, 8, 16])
                        bi = bc(src_i[:, g0:g0 + 8, :].unsqueeze(2), [128, 8, 8, 16])
                        outr = TB[:, kind, 0, :].rearrange("p (g j c) -> p g j c", g=8, j=8)
                        outi = TB[:, kind, 1, :].rearrange("p (g j c) -> p g j c", g=8, j=8)
                        t1 = T1[:].rearrange("p (g j c) -> p g j c", g=8, j=8)
                        t2 = T2[:].rearrange("p (g j c) -> p g j c", g=8, j=8)
                        self.cmul(V, outr, outi, par, pai, br, bi, t1, t2, pa_keys + [src_r.name, src_i.name], ["g_tb%d" % kind])
                    t.op(V, lambda e: e.tensor_scalar(out=Ysn[:], in0=TB[:, 2, 1, :], scalar1=-1.0, scalar2=None, op0=ALU.mult, op1=ALU.bypass),
                         reads=["g_tb2"], writes=["g_ysn"])
                    for gi in range(8):
                        g = g0 + gi
                        sl = slice(gi * 128, (gi + 1) * 128)
                        for ri in range(2):
                            bank = 4 + ri
                            t.op('pe', lambda e, ri=ri, sl=sl, bank=bank: e.transpose(self.ps[bank][:, 0:128], TB[:, 0, ri, sl], self.ident[:]),
                                 reads=["g_tb0", "ident"], writes=["ps%d" % bank], skip_self=True)
                            t.op('act', lambda e, ri=ri, g=g, bank=bank: e.copy(out=MATS[:, g, ri, :], in_=self.ps[bank][:, 0:128]),
                                 reads=["ps%d" % bank], writes=["b_mats"])
                        t.op('pool', lambda e, g=g, sl=sl: e.tensor_copy(out=MATS[:, g, 3, :], in_=TB[:, 1, 0, sl]), reads=["g_tb1"], writes=["b_mats"])
                        t.op('pool', lambda e, g=g, sl=sl: e.tensor_scalar(out=MATS[:, g, 4, :], in0=TB[:, 1, 1, sl], scalar1=-1.0, scalar2=None,
                                                                          op0=ALU.mult, op1=ALU.bypass), reads=["g_tb1"], writes=["b_mats"])
                        for half, (p0, p1) in enumerate([(0, 64), (64, 128)]):
                            bank = 6 + half
                            t.op('pe', lambda e, p0=p0, p1=p1, sl=sl, bank=bank: e.matmul(self.ps[bank][:, 0:128], TB[p0:p1, 2, 0, sl], TB[p0:p1, 3, 0, sl], start=True, stop=False),
                                 reads=["g_tb2", "g_tb3"], writes=["ps%d" % bank], skip_self=True)
                            t.op('pe', lambda e, p0=p0, p1=p1, sl=sl, bank=bank: e.matmul(self.ps[bank][:, 0:128], Ysn[p0:p1, sl], TB[p0:p1, 3, 1, sl], start=False, stop=True),
                                 reads=["g_ysn", "g_tb3"], writes=["ps%d" % bank], skip_self=True)
                        t.op(V, lambda e: e.tensor_tensor(out=T1[:, 0:128], in0=self.ps[6][:, 0:128], in1=maskf[:], op=ALU.mult),
                             reads=["ps6", maskf.name], writes=["cm_t1"])
                        t.op(V, lambda e: e.tensor_tensor(out=T2[:, 0:128], in0=self.ps[7][:, 0:128], in1=maskb[:], op=ALU.mult),
                             reads=["ps7", maskb.name], writes=["cm_t2"])
                        t.op(V, lambda e, g=g: e.tensor_tensor(out=MATS[:, g, 2, :], in0=T1[:, 0:128], in1=T2[:, 0:128], op=ALU.add),
                             reads=["cm_t1", "cm_t2"], writes=["b_mats"])
                t.barrier()
            Up = sb("b_up", [128, 2, 1024], BF16)
            TAB = sb("b_tab", [128, 2, 1024])
            RM = sb("b_rm", [128, 1024])
            W = [sb("b_w%d" % i, [128, 1024]) for i in range(6)]
            Gs = [sb("b_g%d" % i, [128, 1024]) for i in range(2)]
            Hu = [sb("b_h%d" % i, [128, 1024]) for i in range(2)]
            Hs = sb("b_hs", [128, 2, 1024], BF16)
            yd = sb("b_yd", [128, 1024])
            hp = sb("b_hp", [128, 2, 1024], BF16)
            t.op('pool', lambda e: e.memset(Hs[:], 0.0), writes=["b_hs"])
            V = 'dve'
            for g in range(64):
                ub = g % 2
                uk = "b_up%d" % ub
                t.dma(Up[:, ub, :], self.UpD[g, :, :], reads=["UpD"], writes=[uk])
                t.op('pool', lambda e, g=g: e.tensor_scalar(out=RM[:], in0=segm[:], scalar1=RHO[:, g:g + 1], scalar2=None, op0=ALU.mult, op1=ALU.bypass),
                     reads=["b_segm", "b_rho"], writes=["b_rm"])
                t.op(V, lambda e: e.memset(TAB[:, 0, 0:1], 1.0), writes=["b_tab"])
                t.op(V, lambda e: e.memset(TAB[:, 1, 0:1], 0.0), reads=["b_tab"], writes=["b_tab"])
                for L in range(9):
                    n0 = 1 << L
                    cr = PH[:, 0, L, g:g + 1]
                    ci_ = PH[:, 1, L, g:g + 1]
                    src_r = TAB[:, 0, 0:n0]; src_i = TAB[:, 1, 0:n0]
                    dst_r = TAB[:, 0, n0:2 * n0]; dst_i = TAB[:, 1, n0:2 * n0]
                    t.op(V, lambda e, src_i=src_i, ci_=ci_, n0=n0: e.tensor_scalar(out=W[0][:, 0:n0], in0=src_i, scalar1=ci_, scalar2=None, op0=ALU.mult, op1=ALU.bypass),
                         reads=["b_tab", "b_ph"], writes=["b_w0"])
                    t.op(V, lambda e, src_r=src_r, cr=cr, n0=n0, dst_r=dst_r: e.scalar_tensor_tensor(out=dst_r, in0=src_r, scalar=cr, in1=W[0][:, 0:n0], op0=ALU.mult, op1=ALU.subtract),
                         reads=["b_tab", "b_ph", "b_w0"], writes=["b_tab"])
                    t.op(V, lambda e, src_r=src_r, ci_=ci_, n0=n0: e.tensor_scalar(out=W[1][:, 0:n0], in0=src_r, scalar1=ci_, scalar2=None, op0=ALU.mult, op1=ALU.bypass),
                         reads=["b_tab", "b_ph"], writes=["b_w1"])
                    t.op(V, lambda e, src_i=src_i, cr=cr, n0=n0, dst_i=dst_i: e.scalar_tensor_tensor(out=dst_i, in0=src_i, scalar=cr, in1=W[1][:, 0:n0], op0=ALU.mult, op1=ALU.add),
                         reads=["b_tab", "b_ph", "b_w1"], writes=["b_tab"])
                for ri in range(2):
                    t.op('pool', lambda e, ri=ri: e.tensor_copy(out=TAB[:, ri, 512:768], in_=TAB[:, ri, 0:256]), reads=["b_tab"], writes=["b_tab"])
                    t.op('pool', lambda e, ri=ri: e.tensor_copy(out=TAB[:, ri, 768:1024], in_=TAB[:, ri, 0:256]), reads=["b_tab"], writes=["b_tab"])
                for ri in range(2):
                    for h in range(2):
                        bank = ri * 2 + h
                        t.op('pe', lambda e, ri=ri, h=h, bank=bank, g=g, ub=ub: e.matmul(self.ps[bank][:], MATS[:, g, ri, :], Up[:, ub, h * 512:(h + 1) * 512], start=True, stop=True),
                             reads=["b_mats", uk], writes=["ps%d" % bank], skip_self=True)
                cosT = TAB[:, 0, :]; sinT = TAB[:, 1, :]
                for h in range(2):
                    cs = slice(h * 512, (h + 1) * 512)
                    t.op(V, lambda e, h=h, cs=cs: e.tensor_tensor(out=W[0][:, cs], in0=self.ps[h][:], in1=cosT[:, cs], op=ALU.mult), reads=["ps%d" % h, "b_tab"], writes=["b_w0"])
                    t.op(V, lambda e, h=h, cs=cs: e.tensor_tensor(out=W[1][:, cs], in0=self.ps[2 + h][:], in1=sinT[:, cs], op=ALU.mult), reads=["ps%d" % (2 + h), "b_tab"], writes=["b_w1"])
                    t.op(V, lambda e, h=h, cs=cs: e.tensor_tensor(out=W[2][:, cs], in0=self.ps[2 + h][:], in1=cosT[:, cs], op=ALU.mult), reads=["ps%d" % (2 + h), "b_tab"], writes=["b_w2"])
                    t.op(V, lambda e, h=h, cs=cs: e.tensor_tensor(out=W[3][:, cs], in0=self.ps[h][:], in1=sinT[:, cs], op=ALU.mult), reads=["ps%d" % h, "b_tab"], writes=["b_w3"])
                t.op('pool', lambda e: e.tensor_tensor(out=W[4][:], in0=W[0][:], in1=W[1][:], op=ALU.add), reads=["b_w0", "b_w1"], writes=["b_w4"])
                t.op('pool', lambda e: e.tensor_tensor(out=W[5][:], in0=W[2][:], in1=W[3][:], op=ALU.subtract), reads=["b_w2", "b_w3"], writes=["b_w5"])
                for ri in range(2):
                    src = W[4 + ri]
                    t.op(V, lambda e, ri=ri, src=src: e.tensor_tensor_scan(out=Gs[ri][0:64, :], data0=RM[0:64, :], data1=src[0:64, :], initial=0.0, op0=ALU.mult, op1=ALU.add),
                         reads=["b_rm", src.name], writes=["b_g%d_f" % ri])
                    t.op(V, lambda e, ri=ri, src=src: e.tensor_tensor_scan(out=Gs[ri][64:128, ::-1], data0=RM[64:128, ::-1], data1=src[64:128, ::-1], initial=0.0, op0=ALU.mult, op1=ALU.add),
                         reads=["b_rm", src.name], writes=["b_g%d_b" % ri])
                gk = ["b_g0_f", "b_g0_b", "b_g1_f", "b_g1_b"]
                t.op(V, lambda e: e.tensor_tensor(out=W[0][:], in0=Gs[0][:], in1=cosT, op=ALU.mult), reads=gk + ["b_tab"], writes=["b_w0"])
                t.op('pool', lambda e: e.tensor_tensor(out=W[1][:], in0=Gs[1][:], in1=sinT, op=ALU.mult), reads=gk + ["b_tab"], writes=["b_w1"])
                t.op(V, lambda e: e.tensor_tensor(out=W[2][:], in0=Gs[1][:], in1=cosT, op=ALU.mult), reads=gk + ["b_tab"], writes=["b_w2"])
                t.op('pool', lambda e: e.tensor_tensor(out=W[3][:], in0=Gs[0][:], in1=sinT, op=ALU.mult), reads=gk + ["b_tab"], writes=["b_w3"])
                t.op(V, lambda e: e.tensor_tensor(out=Hu[0][:], in0=W[0][:], in1=W[1][:], op=ALU.subtract), reads=["b_w0", "b_w1"], writes=["b_h0"])
                t.op('pool', lambda e: e.tensor_tensor(out=Hu[1][:], in0=W[2][:], in1=W[3][:], op=ALU.add), reads=["b_w2", "b_w3"], writes=["b_h1"])
                for ri in range(2):
                    t.op(V, lambda e, ri=ri: e.tensor_tensor(out=Hs[0:64, ri, 1:1024], in0=Hu[ri][0:64, 0:1023], in1=segm[0:64, 1:1024], op=ALU.mult),
                         reads=["b_h%d" % ri, "b_segm"], writes=["b_hs"])
                    t.op('pool', lambda e, ri=ri: e.tensor_tensor(out=Hs[64:128, ri, 0:1023], in0=Hu[ri][64:128, 1:1024], in1=segm[64:128, 0:1023], op=ALU.mult),
                         reads=["b_h%d" % ri, "b_segm"], writes=["b_hs"])
                for h in range(2):
                    bank = 4 + h
                    cs = slice(h * 512, (h + 1) * 512)
                    t.op('pe', lambda e, g=g, ub=ub, cs=cs, bank=bank: e.matmul(self.ps[bank][:], MATS[:, g, 2, :], Up[:, ub, cs], start=True, stop=False),
                         reads=["b_mats", uk], writes=["ps%d" % bank], skip_self=True)
                    t.op('pe', lambda e, g=g, cs=cs, bank=bank: e.matmul(self.ps[bank][:], MATS[:, g, 3, :], Hs[:, 0, cs], start=False, stop=False),
                         reads=["b_mats", "b_hs"], writes=["ps%d" % bank], skip_self=True)
                    t.op('pe', lambda e, g=g, cs=cs, bank=bank: e.matmul(self.ps[bank][:], MATS[:, g, 4, :], Hs[:, 1, cs], start=False, stop=True),
                         reads=["b_mats", "b_hs"], writes=["ps%d" % bank], skip_self=True)
                    t.op(V, lambda e, g=g, ub=ub, cs=cs, bank=bank: e.scalar_tensor_tensor(out=yd[:, cs], in0=Up[:, ub, cs], scalar=dpk[:, g:g + 1], in1=self.ps[bank][:],
                                                                                         op0=ALU.mult, op1=ALU.add),
                         reads=[uk, "b_dpk", "ps%d" % bank], writes=["b_yd"])
                t.op('act', lambda e, ub=ub: e.activation(out=hp[:, ub, :], in_=yd[:], func=AF.Gelu_apprx_tanh), reads=["b_yd"], writes=["b_hp%d" % ub])
                t.dma(self.HpD[g, :, :], hp[:, ub, :], reads=["b_hp%d" % ub], writes=["HpD"])
            t.barrier()

    def phase_s5c(self):
        nc, t = self.nc, self.t
        with ExitStack() as ph:
            sb = lambda name, shape, dt=F32: ph.enter_context(self.sbuf(name, list(shape), dt))
            self.mk_eps(ph)
            wg = self.load_w_bf16(ph, "s5wglu", self.w_s5_glu, 8, 2 * D, eng_cast='mix')
            hpt = sb("c_hpt", [128, 64, 128], BF16)
            H8 = sb("c_H8", [128, 8, 1024], BF16)
            hT = sb("c_hT", [128, 8, 1024], BF16)
            xf = sb("c_xf", [128, 8, 512])
            z = sb("c_z", [128, 8, 512])
            sg_ = sb("c_sg", [128, 512])
            tmp = {'zc': sb("c_zc", [128, 8, 512]), 'sq': sb("c_sq", [128, 8, 512]), 'sd': sb("c_sd", [128, 512])}
            xo = sb("c_xo", [128, 8, 512])
            xTv = self.xT.rearrange("(k p) n -> p k n", p=128)
            X1v = self.X1.rearrange("(k p) n -> p k n", p=128)
            for tile in range(8):
                t.dma(hpt[:], self.HpD[:, :, tile * 128:(tile + 1) * 128].rearrange("g p c -> p g c"), reads=["HpD"], writes=["c_hpt"])
                for gb in range(8):
                    bank = gb % 2
                    pk = "ps%d" % bank
                    pb = self.ps[bank][:].bitcast(BF16)
                    for gi in range(8):
                        g = gb * 8 + gi
                        t.op('pe', lambda e, g=g, gi=gi, pb=pb: e.transpose(pb[:, gi * 128:(gi + 1) * 128], hpt[:, g, :], self.identb[:]),
                             reads=["c_hpt", "identb"], writes=[pk], skip_self=True)
                    src = pb.rearrange("p (g t c) -> p g t c", g=8, t=8)
                    dst = H8[:, :, gb * 128:(gb + 1) * 128].rearrange("p t (g c) -> p g t c", g=8)
                    if gb % 2 == 0:
                        t.op('dve', lambda e, src=src, dst=dst: e.tensor_copy(out=dst, in_=src), reads=[pk], writes=["c_H8"])
                    else:
                        t.op('act', lambda e, src=src, dst=dst: e.copy(out=dst, in_=src), reads=[pk], writes=["c_H8"])
                for k in range(8):
                    bank = 2 + k % 2
                    pk = "ps%d" % bank
                    pb = self.ps[bank][:].bitcast(BF16)
                    for tt_ in range(8):
                        t.op('pe', lambda e, k=k, tt_=tt_, pb=pb: e.transpose(pb[:, tt_ * 128:(tt_ + 1) * 128], H8[:, tt_, k * 128:(k + 1) * 128], self.identb[:]),
                             reads=["c_H8", "identb"], writes=[pk], skip_self=True)
                    src = pb.rearrange("p (t c) -> p t c", t=8)
                    dst = hT[:, k, :].rearrange("p (c t) -> p t c", t=8)
                    if k % 2 == 0:
                        t.op('dve', lambda e, src=src, dst=dst: e.tensor_copy(out=dst, in_=src), reads=[pk], writes=["c_hT"])
                    else:
                        t.op('act', lambda e, src=src, dst=dst: e.copy(out=dst, in_=src), reads=[pk], writes=["c_hT"])
                for th in range(2):
                    tok0 = tile * 1024 + th * 512
                    t.dma(xf[:], xTv[:, :, tok0:tok0 + 512], writes=["c_xf"])
                    for fo in range(8):
                        bv, bg = 4 + (fo % 2) * 2, 5 + (fo % 2) * 2
                        for k in range(8):
                            t.op('pe', lambda e, k=k, fo=fo, th=th, bv=bv: e.matmul(self.ps[bv][:], wg[:, k, fo * 128:(fo + 1) * 128], hT[:, k, th * 512:(th + 1) * 512],
                                                                              start=(k == 0), stop=(k == 7)), reads=["s5wglu", "c_hT"], writes=["ps%d" % bv], skip_self=True)
                        for k in range(8):
                            t.op('pe', lambda e, k=k, fo=fo, th=th, bg=bg: e.matmul(self.ps[bg][:], wg[:, k, D + fo * 128:D + (fo + 1) * 128], hT[:, k, th * 512:(th + 1) * 512],
                                                                              start=(k == 0), stop=(k == 7)), reads=["s5wglu", "c_hT"], writes=["ps%d" % bg], skip_self=True)
                        t.op('act', lambda e, bg=bg: e.activation(out=sg_[:], in_=self.ps[bg][:], func=AF.Sigmoid), reads=["ps%d" % bg], writes=["c_sg"])
                        t.op('dve', lambda e, bv=bv: e.tensor_tensor(out=sg_[:], in0=self.ps[bv][:], in1=sg_[:], op=ALU.mult), reads=["ps%d" % bv, "c_sg"], writes=["c_sg"])
                        t.op('dve', lambda e, fo=fo: e.scalar_tensor_tensor(out=z[:, fo, :], in0=xf[:, fo, :], scalar=ALPHA, in1=sg_[:], op0=ALU.mult, op1=ALU.add),
                             reads=["c_xf", "c_sg"], writes=["c_z"])
                    self.ln_block(z, "c_z", 0, 0, xo, "c_xo", 512, tmp, 0, 1)
                    t.dma(X1v[:, :, tok0:tok0 + 512], xo[:], reads=["c_xo"], writes=["X1"])
            t.barrier()

    def phase_peer(self, layer, xin, xout):
        sub = getattr(self, "peer_sub", ("p1", "p2", "p3"))
        if "p1" in sub:
            self.peer_p1(layer, xin)
        if "p2" in sub:
            self.peer_p2(layer)
        if "p3" in sub:
            self.peer_p3(layer, xin, xout)

    def peer_p1(self, layer, xin):
        nc, t = self.nc, self.t
        with ExitStack() as ph:
            sb = lambda name, shape, dt=F32: ph.enter_context(self.sbuf(name, list(shape), dt))
            wq = self.load_w_bf16(ph, "p1_wq", self.w_peer_q[layer], 8, 2 * D, eng_cast='mix')
            xf = sb("p1_xf", [128, 2, 8, 512])
            xb = sb("p1_xb", [128, 2, 8, 512], BF16)
            qb = sb("p1_qb", [128, 2, 16, 512], BF16)
            xv = xin.rearrange("(k p) n -> p k n", p=128)
            XBv = self.XB.rearrange("(k p) n -> p k n", p=128)
            QTv = self.QT.rearrange("(k p) n -> p k n", p=128)
            for blk in range(16):
                b = blk % 2
                tok0 = blk * 512
                t.dma(xf[:, b], xv[:, :, tok0:tok0 + 512], reads=["X1"], writes=["p1_xf%d" % b])
                t.op('pool', lambda e, b=b: e.tensor_copy(out=xb[:, b], in_=xf[:, b]), reads=["p1_xf%d" % b], writes=["p1_xb%d" % b])
                t.dma(XBv[:, :, tok0:tok0 + 512], xb[:, b], reads=["p1_xb%d" % b], writes=["XB_%d" % blk])
                for fo in range(16):
                    bank = fo % 4
                    for k in range(8):
                        t.op('pe', lambda e, k=k, fo=fo, b=b, bank=bank: e.matmul(self.ps[bank][:], wq[:, k, fo * 128:(fo + 1) * 128], xb[:, b, k, :],
                                                                             start=(k == 0), stop=(k == 7)),
                             reads=["p1_wq", "p1_xb%d" % b], writes=["ps%d" % bank], skip_self=True)
                    if fo % 2 == 0:
                        t.op('act', lambda e, fo=fo, b=b, bank=bank: e.copy(out=qb[:, b, fo, :], in_=self.ps[bank][:]), reads=["ps%d" % bank], writes=["p1_qb%d" % b])
                    else:
                        t.op('dve', lambda e, fo=fo, b=b, bank=bank: e.tensor_copy(out=qb[:, b, fo, :], in_=self.ps[bank][:]), reads=["ps%d" % bank], writes=["p1_qb%d" % b])
                t.dma(QTv[:, :, tok0:tok0 + 512], qb[:, b], reads=["p1_qb%d" % b], writes=["QT_%d" % blk])
            t.barrier()

    def peer_p2(self, layer):
        nc, t = self.nc, self.t
        NB = self.peer_nblk if hasattr(self, "peer_nblk") else 32
        with ExitStack() as ph:
            sb = lambda name, shape, dt=F32: ph.enter_context(self.sbuf(name, list(shape), dt))
            skf = sb("p2_skf", [128, 2, 128])
            skb = sb("p2_skb", [128, 2, 128], BF16)
            t.dma(skf[:], self.peer_skT[layer].rearrange("c d n -> d c n"), writes=["p2_skf"])
            t.op('dve', lambda e: e.tensor_copy(out=skb[:], in_=skf[:]), reads=["p2_skf"], writes=["p2_skb"])
            xb = sb("p2_xb", [128, 8, 256], BF16)
            qT = sb("p2_qT", [128, 16, 256], BF16)
            sc = sb("p2_sc", [128, 16, 128])
            sc2 = sb("p2_sc2", [128, 2, 128])
            cs = sb("p2_cs", [128, 8, 256])
            cs2 = sb("p2_cs2", [128, 2, 256])
            sv = sb("p2_sv", [128, 16, 16])
            ts_ = sb("p2_ts", [128, 8, 16])
            ex = sb("p2_ex", [128, 8, 16])
            st8 = sb("p2_st8", [128, 4, 8])
            TM = sb("p2_TM", [128, 3, 128])
            SM = sb("p2_SM", [128, 3, 256])
            QR = sb("p2_qr", [128, 2, 2, 16, 128], BF16)
            Pt = sb("p2_P", [128, 8, 128], BF16)
            Et = sb("p2_E", [128, 8, 128])
            Qt = sb("p2_Q", [128, 8, 128], BF16)
            Gs = sb("p2_Gs", [128, 256, 128], BF16)
            UTs = sb("p2_UT", [128, 2, 8, 512], BF16)
            Vs = sb("p2_V", [128, 2, 4, 1024], BF16)
            ga = sb("p2_ga", [128, 2, 256])
            Hh = sb("p2_H", [128, 2, 256], BF16)
            otok = sb("p2_otok", [128, 2, 1024])
            peT = sb("p2_peT", [128, 8, 256])
            XBv = self.XB.rearrange("(k p) n -> p k n", p=128)
            QTv = self.QT.rearrange("(k p) n -> p k n", p=128)
            PEv = self.PEo.rearrange("(k p) n -> p k n", p=128)
            Ubv = self.Ub[layer].rearrange("(k p) e -> p k e", p=128)
            Vbv = self.Vb[layer].rearrange("(i p) f -> p i f", p=128)
            NEG = -1.0e30
            import os as _os2
            TBK = int(_os2.environ.get("TBK", "7"))
            for blk in range(NB):
                tok0 = blk * 256
                b512 = tok0 // 512
                t.dma(xb[:], XBv[:, :, tok0:tok0 + 256], reads=["XB_%d" % b512], writes=["p2_xb"])
                t.dma(qT[:], QTv[:, :, tok0:tok0 + 256], reads=["QT_%d" % b512], writes=["p2_qT"])
                for st in range(2):
                    tsl = slice(st * 128, (st + 1) * 128)
                    for hc in range(16):
                        bank = hc // 4
                        t.op('pe', lambda e, hc=hc, bank=bank, tsl=tsl: e.matmul(self.ps[bank][:, (hc % 4) * 128:(hc % 4 + 1) * 128], qT[:, hc, tsl], skb[:, hc % 2, :],
                                                                              start=True, stop=True),
                             reads=["p2_qT", "p2_skb"], writes=["ps%d" % bank], skip_self=True)
                    for bank in range(4):
                        t.op('act', lambda e, bank=bank: e.copy(out=sc[:, bank * 4:(bank + 1) * 4, :], in_=self.ps[bank][:].rearrange("p (a n) -> p a n", a=4)),
                             reads=["ps%d" % bank], writes=["p2_sc%d" % bank])
                    for hc in range(16):
                        kk = "p2_sc%d" % (hc // 4)
                        t.op('dve', lambda e, hc=hc: e.max(out=sv[:, hc, 0:8], in_=sc[:, hc, :]), reads=[kk], writes=["p2_sv%d" % hc])
                        t.op('dve', lambda e, hc=hc: e.match_replace(out=sc2[:, hc % 2, :], in_to_replace=sv[:, hc, 0:8], in_values=sc[:, hc, :], imm_value=NEG),
                             reads=[kk, "p2_sv%d" % hc], writes=["p2_sc2_%d" % (hc % 2)])
                        t.op('dve', lambda e, hc=hc: e.max(out=sv[:, hc, 8:16], in_=sc2[:, hc % 2, :]), reads=["p2_sc2_%d" % (hc % 2)], writes=["p2_sv%d" % hc])
                    svk = ["p2_sv%d" % hc for hc in range(16)]
                    sv4 = sv[:].rearrange("p (h c) a -> p h c a", c=2)
                    t.op('dve', lambda e: e.tensor_tensor(out=cs[:].rearrange("p h (a b) -> p h a b", a=16),
                                                          in0=bc(sv4[:, :, 0, :].unsqueeze(3), [128, 8, 16, 16]),
                                                          in1=bc(sv4[:, :, 1, :].unsqueeze(2), [128, 8, 16, 16]), op=ALU.add),
                         reads=svk, writes=["p2_cs"])
                    for h in range(8):
                        t.op('dve', lambda e, h=h: e.max(out=ts_[:, h, 0:8], in_=cs[:, h, :]), reads=["p2_cs"], writes=["p2_ts%d" % h])
                        t.op('dve', lambda e, h=h: e.match_replace(out=cs2[:, h % 2, :], in_to_replace=ts_[:, h, 0:8], in_values=cs[:, h, :], imm_value=NEG),
                             reads=["p2_cs", "p2_ts%d" % h], writes=["p2_cs2_%d" % (h % 2)])
                        t.op('dve', lambda e, h=h: e.max(out=ts_[:, h, 8:16], in_=cs2[:, h % 2, :]), reads=["p2_cs2_%d" % (h % 2)], writes=["p2_ts%d" % h])
                    tsk = ["p2_ts%d" % h for h in range(8)]
                    t.op('dve', lambda e: e.tensor_tensor(out=ex[:], in0=ts_[:], in1=bc(ts_[:, :, 0:1], [128, 8, 16]), op=ALU.subtract), reads=tsk, writes=["p2_ex"])
                    t.op('act', lambda e: e.activation(out=ex[:], in_=ex[:], func=AF.Exp), reads=["p2_ex"], writes=["p2_ex"])
                    t.op('dve', lambda e: e.tensor_reduce(out=st8[:, 0, :], in_=ex[:], axis=mybir.AxisListType.X, op=ALU.add), reads=["p2_ex"], writes=["p2_st8"])
                    t.op('act', lambda e: e.activation(out=st8[:, 1, :], in_=st8[:, 0, :], func=AF.Ln), reads=["p2_st8"], writes=["p2_st8"])
                    t.op('dve', lambda e: e.tensor_tensor(out=st8[:, 2, :], in0=st8[:, 1, :], in1=ts_[:, :, 0], op=ALU.add), reads=["p2_st8"] + tsk, writes=["p2_st8"])
                    TMv = TM[:].rearrange("p j (h a) -> p j h a", h=8)
                    t.op('pool', lambda e: e.tensor_copy(out=TMv[:, 0], in_=sv4[:, :, 0, :]), reads=svk, writes=["p2_TM0"])
                    t.op('dve', lambda e: e.scalar_tensor_tensor(out=st8[:, 3, :], in0=ts_[:, :, 15], scalar=-1.0e-5, in1=st8[:, 2, :], op0=ALU.add, op1=ALU.subtract),
                         reads=tsk + ["p2_st8"], writes=["p2_st8"])
                    t.op('act', lambda e: e.activation(out=st8[:, 3, :], in_=st8[:, 3, :], func=AF.Exp), reads=["p2_st8"], writes=["p2_st8"])
                    t.op('dve', lambda e: e.tensor_copy(out=TMv[:, 1], in_=bc(st8[:, 3, :].unsqueeze(2), [128, 8, 16])), reads=["p2_st8"], writes=["p2_TM1"])
                    t.op('dve', lambda e: e.tensor_tensor(out=TMv[:, 2], in0=sv4[:, :, 0, :], in1=bc(st8[:, 2, :].unsqueeze(2), [128, 8, 16]), op=ALU.subtract),
                         reads=svk + ["p2_st8"], writes=["p2_TM2"])
                    for j in range(3):
                        t.op('pe', lambda e, j=j: e.transpose(self.ps[TBK][:, j * 128:(j + 1) * 128], TM[:, j, :], self.ident[:]),
                             reads=["p2_TM%d" % j, "ident"], writes=["ps%d" % TBK], skip_self=True)
                    t.op('act', lambda e, tsl=tsl: e.copy(out=SM[:, :, tsl], in_=self.ps[TBK][:, 0:384].rearrange("p (j n) -> p j n", j=3)),
                         reads=["ps%d" % TBK], writes=["p2_SM%d" % st])
                import os as _os
                _old = _os.environ.get("BANKMAP") == "old"
                B0 = (lambda pg: 5) if _old else (lambda pg: 4 + pg)
                B1 = (lambda pg: 6) if _old else (lambda pg: 6 + pg)
                GB = (lambda pg: [7, 4][pg]) if _old else (lambda pg: pg)
                TB = 4 if _old else 7

                def fill_qr(c16):
                    qb_ = c16 % 2
                    for c in range(2):
                        src = bc(qT[:, c::2, c16 * 16:(c16 + 1) * 16].rearrange("p h t -> p t h").unsqueeze(3), [128, 16, 8, 16])
                        dst = QR[:, qb_, c].rearrange("p t (h a) -> p t h a", h=8)
                        if c == 1 and _os.environ.get("QRENG") == "split":
                            t.op('act', lambda e, src=src, dst=dst: e.copy(out=dst, in_=src), reads=["p2_qT"], writes=["p2_qr%d_%d" % (qb_, c)])
                        else:
                            t.op('pool', lambda e, src=src, dst=dst: e.tensor_copy(out=dst, in_=src), reads=["p2_qT"], writes=["p2_qr%d_%d" % (qb_, c)])

                def stageA(g):
                    if g % 4 == 0:
                        fill_qr(g // 4)
                    qb_ = (g // 4) % 2
                    pg = g % 2
                    for s4 in range(4):
                        tl = (g % 4) * 4 + s4
                        t.op('pe', lambda e, tl=tl, s4=s4, qb_=qb_, pg=pg: e.matmul(self.ps[B0(pg)][:, s4 * 128:(s4 + 1) * 128], QR[:, qb_, 0, tl, :], skb[:, 0, :], start=True, stop=True),
                             reads=["p2_qr%d_0" % qb_, "p2_skb"], writes=["ps%d" % B0(pg)], skip_self=True)
                        t.op('pe', lambda e, tl=tl, s4=s4, qb_=qb_, pg=pg: e.matmul(self.ps[B1(pg)][:, s4 * 128:(s4 + 1) * 128], QR[:, qb_, 1, tl, :], skb[:, 1, :], start=True, stop=True),
                             reads=["p2_qr%d_1" % qb_, "p2_skb"], writes=["ps%d" % B1(pg)], skip_self=True)

                def stageB(g):
                    pg = g % 2
                    for s4 in range(4):
                        tt_ = g * 4 + s4
                        sl = pg * 4 + s4
                        smk = "p2_SM%d" % (tt_ // 128)
                        k0 = "ps%d" % B0(pg)
                        k1 = "ps%d" % B1(pg)
                        t.op('dve', lambda e, s4=s4, tt_=tt_, sl=sl, pg=pg: e.tensor_scalar(out=Pt[:, sl, :], in0=self.ps[B0(pg)][:, s4 * 128:(s4 + 1) * 128], scalar1=SM[:, 0, tt_:tt_ + 1], scalar2=None,
                                                                                       op0=ALU.is_equal, op1=ALU.bypass),
                             reads=[k0, smk], writes=["p2_P%d" % sl])
                        t.op('act', lambda e, s4=s4, tt_=tt_, sl=sl, pg=pg: e.activation(out=Et[:, sl, :], in_=self.ps[B1(pg)][:, s4 * 128:(s4 + 1) * 128], func=AF.Exp, bias=SM[:, 2, tt_:tt_ + 1], scale=1.0),
                             reads=[k1, smk], writes=["p2_E%d" % sl])
                        t.op('dve', lambda e, s4=s4, tt_=tt_, sl=sl, pg=pg: e.scalar_tensor_tensor(out=Qt[:, sl, :], in0=Et[:, sl, :], scalar=SM[:, 1, tt_:tt_ + 1],
                                                                                              in1=Et[:, sl, :], op0=ALU.is_ge, op1=ALU.mult),
                             reads=[smk, "p2_E%d" % sl], writes=["p2_Q%d" % sl])

                def stageC(g):
                    pg = g % 2
                    for s4 in range(4):
                        sl = pg * 4 + s4
                        t.op('pe', lambda e, s4=s4, sl=sl, pg=pg: e.matmul(self.ps[GB(pg)][:, s4 * 128:(s4 + 1) * 128], Qt[:, sl, :], Pt[:, sl, :], start=True, stop=True),
                             reads=["p2_Q%d" % sl, "p2_P%d" % sl], writes=["ps%d" % GB(pg)], skip_self=True)
                    t4 = g * 4
                    t.op('act', lambda e, t4=t4, pg=pg: e.copy(out=Gs[:, t4:t4 + 4, :], in_=self.ps[GB(pg)][:].rearrange("p (t i) -> p t i", t=4)),
                         reads=["ps%d" % GB(pg)], writes=["p2_Gs"])

                opt_tok = getattr(self, "opt_tok", True)
                opt_dense = getattr(self, "opt_dense", True)
                if opt_tok:
                    stageA(0)
                for g in range(64):
                    if opt_tok:
                        if g + 1 < 64:
                            stageA(g + 1)
                    else:
                        stageA(g)
                    stageB(g)
                    stageC(g)
                def load_w(ib):
                    wbuf = ib % 2
                    e0 = ib * 512
                    ukeys = ["Ub%d_%d_%d" % (layer, r0, e0 // 2048) for r0 in range(8)]
                    vkeys = ["Vb%d_%d" % (layer, (ib * 512) // 256 + x) for x in range(2)]
                    t.dma(UTs[:, wbuf], Ubv[:, :, e0:e0 + 512], reads=ukeys, writes=["p2_UT%d" % wbuf])
                    t.dma(Vs[:, wbuf], Vbv[:, ib * 4:(ib + 1) * 4, :], reads=vkeys, writes=["p2_V%d" % wbuf])

                def act_mm(i):
                    ib, ii = i // 4, i % 4
                    wbuf = ib % 2
                    abank = 5 + i % 2
                    for k in range(8):
                        t.op('pe', lambda e, k=k, ii=ii, wbuf=wbuf, abank=abank: e.matmul(self.ps[abank][:, 0:256], UTs[:, wbuf, k, ii * 128:(ii + 1) * 128], xb[:, k, :],
                                                                                    start=(k == 0), stop=(k == 7)),
                             reads=["p2_UT%d" % wbuf, "p2_xb"], writes=["ps%d" % abank], skip_self=True)

                load_w(0)
                if opt_dense:
                    act_mm(0)
                for i in range(128):
                    ib, ii = i // 4, i % 4
                    wbuf = ib % 2
                    ab = i % 2
                    abank = 5 + ab
                    if ii == 0 and ib + 1 < 32:
                        load_w(ib + 1)
                    if opt_dense:
                        if i + 1 < 128:
                            act_mm(i + 1)
                    else:
                        act_mm(i)
                    t.op('act', lambda e, ab=ab, abank=abank: e.activation(out=ga[:, ab, :], in_=self.ps[abank][:, 0:256], func=AF.Gelu_apprx_tanh),
                         reads=["ps%d" % abank], writes=["p2_ga%d" % ab])
                    t.op('dve', lambda e, ab=ab, i=i: e.tensor_tensor(out=Hh[:, ab, :], in0=ga[:, ab, :], in1=Gs[:, :, i], op=ALU.mult),
                         reads=["p2_ga%d" % ab, "p2_Gs"], writes=["p2_H%d" % ab])
                    for st in range(2):
                        for fh in range(2):
                            ob = st * 2 + fh
                            t.op('pe', lambda e, st=st, fh=fh, ob=ob, ab=ab, ii=ii, wbuf=wbuf, i=i: e.matmul(
                                self.ps[ob][:], Hh[:, ab, st * 128:(st + 1) * 128], Vs[:, wbuf, ii, fh * 512:(fh + 1) * 512], start=(i == 0), stop=(i == 127)),
                                reads=["p2_H%d" % ab, "p2_V%d" % wbuf], writes=["ps%d" % ob], skip_self=True)
                for st in range(2):
                    for fh in range(2):
                        ob = st * 2 + fh
                        if ob % 2 == 0:
                            t.op('act', lambda e, st=st, fh=fh, ob=ob: e.copy(out=otok[:, st, fh * 512:(fh + 1) * 512], in_=self.ps[ob][:]), reads=["ps%d" % ob], writes=["p2_otok%d" % st])
                        else:
                            t.op('dve', lambda e, st=st, fh=fh, ob=ob: e.tensor_copy(out=otok[:, st, fh * 512:(fh + 1) * 512], in_=self.ps[ob][:]), reads=["ps%d" % ob], writes=["p2_otok%d" % st])
                for st in range(2):
                    for half in range(2):
                        tb = 5 + half
                        for kk in range(4):
                            fk = half * 4 + kk
                            t.op('pe', lambda e, st=st, fk=fk, kk=kk, tb=tb: e.transpose(self.ps[tb][:, kk * 128:(kk + 1) * 128], otok[:, st, fk * 128:(fk + 1) * 128], self.ident[:]),
                                 reads=["p2_otok%d" % st, "ident"], writes=["ps%d" % tb], skip_self=True)
                        if half == 0:
                            t.op('act', lambda e, st=st, half=half, tb=tb: e.copy(out=peT[:, half * 4:(half + 1) * 4, st * 128:(st + 1) * 128],
                                                                                in_=self.ps[tb][:].rearrange("p (k n) -> p k n", k=4)),
                                 reads=["ps%d" % tb], writes=["p2_peT"])
                        else:
                            t.op('dve', lambda e, st=st, half=half, tb=tb: e.tensor_copy(out=peT[:, half * 4:(half + 1) * 4, st * 128:(st + 1) * 128],
                                                                                       in_=self.ps[tb][:].rearrange("p (k n) -> p k n", k=4)),
                                 reads=["ps%d" % tb], writes=["p2_peT"])
                t.dma(PEv[:, :, tok0:tok0 + 256], peT[:], reads=["p2_peT"], writes=["PEo_%d" % blk])
            t.barrier()

    def peer_p3(self, layer, xin, xout):
        nc, t = self.nc, self.t
        NB = (self.peer_nblk + 1) // 2 if hasattr(self, "peer_nblk") else 16
        with ExitStack() as ph:
            sb = lambda name, shape, dt=F32: ph.enter_context(self.sbuf(name, list(shape), dt))
            self.mk_eps(ph)
            wpg = self.load_w_bf16(ph, "p3_wpg", self.w_ple_gate[layer], 8, D, eng_cast='mix')
            wpp = self.load_w_bf16(ph, "p3_wpp", self.w_ple_proj[layer], 2, D, eng_cast='mix')
            xf = sb("p3_xf", [128, 8, 512])
            pe = sb("p3_pe", [128, 8, 512])
            pf = sb("p3_pf", [128, 2, 512])
            pb = sb("p3_pb", [128, 2, 512], BF16)
            z = sb("p3_z", [128, 8, 512])
            tmp = {'zc': sb("p3_zc", [128, 8, 512]), 'sq': sb("p3_sq", [128, 8, 512]), 'sd': sb("p3_sd", [128, 512])}
            x2 = sb("p3_x2", [128, 8, 512])
            x2b = sb("p3_x2b", [128, 8, 512], BF16)
            sg_ = sb("p3_sg", [128, 2, 512])
            xo = sb("p3_xo", [128, 8, 512])
            xv = xin.rearrange("(k p) n -> p k n", p=128)
            PEv = self.PEo.rearrange("(k p) n -> p k n", p=128)
            pv = self.pT[layer].rearrange("(k p) n -> p k n", p=128)
            ov = xout.rearrange("(k p) n -> p k n", p=128)
            for blk in range(NB):
                tok0 = blk * 512
                t.dma(xf[:], xv[:, :, tok0:tok0 + 512], reads=["X1"], writes=["p3_xf"])
                t.dma(pe[:], PEv[:, :, tok0:tok0 + 512], reads=["PEo_%d" % (2 * blk), "PEo_%d" % (2 * blk + 1)], writes=["p3_pe"])
                t.dma(pf[:], pv[:, :, tok0:tok0 + 512], writes=["p3_pf"])
                t.op('pool', lambda e: e.tensor_copy(out=pb[:], in_=pf[:]), reads=["p3_pf"], writes=["p3_pb"])
                t.op('dve', lambda e: e.scalar_tensor_tensor(out=z[:], in0=xf[:], scalar=ALPHA, in1=pe[:], op0=ALU.mult, op1=ALU.add),
                     reads=["p3_xf", "p3_pe"], writes=["p3_z"])
                self.ln_block(z, "p3_z", layer, 1, x2, "p3_x2", 512, tmp, 0, 1)
                t.op('pool', lambda e: e.tensor_copy(out=x2b[:], in_=x2[:]), reads=["p3_x2"], writes=["p3_x2b"])
                for fo in range(8):
                    bg, bp = 2 + (fo % 2) * 2, 3 + (fo % 2) * 2
                    sgi = fo % 2
                    for k in range(8):
                        t.op('pe', lambda e, k=k, fo=fo, bg=bg: e.matmul(self.ps[bg][:], wpg[:, k, fo * 128:(fo + 1) * 128], x2b[:, k, :], start=(k == 0), stop=(k == 7)),
                             reads=["p3_wpg", "p3_x2b"], writes=["ps%d" % bg], skip_self=True)
                    for k in range(2):
                        t.op('pe', lambda e, k=k, fo=fo, bp=bp: e.matmul(self.ps[bp][:], wpp[:, k, fo * 128:(fo + 1) * 128], pb[:, k, :], start=(k == 0), stop=(k == 1)),
                             reads=["p3_wpp", "p3_pb"], writes=["ps%d" % bp], skip_self=True)
                    t.op('act', lambda e, bg=bg, sgi=sgi: e.activation(out=sg_[:, sgi, :], in_=self.ps[bg][:], func=AF.Sigmoid), reads=["ps%d" % bg], writes=["p3_sg%d" % sgi])
                    t.op('dve', lambda e, bp=bp, sgi=sgi: e.tensor_tensor(out=sg_[:, sgi, :], in0=self.ps[bp][:], in1=sg_[:, sgi, :], op=ALU.mult),
                         reads=["ps%d" % bp, "p3_sg%d" % sgi], writes=["p3_sg%d" % sgi])
                    t.op('pool', lambda e, fo=fo, sgi=sgi: e.tensor_tensor(out=xo[:, fo, :], in0=x2[:, fo, :], in1=sg_[:, sgi, :], op=ALU.add),
                         reads=["p3_x2", "p3_sg%d" % sgi], writes=["p3_xo"])
                t.dma(ov[:, :, tok0:tok0 + 512], xo[:], reads=["p3_xo"], writes=["XOUT%d_%d" % (layer, blk)])
            t.barrier()

    def phase_rg(self):
        self.rg_p1()
        self.rg_p2()
        self.rg_p3()

    def rg_p1(self):
        nc, t = self.nc, self.t
        with ExitStack() as ph:
            sb = lambda name, shape, dt=F32: ph.enter_context(self.sbuf(name, list(shape), dt))
            win = self.load_w_bf16(ph, "r1_win", self.w_rg_in, 8, 2 * D, eng_cast='mix')
            xf = sb("r1_xf", [128, 2, 8, 512])
            xb = sb("r1_xb", [128, 2, 8, 512], BF16)
            gg = sb("r1_gg", [128, 2, 8, 512], BF16)
            rr = sb("r1_rr", [128, 2, 8, 512])
            xv = self.XL1.rearrange("(k p) n -> p k n", p=128)
            ggv = self.RGg.rearrange("(k p) n -> p k n", p=128)
            rrv = self.RGr.rearrange("(k p) n -> p k n", p=128)
            for blk in range(16):
                b = blk % 2
                tok0 = blk * 512
                t.dma(xf[:, b], xv[:, :, tok0:tok0 + 512], writes=["r1_xf%d" % b])
                t.op('pool', lambda e, b=b: e.tensor_copy(out=xb[:, b], in_=xf[:, b]), reads=["r1_xf%d" % b], writes=["r1_xb%d" % b])
                for fo in range(16):
                    bank = fo % 4
                    for k in range(8):
                        t.op('pe', lambda e, k=k, fo=fo, b=b, bank=bank: e.matmul(self.ps[bank][:], win[:, k, fo * 128:(fo + 1) * 128], xb[:, b, k, :],
                                                                             start=(k == 0), stop=(k == 7)),
                             reads=["r1_win", "r1_xb%d" % b], writes=["ps%d" % bank], skip_self=True)
                    if fo < 8:
                        t.op('act', lambda e, fo=fo, b=b, bank=bank: e.activation(out=gg[:, b, fo, :], in_=self.ps[bank][:], func=AF.Gelu_apprx_tanh),
                             reads=["ps%d" % bank], writes=["r1_gg%d" % b])
                    else:
                        t.op('dve', lambda e, fo=fo, b=b, bank=bank: e.tensor_copy(out=rr[:, b, fo - 8, :], in_=self.ps[bank][:]), reads=["ps%d" % bank], writes=["r1_rr%d" % b])
                t.dma(ggv[:, :, tok0:tok0 + 512], gg[:, b], reads=["r1_gg%d" % b], writes=["RGg_%d" % blk])
                t.dma(rrv[:, :, tok0:tok0 + 512], rr[:, b], reads=["r1_rr%d" % b], writes=["RGr_%d" % blk])
            t.barrier()

    def rg_p2(self):
        nc, t = self.nc, self.t
        LM = 4096
        with ExitStack() as ph:
            sb = lambda name, shape, dt=F32: ph.enter_context(self.sbuf(name, list(shape), dt))
            cw = sb("r2_cw", [128, 8, 4]); cbias = sb("r2_cbias", [128, 8])
            bga = sb("r2_bga", [128, 2, 8]); bgx = sb("r2_bgx", [128, 2, 8]); lam = sb("r2_lam", [128, 2, 8])
            sp8 = sb("r2_sp8", [128, 2, 8]); sp16 = sb("r2_sp16", [128, 2, 8])
            for dst, src in [(cw, self.rg_convw), (cbias, self.rg_convb), (bga, self.rg_bga), (bgx, self.rg_bgx), (lam, self.rg_lam)]:
                t.dma(dst[:], src, writes=[dst.name])
            t.op('act', lambda e: e.activation(out=sp8[:], in_=lam[:], func=AF.Exp, scale=-1.0), reads=[lam.name], writes=["r2_sp8"])
            t.op('act', lambda e: e.activation(out=sp8[:], in_=sp8[:], func=AF.Ln, bias=1.0, scale=1.0), reads=["r2_sp8"], writes=["r2_sp8"])
            t.op('dve', lambda e: e.tensor_scalar(out=sp16[:], in0=sp8[:], scalar1=-16.0, scalar2=None, op0=ALU.mult, op1=ALU.bypass), reads=["r2_sp8"], writes=["r2_sp16"])
            t.op('dve', lambda e: e.tensor_scalar(out=sp8[:], in0=sp8[:], scalar1=-8.0, scalar2=None, op0=ALU.mult, op1=ALU.bypass), reads=["r2_sp8", "r2_sp16"], writes=["r2_sp8"])
            wst = sb("r2_wst", [128, 2, 256])
            wgt = sb("r2_wgt", [128, 2, 2, 4, 2, 256], BF16)
            for gi, src in enumerate([self.rg_wga, self.rg_wgx]):
                for d_ in range(2):
                    for h in range(4):
                        t.dma(wst[:], src[d_, h].rearrange("(i p) o -> p i o", p=128), writes=["r2_wst"])
                        t.op('pool', lambda e, gi=gi, d_=d_, h=h: e.tensor_copy(out=wgt[:, gi, d_, h], in_=wst[:]), reads=["r2_wst"], writes=["r2_wgt"])
            rp = sb("r2_rp", [128, 2, LM + 3])
            cc = sb("r2_cc", [128, 2, LM])
            cb = sb("r2_cb", [128, 2, LM], BF16)
            A = sb("r2_A", [128, LM]); B = sb("r2_B", [128, LM])
            HF = sb("r2_HF", [128, LM]); HB = sb("r2_HB", [128, LM])
            gg = sb("r2_gg", [128, LM], BF16); yy = sb("r2_yy", [128, LM], BF16)
            rg_ = sb("r2_rg", [128, 2, 512]); ig_ = sb("r2_ig", [128, 2, 512]); a2_ = sb("r2_a2", [128, 2, 512]); tm_ = sb("r2_tm", [128, 2, 512])
            for si, (s0, L) in enumerate(SEGS):
                for h in range(4):
                    for ct in range(2):
                        ch = 2 * h + ct
                        t.op('pool', lambda e, ct=ct, L=L: e.memset(rp[:, ct, 0:1], 0.0), writes=["r2_rp%d" % ct])
                        t.op('pool', lambda e, ct=ct, L=L: e.memset(rp[:, ct, L + 1:L + 3], 0.0), reads=["r2_rp%d" % ct], writes=["r2_rp%d" % ct])
                        t.dma(rp[:, ct, 1:L + 1], self.RGr[ch * 128:(ch + 1) * 128, s0:s0 + L], reads=["r2_rp%d" % ct], writes=["r2_rp%d" % ct])
                        t.op('dve', lambda e, ct=ct, ch=ch, L=L: e.tensor_scalar(out=cc[:, ct, 0:L], in0=rp[:, ct, 0:L], scalar1=cw[:, ch, 0:1], scalar2=cbias[:, ch:ch + 1],
                                                                            op0=ALU.mult, op1=ALU.add), reads=["r2_rp%d" % ct, cw.name, cbias.name], writes=["r2_cc%d" % ct])
                        for k in range(1, 4):
                            t.op('dve', lambda e, ct=ct, ch=ch, L=L, k=k: e.scalar_tensor_tensor(out=cc[:, ct, 0:L], in0=rp[:, ct, k:k + L], scalar=cw[:, ch, k:k + 1], in1=cc[:, ct, 0:L],
                                                                                           op0=ALU.mult, op1=ALU.add), reads=["r2_rp%d" % ct, cw.name, "r2_cc%d" % ct], writes=["r2_cc%d" % ct])
                        t.op('pool', lambda e, ct=ct, L=L: e.tensor_copy(out=cb[:, ct, 0:L], in_=cc[:, ct, 0:L]), reads=["r2_cc%d" % ct], writes=["r2_cb%d" % ct])
                    for oh in range(2):
                        ch = 2 * h + oh
                        for d_ in range(2):
                            Hd = HF if d_ == 0 else HB
                            for c0 in range(0, L, 512):
                                pi = (c0 // 512) % 2
                                ba, bx = 2 * pi, 2 * pi + 1
                                for ih in range(2):
                                    t.op('pe', lambda e, ih=ih, d_=d_, h=h, oh=oh, c0=c0, ba=ba: e.matmul(self.ps[ba][:], wgt[:, 0, d_, h, ih, oh * 128:(oh + 1) * 128], cb[:, ih, c0:c0 + 512],
                                                                                                    start=(ih == 0), stop=(ih == 1)),
                                         reads=["r2_wgt", "r2_cb0", "r2_cb1"], writes=["ps%d" % ba], skip_self=True)
                                for ih in range(2):
                                    t.op('pe', lambda e, ih=ih, d_=d_, h=h, oh=oh, c0=c0, bx=bx: e.matmul(self.ps[bx][:], wgt[:, 1, d_, h, ih, oh * 128:(oh + 1) * 128], cb[:, ih, c0:c0 + 512],
                                                                                                    start=(ih == 0), stop=(ih == 1)),
                                         reads=["r2_wgt", "r2_cb0", "r2_cb1"], writes=["ps%d" % bx], skip_self=True)
                                t.op('act', lambda e, pi=pi, ba=ba, d_=d_, ch=ch: e.activation(out=rg_[:, pi, :], in_=self.ps[ba][:], func=AF.Sigmoid, bias=bga[:, d_, ch:ch + 1], scale=1.0),
                                     reads=["ps%d" % ba, bga.name], writes=["r2_rg%d" % pi])
                                t.op('act', lambda e, pi=pi, bx=bx, d_=d_, ch=ch: e.activation(out=ig_[:, pi, :], in_=self.ps[bx][:], func=AF.Sigmoid, bias=bgx[:, d_, ch:ch + 1], scale=1.0),
                                     reads=["ps%d" % bx, bgx.name], writes=["r2_ig%d" % pi])
                                t.op('act', lambda e, pi=pi, c0=c0, d_=d_, ch=ch: e.activation(out=A[:, c0:c0 + 512], in_=rg_[:, pi, :], func=AF.Exp, scale=sp8[:, d_, ch:ch + 1]),
                                     reads=["r2_rg%d" % pi, "r2_sp8"], writes=["r2_A"])
                                t.op('act', lambda e, pi=pi, d_=d_, ch=ch: e.activation(out=a2_[:, pi, :], in_=rg_[:, pi, :], func=AF.Exp, scale=sp16[:, d_, ch:ch + 1]),
                                     reads=["r2_rg%d" % pi, "r2_sp16"], writes=["r2_a2%d" % pi])
                                t.op('dve', lambda e, pi=pi: e.tensor_scalar(out=a2_[:, pi, :], in0=a2_[:, pi, :], scalar1=-1.0, scalar2=1.0, op0=ALU.mult, op1=ALU.add),
                                     reads=["r2_a2%d" % pi], writes=["r2_a2%d" % pi])
                                t.op('act', lambda e, pi=pi: e.activation(out=a2_[:, pi, :], in_=a2_[:, pi, :], func=AF.Sqrt), reads=["r2_a2%d" % pi], writes=["r2_a2%d" % pi])
                                t.op('pool', lambda e, pi=pi, oh=oh, c0=c0: e.tensor_tensor(out=tm_[:, pi, :], in0=ig_[:, pi, :], in1=cc[:, oh, c0:c0 + 512], op=ALU.mult),
                                     reads=["r2_ig%d" % pi, "r2_cc%d" % oh], writes=["r2_tm%d" % pi])
                                t.op('dve', lambda e, pi=pi, c0=c0: e.tensor_tensor(out=B[:, c0:c0 + 512], in0=tm_[:, pi, :], in1=a2_[:, pi, :], op=ALU.mult),
                                     reads=["r2_tm%d" % pi, "r2_a2%d" % pi], writes=["r2_B"])
                            if d_ == 0:
                                t.op('dve', lambda e, L=L: e.tensor_tensor_scan(out=HF[:, 0:L], data0=A[:, 0:L], data1=B[:, 0:L], initial=0.0, op0=ALU.mult, op1=ALU.add),
                                     reads=["r2_A", "r2_B"], writes=["r2_HF"])
                            else:
                                t.op('dve', lambda e, L=L: e.tensor_tensor_scan(out=HB[:, 0:L][:, ::-1], data0=A[:, 0:L][:, ::-1], data1=B[:, 0:L][:, ::-1], initial=0.0, op0=ALU.mult, op1=ALU.add),
                                     reads=["r2_A", "r2_B"], writes=["r2_HB"])
                        t.dma(gg[:, 0:L], self.RGg[ch * 128:(ch + 1) * 128, s0:s0 + L], writes=["r2_gg"])
                        t.op('pool', lambda e, L=L: e.tensor_tensor(out=HF[:, 0:L], in0=HF[:, 0:L], in1=HB[:, 0:L], op=ALU.add), reads=["r2_HF", "r2_HB"], writes=["r2_HF"])
                        t.op('dve', lambda e, L=L: e.tensor_tensor(out=yy[:, 0:L], in0=HF[:, 0:L], in1=gg[:, 0:L], op=ALU.mult), reads=["r2_HF", "r2_gg"], writes=["r2_yy"])
                        t.dma(self.RGy[ch * 128:(ch + 1) * 128, s0:s0 + L], yy[:, 0:L], reads=["r2_yy"], writes=["RGy_%d_%d" % (si, ch)])
            t.barrier()

    def rg_p3(self):
        nc, t = self.nc, self.t
        with ExitStack() as ph:
            sb = lambda name, shape, dt=F32: ph.enter_context(self.sbuf(name, list(shape), dt))
            self.mk_eps(ph)
            wo = self.load_w_bf16(ph, "r3_wo", self.w_rg_out, 8, D, eng_cast='mix')
            yb = sb("r3_yb", [128, 8, 512], BF16)
            xf = sb("r3_xf", [128, 8, 512])
            z = sb("r3_z", [128, 8, 512])
            tmp = {'zc': sb("r3_zc", [128, 8, 512]), 'sq': sb("r3_sq", [128, 8, 512]), 'sd': sb("r3_sd", [128, 512])}
            xo = sb("r3_xo", [128, 8, 512])
            xv = self.XL1.rearrange("(k p) n -> p k n", p=128)
            yv = self.RGy.rearrange("(k p) n -> p k n", p=128)
            X1v = self.X1.rearrange("(k p) n -> p k n", p=128)
            for blk in range(16):
                tok0 = blk * 512
                t.dma(yb[:], yv[:, :, tok0:tok0 + 512], writes=["r3_yb"])
                t.dma(xf[:], xv[:, :, tok0:tok0 + 512], writes=["r3_xf"])
                for fo in range(8):
                    bank = 2 + fo % 4
                    for k in range(8):
                        t.op('pe', lambda e, k=k, fo=fo, bank=bank: e.matmul(self.ps[bank][:], wo[:, k, fo * 128:(fo + 1) * 128], yb[:, k, :], start=(k == 0), stop=(k == 7)),
                             reads=["r3_wo", "r3_yb"], writes=["ps%d" % bank], skip_self=True)
                    t.op('dve', lambda e, fo=fo, bank=bank: e.scalar_tensor_tensor(out=z[:, fo, :], in0=xf[:, fo, :], scalar=ALPHA, in1=self.ps[bank][:], op0=ALU.mult, op1=ALU.add),
                         reads=["r3_xf", "ps%d" % bank], writes=["r3_z"])
                self.ln_block(z, "r3_z", 1, 0, xo, "r3_xo", 512, tmp, 0, 1)
                t.dma(X1v[:, :, tok0:tok0 + 512], xo[:], reads=["r3_xo"], writes=["X1"])
            t.barrier()


def _consts():
    c = {}
    c["c_ident"] = np.eye(128, dtype=np.float32)
    c["c_onesm"] = np.full((128, 128), 1.0 / D, dtype=np.float32)
    s = np.arange(128) // 16
    c["c_maskf"] = (s[:, None] <= s[None, :]).astype(np.float32)
    c["c_maskb"] = (s[:, None] >= s[None, :]).astype(np.float32)
    m = np.ones((128, 1024), dtype=np.float32)
    m[0:64, [0, 512, 768]] = 0.0
    m[64:128, [511, 767, 1023]] = 0.0
    c["c_segmask"] = m
    return c


def _shared_weights(inp):
    f = lambda a: np.ascontiguousarray(np.asarray(a, dtype=np.float32))
    w = dict(_consts())
    w["s5_w_in"] = f(inp["s5_w_in"][0])
    w["s5_w_glu"] = f(inp["s5_w_glu"][0])
    w["s5_lamre"] = f(inp["s5_lam_re"][0].transpose(0, 2, 1).reshape(128, 64))
    w["s5_lamim"] = f(inp["s5_lam_im"][0].transpose(0, 2, 1).reshape(128, 64))
    w["s5_lstep"] = f(np.broadcast_to(inp["s5_log_step"][0][:, None, :], (2, 64, 64)).reshape(128, 64))
    w["s5_bre"] = f(inp["s5_b_re"][0].transpose(0, 2, 1, 3).reshape(128, 64, 16))
    w["s5_bim"] = f(inp["s5_b_im"][0].transpose(0, 2, 1, 3).reshape(128, 64, 16))
    w["s5_ctre"] = f(inp["s5_c_re"][0].transpose(0, 3, 1, 2).reshape(128, 64, 16))
    w["s5_ctim"] = f(inp["s5_c_im"][0].transpose(0, 3, 1, 2).reshape(128, 64, 16))
    d = np.asarray(inp["s5_d"][0]).reshape(64, 16)
    w["s5_dpk"] = f(np.broadcast_to(d.T[None, :, :], (8, 16, 64)).reshape(128, 64))
    w["rg_w_in"] = f(inp["rg_w_in"][0])
    w["rg_convw"] = f(inp["rg_conv_w"][0].reshape(4, 8, 128).transpose(2, 1, 0))
    w["rg_convb"] = f(inp["rg_conv_b"][0].reshape(8, 128).T)
    w["rg_wga"] = f(inp["rg_w_gate_a"][0])
    w["rg_wgx"] = f(inp["rg_w_gate_x"][0])
    w["rg_bga"] = f(inp["rg_b_gate_a"][0].reshape(2, 8, 128).transpose(2, 0, 1))
    w["rg_bgx"] = f(inp["rg_b_gate_x"][0].reshape(2, 8, 128).transpose(2, 0, 1))
    w["rg_lam"] = f(inp["rg_lambda"][0].reshape(2, 8, 128).transpose(2, 0, 1))
    w["rg_w_out"] = f(inp["rg_w_out"][0])
    ln = np.stack([np.asarray(inp[k]) for k in ("ln1_g", "ln1_b", "ln2_g", "ln2_b")], 0)
    w["ln_par"] = f(ln.reshape(4, 2, 8, 128).transpose(3, 0, 1, 2))
    w["peer_w_q"] = f(inp["peer_w_q"])
    w["peer_skT"] = f(np.asarray(inp["peer_subkeys"]).transpose(0, 1, 3, 2))
    w["peer_uT"] = f(np.asarray(inp["peer_u"]).transpose(0, 2, 1))
    w["peer_v"] = f(inp["peer_v"])
    w["ple_w_proj"] = f(inp["ple_w_proj"])
    w["ple_w_gate"] = f(inp["ple_w_gate"])
    return w


def _core_acts(inp, core):
    xp = np.asarray(inp["x_prompt"][core])
    xs = np.asarray(inp["x_sample"][2 * core:2 * core + 2]).reshape(4096, D)
    x = np.concatenate([xp, xs], 0)
    pp = np.asarray(inp["p_prompt"][:, core])
    ps = np.asarray(inp["p_sample"][:, 2 * core:2 * core + 2]).reshape(2, 4096, 256)
    p = np.concatenate([pp, ps], 1)
    return {"xT": np.ascontiguousarray(x.T.astype(np.float32)),
            "pT": np.ascontiguousarray(p.transpose(0, 2, 1).astype(np.float32))}


_CACHE = {}


def kernel(**inputs):
    if "nc" not in _CACHE:
        k = Ker()
        _CACHE["nc"] = k.build()
        _CACHE["in_names"] = list(k.in_names)
    nc = _CACHE["nc"]
    names = _CACHE["in_names"]
    w = _shared_weights(inputs)
    in_maps = []
    for core in range(8):
        m = dict(w)
        m.update(_core_acts(inputs, core))
        in_maps.append({n: m[n] for n in names})
    res = run_bass_kernel_spmd(nc, in_maps, core_ids=list(range(8)))
    yp = np.empty((8, 4096, D), dtype=np.float32)
    ys = np.empty((16, 2048, D), dtype=np.float32)
    for core in range(8):
        y = np.asarray(res.results[core]["yT"]).T
        yp[core] = y[0:4096]
        ys[2 * core] = y[4096:6144]
        ys[2 * core + 1] = y[6144:8192]
    return (yp, ys)
```

```python
from contextlib import ExitStack
import math
import numpy as np
import concourse.bass as bass
import concourse.mybir as mybir
from concourse.bass_utils import run_bass_kernel_spmd

F32 = mybir.dt.float32
BF16 = mybir.dt.bfloat16
I32 = mybir.dt.int32
AF = mybir.ActivationFunctionType
ALU = mybir.AluOpType

NTOK = 8192
D = 1024
ALPHA = 4.0 ** 0.25
LN_EPS = 1e-5
SEGS = [(0, 4096), (4096, 2048), (6144, 2048)]
TWO_PI = 2.0 * math.pi


class Trk:
    ENGS = ['pe', 'act', 'dve', 'pool', 'sp']

    def __init__(self, nc, stack, nslots=12):
        self.nc = nc
        self.e = {'pe': nc.tensor, 'act': nc.scalar, 'dve': nc.vector, 'pool': nc.gpsimd, 'sp': nc.sync}
        self.sem = {}
        self.cnt = {}
        for n in self.ENGS:
            self.sem[n] = stack.enter_context(nc.semaphore('s_' + n))
            self.cnt[n] = 0
        self.slots = ['d%d' % i for i in range(nslots)]
        for s in self.slots:
            self.sem[s] = stack.enter_context(nc.semaphore('s_' + s))
            self.cnt[s] = 0
        self.rr = 0
        self.seen = {n: {} for n in self.ENGS}
        self.lw = {}
        self.lr = {}
        self.ninst = 0

    def _deps(self, reads, writes):
        need = {}
        for k in reads:
            for e, c in self.lw.get(k, {}).items():
                if c > need.get(e, 0):
                    need[e] = c
        for k in writes:
            for e, c in self.lw.get(k, {}).items():
                if c > need.get(e, 0):
                    need[e] = c
            for e, c in self.lr.get(k, {}).items():
                if c > need.get(e, 0):
                    need[e] = c
        return need

    def _wait(self, eng, need, skip_self=False):
        for e, c in need.items():
            if skip_self and e == eng:
                continue
            if self.seen[eng].get(e, 0) >= c:
                continue
            self.e[eng].wait_ge(self.sem[e], c)
            self.seen[eng][e] = c

    def _record(self, who, c, reads, writes):
        for k in writes:
            self.lw[k] = {who: c}
            self.lr[k] = {}
        for k in reads:
            self.lr.setdefault(k, {})[who] = c

    def op(self, eng, fn, reads=(), writes=(), skip_self=False):
        need = self._deps(reads, writes)
        self._wait(eng, need, skip_self)
        ins = fn(self.e[eng])
        self.cnt[eng] += 1
        ins.then_inc(self.sem[eng], 1)
        self._record(eng, self.cnt[eng], reads, writes)
        self.ninst += 1

    def dma(self, out, in_, reads=(), writes=(), eng='sp'):
        need = self._deps(reads, writes)
        slot = self.slots[self.rr]
        self.rr = (self.rr + 1) % len(self.slots)
        if self.cnt[slot] > 0:
            need[slot] = max(need.get(slot, 0), self.cnt[slot])
        self._wait(eng, need)
        ins = self.e[eng].dma_start(out=out, in_=in_)
        self.cnt[slot] += 16
        ins.then_inc(self.sem[slot], 16)
        self._record(slot, self.cnt[slot], reads, writes)
        self.ninst += 1

    def barrier(self):
        self.min_rem = min(getattr(self, "min_rem", 1 << 30), self.nc.sbuf_bytes_remaining)
        allc = {k: v for k, v in self.cnt.items() if v > 0}
        for eng in self.ENGS:
            self._wait(eng, dict(allc), skip_self=True)

    def finish(self):
        allc = {k: v for k, v in self.cnt.items() if v > 0}
        self._wait('sp', dict(allc), skip_self=True)


def bc(ap, shape):
    return ap.to_broadcast(list(shape))


class Ker:
    def __init__(self, dbg_out=(), dbg_in=(), phases=None):
        self.dbg_out = set(dbg_out)
        self.dbg_in = set(dbg_in)
        self.phases = phases
        self.nc = bass.Bass("TRN2", target_bir_lowering=False)
        self.in_names = []
        self.out_names = []
        self.tmp_id = 0

    def sbuf(self, name, shape, dt):
        self.tmp_id += 1
        return self.nc.sbuf_tensor("%s_u%d" % (name, self.tmp_id), list(shape), dt)

    def din(self, name, shape, dt=F32):
        self.in_names.append(name)
        return self.nc.dram_tensor(name, list(shape), dt, kind="ExternalInput").ap()

    def dout(self, name, shape, dt=F32):
        self.out_names.append(name)
        return self.nc.dram_tensor(name, list(shape), dt, kind="ExternalOutput").ap()

    def scratch(self, name, shape, dt=F32):
        if name in self.dbg_in:
            return self.din(name, shape, dt)
        if name in self.dbg_out:
            return self.dout(name, shape, dt)
        return self.nc.dram_tensor(name, list(shape), dt, kind="Internal").ap()

    def __getattr__(self, attr):
        specs = self.__dict__.get("specs", {})
        if attr in specs:
            kind, name, shape = specs[attr]
            ap = self.din(name, shape)
            self.__dict__[attr] = ap
            return ap
        raise AttributeError(attr)

    def on(self, ph):
        return self.phases is None or ph in self.phases

    def build(self):
        nc = self.nc
        with ExitStack() as st:
            self.st = st
            self.t = Trk(nc, st)
            t = self.t
            self.specs = {
                "xT": ("din", "xT", [D, NTOK]),
                "pT": ("din", "pT", [2, 256, NTOK]),
                "yT": ("dout", "yT", [D, NTOK]),
                "c_ident": ("din", "c_ident", [128, 128]),
                "c_onesm": ("din", "c_onesm", [128, 128]),
                "c_maskf": ("din", "c_maskf", [128, 128]),
                "c_maskb": ("din", "c_maskb", [128, 128]),
                "c_segmask": ("din", "c_segmask", [128, 1024]),
                "w_s5_in": ("din", "s5_w_in", [D, D]),
                "w_s5_glu": ("din", "s5_w_glu", [D, 2 * D]),
                "s5_lamre": ("din", "s5_lamre", [128, 64]),
                "s5_lamim": ("din", "s5_lamim", [128, 64]),
                "s5_lstep": ("din", "s5_lstep", [128, 64]),
                "s5_bre": ("din", "s5_bre", [128, 64, 16]),
                "s5_bim": ("din", "s5_bim", [128, 64, 16]),
                "s5_ctre": ("din", "s5_ctre", [128, 64, 16]),
                "s5_ctim": ("din", "s5_ctim", [128, 64, 16]),
                "s5_dpk": ("din", "s5_dpk", [128, 64]),
                "w_rg_in": ("din", "rg_w_in", [D, 2 * D]),
                "rg_convw": ("din", "rg_convw", [128, 8, 4]),
                "rg_convb": ("din", "rg_convb", [128, 8]),
                "rg_wga": ("din", "rg_wga", [2, 4, 256, 256]),
                "rg_wgx": ("din", "rg_wgx", [2, 4, 256, 256]),
                "rg_bga": ("din", "rg_bga", [128, 2, 8]),
                "rg_bgx": ("din", "rg_bgx", [128, 2, 8]),
                "rg_lam": ("din", "rg_lam", [128, 2, 8]),
                "w_rg_out": ("din", "rg_w_out", [D, D]),
                "ln_par": ("din", "ln_par", [128, 4, 2, 8]),
                "w_peer_q": ("din", "peer_w_q", [2, D, 2 * D]),
                "peer_skT": ("din", "peer_skT", [2, 2, 128, 128]),
                "peer_uT": ("din", "peer_uT", [2, D, 16384]),
                "peer_v": ("din", "peer_v", [2, 16384, D]),
                "w_ple_proj": ("din", "ple_w_proj", [2, 256, D]),
                "w_ple_gate": ("din", "ple_w_gate", [2, D, D]),
            }
            self.yT = self.dout("yT", [D, NTOK]) if self.on("peer1") else None
            self.UpD = self.scratch("UpD", [64, 128, 1024], BF16)
            self.HpD = self.scratch("HpD", [64, 128, 1024], BF16)
            self.X1 = self.scratch("X1", [D, NTOK])
            self.XL1 = self.scratch("XL1", [D, NTOK])
            self.RGr = self.scratch("RGr", [D, NTOK])
            self.RGg = self.scratch("RGg", [D, NTOK], BF16)
            self.RGy = self.scratch("RGy", [D, NTOK], BF16)
            self.XB = self.scratch("XB", [D, NTOK], BF16)
            self.QT = self.scratch("QT", [2 * D, NTOK], BF16)
            self.PEo = self.scratch("PEo", [D, NTOK])
            self.Ub = self.scratch("Ub", [2, D, 16384], BF16)
            self.Vb = self.scratch("Vb", [2, 16384, D], BF16)
            self.ident = st.enter_context(self.sbuf("ident", [128, 128], F32))
            self.identb = st.enter_context(self.sbuf("identb", [128, 128], BF16))
            self.onesm = st.enter_context(self.sbuf("onesm", [128, 128], F32))
            self.lnp = st.enter_context(self.sbuf("lnp", [128, 4, 2, 8], F32))
            self.ps = [st.enter_context(nc.psum_tensor("ps%d" % i, [128, 512], F32)) for i in range(8)]
            t.dma(self.ident[:], self.c_ident, writes=["ident"])
            t.dma(self.onesm[:], self.c_onesm, writes=["onesm"])
            t.dma(self.lnp[:], self.ln_par, writes=["lnp"])
            t.op('dve', lambda e: e.tensor_copy(out=self.identb[:], in_=self.ident[:]), reads=["ident"], writes=["identb"])

            if self.on("tabcast"):
                self.phase_tabcast()
            if self.on("s5a"):
                self.phase_s5a()
            if self.on("s5b"):
                self.phase_s5b()
            if self.on("s5c"):
                self.phase_s5c()
            if self.on("peer0"):
                self.phase_peer(0, self.X1, self.XL1)
            if self.on("rg"):
                self.phase_rg()
            if self.on("peer1"):
                self.phase_peer(1, self.X1, self.yT)
            t.barrier()
            t.finish()
        return nc

    def load_w_bf16(self, ph, name, dram_ap, kt, ncols, eng_cast='pool'):
        nc, t = self.nc, self.t
        wb = ph.enter_context(self.sbuf(name, [128, kt, ncols], BF16))
        stg = ph.enter_context(self.sbuf(name + "_stg", [128, 2, 2048], F32))
        i = 0
        for k in range(kt):
            for c0 in range(0, ncols, 2048):
                cw = min(2048, ncols - c0)
                b = i % 2
                t.dma(stg[:, b, 0:cw], dram_ap[k * 128:(k + 1) * 128, c0:c0 + cw],
                      writes=[name + "_stg%d" % b])
                eng = ['pool', 'act'][i % 2] if eng_cast == 'mix' else eng_cast
                if eng == 'act':
                    t.op('act', lambda e, b=b, k=k, c0=c0, cw=cw: e.copy(out=wb[:, k, c0:c0 + cw], in_=stg[:, b, 0:cw]),
                         reads=[name + "_stg%d" % b], writes=[name])
                else:
                    t.op(eng, lambda e, b=b, k=k, c0=c0, cw=cw: e.tensor_copy(out=wb[:, k, c0:c0 + cw], in_=stg[:, b, 0:cw]),
                         reads=[name + "_stg%d" % b], writes=[name])
                i += 1
        return wb

    def ln_block(self, z, zkey, layer, which, out, outkey, N, tmp, pbank_a, pbank_b):
        t = self.t
        pm = self.ps[pbank_a]
        pv = self.ps[pbank_b]
        ka, kb = "ps%d" % pbank_a, "ps%d" % pbank_b
        for k in range(8):
            t.op('pe', lambda e, k=k: e.matmul(pm[:, 0:N], self.onesm[:], z[:, k, :], start=(k == 0), stop=(k == 7)),
                 reads=[zkey, "onesm"], writes=[ka], skip_self=True)
        zc, sq, sd = tmp['zc'], tmp['sq'], tmp['sd']
        t.op('dve', lambda e: e.tensor_tensor(out=zc[:], in0=z[:], in1=bc(pm[:, 0:N].unsqueeze(1), [128, 8, N]), op=ALU.subtract),
             reads=[zkey, ka], writes=[zc.name])
        t.op('act', lambda e: e.activation(out=sq[:], in_=zc[:], func=AF.Square), reads=[zc.name], writes=[sq.name])
        for k in range(8):
            t.op('pe', lambda e, k=k: e.matmul(pv[:, 0:N], self.onesm[:], sq[:, k, :], start=(k == 0), stop=(k == 7)),
                 reads=[sq.name, "onesm"], writes=[kb], skip_self=True)
        t.op('act', lambda e: e.activation(out=sd[:], in_=pv[:, 0:N], func=AF.Sqrt, bias=self.epsc[:, 0:1], scale=1.0),
             reads=[kb, "epsc"], writes=[sd.name])
        t.op('dve', lambda e: e.reciprocal(out=sd[:], in_=sd[:]), reads=[sd.name], writes=[sd.name])
        t.op('dve', lambda e: e.tensor_tensor(out=zc[:], in0=zc[:], in1=bc(sd[:].unsqueeze(1), [128, 8, N]), op=ALU.mult),
             reads=[zc.name, sd.name], writes=[zc.name])
        for k in range(8):
            t.op('act', lambda e, k=k: e.activation(out=out[:, k, :], in_=zc[:, k, :], func=AF.Identity,
                                                    bias=self.lnp[:, 2 * which + 1, layer, k:k + 1],
                                                    scale=self.lnp[:, 2 * which, layer, k:k + 1]),
                 reads=[zc.name, "lnp"], writes=[outkey])

    def mk_eps(self, ph):
        nc, t = self.nc, self.t
        self.epsc = ph.enter_context(self.sbuf("epsc", [128, 1], F32))
        t.op('dve', lambda e: e.memset(self.epsc[:], LN_EPS), writes=["epsc"])

    def phase_tabcast(self):
        t = self.t
        for l in range(2):
            for r0 in range(0, D, 128):
                for c0 in range(0, 16384, 2048):
                    t.dma(self.Ub[l, r0:r0 + 128, c0:c0 + 2048], self.peer_uT[l, r0:r0 + 128, c0:c0 + 2048],
                          writes=["Ub%d_%d_%d" % (l, r0 // 128, c0 // 2048)], eng='pool')
            for r0 in range(0, 16384, 256):
                t.dma(self.Vb[l, r0:r0 + 256, :], self.peer_v[l, r0:r0 + 256, :], writes=["Vb%d_%d" % (l, r0 // 256)], eng='pool')

    def phase_s5a(self):
        nc, t = self.nc, self.t
        with ExitStack() as ph:
            wb = self.load_w_bf16(ph, "s5win", self.w_s5_in, 8, D, eng_cast='mix')
            xf = ph.enter_context(self.sbuf("a_xf", [128, 2, 8, 512], F32))
            xb = ph.enter_context(self.sbuf("a_xb", [128, 8, 1024], BF16))
            U8 = ph.enter_context(self.sbuf("a_U8", [128, 64, 8, 16], BF16))
            Upt = ph.enter_context(self.sbuf("a_Upt", [128, 64, 128], BF16))
            xTv = self.xT.rearrange("(k p) n -> p k n", p=128)
            for tile in range(8):
                t0 = tile * 1024
                for h in range(2):
                    t.dma(xf[:, h, :, :], xTv[:, :, t0 + h * 512:t0 + (h + 1) * 512], writes=["a_xf%d" % h])
                    t.op('act' if h == 0 else 'pool',
                         (lambda e, h=h: e.copy(out=xb[:, :, h * 512:(h + 1) * 512], in_=xf[:, h, :, :])) if h == 0 else
                         (lambda e, h=h: e.tensor_copy(out=xb[:, :, h * 512:(h + 1) * 512], in_=xf[:, h, :, :])),
                         reads=["a_xf%d" % h], writes=["a_xb"])
                for s in range(8):
                    for fh in range(2):
                        bank = (s * 2 + fh) % 4
                        pk = "ps%d" % bank
                        for k in range(8):
                            t.op('pe', lambda e, k=k, s=s, fh=fh, bank=bank: e.matmul(
                                self.ps[bank][:], xb[:, k, s::8], wb[:, k, fh * 512:(fh + 1) * 512],
                                start=(k == 0), stop=(k == 7)),
                                reads=["a_xb", "s5win"], writes=[pk], skip_self=True)
                        if (s * 2 + fh) % 2 == 0:
                            t.op('act', lambda e, s=s, fh=fh, bank=bank: e.copy(out=U8[:, fh * 32:(fh + 1) * 32, s, :], in_=self.ps[bank][:].rearrange("p (g c) -> p g c", c=16)),
                                 reads=[pk], writes=["a_U8"])
                        else:
                            t.op('dve', lambda e, s=s, fh=fh, bank=bank: e.tensor_copy(out=U8[:, fh * 32:(fh + 1) * 32, s, :], in_=self.ps[bank][:].rearrange("p (g c) -> p g c", c=16)),
                                 reads=[pk], writes=["a_U8"])
                for gb in range(8):
                    bank = 4 + gb % 2
                    pk = "ps%d" % bank
                    pb = self.ps[bank][:].bitcast(BF16)
                    for gi in range(8):
                        g = gb * 8 + gi
                        t.op('pe', lambda e, g=g, gi=gi, pb=pb: e.transpose(pb[:, gi * 128:(gi + 1) * 128],
                                                                          U8[:, g, :, :].rearrange("p s c -> p (s c)"), self.identb[:]),
                             reads=["a_U8", "identb"], writes=[pk], skip_self=True)
                    if gb % 2 == 0:
                        t.op('dve', lambda e, gb=gb, pb=pb: e.tensor_copy(out=Upt[:, gb * 8:(gb + 1) * 8, :],
                                                                        in_=pb.rearrange("p (g c) -> p g c", g=8)),
                             reads=[pk], writes=["a_Upt"])
                    else:
                        t.op('act', lambda e, gb=gb, pb=pb: e.copy(out=Upt[:, gb * 8:(gb + 1) * 8, :],
                                                                 in_=pb.rearrange("p (g c) -> p g c", g=8)),
                             reads=[pk], writes=["a_Upt"])
                t.dma(self.UpD[:, :, tile * 128:(tile + 1) * 128].rearrange("g p c -> p g c"), Upt[:],
                      reads=["a_Upt"], writes=["UpD"])
            t.barrier()

    def cmul(self, eng, outr, outi, ar, ai, br, bi, tmp1, tmp2, rk, wk):
        t = self.t
        t.op(eng, lambda e: e.tensor_tensor(out=tmp1, in0=ar, in1=br, op=ALU.mult), reads=rk, writes=["cm_t1"])
        t.op(eng, lambda e: e.tensor_tensor(out=tmp2, in0=ai, in1=bi, op=ALU.mult), reads=rk, writes=["cm_t2"])
        t.op(eng, lambda e: e.tensor_tensor(out=outr, in0=tmp1, in1=tmp2, op=ALU.subtract), reads=["cm_t1", "cm_t2"] + rk, writes=wk)
        t.op(eng, lambda e: e.tensor_tensor(out=tmp1, in0=ar, in1=bi, op=ALU.mult), reads=rk + wk, writes=["cm_t1"])
        t.op(eng, lambda e: e.tensor_tensor(out=tmp2, in0=ai, in1=br, op=ALU.mult), reads=rk + wk, writes=["cm_t2"])
        t.op(eng, lambda e: e.tensor_tensor(out=outi, in0=tmp1, in1=tmp2, op=ALU.add), reads=["cm_t1", "cm_t2"] + rk, writes=wk)

    def phase_s5b(self):
        nc, t = self.nc, self.t
        with ExitStack() as ph:
            sb = lambda name, shape, dt=F32: ph.enter_context(self.sbuf(name, list(shape), dt))
            MATS = sb("b_mats", [128, 64, 5, 128], BF16)
            POW = sb("b_pow", [128, 2, 16, 64])
            PH = sb("b_ph", [128, 2, 10, 64])
            RHO = sb("b_rho", [128, 64])
            dpk = sb("b_dpk", [128, 64])
            segm = sb("b_segm", [128, 1024])
            t.dma(dpk[:], self.s5_dpk, writes=["b_dpk"])
            t.dma(segm[:], self.c_segmask, writes=["b_segm"])
            with ExitStack() as pg:
                sg = lambda name, shape, dt=F32: pg.enter_context(self.sbuf(name, list(shape), dt))
                lre = sg("g_lre", [128, 64]); lim = sg("g_lim", [128, 64]); lst = sg("g_lst", [128, 64])
                Bre = sg("g_bre", [128, 64, 16]); Bim = sg("g_bim", [128, 64, 16])
                Cre = sg("g_cre", [128, 64, 16]); Cim = sg("g_cim", [128, 64, 16])
                maskf = sg("g_maskf", [128, 128]); maskb = sg("g_maskb", [128, 128])
                for dst, src in [(lre, self.s5_lamre), (lim, self.s5_lamim), (lst, self.s5_lstep), (Bre, self.s5_bre),
                                 (Bim, self.s5_bim), (Cre, self.s5_ctre), (Cim, self.s5_ctim), (maskf, self.c_maskf),
                                 (maskb, self.c_maskb)]:
                    t.dma(dst[:], src, writes=[dst.name])
                S = {}
                for nm in ["step", "ang", "lrs", "mag", "magi", "kf", "r", "m1", "s1", "c1", "ar", "ai", "ari", "aii",
                           "den", "zr", "qr", "qi", "u1", "u2", "e8"]:
                    S[nm] = sg("g_" + nm, [128, 64])
                ki = sg("g_ki", [128, 64], I32)
                V = 'dve'

                def tt(out, a, b, op, rk, wk):
                    t.op(V, lambda e: e.tensor_tensor(out=out, in0=a, in1=b, op=op), reads=rk, writes=wk)

                def ts(out, a, s1, s2, op0, op1, rk, wk):
                    t.op(V, lambda e: e.tensor_scalar(out=out, in0=a, scalar1=s1, scalar2=s2, op0=op0, op1=op1), reads=rk, writes=wk)

                def act(out, a, func, rk, wk, scale=1.0):
                    t.op('act', lambda e: e.activation(out=out, in_=a, func=func, scale=scale), reads=rk, writes=wk)

                n = lambda k: S[k].name
                act(S["step"][:], lst[:], AF.Exp, [lst.name], [n("step")])
                tt(S["ang"][:], lim[:], S["step"][:], ALU.mult, [lim.name, n("step")], [n("ang")])
                tt(S["lrs"][:], lre[:], S["step"][:], ALU.mult, [lre.name, n("step")], [n("lrs")])
                act(S["mag"][:], S["lrs"][:], AF.Exp, [n("lrs")], [n("mag")])
                act(S["magi"][:], S["lrs"][:], AF.Exp, [n("lrs")], [n("magi")], scale=-1.0)
                act(S["e8"][:], S["lrs"][:], AF.Exp, [n("lrs")], [n("e8")], scale=-8.0)
                act(RHO[:], S["lrs"][:], AF.Exp, [n("lrs")], ["b_rho"], scale=8.0)

                def range_reduce(dst, src, shift):
                    ts(S["kf"][:], src, 1.0 / TWO_PI, shift / TWO_PI + 0.5, ALU.mult, ALU.add, [n("ang")], [n("kf")])
                    t.op(V, lambda e: e.tensor_copy(out=ki[:], in_=S["kf"][:]), reads=[n("kf")], writes=[ki.name])
                    t.op(V, lambda e: e.tensor_copy(out=S["kf"][:], in_=ki[:]), reads=[ki.name], writes=[n("kf")])
                    ts(S["kf"][:], S["kf"][:], -TWO_PI, shift, ALU.mult, ALU.add, [n("kf")], [n("kf")])
                    tt(dst, src, S["kf"][:], ALU.add, [n("ang"), n("kf")], [n("r")])
                    ts(S["m1"][:], dst, math.pi, -TWO_PI, ALU.is_gt, ALU.mult, [n("r")], [n("m1")])
                    tt(dst, dst, S["m1"][:], ALU.add, [n("r"), n("m1")], [n("r")])
                    ts(S["m1"][:], dst, -math.pi, TWO_PI, ALU.is_lt, ALU.mult, [n("r")], [n("m1")])
                    tt(dst, dst, S["m1"][:], ALU.add, [n("r"), n("m1")], [n("r")])
                    ts(dst, dst, math.pi, -math.pi, ALU.min, ALU.max, [n("r")], [n("r")])

                range_reduce(S["r"][:], S["ang"][:], 0.0)
                act(S["s1"][:], S["r"][:], AF.Sin, [n("r")], [n("s1")])
                range_reduce(S["r"][:], S["ang"][:], math.pi / 2)
                act(S["c1"][:], S["r"][:], AF.Sin, [n("r")], [n("c1")])
                tt(S["ar"][:], S["mag"][:], S["c1"][:], ALU.mult, [n("mag"), n("c1")], [n("ar")])
                tt(S["ai"][:], S["mag"][:], S["s1"][:], ALU.mult, [n("mag"), n("s1")], [n("ai")])
                tt(S["ari"][:], S["magi"][:], S["c1"][:], ALU.mult, [n("magi"), n("c1")], [n("ari")])
                tt(S["aii"][:], S["magi"][:], S["s1"][:], ALU.mult, [n("magi"), n("s1")], [n("aii")])
                ts(S["aii"][:], S["aii"][:], -1.0, None, ALU.mult, ALU.bypass, [n("aii")], [n("aii")])
                tt(S["den"][:], lre[:], lre[:], ALU.mult, [lre.name], [n("den")])
                tt(S["u1"][:], lim[:], lim[:], ALU.mult, [lim.name], [n("u1")])
                tt(S["den"][:], S["den"][:], S["u1"][:], ALU.add, [n("den"), n("u1")], [n("den")])
                t.op(V, lambda e: e.reciprocal(out=S["den"][:], in_=S["den"][:]), reads=[n("den")], writes=[n("den")])
                ts(S["zr"][:], S["ar"][:], -1.0, None, ALU.add, ALU.bypass, [n("ar")], [n("zr")])
                tt(S["u1"][:], S["zr"][:], lre[:], ALU.mult, [n("zr"), lre.name], [n("u1")])
                tt(S["u2"][:], S["ai"][:], lim[:], ALU.mult, [n("ai"), lim.name], [n("u2")])
                tt(S["u1"][:], S["u1"][:], S["u2"][:], ALU.add, [n("u1"), n("u2")], [n("u1")])
                tt(S["qr"][:], S["u1"][:], S["den"][:], ALU.mult, [n("u1"), n("den")], [n("qr")])
                tt(S["u1"][:], S["ai"][:], lre[:], ALU.mult, [n("ai"), lre.name], [n("u1")])
                tt(S["u2"][:], S["zr"][:], lim[:], ALU.mult, [n("zr"), lim.name], [n("u2")])
                tt(S["u1"][:], S["u1"][:], S["u2"][:], ALU.subtract, [n("u1"), n("u2")], [n("u1")])
                tt(S["qi"][:], S["u1"][:], S["den"][:], ALU.mult, [n("u1"), n("den")], [n("qi")])
                BBr = sg("g_bbr", [128, 64, 16]); BBi = sg("g_bbi", [128, 64, 16])
                T1 = sg("g_T1", [128, 1024]); T2 = sg("g_T2", [128, 1024])
                T1v = T1[:].rearrange("p (g c) -> p g c", c=16)
                T2v = T2[:].rearrange("p (g c) -> p g c", c=16)
                qrb = bc(S["qr"][:].unsqueeze(2), [128, 64, 16]); qib = bc(S["qi"][:].unsqueeze(2), [128, 64, 16])
                self.cmul(V, BBr[:], BBi[:], qrb, qib, Bre[:], Bim[:], T1v, T2v, [n("qr"), n("qi"), Bre.name, Bim.name], [BBr.name, BBi.name])
                t.op(V, lambda e: e.memset(POW[:, 0, 7, :], 1.0), writes=["b_pow"])
                t.op(V, lambda e: e.memset(POW[:, 1, 7, :], 0.0), reads=["b_pow"], writes=["b_pow"])
                for k in range(0, 8):
                    self.cmul(V, POW[:, 0, 8 + k, :], POW[:, 1, 8 + k, :], POW[:, 0, 7 + k, :], POW[:, 1, 7 + k, :],
                              S["ar"][:], S["ai"][:], T1[:, 0:64], T2[:, 0:64], ["b_pow", n("ar"), n("ai")], ["b_pow"])
                for k in range(0, 7):
                    self.cmul(V, POW[:, 0, 6 - k, :], POW[:, 1, 6 - k, :], POW[:, 0, 7 - k, :], POW[:, 1, 7 - k, :],
                              S["ari"][:], S["aii"][:], T1[:, 0:64], T2[:, 0:64], ["b_pow", n("ari"), n("aii")], ["b_pow"])
                tt(PH[:, 0, 0, :], POW[:, 0, 15, :], S["e8"][:], ALU.mult, ["b_pow", n("e8")], ["b_ph"])
                tt(PH[:, 1, 0, :], POW[:, 1, 15, :], S["e8"][:], ALU.mult, ["b_pow", n("e8"), "b_ph"], ["b_ph"])
                t.op(V, lambda e: e.tensor_scalar(out=PH[64:128, 1, 0, :], in0=PH[64:128, 1, 0, :], scalar1=-1.0, scalar2=None,
                                                  op0=ALU.mult, op1=ALU.bypass), reads=["b_ph"], writes=["b_ph"])
                for L in range(9):
                    self.cmul(V, PH[:, 0, L + 1, :], PH[:, 1, L + 1, :], PH[:, 0, L, :], PH[:, 1, L, :],
                              PH[:, 0, L, :], PH[:, 1, L, :], T1[:, 0:64], T2[:, 0:64], ["b_ph"], ["b_ph"])
                PA = sg("g_pa", [128, 4, 2, 8, 64])
                kmap = {0: (lambda j: 7 - j, lambda j: j), 1: (lambda j: j + 1, lambda j: 8 - j),
                        2: (lambda j: -j, lambda j: j), 3: (lambda j: j, lambda j: -j)}
                ci = 0
                for kind in range(4):
                    for j in range(8):
                        for half, (p0, p1) in enumerate([(0, 64), (64, 128)]):
                            kk = kmap[kind][half](j) + 7
                            eng = ['act', 'pool'][ci % 2]
                            ci += 1
                            if eng == 'act':
                                t.op('act', lambda e, kind=kind, j=j, p0=p0, p1=p1, kk=kk: e.copy(out=PA[p0:p1, kind, :, j, :], in_=POW[p0:p1, :, kk, :]),
                                     reads=["b_pow"], writes=["g_pa%d_%d_%d" % (kind, j, half)])
                            else:
                                t.op('pool', lambda e, kind=kind, j=j, p0=p0, p1=p1, kk=kk: e.tensor_copy(out=PA[p0:p1, kind, :, j, :], in_=POW[p0:p1, :, kk, :]),
                                     reads=["b_pow"], writes=["g_pa%d_%d_%d" % (kind, j, half)])
                pa_keys = ["g_pa%d_%d_%d" % (kind, j, half) for kind in range(4) for j in range(8) for half in range(2)]
                TB = sg("g_tb", [128, 4, 2, 1024])
                Ysn = sg("g_ysn", [128, 1024])
                for gb in range(8):
                    g0 = gb * 8
                    for kind in range(4):
                        src_r, src_i = (BBr, BBi) if kind in (0, 2) else (Cre, Cim)
                        par = bc(PA[:, kind, 0, :, g0:g0 + 8].rearrange("p j g -> p g j").unsqueeze(3), [128, 8, 8, 16])
                        pai = bc(PA[:, kind, 1, :, g0:g0 + 8].rearrange("p j g -> p g j").unsqueeze(3), [128, 8, 8, 16])
                        br = bc(src_r[:, g0:g0 + 8, :].unsqueeze(2), [128, 8, 8, 16])
                        bi = bc(src_i[:, g0:g0 + 8, :].unsqueeze(2), [128, 8, 8, 16])
                        outr = TB[:, kind, 0, :].rearrange("p (g j c) -> p g j c", g=8, j=8)
                        outi = TB[:, kind, 1, :].rearrange("p (g j c) -> p g j c", g=8, j=8)
                        t1 = T1[:].rearrange("p (g j c) -> p g j c", g=8, j=8)
                        t2 = T2[:].rearrange("p (g j c) -> p g j c", g=8, j=8)
                        self.cmul(V, outr, outi, par, pai, br, bi, t1, t2, pa_keys + [src_r.name, src_i.name], ["g_tb%d" % kind])
                    t.op(V, lambda e: e.tensor_scalar(out=Ysn[:], in0=TB[:, 2, 1, :], scalar1=-1.0, scalar2=None, op0=ALU.mult, op1=ALU.bypass),
                         reads=["g_tb2"], writes=["g_ysn"])
                    for gi in range(8):
                        g = g0 + gi
                        sl = slice(gi * 128, (gi + 1) * 128)
                        for ri in range(2):
                            bank = 4 + ri
                            t.op('pe', lambda e, ri=ri, sl=sl, bank=bank: e.transpose(self.ps[bank][:, 0:128], TB[:, 0, ri, sl], self.ident[:]),
                                 reads=["g_tb0", "ident"], writes=["ps%d" % bank], skip_self=True)
                            t.op('act', lambda e, ri=ri, g=g, bank=bank: e.copy(out=MATS[:, g, ri, :], in_=self.ps[bank][:, 0:128]),
                                 reads=["ps%d" % bank], writes=["b_mats"])
                        t.op('pool', lambda e, g=g, sl=sl: e.tensor_copy(out=MATS[:, g, 3, :], in_=TB[:, 1, 0, sl]), reads=["g_tb1"], writes=["b_mats"])
                        t.op('pool', lambda e, g=g, sl=sl: e.tensor_scalar(out=MATS[:, g, 4, :], in0=TB[:, 1, 1, sl], scalar1=-1.0, scalar2=None,
                                                                          op0=ALU.mult, op1=ALU.bypass), reads=["g_tb1"], writes=["b_mats"])
                        for half, (p0, p1) in enumerate([(0, 64), (64, 128)]):
                            bank = 6 + half
                            t.op('pe', lambda e, p0=p0, p1=p1, sl=sl, bank=bank: e.matmul(self.ps[bank][:, 0:128], TB[p0:p1, 2, 0, sl], TB[p0:p1, 3, 0, sl], start=True, stop=False),
                                 reads=["g_tb2", "g_tb3"], writes=["ps%d" % bank], skip_self=True)
                            t.op('pe', lambda e, p0=p0, p1=p1, sl=sl, bank=bank: e.matmul(self.ps[bank][:, 0:128], Ysn[p0:p1, sl], TB[p0:p1, 3, 1, sl], start=False, stop=True),
                                 reads=["g_ysn", "g_tb3"], writes=["ps%d" % bank], skip_self=True)
                        t.op(V, lambda e: e.tensor_tensor(out=T1[:, 0:128], in0=self.ps[6][:, 0:128], in1=maskf[:], op=ALU.mult),
                             reads=["ps6", maskf.name], writes=["cm_t1"])
                        t.op(V, lambda e: e.tensor_tensor(out=T2[:, 0:128], in0=self.ps[7][:, 0:128], in1=maskb[:], op=ALU.mult),
                             reads=["ps7", maskb.name], writes=["cm_t2"])
                        t.op(V, lambda e, g=g: e.tensor_tensor(out=MATS[:, g, 2, :], in0=T1[:, 0:128], in1=T2[:, 0:128], op=ALU.add),
                             reads=["cm_t1", "cm_t2"], writes=["b_mats"])
                t.barrier()
            Up = sb("b_up", [128, 2, 1024], BF16)
            TAB = sb("b_tab", [128, 2, 1024])
            RM = sb("b_rm", [128, 1024])
            W = [sb("b_w%d" % i, [128, 1024]) for i in range(6)]
            Gs = [sb("b_g%d" % i, [128, 1024]) for i in range(2)]
            Hu = [sb("b_h%d" % i, [128, 1024]) for i in range(2)]
            Hs = sb("b_hs", [128, 2, 1024], BF16)
            yd = sb("b_yd", [128, 1024])
            hp = sb("b_hp", [128, 2, 1024], BF16)
            t.op('pool', lambda e: e.memset(Hs[:], 0.0), writes=["b_hs"])
            V = 'dve'
            for g in range(64):
                ub = g % 2
                uk = "b_up%d" % ub
                t.dma(Up[:, ub, :], self.UpD[g, :, :], reads=["UpD"], writes=[uk])
                t.op('act', lambda e, g=g: e.activation(out=RM[:], in_=segm[:], func=AF.Copy, scale=RHO[:, g:g + 1]),
                     reads=["b_segm", "b_rho"], writes=["b_rm"])
                t.op(V, lambda e: e.memset(TAB[:, 0, 0:1], 1.0), writes=["b_tab"])
                t.op(V, lambda e: e.memset(TAB[:, 1, 0:1], 0.0), reads=["b_tab"], writes=["b_tab"])
                for L in range(9):
                    n0 = 1 << L
                    cr = PH[:, 0, L, g:g + 1]
                    ci_ = PH[:, 1, L, g:g + 1]
                    src_r = TAB[:, 0, 0:n0]; src_i = TAB[:, 1, 0:n0]
                    dst_r = TAB[:, 0, n0:2 * n0]; dst_i = TAB[:, 1, n0:2 * n0]
                    t.op(V, lambda e, src_i=src_i, ci_=ci_, n0=n0: e.tensor_scalar(out=W[0][:, 0:n0], in0=src_i, scalar1=ci_, scalar2=None, op0=ALU.mult, op1=ALU.bypass),
                         reads=["b_tab", "b_ph"], writes=["b_w0"])
                    t.op(V, lambda e, src_r=src_r, cr=cr, n0=n0, dst_r=dst_r: e.scalar_tensor_tensor(out=dst_r, in0=src_r, scalar=cr, in1=W[0][:, 0:n0], op0=ALU.mult, op1=ALU.subtract),
                         reads=["b_tab", "b_ph", "b_w0"], writes=["b_tab"])
                    t.op(V, lambda e, src_r=src_r, ci_=ci_, n0=n0: e.tensor_scalar(out=W[1][:, 0:n0], in0=src_r, scalar1=ci_, scalar2=None, op0=ALU.mult, op1=ALU.bypass),
                         reads=["b_tab", "b_ph"], writes=["b_w1"])
                    t.op(V, lambda e, src_i=src_i, cr=cr, n0=n0, dst_i=dst_i: e.scalar_tensor_tensor(out=dst_i, in0=src_i, scalar=cr, in1=W[1][:, 0:n0], op0=ALU.mult, op1=ALU.add),
                         reads=["b_tab", "b_ph", "b_w1"], writes=["b_tab"])
                for ri in range(2):
                    t.op('act', lambda e, ri=ri: e.copy(out=TAB[:, ri, 512:768], in_=TAB[:, ri, 0:256]), reads=["b_tab"], writes=["b_tab"])
                    t.op('act', lambda e, ri=ri: e.copy(out=TAB[:, ri, 768:1024], in_=TAB[:, ri, 0:256]), reads=["b_tab"], writes=["b_tab"])
                for ri in range(2):
                    for h in range(2):
                        bank = ri * 2 + h
                        t.op('pe', lambda e, ri=ri, h=h, bank=bank, g=g, ub=ub: e.matmul(self.ps[bank][:], MATS[:, g, ri, :], Up[:, ub, h * 512:(h + 1) * 512], start=True, stop=True),
                             reads=["b_mats", uk], writes=["ps%d" % bank], skip_self=True)
                cosT = TAB[:, 0, :]; sinT = TAB[:, 1, :]
                for h in range(2):
                    cs = slice(h * 512, (h + 1) * 512)
                    t.op(V, lambda e, h=h, cs=cs: e.tensor_tensor(out=W[0][:, cs], in0=self.ps[h][:], in1=cosT[:, cs], op=ALU.mult), reads=["ps%d" % h, "b_tab"], writes=["b_w0"])
                    t.op(V, lambda e, h=h, cs=cs: e.tensor_tensor(out=W[1][:, cs], in0=self.ps[2 + h][:], in1=sinT[:, cs], op=ALU.mult), reads=["ps%d" % (2 + h), "b_tab"], writes=["b_w1"])
                    t.op(V, lambda e, h=h, cs=cs: e.tensor_tensor(out=W[2][:, cs], in0=self.ps[2 + h][:], in1=cosT[:, cs], op=ALU.mult), reads=["ps%d" % (2 + h), "b_tab"], writes=["b_w2"])
                    t.op(V, lambda e, h=h, cs=cs: e.tensor_tensor(out=W[3][:, cs], in0=self.ps[h][:], in1=sinT[:, cs], op=ALU.mult), reads=["ps%d" % h, "b_tab"], writes=["b_w3"])
                t.op('pool', lambda e: e.tensor_tensor(out=W[4][:], in0=W[0][:], in1=W[1][:], op=ALU.add), reads=["b_w0", "b_w1"], writes=["b_w4"])
                t.op('pool', lambda e: e.tensor_tensor(out=W[5][:], in0=W[2][:], in1=W[3][:], op=ALU.subtract), reads=["b_w2", "b_w3"], writes=["b_w5"])
                for ri in range(2):
                    src = W[4 + ri]
                    t.op(V, lambda e, ri=ri, src=src: e.tensor_tensor_scan(out=Gs[ri][0:64, :], data0=RM[0:64, :], data1=src[0:64, :], initial=0.0, op0=ALU.mult, op1=ALU.add),
                         reads=["b_rm", src.name], writes=["b_g%d_f" % ri])
                    t.op(V, lambda e, ri=ri, src=src: e.tensor_tensor_scan(out=Gs[ri][64:128, ::-1], data0=RM[64:128, ::-1], data1=src[64:128, ::-1], initial=0.0, op0=ALU.mult, op1=ALU.add),
                         reads=["b_rm", src.name], writes=["b_g%d_b" % ri])
                gk = ["b_g0_f", "b_g0_b", "b_g1_f", "b_g1_b"]
                t.op(V, lambda e: e.tensor_tensor(out=W[0][:], in0=Gs[0][:], in1=cosT, op=ALU.mult), reads=gk + ["b_tab"], writes=["b_w0"])
                t.op('pool', lambda e: e.tensor_tensor(out=W[1][:], in0=Gs[1][:], in1=sinT, op=ALU.mult), reads=gk + ["b_tab"], writes=["b_w1"])
                t.op(V, lambda e: e.tensor_tensor(out=W[2][:], in0=Gs[1][:], in1=cosT, op=ALU.mult), reads=gk + ["b_tab"], writes=["b_w2"])
                t.op('pool', lambda e: e.tensor_tensor(out=W[3][:], in0=Gs[0][:], in1=sinT, op=ALU.mult), reads=gk + ["b_tab"], writes=["b_w3"])
                t.op(V, lambda e: e.tensor_tensor(out=Hu[0][:], in0=W[0][:], in1=W[1][:], op=ALU.subtract), reads=["b_w0", "b_w1"], writes=["b_h0"])
                t.op('pool', lambda e: e.tensor_tensor(out=Hu[1][:], in0=W[2][:], in1=W[3][:], op=ALU.add), reads=["b_w2", "b_w3"], writes=["b_h1"])
                for ri in range(2):
                    t.op(V, lambda e, ri=ri: e.tensor_tensor(out=Hs[0:64, ri, 1:1024], in0=Hu[ri][0:64, 0:1023], in1=segm[0:64, 1:1024], op=ALU.mult),
                         reads=["b_h%d" % ri, "b_segm"], writes=["b_hs"])
                    t.op('pool', lambda e, ri=ri: e.tensor_tensor(out=Hs[64:128, ri, 0:1023], in0=Hu[ri][64:128, 1:1024], in1=segm[64:128, 0:1023], op=ALU.mult),
                         reads=["b_h%d" % ri, "b_segm"], writes=["b_hs"])
                for h in range(2):
                    bank = 4 + h
                    cs = slice(h * 512, (h + 1) * 512)
                    t.op('pe', lambda e, g=g, ub=ub, cs=cs, bank=bank: e.matmul(self.ps[bank][:], MATS[:, g, 2, :], Up[:, ub, cs], start=True, stop=False),
                         reads=["b_mats", uk], writes=["ps%d" % bank], skip_self=True)
                    t.op('pe', lambda e, g=g, cs=cs, bank=bank: e.matmul(self.ps[bank][:], MATS[:, g, 3, :], Hs[:, 0, cs], start=False, stop=False),
                         reads=["b_mats", "b_hs"], writes=["ps%d" % bank], skip_self=True)
                    t.op('pe', lambda e, g=g, cs=cs, bank=bank: e.matmul(self.ps[bank][:], MATS[:, g, 4, :], Hs[:, 1, cs], start=False, stop=True),
                         reads=["b_mats", "b_hs"], writes=["ps%d" % bank], skip_self=True)
                    t.op(V, lambda e, g=g, ub=ub, cs=cs, bank=bank: e.scalar_tensor_tensor(out=yd[:, cs], in0=Up[:, ub, cs], scalar=dpk[:, g:g + 1], in1=self.ps[bank][:],
                                                                                         op0=ALU.mult, op1=ALU.add),
                         reads=[uk, "b_dpk", "ps%d" % bank], writes=["b_yd"])
                t.op('act', lambda e, ub=ub: e.activation(out=hp[:, ub, :], in_=yd[:], func=AF.Gelu_apprx_tanh), reads=["b_yd"], writes=["b_hp%d" % ub])
                t.dma(self.HpD[g, :, :], hp[:, ub, :], reads=["b_hp%d" % ub], writes=["HpD"])
            t.barrier()

    def phase_s5c(self):
        nc, t = self.nc, self.t
        with ExitStack() as ph:
            sb = lambda name, shape, dt=F32: ph.enter_context(self.sbuf(name, list(shape), dt))
            self.mk_eps(ph)
            wg = self.load_w_bf16(ph, "s5wglu", self.w_s5_glu, 8, 2 * D, eng_cast='mix')
            hpt = sb("c_hpt", [128, 64, 128], BF16)
            H8 = sb("c_H8", [128, 8, 1024], BF16)
            hT = sb("c_hT", [128, 8, 1024], BF16)
            xf = sb("c_xf", [128, 8, 512])
            z = sb("c_z", [128, 8, 512])
            sg_ = sb("c_sg", [128, 512])
            tmp = {'zc': sb("c_zc", [128, 8, 512]), 'sq': sb("c_sq", [128, 8, 512]), 'sd': sb("c_sd", [128, 512])}
            xo = sb("c_xo", [128, 8, 512])
            xTv = self.xT.rearrange("(k p) n -> p k n", p=128)
            X1v = self.X1.rearrange("(k p) n -> p k n", p=128)
            for tile in range(8):
                t.dma(hpt[:], self.HpD[:, :, tile * 128:(tile + 1) * 128].rearrange("g p c -> p g c"), reads=["HpD"], writes=["c_hpt"])
                for gb in range(8):
                    bank = gb % 2
                    pk = "ps%d" % bank
                    pb = self.ps[bank][:].bitcast(BF16)
                    for gi in range(8):
                        g = gb * 8 + gi
                        t.op('pe', lambda e, g=g, gi=gi, pb=pb: e.transpose(pb[:, gi * 128:(gi + 1) * 128], hpt[:, g, :], self.identb[:]),
                             reads=["c_hpt", "identb"], writes=[pk], skip_self=True)
                    src = pb.rearrange("p (g t c) -> p g t c", g=8, t=8)
                    dst = H8[:, :, gb * 128:(gb + 1) * 128].rearrange("p t (g c) -> p g t c", g=8)
                    if gb % 2 == 0:
                        t.op('dve', lambda e, src=src, dst=dst: e.tensor_copy(out=dst, in_=src), reads=[pk], writes=["c_H8"])
                    else:
                        t.op('act', lambda e, src=src, dst=dst: e.copy(out=dst, in_=src), reads=[pk], writes=["c_H8"])
                for k in range(8):
                    bank = 2 + k % 2
                    pk = "ps%d" % bank
                    pb = self.ps[bank][:].bitcast(BF16)
                    for tt_ in range(8):
                        t.op('pe', lambda e, k=k, tt_=tt_, pb=pb: e.transpose(pb[:, tt_ * 128:(tt_ + 1) * 128], H8[:, tt_, k * 128:(k + 1) * 128], self.identb[:]),
                             reads=["c_H8", "identb"], writes=[pk], skip_self=True)
                    src = pb.rearrange("p (t c) -> p t c", t=8)
                    dst = hT[:, k, :].rearrange("p (c t) -> p t c", t=8)
                    if k % 2 == 0:
                        t.op('dve', lambda e, src=src, dst=dst: e.tensor_copy(out=dst, in_=src), reads=[pk], writes=["c_hT"])
                    else:
                        t.op('act', lambda e, src=src, dst=dst: e.copy(out=dst, in_=src), reads=[pk], writes=["c_hT"])
                for th in range(2):
                    tok0 = tile * 1024 + th * 512
                    t.dma(xf[:], xTv[:, :, tok0:tok0 + 512], writes=["c_xf"])
                    for fo in range(8):
                        bv, bg = 4 + (fo % 2) * 2, 5 + (fo % 2) * 2
                        for k in range(8):
                            t.op('pe', lambda e, k=k, fo=fo, th=th, bv=bv: e.matmul(self.ps[bv][:], wg[:, k, fo * 128:(fo + 1) * 128], hT[:, k, th * 512:(th + 1) * 512],
                                                                              start=(k == 0), stop=(k == 7)), reads=["s5wglu", "c_hT"], writes=["ps%d" % bv], skip_self=True)
                        for k in range(8):
                            t.op('pe', lambda e, k=k, fo=fo, th=th, bg=bg: e.matmul(self.ps[bg][:], wg[:, k, D + fo * 128:D + (fo + 1) * 128], hT[:, k, th * 512:(th + 1) * 512],
                                                                              start=(k == 0), stop=(k == 7)), reads=["s5wglu", "c_hT"], writes=["ps%d" % bg], skip_self=True)
                        t.op('act', lambda e, bg=bg: e.activation(out=sg_[:], in_=self.ps[bg][:], func=AF.Sigmoid), reads=["ps%d" % bg], writes=["c_sg"])
                        t.op('dve', lambda e, bv=bv: e.tensor_tensor(out=sg_[:], in0=self.ps[bv][:], in1=sg_[:], op=ALU.mult), reads=["ps%d" % bv, "c_sg"], writes=["c_sg"])
                        t.op('dve', lambda e, fo=fo: e.scalar_tensor_tensor(out=z[:, fo, :], in0=xf[:, fo, :], scalar=ALPHA, in1=sg_[:], op0=ALU.mult, op1=ALU.add),
                             reads=["c_xf", "c_sg"], writes=["c_z"])
                    self.ln_block(z, "c_z", 0, 0, xo, "c_xo", 512, tmp, 0, 1)
                    t.dma(X1v[:, :, tok0:tok0 + 512], xo[:], reads=["c_xo"], writes=["X1"])
            t.barrier()

    def phase_peer(self, layer, xin, xout):
        sub = getattr(self, "peer_sub", ("p1", "p2", "p3"))
        if "p1" in sub:
            self.peer_p1(layer, xin)
        if "p2" in sub:
            self.peer_p2(layer)
        if "p3" in sub:
            self.peer_p3(layer, xin, xout)

    def peer_p1(self, layer, xin):
        nc, t = self.nc, self.t
        with ExitStack() as ph:
            sb = lambda name, shape, dt=F32: ph.enter_context(self.sbuf(name, list(shape), dt))
            wq = self.load_w_bf16(ph, "p1_wq", self.w_peer_q[layer], 8, 2 * D, eng_cast='mix')
            xf = sb("p1_xf", [128, 2, 8, 512])
            xb = sb("p1_xb", [128, 2, 8, 512], BF16)
            qb = sb("p1_qb", [128, 2, 16, 512], BF16)
            xv = xin.rearrange("(k p) n -> p k n", p=128)
            XBv = self.XB.rearrange("(k p) n -> p k n", p=128)
            QTv = self.QT.rearrange("(k p) n -> p k n", p=128)
            for blk in range(16):
                b = blk % 2
                tok0 = blk * 512
                t.dma(xf[:, b], xv[:, :, tok0:tok0 + 512], reads=["X1"], writes=["p1_xf%d" % b])
                t.op('pool', lambda e, b=b: e.tensor_copy(out=xb[:, b], in_=xf[:, b]), reads=["p1_xf%d" % b], writes=["p1_xb%d" % b])
                t.dma(XBv[:, :, tok0:tok0 + 512], xb[:, b], reads=["p1_xb%d" % b], writes=["XB_%d" % blk])
                for fo in range(16):
                    bank = fo % 4
                    for k in range(8):
                        t.op('pe', lambda e, k=k, fo=fo, b=b, bank=bank: e.matmul(self.ps[bank][:], wq[:, k, fo * 128:(fo + 1) * 128], xb[:, b, k, :],
                                                                             start=(k == 0), stop=(k == 7)),
                             reads=["p1_wq", "p1_xb%d" % b], writes=["ps%d" % bank], skip_self=True)
                    if fo % 2 == 0:
                        t.op('act', lambda e, fo=fo, b=b, bank=bank: e.copy(out=qb[:, b, fo, :], in_=self.ps[bank][:]), reads=["ps%d" % bank], writes=["p1_qb%d" % b])
                    else:
                        t.op('dve', lambda e, fo=fo, b=b, bank=bank: e.tensor_copy(out=qb[:, b, fo, :], in_=self.ps[bank][:]), reads=["ps%d" % bank], writes=["p1_qb%d" % b])
                t.dma(QTv[:, :, tok0:tok0 + 512], qb[:, b], reads=["p1_qb%d" % b], writes=["QT_%d" % blk])
            t.barrier()

    def peer_p2(self, layer):
        nc, t = self.nc, self.t
        NB = self.peer_nblk if hasattr(self, "peer_nblk") else 32
        with ExitStack() as ph:
            sb = lambda name, shape, dt=F32: ph.enter_context(self.sbuf(name, list(shape), dt))
            skf = sb("p2_skf", [128, 2, 128])
            skb = sb("p2_skb", [128, 2, 128], BF16)
            t.dma(skf[:], self.peer_skT[layer].rearrange("c d n -> d c n"), writes=["p2_skf"])
            t.op('dve', lambda e: e.tensor_copy(out=skb[:], in_=skf[:]), reads=["p2_skf"], writes=["p2_skb"])
            xb = sb("p2_xb", [128, 8, 256], BF16)
            qT = sb("p2_qT", [128, 16, 256], BF16)
            sc = sb("p2_sc", [128, 16, 128])
            sc2 = sb("p2_sc2", [128, 2, 128])
            cs = sb("p2_cs", [128, 8, 256])
            cs2 = sb("p2_cs2", [128, 2, 256])
            sv = sb("p2_sv", [128, 16, 16])
            ts_ = sb("p2_ts", [128, 8, 16])
            ex = sb("p2_ex", [128, 8, 16])
            st8 = sb("p2_st8", [128, 4, 8])
            TM = sb("p2_TM", [128, 3, 128])
            SM = sb("p2_SM", [128, 3, 256])
            QR = sb("p2_qr", [128, 2, 2, 16, 128], BF16)
            Pt = sb("p2_P", [128, 8, 128], BF16)
            Et = sb("p2_E", [128, 8, 128])
            Qt = sb("p2_Q", [128, 8, 128], BF16)
            Gs = sb("p2_Gs", [128, 256, 128], BF16)
            UTs = sb("p2_UT", [128, 2, 8, 512], BF16)
            Vs = sb("p2_V", [128, 2, 4, 1024], BF16)
            ga = sb("p2_ga", [128, 2, 256])
            Hh = sb("p2_H", [128, 2, 256], BF16)
            otok = sb("p2_otok", [128, 2, 1024])
            peT = sb("p2_peT", [128, 8, 256])
            XBv = self.XB.rearrange("(k p) n -> p k n", p=128)
            QTv = self.QT.rearrange("(k p) n -> p k n", p=128)
            PEv = self.PEo.rearrange("(k p) n -> p k n", p=128)
            Ubv = self.Ub[layer].rearrange("(k p) e -> p k e", p=128)
            Vbv = self.Vb[layer].rearrange("(i p) f -> p i f", p=128)
            NEG = -1.0e30
            import os as _os2
            TBK = int(_os2.environ.get("TBK", "7"))
            for blk in range(NB):
                tok0 = blk * 256
                b512 = tok0 // 512
                t.dma(xb[:], XBv[:, :, tok0:tok0 + 256], reads=["XB_%d" % b512], writes=["p2_xb"])
                t.dma(qT[:], QTv[:, :, tok0:tok0 + 256], reads=["QT_%d" % b512], writes=["p2_qT"])
                for st in range(2):
                    tsl = slice(st * 128, (st + 1) * 128)
                    for hc in range(16):
                        bank = hc // 4
                        t.op('pe', lambda e, hc=hc, bank=bank, tsl=tsl: e.matmul(self.ps[bank][:, (hc % 4) * 128:(hc % 4 + 1) * 128], qT[:, hc, tsl], skb[:, hc % 2, :],
                                                                              start=True, stop=True),
                             reads=["p2_qT", "p2_skb"], writes=["ps%d" % bank], skip_self=True)
                    for bank in range(4):
                        t.op('act', lambda e, bank=bank: e.copy(out=sc[:, bank * 4:(bank + 1) * 4, :], in_=self.ps[bank][:].rearrange("p (a n) -> p a n", a=4)),
                             reads=["ps%d" % bank], writes=["p2_sc%d" % bank])
                    for hc in range(16):
                        kk = "p2_sc%d" % (hc // 4)
                        t.op('dve', lambda e, hc=hc: e.max(out=sv[:, hc, 0:8], in_=sc[:, hc, :]), reads=[kk], writes=["p2_sv%d" % hc])
                        t.op('dve', lambda e, hc=hc: e.match_replace(out=sc2[:, hc % 2, :], in_to_replace=sv[:, hc, 0:8], in_values=sc[:, hc, :], imm_value=NEG),
                             reads=[kk, "p2_sv%d" % hc], writes=["p2_sc2_%d" % (hc % 2)])
                        t.op('dve', lambda e, hc=hc: e.max(out=sv[:, hc, 8:16], in_=sc2[:, hc % 2, :]), reads=["p2_sc2_%d" % (hc % 2)], writes=["p2_sv%d" % hc])
                    svk = ["p2_sv%d" % hc for hc in range(16)]
                    sv4 = sv[:].rearrange("p (h c) a -> p h c a", c=2)
                    t.op('dve', lambda e: e.tensor_tensor(out=cs[:].rearrange("p h (a b) -> p h a b", a=16),
                                                          in0=bc(sv4[:, :, 0, :].unsqueeze(3), [128, 8, 16, 16]),
                                                          in1=bc(sv4[:, :, 1, :].unsqueeze(2), [128, 8, 16, 16]), op=ALU.add),
                         reads=svk, writes=["p2_cs"])
                    for h in range(8):
                        t.op('dve', lambda e, h=h: e.max(out=ts_[:, h, 0:8], in_=cs[:, h, :]), reads=["p2_cs"], writes=["p2_ts%d" % h])
                        t.op('dve', lambda e, h=h: e.match_replace(out=cs2[:, h % 2, :], in_to_replace=ts_[:, h, 0:8], in_values=cs[:, h, :], imm_value=NEG),
                             reads=["p2_cs", "p2_ts%d" % h], writes=["p2_cs2_%d" % (h % 2)])
                        t.op('dve', lambda e, h=h: e.max(out=ts_[:, h, 8:16], in_=cs2[:, h % 2, :]), reads=["p2_cs2_%d" % (h % 2)], writes=["p2_ts%d" % h])
                    tsk = ["p2_ts%d" % h for h in range(8)]
                    t.op('dve', lambda e: e.tensor_tensor(out=ex[:], in0=ts_[:], in1=bc(ts_[:, :, 0:1], [128, 8, 16]), op=ALU.subtract), reads=tsk, writes=["p2_ex"])
                    t.op('act', lambda e: e.activation(out=ex[:], in_=ex[:], func=AF.Exp), reads=["p2_ex"], writes=["p2_ex"])
                    t.op('dve', lambda e: e.tensor_reduce(out=st8[:, 0, :], in_=ex[:], axis=mybir.AxisListType.X, op=ALU.add), reads=["p2_ex"], writes=["p2_st8"])
                    t.op('act', lambda e: e.activation(out=st8[:, 1, :], in_=st8[:, 0, :], func=AF.Ln), reads=["p2_st8"], writes=["p2_st8"])
                    t.op('dve', lambda e: e.tensor_tensor(out=st8[:, 2, :], in0=st8[:, 1, :], in1=ts_[:, :, 0], op=ALU.add), reads=["p2_st8"] + tsk, writes=["p2_st8"])
                    TMv = TM[:].rearrange("p j (h a) -> p j h a", h=8)
                    t.op('pool', lambda e: e.tensor_copy(out=TMv[:, 0], in_=sv4[:, :, 0, :]), reads=svk, writes=["p2_TM0"])
                    t.op('dve', lambda e: e.scalar_tensor_tensor(out=st8[:, 3, :], in0=ts_[:, :, 15], scalar=-1.0e-5, in1=st8[:, 2, :], op0=ALU.add, op1=ALU.subtract),
                         reads=tsk + ["p2_st8"], writes=["p2_st8"])
                    t.op('act', lambda e: e.activation(out=st8[:, 3, :], in_=st8[:, 3, :], func=AF.Exp), reads=["p2_st8"], writes=["p2_st8"])
                    t.op('dve', lambda e: e.tensor_copy(out=TMv[:, 1], in_=bc(st8[:, 3, :].unsqueeze(2), [128, 8, 16])), reads=["p2_st8"], writes=["p2_TM1"])
                    t.op('dve', lambda e: e.tensor_tensor(out=TMv[:, 2], in0=sv4[:, :, 0, :], in1=bc(st8[:, 2, :].unsqueeze(2), [128, 8, 16]), op=ALU.subtract),
                         reads=svk + ["p2_st8"], writes=["p2_TM2"])
                    for j in range(3):
                        t.op('pe', lambda e, j=j: e.transpose(self.ps[TBK][:, j * 128:(j + 1) * 128], TM[:, j, :], self.ident[:]),
                             reads=["p2_TM%d" % j, "ident"], writes=["ps%d" % TBK], skip_self=True)
                    t.op('act', lambda e, tsl=tsl: e.copy(out=SM[:, :, tsl], in_=self.ps[TBK][:, 0:384].rearrange("p (j n) -> p j n", j=3)),
                         reads=["ps%d" % TBK], writes=["p2_SM%d" % st])
                import os as _os
                _old = _os.environ.get("BANKMAP") == "old"
                B0 = (lambda pg: 5) if _old else (lambda pg: 4 + pg)
                B1 = (lambda pg: 6) if _old else (lambda pg: 6 + pg)
                GB = (lambda pg: [7, 4][pg]) if _old else (lambda pg: pg)
                TB = 4 if _old else 7

                def fill_qr(c16):
                    qb_ = c16 % 2
                    for c in range(2):
                        src = bc(qT[:, c::2, c16 * 16:(c16 + 1) * 16].rearrange("p h t -> p t h").unsqueeze(3), [128, 16, 8, 16])
                        dst = QR[:, qb_, c].rearrange("p t (h a) -> p t h a", h=8)
                        t.op('act', lambda e, src=src, dst=dst: e.copy(out=dst, in_=src), reads=["p2_qT"], writes=["p2_qr%d_%d" % (qb_, c)])

                def stageA(g):
                    if g % 4 == 0:
                        fill_qr(g // 4)
                    qb_ = (g // 4) % 2
                    pg = g % 2
                    for s4 in range(4):
                        tl = (g % 4) * 4 + s4
                        t.op('pe', lambda e, tl=tl, s4=s4, qb_=qb_, pg=pg: e.matmul(self.ps[B0(pg)][:, s4 * 128:(s4 + 1) * 128], QR[:, qb_, 0, tl, :], skb[:, 0, :], start=True, stop=True),
                             reads=["p2_qr%d_0" % qb_, "p2_skb"], writes=["ps%d" % B0(pg)], skip_self=True)
                        t.op('pe', lambda e, tl=tl, s4=s4, qb_=qb_, pg=pg: e.matmul(self.ps[B1(pg)][:, s4 * 128:(s4 + 1) * 128], QR[:, qb_, 1, tl, :], skb[:, 1, :], start=True, stop=True),
                             reads=["p2_qr%d_1" % qb_, "p2_skb"], writes=["ps%d" % B1(pg)], skip_self=True)

                def stageB(g):
                    pg = g % 2
                    for s4 in range(4):
                        tt_ = g * 4 + s4
                        sl = pg * 4 + s4
                        smk = "p2_SM%d" % (tt_ // 128)
                        k0 = "ps%d" % B0(pg)
                        k1 = "ps%d" % B1(pg)
                        t.op('dve', lambda e, s4=s4, tt_=tt_, sl=sl, pg=pg: e.tensor_scalar(out=Pt[:, sl, :], in0=self.ps[B0(pg)][:, s4 * 128:(s4 + 1) * 128], scalar1=SM[:, 0, tt_:tt_ + 1], scalar2=None,
                                                                                       op0=ALU.is_equal, op1=ALU.bypass),
                             reads=[k0, smk], writes=["p2_P%d" % sl])
                        t.op('act', lambda e, s4=s4, tt_=tt_, sl=sl, pg=pg: e.activation(out=Et[:, sl, :], in_=self.ps[B1(pg)][:, s4 * 128:(s4 + 1) * 128], func=AF.Exp, bias=SM[:, 2, tt_:tt_ + 1], scale=1.0),
                             reads=[k1, smk], writes=["p2_E%d" % sl])
                        t.op('dve', lambda e, s4=s4, tt_=tt_, sl=sl, pg=pg: e.scalar_tensor_tensor(out=Qt[:, sl, :], in0=Et[:, sl, :], scalar=SM[:, 1, tt_:tt_ + 1],
                                                                                              in1=Et[:, sl, :], op0=ALU.is_ge, op1=ALU.mult),
                             reads=[smk, "p2_E%d" % sl], writes=["p2_Q%d" % sl])

                def stageC(g):
                    pg = g % 2
                    for s4 in range(4):
                        sl = pg * 4 + s4
                        t.op('pe', lambda e, s4=s4, sl=sl, pg=pg: e.matmul(self.ps[GB(pg)][:, s4 * 128:(s4 + 1) * 128], Qt[:, sl, :], Pt[:, sl, :], start=True, stop=True),
                             reads=["p2_Q%d" % sl, "p2_P%d" % sl], writes=["ps%d" % GB(pg)], skip_self=True)
                    t4 = g * 4
                    t.op('act', lambda e, t4=t4, pg=pg: e.copy(out=Gs[:, t4:t4 + 4, :], in_=self.ps[GB(pg)][:].rearrange("p (t i) -> p t i", t=4)),
                         reads=["ps%d" % GB(pg)], writes=["p2_Gs"])

                opt_tok = getattr(self, "opt_tok", True)
                opt_dense = getattr(self, "opt_dense", True)
                if opt_tok:
                    stageA(0)
                for g in range(64):
                    if opt_tok:
                        if g + 1 < 64:
                            stageA(g + 1)
                    else:
                        stageA(g)
                    stageB(g)
                    stageC(g)
                def load_w(ib):
                    wbuf = ib % 2
                    e0 = ib * 512
                    ukeys = ["Ub%d_%d_%d" % (layer, r0, e0 // 2048) for r0 in range(8)]
                    vkeys = ["Vb%d_%d" % (layer, (ib * 512) // 256 + x) for x in range(2)]
                    t.dma(UTs[:, wbuf], Ubv[:, :, e0:e0 + 512], reads=ukeys, writes=["p2_UT%d" % wbuf])
                    t.dma(Vs[:, wbuf], Vbv[:, ib * 4:(ib + 1) * 4, :], reads=vkeys, writes=["p2_V%d" % wbuf])

                def act_mm(i):
                    ib, ii = i // 4, i % 4
                    wbuf = ib % 2
                    abank = 5 + i % 2
                    for k in range(8):
                        t.op('pe', lambda e, k=k, ii=ii, wbuf=wbuf, abank=abank: e.matmul(self.ps[abank][:, 0:256], UTs[:, wbuf, k, ii * 128:(ii + 1) * 128], xb[:, k, :],
                                                                                    start=(k == 0), stop=(k == 7)),
                             reads=["p2_UT%d" % wbuf, "p2_xb"], writes=["ps%d" % abank], skip_self=True)

                load_w(0)
                if opt_dense:
                    act_mm(0)
                for i in range(128):
                    ib, ii = i // 4, i % 4
                    wbuf = ib % 2
                    ab = i % 2
                    abank = 5 + ab
                    if ii == 0 and ib + 1 < 32:
                        load_w(ib + 1)
                    if opt_dense:
                        if i + 1 < 128:
                            act_mm(i + 1)
                    else:
                        act_mm(i)
                    t.op('act', lambda e, ab=ab, abank=abank: e.activation(out=ga[:, ab, :], in_=self.ps[abank][:, 0:256], func=AF.Gelu_apprx_tanh),
                         reads=["ps%d" % abank], writes=["p2_ga%d" % ab])
                    t.op('dve', lambda e, ab=ab, i=i: e.tensor_tensor(out=Hh[:, ab, :], in0=ga[:, ab, :], in1=Gs[:, :, i], op=ALU.mult),
                         reads=["p2_ga%d" % ab, "p2_Gs"], writes=["p2_H%d" % ab])
                    for st in range(2):
                        for fh in range(2):
                            ob = st * 2 + fh
                            t.op('pe', lambda e, st=st, fh=fh, ob=ob, ab=ab, ii=ii, wbuf=wbuf, i=i: e.matmul(
                                self.ps[ob][:], Hh[:, ab, st * 128:(st + 1) * 128], Vs[:, wbuf, ii, fh * 512:(fh + 1) * 512], start=(i == 0), stop=(i == 127)),
                                reads=["p2_H%d" % ab, "p2_V%d" % wbuf], writes=["ps%d" % ob], skip_self=True)
                for st in range(2):
                    for fh in range(2):
                        ob = st * 2 + fh
                        if ob % 2 == 0:
                            t.op('act', lambda e, st=st, fh=fh, ob=ob: e.copy(out=otok[:, st, fh * 512:(fh + 1) * 512], in_=self.ps[ob][:]), reads=["ps%d" % ob], writes=["p2_otok%d" % st])
                        else:
                            t.op('dve', lambda e, st=st, fh=fh, ob=ob: e.tensor_copy(out=otok[:, st, fh * 512:(fh + 1) * 512], in_=self.ps[ob][:]), reads=["ps%d" % ob], writes=["p2_otok%d" % st])
                for st in range(2):
                    for half in range(2):
                        tb = 5 + half
                        for kk in range(4):
                            fk = half * 4 + kk
                            t.op('pe', lambda e, st=st, fk=fk, kk=kk, tb=tb: e.transpose(self.ps[tb][:, kk * 128:(kk + 1) * 128], otok[:, st, fk * 128:(fk + 1) * 128], self.ident[:]),
                                 reads=["p2_otok%d" % st, "ident"], writes=["ps%d" % tb], skip_self=True)
                        if half == 0:
                            t.op('act', lambda e, st=st, half=half, tb=tb: e.copy(out=peT[:, half * 4:(half + 1) * 4, st * 128:(st + 1) * 128],
                                                                                in_=self.ps[tb][:].rearrange("p (k n) -> p k n", k=4)),
                                 reads=["ps%d" % tb], writes=["p2_peT"])
                        else:
                            t.op('dve', lambda e, st=st, half=half, tb=tb: e.tensor_copy(out=peT[:, half * 4:(half + 1) * 4, st * 128:(st + 1) * 128],
                                                                                       in_=self.ps[tb][:].rearrange("p (k n) -> p k n", k=4)),
                                 reads=["ps%d" % tb], writes=["p2_peT"])
                t.dma(PEv[:, :, tok0:tok0 + 256], peT[:], reads=["p2_peT"], writes=["PEo_%d" % blk])
            t.barrier()

    def peer_p3(self, layer, xin, xout):
        nc, t = self.nc, self.t
        NB = (self.peer_nblk + 1) // 2 if hasattr(self, "peer_nblk") else 16
        with ExitStack() as ph:
            sb = lambda name, shape, dt=F32: ph.enter_context(self.sbuf(name, list(shape), dt))
            self.mk_eps(ph)
            wpg = self.load_w_bf16(ph, "p3_wpg", self.w_ple_gate[layer], 8, D, eng_cast='mix')
            wpp = self.load_w_bf16(ph, "p3_wpp", self.w_ple_proj[layer], 2, D, eng_cast='mix')
            xf = sb("p3_xf", [128, 8, 512])
            pe = sb("p3_pe", [128, 8, 512])
            pf = sb("p3_pf", [128, 2, 512])
            pb = sb("p3_pb", [128, 2, 512], BF16)
            z = sb("p3_z", [128, 8, 512])
            tmp = {'zc': sb("p3_zc", [128, 8, 512]), 'sq': sb("p3_sq", [128, 8, 512]), 'sd': sb("p3_sd", [128, 512])}
            x2 = sb("p3_x2", [128, 8, 512])
            x2b = sb("p3_x2b", [128, 8, 512], BF16)
            sg_ = sb("p3_sg", [128, 2, 512])
            xo = sb("p3_xo", [128, 8, 512])
            xv = xin.rearrange("(k p) n -> p k n", p=128)
            PEv = self.PEo.rearrange("(k p) n -> p k n", p=128)
            pv = self.pT[layer].rearrange("(k p) n -> p k n", p=128)
            ov = xout.rearrange("(k p) n -> p k n", p=128)
            for blk in range(NB):
                tok0 = blk * 512
                t.dma(xf[:], xv[:, :, tok0:tok0 + 512], reads=["X1"], writes=["p3_xf"])
                t.dma(pe[:], PEv[:, :, tok0:tok0 + 512], reads=["PEo_%d" % (2 * blk), "PEo_%d" % (2 * blk + 1)], writes=["p3_pe"])
                t.dma(pf[:], pv[:, :, tok0:tok0 + 512], writes=["p3_pf"])
                t.op('pool', lambda e: e.tensor_copy(out=pb[:], in_=pf[:]), reads=["p3_pf"], writes=["p3_pb"])
                t.op('dve', lambda e: e.scalar_tensor_tensor(out=z[:], in0=xf[:], scalar=ALPHA, in1=pe[:], op0=ALU.mult, op1=ALU.add),
                     reads=["p3_xf", "p3_pe"], writes=["p3_z"])
                self.ln_block(z, "p3_z", layer, 1, x2, "p3_x2", 512, tmp, 0, 1)
                t.op('pool', lambda e: e.tensor_copy(out=x2b[:], in_=x2[:]), reads=["p3_x2"], writes=["p3_x2b"])
                for fo in range(8):
                    bg, bp = 2 + (fo % 2) * 2, 3 + (fo % 2) * 2
                    sgi = fo % 2
                    for k in range(8):
                        t.op('pe', lambda e, k=k, fo=fo, bg=bg: e.matmul(self.ps[bg][:], wpg[:, k, fo * 128:(fo + 1) * 128], x2b[:, k, :], start=(k == 0), stop=(k == 7)),
                             reads=["p3_wpg", "p3_x2b"], writes=["ps%d" % bg], skip_self=True)
                    for k in range(2):
                        t.op('pe', lambda e, k=k, fo=fo, bp=bp: e.matmul(self.ps[bp][:], wpp[:, k, fo * 128:(fo + 1) * 128], pb[:, k, :], start=(k == 0), stop=(k == 1)),
                             reads=["p3_wpp", "p3_pb"], writes=["ps%d" % bp], skip_self=True)
                    t.op('act', lambda e, bg=bg, sgi=sgi: e.activation(out=sg_[:, sgi, :], in_=self.ps[bg][:], func=AF.Sigmoid), reads=["ps%d" % bg], writes=["p3_sg%d" % sgi])
                    t.op('dve', lambda e, bp=bp, sgi=sgi: e.tensor_tensor(out=sg_[:, sgi, :], in0=self.ps[bp][:], in1=sg_[:, sgi, :], op=ALU.mult),
                         reads=["ps%d" % bp, "p3_sg%d" % sgi], writes=["p3_sg%d" % sgi])
                    t.op('pool', lambda e, fo=fo, sgi=sgi: e.tensor_tensor(out=xo[:, fo, :], in0=x2[:, fo, :], in1=sg_[:, sgi, :], op=ALU.add),
                         reads=["p3_x2", "p3_sg%d" % sgi], writes=["p3_xo"])
                t.dma(ov[:, :, tok0:tok0 + 512], xo[:], reads=["p3_xo"], writes=["XOUT%d_%d" % (layer, blk)])
            t.barrier()

    def phase_rg(self):
        self.rg_p1()
        self.rg_p2()
        self.rg_p3()

    def rg_p1(self):
        nc, t = self.nc, self.t
        with ExitStack() as ph:
            sb = lambda name, shape, dt=F32: ph.enter_context(self.sbuf(name, list(shape), dt))
            win = self.load_w_bf16(ph, "r1_win", self.w_rg_in, 8, 2 * D, eng_cast='mix')
            xf = sb("r1_xf", [128, 2, 8, 512])
            xb = sb("r1_xb", [128, 2, 8, 512], BF16)
            gg = sb("r1_gg", [128, 2, 8, 512], BF16)
            rr = sb("r1_rr", [128, 2, 8, 512])
            xv = self.XL1.rearrange("(k p) n -> p k n", p=128)
            ggv = self.RGg.rearrange("(k p) n -> p k n", p=128)
            rrv = self.RGr.rearrange("(k p) n -> p k n", p=128)
            for blk in range(16):
                b = blk % 2
                tok0 = blk * 512
                t.dma(xf[:, b], xv[:, :, tok0:tok0 + 512], writes=["r1_xf%d" % b])
                t.op('pool', lambda e, b=b: e.tensor_copy(out=xb[:, b], in_=xf[:, b]), reads=["r1_xf%d" % b], writes=["r1_xb%d" % b])
                for fo in range(16):
                    bank = fo % 4
                    for k in range(8):
                        t.op('pe', lambda e, k=k, fo=fo, b=b, bank=bank: e.matmul(self.ps[bank][:], win[:, k, fo * 128:(fo + 1) * 128], xb[:, b, k, :],
                                                                             start=(k == 0), stop=(k == 7)),
                             reads=["r1_win", "r1_xb%d" % b], writes=["ps%d" % bank], skip_self=True)
                    if fo < 8:
                        t.op('act', lambda e, fo=fo, b=b, bank=bank: e.activation(out=gg[:, b, fo, :], in_=self.ps[bank][:], func=AF.Gelu_apprx_tanh),
                             reads=["ps%d" % bank], writes=["r1_gg%d" % b])
                    else:
                        t.op('dve', lambda e, fo=fo, b=b, bank=bank: e.tensor_copy(out=rr[:, b, fo - 8, :], in_=self.ps[bank][:]), reads=["ps%d" % bank], writes=["r1_rr%d" % b])
                t.dma(ggv[:, :, tok0:tok0 + 512], gg[:, b], reads=["r1_gg%d" % b], writes=["RGg_%d" % blk])
                t.dma(rrv[:, :, tok0:tok0 + 512], rr[:, b], reads=["r1_rr%d" % b], writes=["RGr_%d" % blk])
            t.barrier()

    def rg_p2(self):
        nc, t = self.nc, self.t
        LM = 4096
        with ExitStack() as ph:
            sb = lambda name, shape, dt=F32: ph.enter_context(self.sbuf(name, list(shape), dt))
            cw = sb("r2_cw", [128, 8, 4]); cbias = sb("r2_cbias", [128, 8])
            bga = sb("r2_bga", [128, 2, 8]); bgx = sb("r2_bgx", [128, 2, 8]); lam = sb("r2_lam", [128, 2, 8])
            sp8 = sb("r2_sp8", [128, 2, 8]); sp16 = sb("r2_sp16", [128, 2, 8])
            for dst, src in [(cw, self.rg_convw), (cbias, self.rg_convb), (bga, self.rg_bga), (bgx, self.rg_bgx), (lam, self.rg_lam)]:
                t.dma(dst[:], src, writes=[dst.name])
            t.op('act', lambda e: e.activation(out=sp8[:], in_=lam[:], func=AF.Exp, scale=-1.0), reads=[lam.name], writes=["r2_sp8"])
            t.op('act', lambda e: e.activation(out=sp8[:], in_=sp8[:], func=AF.Ln, bias=1.0, scale=1.0), reads=["r2_sp8"], writes=["r2_sp8"])
            t.op('dve', lambda e: e.tensor_scalar(out=sp16[:], in0=sp8[:], scalar1=-16.0, scalar2=None, op0=ALU.mult, op1=ALU.bypass), reads=["r2_sp8"], writes=["r2_sp16"])
            t.op('dve', lambda e: e.tensor_scalar(out=sp8[:], in0=sp8[:], scalar1=-8.0, scalar2=None, op0=ALU.mult, op1=ALU.bypass), reads=["r2_sp8", "r2_sp16"], writes=["r2_sp8"])
            wst = sb("r2_wst", [128, 2, 256])
            wgt = sb("r2_wgt", [128, 2, 2, 4, 2, 256], BF16)
            for gi, src in enumerate([self.rg_wga, self.rg_wgx]):
                for d_ in range(2):
                    for h in range(4):
                        t.dma(wst[:], src[d_, h].rearrange("(i p) o -> p i o", p=128), writes=["r2_wst"])
                        t.op('pool', lambda e, gi=gi, d_=d_, h=h: e.tensor_copy(out=wgt[:, gi, d_, h], in_=wst[:]), reads=["r2_wst"], writes=["r2_wgt"])
            rp = sb("r2_rp", [128, 2, LM + 3])
            cc = sb("r2_cc", [128, 2, LM])
            cb = sb("r2_cb", [128, 2, LM], BF16)
            A = sb("r2_A", [128, LM]); B = sb("r2_B", [128, LM])
            HF = sb("r2_HF", [128, LM]); HB = sb("r2_HB", [128, LM])
            gg = sb("r2_gg", [128, LM], BF16); yy = sb("r2_yy", [128, LM], BF16)
            rg_ = sb("r2_rg", [128, 2, 512]); ig_ = sb("r2_ig", [128, 2, 512]); a2_ = sb("r2_a2", [128, 2, 512]); tm_ = sb("r2_tm", [128, 2, 512])
            for si, (s0, L) in enumerate(SEGS):
                for h in range(4):
                    for ct in range(2):
                        ch = 2 * h + ct
                        t.op('pool', lambda e, ct=ct, L=L: e.memset(rp[:, ct, 0:1], 0.0), writes=["r2_rp%d" % ct])
                        t.op('pool', lambda e, ct=ct, L=L: e.memset(rp[:, ct, L + 1:L + 3], 0.0), reads=["r2_rp%d" % ct], writes=["r2_rp%d" % ct])
                        t.dma(rp[:, ct, 1:L + 1], self.RGr[ch * 128:(ch + 1) * 128, s0:s0 + L], reads=["r2_rp%d" % ct], writes=["r2_rp%d" % ct])
                        t.op('dve', lambda e, ct=ct, ch=ch, L=L: e.tensor_scalar(out=cc[:, ct, 0:L], in0=rp[:, ct, 0:L], scalar1=cw[:, ch, 0:1], scalar2=cbias[:, ch:ch + 1],
                                                                            op0=ALU.mult, op1=ALU.add), reads=["r2_rp%d" % ct, cw.name, cbias.name], writes=["r2_cc%d" % ct])
                        for k in range(1, 4):
                            t.op('dve', lambda e, ct=ct, ch=ch, L=L, k=k: e.scalar_tensor_tensor(out=cc[:, ct, 0:L], in0=rp[:, ct, k:k + L], scalar=cw[:, ch, k:k + 1], in1=cc[:, ct, 0:L],
                                                                                           op0=ALU.mult, op1=ALU.add), reads=["r2_rp%d" % ct, cw.name, "r2_cc%d" % ct], writes=["r2_cc%d" % ct])
                        t.op('pool', lambda e, ct=ct, L=L: e.tensor_copy(out=cb[:, ct, 0:L], in_=cc[:, ct, 0:L]), reads=["r2_cc%d" % ct], writes=["r2_cb%d" % ct])
                    for oh in range(2):
                        ch = 2 * h + oh
                        for d_ in range(2):
                            Hd = HF if d_ == 0 else HB
                            for c0 in range(0, L, 512):
                                pi = (c0 // 512) % 2
                                ba, bx = 2 * pi, 2 * pi + 1
                                for ih in range(2):
                                    t.op('pe', lambda e, ih=ih, d_=d_, h=h, oh=oh, c0=c0, ba=ba: e.matmul(self.ps[ba][:], wgt[:, 0, d_, h, ih, oh * 128:(oh + 1) * 128], cb[:, ih, c0:c0 + 512],
                                                                                                    start=(ih == 0), stop=(ih == 1)),
                                         reads=["r2_wgt", "r2_cb0", "r2_cb1"], writes=["ps%d" % ba], skip_self=True)
                                for ih in range(2):
                                    t.op('pe', lambda e, ih=ih, d_=d_, h=h, oh=oh, c0=c0, bx=bx: e.matmul(self.ps[bx][:], wgt[:, 1, d_, h, ih, oh * 128:(oh + 1) * 128], cb[:, ih, c0:c0 + 512],
                                                                                                    start=(ih == 0), stop=(ih == 1)),
                                         reads=["r2_wgt", "r2_cb0", "r2_cb1"], writes=["ps%d" % bx], skip_self=True)
                                t.op('act', lambda e, pi=pi, ba=ba, d_=d_, ch=ch: e.activation(out=rg_[:, pi, :], in_=self.ps[ba][:], func=AF.Sigmoid, bias=bga[:, d_, ch:ch + 1], scale=1.0),
                                     reads=["ps%d" % ba, bga.name], writes=["r2_rg%d" % pi])
                                t.op('act', lambda e, pi=pi, bx=bx, d_=d_, ch=ch: e.activation(out=ig_[:, pi, :], in_=self.ps[bx][:], func=AF.Sigmoid, bias=bgx[:, d_, ch:ch + 1], scale=1.0),
                                     reads=["ps%d" % bx, bgx.name], writes=["r2_ig%d" % pi])
                                t.op('act', lambda e, pi=pi, c0=c0, d_=d_, ch=ch: e.activation(out=A[:, c0:c0 + 512], in_=rg_[:, pi, :], func=AF.Exp, scale=sp8[:, d_, ch:ch + 1]),
                                     reads=["r2_rg%d" % pi, "r2_sp8"], writes=["r2_A"])
                                t.op('act', lambda e, pi=pi, d_=d_, ch=ch: e.activation(out=a2_[:, pi, :], in_=rg_[:, pi, :], func=AF.Exp, scale=sp16[:, d_, ch:ch + 1]),
                                     reads=["r2_rg%d" % pi, "r2_sp16"], writes=["r2_a2%d" % pi])
                                t.op('dve', lambda e, pi=pi: e.tensor_scalar(out=a2_[:, pi, :], in0=a2_[:, pi, :], scalar1=-1.0, scalar2=1.0, op0=ALU.mult, op1=ALU.add),
                                     reads=["r2_a2%d" % pi], writes=["r2_a2%d" % pi])
                                t.op('act', lambda e, pi=pi: e.activation(out=a2_[:, pi, :], in_=a2_[:, pi, :], func=AF.Sqrt), reads=["r2_a2%d" % pi], writes=["r2_a2%d" % pi])
                                t.op('pool', lambda e, pi=pi, oh=oh, c0=c0: e.tensor_tensor(out=tm_[:, pi, :], in0=ig_[:, pi, :], in1=cc[:, oh, c0:c0 + 512], op=ALU.mult),
                                     reads=["r2_ig%d" % pi, "r2_cc%d" % oh], writes=["r2_tm%d" % pi])
                                t.op('dve', lambda e, pi=pi, c0=c0: e.tensor_tensor(out=B[:, c0:c0 + 512], in0=tm_[:, pi, :], in1=a2_[:, pi, :], op=ALU.mult),
                                     reads=["r2_tm%d" % pi, "r2_a2%d" % pi], writes=["r2_B"])
                            if d_ == 0:
                                t.op('dve', lambda e, L=L: e.tensor_tensor_scan(out=HF[:, 0:L], data0=A[:, 0:L], data1=B[:, 0:L], initial=0.0, op0=ALU.mult, op1=ALU.add),
                                     reads=["r2_A", "r2_B"], writes=["r2_HF"])
                            else:
                                t.op('dve', lambda e, L=L: e.tensor_tensor_scan(out=HB[:, 0:L][:, ::-1], data0=A[:, 0:L][:, ::-1], data1=B[:, 0:L][:, ::-1], initial=0.0, op0=ALU.mult, op1=ALU.add),
                                     reads=["r2_A", "r2_B"], writes=["r2_HB"])
                        t.dma(gg[:, 0:L], self.RGg[ch * 128:(ch + 1) * 128, s0:s0 + L], writes=["r2_gg"])
                        t.op('pool', lambda e, L=L: e.tensor_tensor(out=HF[:, 0:L], in0=HF[:, 0:L], in1=HB[:, 0:L], op=ALU.add), reads=["r2_HF", "r2_HB"], writes=["r2_HF"])
                        t.op('dve', lambda e, L=L: e.tensor_tensor(out=yy[:, 0:L], in0=HF[:, 0:L], in1=gg[:, 0:L], op=ALU.mult), reads=["r2_HF", "r2_gg"], writes=["r2_yy"])
                        t.dma(self.RGy[ch * 128:(ch + 1) * 128, s0:s0 + L], yy[:, 0:L], reads=["r2_yy"], writes=["RGy_%d_%d" % (si, ch)])
            t.barrier()

    def rg_p3(self):
        nc, t = self.nc, self.t
        with ExitStack() as ph:
            sb = lambda name, shape, dt=F32: ph.enter_context(self.sbuf(name, list(shape), dt))
            self.mk_eps(ph)
            wo = self.load_w_bf16(ph, "r3_wo", self.w_rg_out, 8, D, eng_cast='mix')
            yb = sb("r3_yb", [128, 8, 512], BF16)
            xf = sb("r3_xf", [128, 8, 512])
            z = sb("r3_z", [128, 8, 512])
            tmp = {'zc': sb("r3_zc", [128, 8, 512]), 'sq': sb("r3_sq", [128, 8, 512]), 'sd': sb("r3_sd", [128, 512])}
            xo = sb("r3_xo", [128, 8, 512])
            xv = self.XL1.rearrange("(k p) n -> p k n", p=128)
            yv = self.RGy.rearrange("(k p) n -> p k n", p=128)
            X1v = self.X1.rearrange("(k p) n -> p k n", p=128)
            for blk in range(16):
                tok0 = blk * 512
                t.dma(yb[:], yv[:, :, tok0:tok0 + 512], writes=["r3_yb"])
                t.dma(xf[:], xv[:, :, tok0:tok0 + 512], writes=["r3_xf"])
                for fo in range(8):
                    bank = 2 + fo % 4
                    for k in range(8):
                        t.op('pe', lambda e, k=k, fo=fo, bank=bank: e.matmul(self.ps[bank][:], wo[:, k, fo * 128:(fo + 1) * 128], yb[:, k, :], start=(k == 0), stop=(k == 7)),
                             reads=["r3_wo", "r3_yb"], writes=["ps%d" % bank], skip_self=True)
                    t.op('dve', lambda e, fo=fo, bank=bank: e.scalar_tensor_tensor(out=z[:, fo, :], in0=xf[:, fo, :], scalar=ALPHA, in1=self.ps[bank][:], op0=ALU.mult, op1=ALU.add),
                         reads=["r3_xf", "ps%d" % bank], writes=["r3_z"])
                self.ln_block(z, "r3_z", 1, 0, xo, "r3_xo", 512, tmp, 0, 1)
                t.dma(X1v[:, :, tok0:tok0 + 512], xo[:], reads=["r3_xo"], writes=["X1"])
            t.barrier()


def _consts():
    c = {}
    c["c_ident"] = np.eye(128, dtype=np.float32)
    c["c_onesm"] = np.full((128, 128), 1.0 / D, dtype=np.float32)
    s = np.arange(128) // 16
    c["c_maskf"] = (s[:, None] <= s[None, :]).astype(np.float32)
    c["c_maskb"] = (s[:, None] >= s[None, :]).astype(np.float32)
    m = np.ones((128, 1024), dtype=np.float32)
    m[0:64, [0, 512, 768]] = 0.0
    m[64:128, [511, 767, 1023]] = 0.0
    c["c_segmask"] = m
    return c


def _shared_weights(inp):
    f = lambda a: np.ascontiguousarray(np.asarray(a, dtype=np.float32))
    w = dict(_consts())
    w["s5_w_in"] = f(inp["s5_w_in"][0])
    w["s5_w_glu"] = f(inp["s5_w_glu"][0])
    w["s5_lamre"] = f(inp["s5_lam_re"][0].transpose(0, 2, 1).reshape(128, 64))
    w["s5_lamim"] = f(inp["s5_lam_im"][0].transpose(0, 2, 1).reshape(128, 64))
    w["s5_lstep"] = f(np.broadcast_to(inp["s5_log_step"][0][:, None, :], (2, 64, 64)).reshape(128, 64))
    w["s5_bre"] = f(inp["s5_b_re"][0].transpose(0, 2, 1, 3).reshape(128, 64, 16))
    w["s5_bim"] = f(inp["s5_b_im"][0].transpose(0, 2, 1, 3).reshape(128, 64, 16))
    w["s5_ctre"] = f(inp["s5_c_re"][0].transpose(0, 3, 1, 2).reshape(128, 64, 16))
    w["s5_ctim"] = f(inp["s5_c_im"][0].transpose(0, 3, 1, 2).reshape(128, 64, 16))
    d = np.asarray(inp["s5_d"][0]).reshape(64, 16)
    w["s5_dpk"] = f(np.broadcast_to(d.T[None, :, :], (8, 16, 64)).reshape(128, 64))
    w["rg_w_in"] = f(inp["rg_w_in"][0])
    w["rg_convw"] = f(inp["rg_conv_w"][0].reshape(4, 8, 128).transpose(2, 1, 0))
    w["rg_convb"] = f(inp["rg_conv_b"][0].reshape(8, 128).T)
    w["rg_wga"] = f(inp["rg_w_gate_a"][0])
    w["rg_wgx"] = f(inp["rg_w_gate_x"][0])
    w["rg_bga"] = f(inp["rg_b_gate_a"][0].reshape(2, 8, 128).transpose(2, 0, 1))
    w["rg_bgx"] = f(inp["rg_b_gate_x"][0].reshape(2, 8, 128).transpose(2, 0, 1))
    w["rg_lam"] = f(inp["rg_lambda"][0].reshape(2, 8, 128).transpose(2, 0, 1))
    w["rg_w_out"] = f(inp["rg_w_out"][0])
    ln = np.stack([np.asarray(inp[k]) for k in ("ln1_g", "ln1_b", "ln2_g", "ln2_b")], 0)
    w["ln_par"] = f(ln.reshape(4, 2, 8, 128).transpose(3, 0, 1, 2))
    w["peer_w_q"] = f(inp["peer_w_q"])
    w["peer_skT"] = f(np.asarray(inp["peer_subkeys"]).transpose(0, 1, 3, 2))
    w["peer_uT"] = f(np.asarray(inp["peer_u"]).transpose(0, 2, 1))
    w["peer_v"] = f(inp["peer_v"])
    w["ple_w_proj"] = f(inp["ple_w_proj"])
    w["ple_w_gate"] = f(inp["ple_w_gate"])
    return w


def _core_acts(inp, core):
    xp = np.asarray(inp["x_prompt"][core])
    xs = np.asarray(inp["x_sample"][2 * core:2 * core + 2]).reshape(4096, D)
    x = np.concatenate([xp, xs], 0)
    pp = np.asarray(inp["p_prompt"][:, core])
    ps = np.asarray(inp["p_sample"][:, 2 * core:2 * core + 2]).reshape(2, 4096, 256)
    p = np.concatenate([pp, ps], 1)
    return {"xT": np.ascontiguousarray(x.T.astype(np.float32)),
            "pT": np.ascontiguousarray(p.transpose(0, 2, 1).astype(np.float32))}


_CACHE = {}


def kernel(**inputs):
    if "nc" not in _CACHE:
        k = Ker()
        _CACHE["nc"] = k.build()
        _CACHE["in_names"] = list(k.in_names)
    nc = _CACHE["nc"]
    names = _CACHE["in_names"]
    w = _shared_weights(inputs)
    in_maps = []
    for core in range(8):
        m = dict(w)
        m.update(_core_acts(inputs, core))
        in_maps.append({n: m[n] for n in names})
    res = run_bass_kernel_spmd(nc, in_maps, core_ids=list(range(8)))
    yp = np.empty((8, 4096, D), dtype=np.float32)
    ys = np.empty((16, 2048, D), dtype=np.float32)
    for core in range(8):
        y = np.asarray(res.results[core]["yT"]).T
        yp[core] = y[0:4096]
        ys[2 * core] = y[4096:6144]
        ys[2 * core + 1] = y[6144:8192]
    return (yp, ys)
```

```python
from contextlib import ExitStack
import math
import numpy as np
import concourse.bass as bass
import concourse.mybir as mybir
from concourse.bass_utils import run_bass_kernel_spmd

F32 = mybir.dt.float32
BF16 = mybir.dt.bfloat16
I32 = mybir.dt.int32
AF = mybir.ActivationFunctionType
ALU = mybir.AluOpType

NTOK = 8192
D = 1024
ALPHA = 4.0 ** 0.25
LN_EPS = 1e-5
SEGS = [(0, 4096), (4096, 2048), (6144, 2048)]
TWO_PI = 2.0 * math.pi


class Trk:
    ENGS = ['pe', 'act', 'dve', 'pool', 'sp']

    def __init__(self, nc, stack, nslots=12):
        self.nc = nc
        self.e = {'pe': nc.tensor, 'act': nc.scalar, 'dve': nc.vector, 'pool': nc.gpsimd, 'sp': nc.sync}
        self.sem = {}
        self.cnt = {}
        for n in self.ENGS:
            self.sem[n] = stack.enter_context(nc.semaphore('s_' + n))
            self.cnt[n] = 0
        self.slots = ['d%d' % i for i in range(nslots)]
        for s in self.slots:
            self.sem[s] = stack.enter_context(nc.semaphore('s_' + s))
            self.cnt[s] = 0
        self.rr = 0
        self.seen = {n: {} for n in self.ENGS}
        self.lw = {}
        self.lr = {}
        self.ninst = 0

    def _deps(self, reads, writes):
        need = {}
        for k in reads:
            for e, c in self.lw.get(k, {}).items():
                if c > need.get(e, 0):
                    need[e] = c
        for k in writes:
            for e, c in self.lw.get(k, {}).items():
                if c > need.get(e, 0):
                    need[e] = c
            for e, c in self.lr.get(k, {}).items():
                if c > need.get(e, 0):
                    need[e] = c
        return need

    def _wait(self, eng, need, skip_self=False):
        for e, c in need.items():
            if skip_self and e == eng:
                continue
            if self.seen[eng].get(e, 0) >= c:
                continue
            self.e[eng].wait_ge(self.sem[e], c)
            self.seen[eng][e] = c

    def _record(self, who, c, reads, writes):
        for k in writes:
            self.lw[k] = {who: c}
            self.lr[k] = {}
        for k in reads:
            self.lr.setdefault(k, {})[who] = c

    def op(self, eng, fn, reads=(), writes=(), skip_self=False):
        need = self._deps(reads, writes)
        self._wait(eng, need, skip_self)
        ins = fn(self.e[eng])
        self.cnt[eng] += 1
        ins.then_inc(self.sem[eng], 1)
        self._record(eng, self.cnt[eng], reads, writes)
        self.ninst += 1

    def dma(self, out, in_, reads=(), writes=(), eng='sp'):
        need = self._deps(reads, writes)
        slot = self.slots[self.rr]
        self.rr = (self.rr + 1) % len(self.slots)
        if self.cnt[slot] > 0:
            need[slot] = max(need.get(slot, 0), self.cnt[slot])
        self._wait(eng, need)
        ins = self.e[eng].dma_start(out=out, in_=in_)
        self.cnt[slot] += 16
        ins.then_inc(self.sem[slot], 16)
        self._record(slot, self.cnt[slot], reads, writes)
        self.ninst += 1

    def barrier(self):
        self.min_rem = min(getattr(self, "min_rem", 1 << 30), self.nc.sbuf_bytes_remaining)
        allc = {k: v for k, v in self.cnt.items() if v > 0}
        for eng in self.ENGS:
            self._wait(eng, dict(allc), skip_self=True)

    def finish(self):
        allc = {k: v for k, v in self.cnt.items() if v > 0}
        self._wait('sp', dict(allc), skip_self=True)


def bc(ap, shape):
    return ap.to_broadcast(list(shape))


class Ker:
    def __init__(self, dbg_out=(), dbg_in=(), phases=None):
        self.dbg_out = set(dbg_out)
        self.dbg_in = set(dbg_in)
        self.phases = phases
        self.nc = bass.Bass("TRN2", target_bir_lowering=False)
        self.in_names = []
        self.out_names = []
        self.tmp_id = 0

    def sbuf(self, name, shape, dt):
        self.tmp_id += 1
        return self.nc.sbuf_tensor("%s_u%d" % (name, self.tmp_id), list(shape), dt)

    def din(self, name, shape, dt=F32):
        self.in_names.append(name)
        return self.nc.dram_tensor(name, list(shape), dt, kind="ExternalInput").ap()

    def dout(self, name, shape, dt=F32):
        self.out_names.append(name)
        return self.nc.dram_tensor(name, list(shape), dt, kind="ExternalOutput").ap()

    def scratch(self, name, shape, dt=F32):
        if name in self.dbg_in:
            return self.din(name, shape, dt)
        if name in self.dbg_out:
            return self.dout(name, shape, dt)
        return self.nc.dram_tensor(name, list(shape), dt, kind="Internal").ap()

    def __getattr__(self, attr):
        specs = self.__dict__.get("specs", {})
        if attr in specs:
            kind, name, shape = specs[attr]
            ap = self.din(name, shape)
            self.__dict__[attr] = ap
            return ap
        raise AttributeError(attr)

    def on(self, ph):
        return self.phases is None or ph in self.phases

    def build(self):
        nc = self.nc
        with ExitStack() as st:
            self.st = st
            self.t = Trk(nc, st)
            t = self.t
            self.specs = {
                "xT": ("din", "xT", [D, NTOK]),
                "pT": ("din", "pT", [2, 256, NTOK]),
                "yT": ("dout", "yT", [D, NTOK]),
                "c_ident": ("din", "c_ident", [128, 128]),
                "c_onesm": ("din", "c_onesm", [128, 128]),
                "c_maskf": ("din", "c_maskf", [128, 128]),
                "c_maskb": ("din", "c_maskb", [128, 128]),
                "c_segmask": ("din", "c_segmask", [128, 1024]),
                "w_s5_in": ("din", "s5_w_in", [D, D]),
                "w_s5_glu": ("din", "s5_w_glu", [D, 2 * D]),
                "s5_lamre": ("din", "s5_lamre", [128, 64]),
                "s5_lamim": ("din", "s5_lamim", [128, 64]),
                "s5_lstep": ("din", "s5_lstep", [128, 64]),
                "s5_bre": ("din", "s5_bre", [128, 64, 16]),
                "s5_bim": ("din", "s5_bim", [128, 64, 16]),
                "s5_ctre": ("din", "s5_ctre", [128, 64, 16]),
                "s5_ctim": ("din", "s5_ctim", [128, 64, 16]),
                "s5_dpk": ("din", "s5_dpk", [128, 64]),
                "w_rg_in": ("din", "rg_w_in", [D, 2 * D]),
                "rg_convw": ("din", "rg_convw", [128, 8, 4]),
                "rg_convb": ("din", "rg_convb", [128, 8]),
                "rg_wga": ("din", "rg_wga", [2, 4, 256, 256]),
                "rg_wgx": ("din", "rg_wgx", [2, 4, 256, 256]),
                "rg_bga": ("din", "rg_bga", [128, 2, 8]),
                "rg_bgx": ("din", "rg_bgx", [128, 2, 8]),
                "rg_lam": ("din", "rg_lam", [128, 2, 8]),
                "w_rg_out": ("din", "rg_w_out", [D, D]),
                "ln_par": ("din", "ln_par", [128, 4, 2, 8]),
                "w_peer_q": ("din", "peer_w_q", [2, D, 2 * D]),
                "peer_skT": ("din", "peer_skT", [2, 2, 128, 128]),
                "peer_uT": ("din", "peer_uT", [2, D, 16384]),
                "peer_v": ("din", "peer_v", [2, 16384, D]),
                "w_ple_proj": ("din", "ple_w_proj", [2, 256, D]),
                "w_ple_gate": ("din", "ple_w_gate", [2, D, D]),
            }
            self.yT = self.dout("yT", [D, NTOK]) if self.on("peer1") else None
            self.UpD = self.scratch("UpD", [64, 128, 1024], BF16)
            self.HpD = self.scratch("HpD", [64, 128, 1024], BF16)
            self.X1 = self.scratch("X1", [D, NTOK])
            self.XL1 = self.scratch("XL1", [D, NTOK])
            self.RGr = self.scratch("RGr", [D, NTOK])
            self.RGg = self.scratch("RGg", [D, NTOK], BF16)
            self.RGy = self.scratch("RGy", [D, NTOK], BF16)
            self.XB = self.scratch("XB", [D, NTOK], BF16)
            self.QT = self.scratch("QT", [2 * D, NTOK], BF16)
            self.PEo = self.scratch("PEo", [D, NTOK])
            self.Ub = self.scratch("Ub", [2, D, 16384], BF16)
            self.Vb = self.scratch("Vb", [2, 16384, D], BF16)
            self.ident = st.enter_context(self.sbuf("ident", [128, 128], F32))
            self.identb = st.enter_context(self.sbuf("identb", [128, 128], BF16))
            self.onesm = st.enter_context(self.sbuf("onesm", [128, 128], F32))
            self.lnp = st.enter_context(self.sbuf("lnp", [128, 4, 2, 8], F32))
            self.ps = [st.enter_context(nc.psum_tensor("ps%d" % i, [128, 512], F32)) for i in range(8)]
            t.dma(self.ident[:], self.c_ident, writes=["ident"])
            t.dma(self.onesm[:], self.c_onesm, writes=["onesm"])
            t.dma(self.lnp[:], self.ln_par, writes=["lnp"])
            t.op('dve', lambda e: e.tensor_copy(out=self.identb[:], in_=self.ident[:]), reads=["ident"], writes=["identb"])

            if self.on("tabcast"):
                self.phase_tabcast()
            if self.on("s5a"):
                self.phase_s5a()
            if self.on("s5b"):
                self.phase_s5b()
            if self.on("s5c"):
                self.phase_s5c()
            if self.on("peer0"):
                self.phase_peer(0, self.X1, self.XL1)
            if self.on("rg"):
                self.phase_rg()
            if self.on("peer1"):
                self.phase_peer(1, self.X1, self.yT)
            t.barrier()
            t.finish()
        return nc

    def load_w_bf16(self, ph, name, dram_ap, kt, ncols, eng_cast='pool'):
        nc, t = self.nc, self.t
        wb = ph.enter_context(self.sbuf(name, [128, kt, ncols], BF16))
        stg = ph.enter_context(self.sbuf(name + "_stg", [128, 2, 2048], F32))
        i = 0
        for k in range(kt):
            for c0 in range(0, ncols, 2048):
                cw = min(2048, ncols - c0)
                b = i % 2
                t.dma(stg[:, b, 0:cw], dram_ap[k * 128:(k + 1) * 128, c0:c0 + cw],
                      writes=[name + "_stg%d" % b])
                eng = ['pool', 'act'][i % 2] if eng_cast == 'mix' else eng_cast
                if eng == 'act':
                    t.op('act', lambda e, b=b, k=k, c0=c0, cw=cw: e.copy(out=wb[:, k, c0:c0 + cw], in_=stg[:, b, 0:cw]),
                         reads=[name + "_stg%d" % b], writes=[name])
                else:
                    t.op(eng, lambda e, b=b, k=k, c0=c0, cw=cw: e.tensor_copy(out=wb[:, k, c0:c0 + cw], in_=stg[:, b, 0:cw]),
                         reads=[name + "_stg%d" % b], writes=[name])
                i += 1
        return wb

    def ln_block(self, z, zkey, layer, which, out, outkey, N, tmp, pbank_a, pbank_b):
        t = self.t
        pm = self.ps[pbank_a]
        pv = self.ps[pbank_b]
        ka, kb = "ps%d" % pbank_a, "ps%d" % pbank_b
        for k in range(8):
            t.op('pe', lambda e, k=k: e.matmul(pm[:, 0:N], self.onesm[:], z[:, k, :], start=(k == 0), stop=(k == 7)),
                 reads=[zkey, "onesm"], writes=[ka], skip_self=True)
        zc, sq, sd = tmp['zc'], tmp['sq'], tmp['sd']
        t.op('dve', lambda e: e.tensor_tensor(out=zc[:], in0=z[:], in1=bc(pm[:, 0:N].unsqueeze(1), [128, 8, N]), op=ALU.subtract),
             reads=[zkey, ka], writes=[zc.name])
        t.op('act', lambda e: e.activation(out=sq[:], in_=zc[:], func=AF.Square), reads=[zc.name], writes=[sq.name])
        for k in range(8):
            t.op('pe', lambda e, k=k: e.matmul(pv[:, 0:N], self.onesm[:], sq[:, k, :], start=(k == 0), stop=(k == 7)),
                 reads=[sq.name, "onesm"], writes=[kb], skip_self=True)
        t.op('act', lambda e: e.activation(out=sd[:], in_=pv[:, 0:N], func=AF.Sqrt, bias=self.epsc[:, 0:1], scale=1.0),
             reads=[kb, "epsc"], writes=[sd.name])
        t.op('dve', lambda e: e.reciprocal(out=sd[:], in_=sd[:]), reads=[sd.name], writes=[sd.name])
        t.op('dve', lambda e: e.tensor_tensor(out=zc[:], in0=zc[:], in1=bc(sd[:].unsqueeze(1), [128, 8, N]), op=ALU.mult),
             reads=[zc.name, sd.name], writes=[zc.name])
        for k in range(8):
            t.op('act', lambda e, k=k: e.activation(out=out[:, k, :], in_=zc[:, k, :], func=AF.Identity,
                                                    bias=self.lnp[:, 2 * which + 1, layer, k:k + 1],
                                                    scale=self.lnp[:, 2 * which, layer, k:k + 1]),
                 reads=[zc.name, "lnp"], writes=[outkey])

    def mk_eps(self, ph):
        nc, t = self.nc, self.t
        self.epsc = ph.enter_context(self.sbuf("epsc", [128, 1], F32))
        t.op('dve', lambda e: e.memset(self.epsc[:], LN_EPS), writes=["epsc"])

    def phase_tabcast(self):
        t = self.t
        for l in range(2):
            for r0 in range(0, D, 128):
                for c0 in range(0, 16384, 2048):
                    t.dma(self.Ub[l, r0:r0 + 128, c0:c0 + 2048], self.peer_uT[l, r0:r0 + 128, c0:c0 + 2048],
                          writes=["Ub%d_%d_%d" % (l, r0 // 128, c0 // 2048)], eng='pool')
            for r0 in range(0, 16384, 256):
                t.dma(self.Vb[l, r0:r0 + 256, :], self.peer_v[l, r0:r0 + 256, :], writes=["Vb%d_%d" % (l, r0 // 256)], eng='pool')

    def phase_s5a(self):
        nc, t = self.nc, self.t
        with ExitStack() as ph:
            wb = self.load_w_bf16(ph, "s5win", self.w_s5_in, 8, D, eng_cast='mix')
            xf = ph.enter_context(self.sbuf("a_xf", [128, 2, 8, 512], F32))
            xb = ph.enter_context(self.sbuf("a_xb", [128, 8, 1024], BF16))
            U8 = ph.enter_context(self.sbuf("a_U8", [128, 64, 8, 16], BF16))
            Upt = ph.enter_context(self.sbuf("a_Upt", [128, 64, 128], BF16))
            xTv = self.xT.rearrange("(k p) n -> p k n", p=128)
            for tile in range(8):
                t0 = tile * 1024
                for h in range(2):
                    t.dma(xf[:, h, :, :], xTv[:, :, t0 + h * 512:t0 + (h + 1) * 512], writes=["a_xf%d" % h])
                    t.op('act' if h == 0 else 'pool',
                         (lambda e, h=h: e.copy(out=xb[:, :, h * 512:(h + 1) * 512], in_=xf[:, h, :, :])) if h == 0 else
                         (lambda e, h=h: e.tensor_copy(out=xb[:, :, h * 512:(h + 1) * 512], in_=xf[:, h, :, :])),
                         reads=["a_xf%d" % h], writes=["a_xb"])
                for s in range(8):
                    for fh in range(2):
                        bank = (s * 2 + fh) % 4
                        pk = "ps%d" % bank
                        for k in range(8):
                            t.op('pe', lambda e, k=k, s=s, fh=fh, bank=bank: e.matmul(
                                self.ps[bank][:], xb[:, k, s::8], wb[:, k, fh * 512:(fh + 1) * 512],
                                start=(k == 0), stop=(k == 7)),
                                reads=["a_xb", "s5win"], writes=[pk], skip_self=True)
                        if (s * 2 + fh) % 2 == 0:
                            t.op('act', lambda e, s=s, fh=fh, bank=bank: e.copy(out=U8[:, fh * 32:(fh + 1) * 32, s, :], in_=self.ps[bank][:].rearrange("p (g c) -> p g c", c=16)),
                                 reads=[pk], writes=["a_U8"])
                        else:
                            t.op('dve', lambda e, s=s, fh=fh, bank=bank: e.tensor_copy(out=U8[:, fh * 32:(fh + 1) * 32, s, :], in_=self.ps[bank][:].rearrange("p (g c) -> p g c", c=16)),
                                 reads=[pk], writes=["a_U8"])
                for gb in range(8):
                    bank = 4 + gb % 2
                    pk = "ps%d" % bank
                    pb = self.ps[bank][:].bitcast(BF16)
                    for gi in range(8):
                        g = gb * 8 + gi
                        t.op('pe', lambda e, g=g, gi=gi, pb=pb: e.transpose(pb[:, gi * 128:(gi + 1) * 128],
                                                                          U8[:, g, :, :].rearrange("p s c -> p (s c)"), self.identb[:]),
                             reads=["a_U8", "identb"], writes=[pk], skip_self=True)
                    if gb % 2 == 0:
                        t.op('dve', lambda e, gb=gb, pb=pb: e.tensor_copy(out=Upt[:, gb * 8:(gb + 1) * 8, :],
                                                                        in_=pb.rearrange("p (g c) -> p g c", g=8)),
                             reads=[pk], writes=["a_Upt"])
                    else:
                        t.op('act', lambda e, gb=gb, pb=pb: e.copy(out=Upt[:, gb * 8:(gb + 1) * 8, :],
                                                                 in_=pb.rearrange("p (g c) -> p g c", g=8)),
                             reads=[pk], writes=["a_Upt"])
                t.dma(self.UpD[:, :, tile * 128:(tile + 1) * 128].rearrange("g p c -> p g c"), Upt[:],
                      reads=["a_Upt"], writes=["UpD"])
            t.barrier()

    def cmul(self, eng, outr, outi, ar, ai, br, bi, tmp1, tmp2, rk, wk):
        t = self.t
        t.op(eng, lambda e: e.tensor_tensor(out=tmp1, in0=ar, in1=br, op=ALU.mult), reads=rk, writes=["cm_t1"])
        t.op(eng, lambda e: e.tensor_tensor(out=tmp2, in0=ai, in1=bi, op=ALU.mult), reads=rk, writes=["cm_t2"])
        t.op(eng, lambda e: e.tensor_tensor(out=outr, in0=tmp1, in1=tmp2, op=ALU.subtract), reads=["cm_t1", "cm_t2"] + rk, writes=wk)
        t.op(eng, lambda e: e.tensor_tensor(out=tmp1, in0=ar, in1=bi, op=ALU.mult), reads=rk + wk, writes=["cm_t1"])
        t.op(eng, lambda e: e.tensor_tensor(out=tmp2, in0=ai, in1=br, op=ALU.mult), reads=rk + wk, writes=["cm_t2"])
        t.op(eng, lambda e: e.tensor_tensor(out=outi, in0=tmp1, in1=tmp2, op=ALU.add), reads=["cm_t1", "cm_t2"] + rk, writes=wk)

    def phase_s5b(self):
        nc, t = self.nc, self.t
        with ExitStack() as ph:
            sb = lambda name, shape, dt=F32: ph.enter_context(self.sbuf(name, list(shape), dt))
            MATS = sb("b_mats", [128, 64, 5, 128], BF16)
            POW = sb("b_pow", [128, 2, 16, 64])
            PH = sb("b_ph", [128, 2, 10, 64])
            RHO = sb("b_rho", [128, 64])
            dpk = sb("b_dpk", [128, 64])
            segm = sb("b_segm", [128, 1024])
            t.dma(dpk[:], self.s5_dpk, writes=["b_dpk"])
            t.dma(segm[:], self.c_segmask, writes=["b_segm"])
            with ExitStack() as pg:
                sg = lambda name, shape, dt=F32: pg.enter_context(self.sbuf(name, list(shape), dt))
                lre = sg("g_lre", [128, 64]); lim = sg("g_lim", [128, 64]); lst = sg("g_lst", [128, 64])
                Bre = sg("g_bre", [128, 64, 16]); Bim = sg("g_bim", [128, 64, 16])
                Cre = sg("g_cre", [128, 64, 16]); Cim = sg("g_cim", [128, 64, 16])
                maskf = sg("g_maskf", [128, 128]); maskb = sg("g_maskb", [128, 128])
                for dst, src in [(lre, self.s5_lamre), (lim, self.s5_lamim), (lst, self.s5_lstep), (Bre, self.s5_bre),
                                 (Bim, self.s5_bim), (Cre, self.s5_ctre), (Cim, self.s5_ctim), (maskf, self.c_maskf),
                                 (maskb, self.c_maskb)]:
                    t.dma(dst[:], src, writes=[dst.name])
                S = {}
                for nm in ["step", "ang", "lrs", "mag", "magi", "kf", "r", "m1", "s1", "c1", "ar", "ai", "ari", "aii",
                           "den", "zr", "qr", "qi", "u1", "u2", "e8"]:
                    S[nm] = sg("g_" + nm, [128, 64])
                ki = sg("g_ki", [128, 64], I32)
                V = 'dve'

                def tt(out, a, b, op, rk, wk):
                    t.op(V, lambda e: e.tensor_tensor(out=out, in0=a, in1=b, op=op), reads=rk, writes=wk)

                def ts(out, a, s1, s2, op0, op1, rk, wk):
                    t.op(V, lambda e: e.tensor_scalar(out=out, in0=a, scalar1=s1, scalar2=s2, op0=op0, op1=op1), reads=rk, writes=wk)

                def act(out, a, func, rk, wk, scale=1.0):
                    t.op('act', lambda e: e.activation(out=out, in_=a, func=func, scale=scale), reads=rk, writes=wk)

                n = lambda k: S[k].name
                act(S["step"][:], lst[:], AF.Exp, [lst.name], [n("step")])
                tt(S["ang"][:], lim[:], S["step"][:], ALU.mult, [lim.name, n("step")], [n("ang")])
                tt(S["lrs"][:], lre[:], S["step"][:], ALU.mult, [lre.name, n("step")], [n("lrs")])
                act(S["mag"][:], S["lrs"][:], AF.Exp, [n("lrs")], [n("mag")])
                act(S["magi"][:], S["lrs"][:], AF.Exp, [n("lrs")], [n("magi")], scale=-1.0)
                act(S["e8"][:], S["lrs"][:], AF.Exp, [n("lrs")], [n("e8")], scale=-8.0)
                act(RHO[:], S["lrs"][:], AF.Exp, [n("lrs")], ["b_rho"], scale=8.0)

                def range_reduce(dst, src, shift):
                    ts(S["kf"][:], src, 1.0 / TWO_PI, shift / TWO_PI + 0.5, ALU.mult, ALU.add, [n("ang")], [n("kf")])
                    t.op(V, lambda e: e.tensor_copy(out=ki[:], in_=S["kf"][:]), reads=[n("kf")], writes=[ki.name])
                    t.op(V, lambda e: e.tensor_copy(out=S["kf"][:], in_=ki[:]), reads=[ki.name], writes=[n("kf")])
                    ts(S["kf"][:], S["kf"][:], -TWO_PI, shift, ALU.mult, ALU.add, [n("kf")], [n("kf")])
                    tt(dst, src, S["kf"][:], ALU.add, [n("ang"), n("kf")], [n("r")])
                    ts(S["m1"][:], dst, math.pi, -TWO_PI, ALU.is_gt, ALU.mult, [n("r")], [n("m1")])
                    tt(dst, dst, S["m1"][:], ALU.add, [n("r"), n("m1")], [n("r")])
                    ts(S["m1"][:], dst, -math.pi, TWO_PI, ALU.is_lt, ALU.mult, [n("r")], [n("m1")])
                    tt(dst, dst, S["m1"][:], ALU.add, [n("r"), n("m1")], [n("r")])
                    ts(dst, dst, math.pi, -math.pi, ALU.min, ALU.max, [n("r")], [n("r")])

                range_reduce(S["r"][:], S["ang"][:], 0.0)
                act(S["s1"][:], S["r"][:], AF.Sin, [n("r")], [n("s1")])
                range_reduce(S["r"][:], S["ang"][:], math.pi / 2)
                act(S["c1"][:], S["r"][:], AF.Sin, [n("r")], [n("c1")])
                tt(S["ar"][:], S["mag"][:], S["c1"][:], ALU.mult, [n("mag"), n("c1")], [n("ar")])
                tt(S["ai"][:], S["mag"][:], S["s1"][:], ALU.mult, [n("mag"), n("s1")], [n("ai")])
                tt(S["ari"][:], S["magi"][:], S["c1"][:], ALU.mult, [n("magi"), n("c1")], [n("ari")])
                tt(S["aii"][:], S["magi"][:], S["s1"][:], ALU.mult, [n("magi"), n("s1")], [n("aii")])
                ts(S["aii"][:], S["aii"][:], -1.0, None, ALU.mult, ALU.bypass, [n("aii")], [n("aii")])
                tt(S["den"][:], lre[:], lre[:], ALU.mult, [lre.name], [n("den")])
                tt(S["u1"][:], lim[:], lim[:], ALU.mult, [lim.name], [n("u1")])
                tt(S["den"][:], S["den"][:], S["u1"][:], ALU.add, [n("den"), n("u1")], [n("den")])
                t.op(V, lambda e: e.reciprocal(out=S["den"][:], in_=S["den"][:]), reads=[n("den")], writes=[n("den")])
                ts(S["zr"][:], S["ar"][:], -1.0, None, ALU.add, ALU.bypass, [n("ar")], [n("zr")])
                tt(S["u1"][:], S["zr"][:], lre[:], ALU.mult, [n("zr"), lre.name], [n("u1")])
                tt(S["u2"][:], S["ai"][:], lim[:], ALU.mult, [n("ai"), lim.name], [n("u2")])
                tt(S["u1"][:], S["u1"][:], S["u2"][:], ALU.add, [n("u1"), n("u2")], [n("u1")])
                tt(S["qr"][:], S["u1"][:], S["den"][:], ALU.mult, [n("u1"), n("den")], [n("qr")])
                tt(S["u1"][:], S["ai"][:], lre[:], ALU.mult, [n("ai"), lre.name], [n("u1")])
                tt(S["u2"][:], S["zr"][:], lim[:], ALU.mult, [n("zr"), lim.name], [n("u2")])
                tt(S["u1"][:], S["u1"][:], S["u2"][:], ALU.subtract, [n("u1"), n("u2")], [n("u1")])
                tt(S["qi"][:], S["u1"][:], S["den"][:], ALU.mult, [n("u1"), n("den")], [n("qi")])
                BBr = sg("g_bbr", [128, 64, 16]); BBi = sg("g_bbi", [128, 64, 16])
                T1 = sg("g_T1", [128, 1024]); T2 = sg("g_T2", [128, 1024])
                T1v = T1[:].rearrange("p (g c) -> p g c", c=16)
                T2v = T2[:].rearrange("p (g c) -> p g c", c=16)
                qrb = bc(S["qr"][:].unsqueeze(2), [128, 64, 16]); qib = bc(S["qi"][:].unsqueeze(2), [128, 64, 16])
                self.cmul(V, BBr[:], BBi[:], qrb, qib, Bre[:], Bim[:], T1v, T2v, [n("qr"), n("qi"), Bre.name, Bim.name], [BBr.name, BBi.name])
                t.op(V, lambda e: e.memset(POW[:, 0, 7, :], 1.0), writes=["b_pow"])
                t.op(V, lambda e: e.memset(POW[:, 1, 7, :], 0.0), reads=["b_pow"], writes=["b_pow"])
                for k in range(0, 8):
                    self.cmul(V, POW[:, 0, 8 + k, :], POW[:, 1, 8 + k, :], POW[:, 0, 7 + k, :], POW[:, 1, 7 + k, :],
                              S["ar"][:], S["ai"][:], T1[:, 0:64], T2[:, 0:64], ["b_pow", n("ar"), n("ai")], ["b_pow"])
                for k in range(0, 7):
                    self.cmul(V, POW[:, 0, 6 - k, :], POW[:, 1, 6 - k, :], POW[:, 0, 7 - k, :], POW[:, 1, 7 - k, :],
                              S["ari"][:], S["aii"][:], T1[:, 0:64], T2[:, 0:64], ["b_pow", n("ari"), n("aii")], ["b_pow"])
                tt(PH[:, 0, 0, :], POW[:, 0, 15, :], S["e8"][:], ALU.mult, ["b_pow", n("e8")], ["b_ph"])
                tt(PH[:, 1, 0, :], POW[:, 1, 15, :], S["e8"][:], ALU.mult, ["b_pow", n("e8"), "b_ph"], ["b_ph"])
                t.op(V, lambda e: e.tensor_scalar(out=PH[64:128, 1, 0, :], in0=PH[64:128, 1, 0, :], scalar1=-1.0, scalar2=None,
                                                  op0=ALU.mult, op1=ALU.bypass), reads=["b_ph"], writes=["b_ph"])
                for L in range(9):
                    self.cmul(V, PH[:, 0, L + 1, :], PH[:, 1, L + 1, :], PH[:, 0, L, :], PH[:, 1, L, :],
                              PH[:, 0, L, :], PH[:, 1, L, :], T1[:, 0:64], T2[:, 0:64], ["b_ph"], ["b_ph"])
                PA = sg("g_pa", [128, 4, 2, 8, 64])
                kmap = {0: (lambda j: 7 - j, lambda j: j), 1: (lambda j: j + 1, lambda j: 8 - j),
                        2: (lambda j: -j, lambda j: j), 3: (lambda j: j, lambda j: -j)}
                ci = 0
                for kind in range(4):
                    for j in range(8):
                        for half, (p0, p1) in enumerate([(0, 64), (64, 128)]):
                            kk = kmap[kind][half](j) + 7
                            eng = ['act', 'pool'][ci % 2]
                            ci += 1
                            if eng == 'act':
                                t.op('act', lambda e, kind=kind, j=j, p0=p0, p1=p1, kk=kk: e.copy(out=PA[p0:p1, kind, :, j, :], in_=POW[p0:p1, :, kk, :]),
                                     reads=["b_pow"], writes=["g_pa%d_%d_%d" % (kind, j, half)])
                            else:
                                t.op('pool', lambda e, kind=kind, j=j, p0=p0, p1=p1, kk=kk: e.tensor_copy(out=PA[p0:p1, kind, :, j, :], in_=POW[p0:p1, :, kk, :]),
                                     reads=["b_pow"], writes=["g_pa%d_%d_%d" % (kind, j, half)])
                pa_keys = ["g_pa%d_%d_%d" % (kind, j, half) for kind in range(4) for j in range(8) for half in range(2)]
                TB = sg("g_tb", [128, 4, 2, 1024])
                Ysn = sg("g_ysn", [128, 1024])
                for gb in range(8):
                    g0 = gb * 8
                    for kind in range(4):
                        src_r, src_i = (BBr, BBi) if kind in (0, 2) else (Cre, Cim)
                        par = bc(PA[:, kind, 0, :, g0:g0 + 8].rearrange("p j g -> p g j").unsqueeze(3), [128, 8, 8, 16])
                        pai = bc(PA[:, kind, 1, :, g0:g0 + 8].rearrange("p j g -> p g j").unsqueeze(3), [128, 8, 8, 16])
                        br = bc(src_r[:, g0:g0 + 8, :].unsqueeze(2), [128, 8, 8, 16])
                        bi = bc(src_i[:, g0:g0 + 8, :].unsqueeze(2), [128, 8, 8, 16])
                        outr = TB[:, kind, 0, :].rearrange("p (g j c) -> p g j c", g=8, j=8)
                        outi = TB[:, kind, 1, :].rearrange("p (g j c) -> p g j c", g=8, j=8)
                        t1 = T1[:].rearrange("p (g j c) -> p g j c", g=8, j=8)
                        t2 = T2[:].rearrange("p (g j c) -> p g j c", g=8, j=8)
                        self.cmul(V, outr, outi, par, pai, br, bi, t1, t2, pa_keys + [src_r.name, src_i.name], ["g_tb%d" % kind])
                    t.op(V, lambda e: e.tensor_scalar(out=Ysn[:], in0=TB[:, 2, 1, :], scalar1=-1.0, scalar2=None, op0=ALU.mult, op1=ALU.bypass),
                         reads=["g_tb2"], writes=["g_ysn"])
                    for gi in range(8):
                        g = g0 + gi
                        sl = slice(gi * 128, (gi + 1) * 128)
                        for ri in range(2):
                            bank = 4 + ri
                            t.op('pe', lambda e, ri=ri, sl=sl, bank=bank: e.transpose(self.ps[bank][:, 0:128], TB[:, 0, ri, sl], self.ident[:]),
                                 reads=["g_tb0", "ident"], writes=["ps%d" % bank], skip_self=True)
                            t.op('act', lambda e, ri=ri, g=g, bank=bank: e.copy(out=MATS[:, g, ri, :], in_=self.ps[bank][:, 0:128]),
                                 reads=["ps%d" % bank], writes=["b_mats"])
                        t.op('pool', lambda e, g=g, sl=sl: e.tensor_copy(out=MATS[:, g, 3, :], in_=TB[:, 1, 0, sl]), reads=["g_tb1"], writes=["b_mats"])
                        t.op('pool', lambda e, g=g, sl=sl: e.tensor_scalar(out=MATS[:, g, 4, :], in0=TB[:, 1, 1, sl], scalar1=-1.0, scalar2=None,
                                                                          op0=ALU.mult, op1=ALU.bypass), reads=["g_tb1"], writes=["b_mats"])
                        for half, (p0, p1) in enumerate([(0, 64), (64, 128)]):
                            bank = 6 + half
                            t.op('pe', lambda e, p0=p0, p1=p1, sl=sl, bank=bank: e.matmul(self.ps[bank][:, 0:128], TB[p0:p1, 2, 0, sl], TB[p0:p1, 3, 0, sl], start=True, stop=False),
                                 reads=["g_tb2", "g_tb3"], writes=["ps%d" % bank], skip_self=True)
                            t.op('pe', lambda e, p0=p0, p1=p1, sl=sl, bank=bank: e.matmul(self.ps[bank][:, 0:128], Ysn[p0:p1, sl], TB[p0:p1, 3, 1, sl], start=False, stop=True),
                                 reads=["g_ysn", "g_tb3"], writes=["ps%d" % bank], skip_self=True)
                        t.op(V, lambda e: e.tensor_tensor(out=T1[:, 0:128], in0=self.ps[6][:, 0:128], in1=maskf[:], op=ALU.mult),
                             reads=["ps6", maskf.name], writes=["cm_t1"])
                        t.op(V, lambda e: e.tensor_tensor(out=T2[:, 0:128], in0=self.ps[7][:, 0:128], in1=maskb[:], op=ALU.mult),
                             reads=["ps7", maskb.name], writes=["cm_t2"])
                        t.op(V, lambda e, g=g: e.tensor_tensor(out=MATS[:, g, 2, :], in0=T1[:, 0:128], in1=T2[:, 0:128], op=ALU.add),
                             reads=["cm_t1", "cm_t2"], writes=["b_mats"])
                t.barrier()
            Up = sb("b_up", [128, 2, 1024], BF16)
            TAB = sb("b_tab", [128, 2, 1024])
            RM = sb("b_rm", [128, 1024])
            W = [sb("b_w%d" % i, [128, 1024]) for i in range(6)]
            Gs = [sb("b_g%d" % i, [128, 1024]) for i in range(2)]
            Hu = [sb("b_h%d" % i, [128, 1024]) for i in range(2)]
            Hs = sb("b_hs", [128, 2, 1024], BF16)
            yd = sb("b_yd", [128, 1024])
            hp = sb("b_hp", [128, 2, 1024], BF16)
            t.op('pool', lambda e: e.memset(Hs[:], 0.0), writes=["b_hs"])
            V = 'dve'
            for g in range(64):
                ub = g % 2
                uk = "b_up%d" % ub
                t.dma(Up[:, ub, :], self.UpD[g, :, :], reads=["UpD"], writes=[uk])
                t.op('act', lambda e, g=g: e.activation(out=RM[:], in_=segm[:], func=AF.Copy, scale=RHO[:, g:g + 1]),
                     reads=["b_segm", "b_rho"], writes=["b_rm"])
                t.op(V, lambda e: e.memset(TAB[:, 0, 0:1], 1.0), writes=["b_tab"])
                t.op(V, lambda e: e.memset(TAB[:, 1, 0:1], 0.0), reads=["b_tab"], writes=["b_tab"])
                for L in range(9):
                    n0 = 1 << L
                    cr = PH[:, 0, L, g:g + 1]
                    ci_ = PH[:, 1, L, g:g + 1]
                    src_r = TAB[:, 0, 0:n0]; src_i = TAB[:, 1, 0:n0]
                    dst_r = TAB[:, 0, n0:2 * n0]; dst_i = TAB[:, 1, n0:2 * n0]
                    t.op(V, lambda e, src_i=src_i, ci_=ci_, n0=n0: e.tensor_scalar(out=W[0][:, 0:n0], in0=src_i, scalar1=ci_, scalar2=None, op0=ALU.mult, op1=ALU.bypass),
                         reads=["b_tab", "b_ph"], writes=["b_w0"])
                    t.op(V, lambda e, src_r=src_r, cr=cr, n0=n0, dst_r=dst_r: e.scalar_tensor_tensor(out=dst_r, in0=src_r, scalar=cr, in1=W[0][:, 0:n0], op0=ALU.mult, op1=ALU.subtract),
                         reads=["b_tab", "b_ph", "b_w0"], writes=["b_tab"])
                    t.op(V, lambda e, src_r=src_r, ci_=ci_, n0=n0: e.tensor_scalar(out=W[1][:, 0:n0], in0=src_r, scalar1=ci_, scalar2=None, op0=ALU.mult, op1=ALU.bypass),
                         reads=["b_tab", "b_ph"], writes=["b_w1"])
                    t.op(V, lambda e, src_i=src_i, cr=cr, n0=n0, dst_i=dst_i: e.scalar_tensor_tensor(out=dst_i, in0=src_i, scalar=cr, in1=W[1][:, 0:n0], op0=ALU.mult, op1=ALU.add),
                         reads=["b_tab", "b_ph", "b_w1"], writes=["b_tab"])
                for ri in range(2):
                    t.op('act', lambda e, ri=ri: e.copy(out=TAB[:, ri, 512:768], in_=TAB[:, ri, 0:256]), reads=["b_tab"], writes=["b_tab"])
                    t.op('act', lambda e, ri=ri: e.copy(out=TAB[:, ri, 768:1024], in_=TAB[:, ri, 0:256]), reads=["b_tab"], writes=["b_tab"])
                for ri in range(2):
                    for h in range(2):
                        bank = ri * 2 + h
                        t.op('pe', lambda e, ri=ri, h=h, bank=bank, g=g, ub=ub: e.matmul(self.ps[bank][:], MATS[:, g, ri, :], Up[:, ub, h * 512:(h + 1) * 512], start=True, stop=True),
                             reads=["b_mats", uk], writes=["ps%d" % bank], skip_self=True)
                cosT = TAB[:, 0, :]; sinT = TAB[:, 1, :]
                for h in range(2):
                    cs = slice(h * 512, (h + 1) * 512)
                    t.op(V, lambda e, h=h, cs=cs: e.tensor_tensor(out=W[0][:, cs], in0=self.ps[h][:], in1=cosT[:, cs], op=ALU.mult), reads=["ps%d" % h, "b_tab"], writes=["b_w0"])
                    t.op(V, lambda e, h=h, cs=cs: e.tensor_tensor(out=W[1][:, cs], in0=self.ps[2 + h][:], in1=sinT[:, cs], op=ALU.mult), reads=["ps%d" % (2 + h), "b_tab"], writes=["b_w1"])
                    t.op(V, lambda e, h=h, cs=cs: e.tensor_tensor(out=W[2][:, cs], in0=self.ps[2 + h][:], in1=cosT[:, cs], op=ALU.mult), reads=["ps%d" % (2 + h), "b_tab"], writes=["b_w2"])
                    t.op(V, lambda e, h=h, cs=cs: e.tensor_tensor(out=W[3][:, cs], in0=self.ps[h][:], in1=sinT[:, cs], op=ALU.mult), reads=["ps%d" % h, "b_tab"], writes=["b_w3"])
                t.op('pool', lambda e: e.tensor_tensor(out=W[4][:], in0=W[0][:], in1=W[1][:], op=ALU.add), reads=["b_w0", "b_w1"], writes=["b_w4"])
                t.op('pool', lambda e: e.tensor_tensor(out=W[5][:], in0=W[2][:], in1=W[3][:], op=ALU.subtract), reads=["b_w2", "b_w3"], writes=["b_w5"])
                for ri in range(2):
                    src = W[4 + ri]
                    t.op(V, lambda e, ri=ri, src=src: e.tensor_tensor_scan(out=Gs[ri][0:64, :], data0=RM[0:64, :], data1=src[0:64, :], initial=0.0, op0=ALU.mult, op1=ALU.add),
                         reads=["b_rm", src.name], writes=["b_g%d_f" % ri])
                    t.op(V, lambda e, ri=ri, src=src: e.tensor_tensor_scan(out=Gs[ri][64:128, ::-1], data0=RM[64:128, ::-1], data1=src[64:128, ::-1], initial=0.0, op0=ALU.mult, op1=ALU.add),
                         reads=["b_rm", src.name], writes=["b_g%d_b" % ri])
                gk = ["b_g0_f", "b_g0_b", "b_g1_f", "b_g1_b"]
                t.op(V, lambda e: e.tensor_tensor(out=W[0][:], in0=Gs[0][:], in1=cosT, op=ALU.mult), reads=gk + ["b_tab"], writes=["b_w0"])
                t.op('pool', lambda e: e.tensor_tensor(out=W[1][:], in0=Gs[1][:], in1=sinT, op=ALU.mult), reads=gk + ["b_tab"], writes=["b_w1"])
                t.op(V, lambda e: e.tensor_tensor(out=W[2][:], in0=Gs[1][:], in1=cosT, op=ALU.mult), reads=gk + ["b_tab"], writes=["b_w2"])
                t.op('pool', lambda e: e.tensor_tensor(out=W[3][:], in0=Gs[0][:], in1=sinT, op=ALU.mult), reads=gk + ["b_tab"], writes=["b_w3"])
                t.op(V, lambda e: e.tensor_tensor(out=Hu[0][:], in0=W[0][:], in1=W[1][:], op=ALU.subtract), reads=["b_w0", "b_w1"], writes=["b_h0"])
                t.op('pool', lambda e: e.tensor_tensor(out=Hu[1][:], in0=W[2][:], in1=W[3][:], op=ALU.add), reads=["b_w2", "b_w3"], writes=["b_h1"])
                for ri in range(2):
                    t.op(V, lambda e, ri=ri: e.tensor_tensor(out=Hs[0:64, ri, 1:1024], in0=Hu[ri][0:64, 0:1023], in1=segm[0:64, 1:1024], op=ALU.mult),
                         reads=["b_h%d" % ri, "b_segm"], writes=["b_hs"])
                    t.op('pool', lambda e, ri=ri: e.tensor_tensor(out=Hs[64:128, ri, 0:1023], in0=Hu[ri][64:128, 1:1024], in1=segm[64:128, 0:1023], op=ALU.mult),
                         reads=["b_h%d" % ri, "b_segm"], writes=["b_hs"])
                for h in range(2):
                    bank = 4 + h
                    cs = slice(h * 512, (h + 1) * 512)
                    t.op('pe', lambda e, g=g, ub=ub, cs=cs, bank=bank: e.matmul(self.ps[bank][:], MATS[:, g, 2, :], Up[:, ub, cs], start=True, stop=False),
                         reads=["b_mats", uk], writes=["ps%d" % bank], skip_self=True)
                    t.op('pe', lambda e, g=g, cs=cs, bank=bank: e.matmul(self.ps[bank][:], MATS[:, g, 3, :], Hs[:, 0, cs], start=False, stop=False),
                         reads=["b_mats", "b_hs"], writes=["ps%d" % bank], skip_self=True)
                    t.op('pe', lambda e, g=g, cs=cs, bank=bank: e.matmul(self.ps[bank][:], MATS[:, g, 4, :], Hs[:, 1, cs], start=False, stop=True),
                         reads=["b_mats", "b_hs"], writes=["ps%d" % bank], skip_self=True)
                    t.op(V, lambda e, g=g, ub=ub, cs=cs, bank=bank: e.scalar_tensor_tensor(out=yd[:, cs], in0=Up[:, ub, cs], scalar=dpk[:, g:g + 1], in1=self.ps[bank][:],
                                                                                         op0=ALU.mult, op1=ALU.add),
                         reads=[uk, "b_dpk", "ps%d" % bank], writes=["b_yd"])
                t.op('act', lambda e, ub=ub: e.activation(out=hp[:, ub, :], in_=yd[:], func=AF.Gelu_apprx_tanh), reads=["b_yd"], writes=["b_hp%d" % ub])
                t.dma(self.HpD[g, :, :], hp[:, ub, :], reads=["b_hp%d" % ub], writes=["HpD"])
            t.barrier()

    def phase_s5c(self):
        nc, t = self.nc, self.t
        with ExitStack() as ph:
            sb = lambda name, shape, dt=F32: ph.enter_context(self.sbuf(name, list(shape), dt))
            self.mk_eps(ph)
            wg = self.load_w_bf16(ph, "s5wglu", self.w_s5_glu, 8, 2 * D, eng_cast='mix')
            hpt = sb("c_hpt", [128, 64, 128], BF16)
            H8 = sb("c_H8", [128, 8, 1024], BF16)
            hT = sb("c_hT", [128, 8, 1024], BF16)
            xf = sb("c_xf", [128, 8, 512])
            z = sb("c_z", [128, 8, 512])
            sg_ = sb("c_sg", [128, 512])
            tmp = {'zc': sb("c_zc", [128, 8, 512]), 'sq': sb("c_sq", [128, 8, 512]), 'sd': sb("c_sd", [128, 512])}
            xo = sb("c_xo", [128, 8, 512])
            xTv = self.xT.rearrange("(k p) n -> p k n", p=128)
            X1v = self.X1.rearrange("(k p) n -> p k n", p=128)
            for tile in range(8):
                t.dma(hpt[:], self.HpD[:, :, tile * 128:(tile + 1) * 128].rearrange("g p c -> p g c"), reads=["HpD"], writes=["c_hpt"])
                for gb in range(8):
                    bank = gb % 2
                    pk = "ps%d" % bank
                    pb = self.ps[bank][:].bitcast(BF16)
                    for gi in range(8):
                        g = gb * 8 + gi
                        t.op('pe', lambda e, g=g, gi=gi, pb=pb: e.transpose(pb[:, gi * 128:(gi + 1) * 128], hpt[:, g, :], self.identb[:]),
                             reads=["c_hpt", "identb"], writes=[pk], skip_self=True)
                    src = pb.rearrange("p (g t c) -> p g t c", g=8, t=8)
                    dst = H8[:, :, gb * 128:(gb + 1) * 128].rearrange("p t (g c) -> p g t c", g=8)
                    if gb % 2 == 0:
                        t.op('dve', lambda e, src=src, dst=dst: e.tensor_copy(out=dst, in_=src), reads=[pk], writes=["c_H8"])
                    else:
                        t.op('act', lambda e, src=src, dst=dst: e.copy(out=dst, in_=src), reads=[pk], writes=["c_H8"])
                for k in range(8):
                    bank = 2 + k % 2
                    pk = "ps%d" % bank
                    pb = self.ps[bank][:].bitcast(BF16)
                    for tt_ in range(8):
                        t.op('pe', lambda e, k=k, tt_=tt_, pb=pb: e.transpose(pb[:, tt_ * 128:(tt_ + 1) * 128], H8[:, tt_, k * 128:(k + 1) * 128], self.identb[:]),
                             reads=["c_H8", "identb"], writes=[pk], skip_self=True)
                    src = pb.rearrange("p (t c) -> p t c", t=8)
                    dst = hT[:, k, :].rearrange("p (c t) -> p t c", t=8)
                    if k % 2 == 0:
                        t.op('dve', lambda e, src=src, dst=dst: e.tensor_copy(out=dst, in_=src), reads=[pk], writes=["c_hT"])
                    else:
                        t.op('act', lambda e, src=src, dst=dst: e.copy(out=dst, in_=src), reads=[pk], writes=["c_hT"])
                for th in range(2):
                    tok0 = tile * 1024 + th * 512
                    t.dma(xf[:], xTv[:, :, tok0:tok0 + 512], writes=["c_xf"])
                    for fo in range(8):
                        bv, bg = 4 + (fo % 2) * 2, 5 + (fo % 2) * 2
                        for k in range(8):
                            t.op('pe', lambda e, k=k, fo=fo, th=th, bv=bv: e.matmul(self.ps[bv][:], wg[:, k, fo * 128:(fo + 1) * 128], hT[:, k, th * 512:(th + 1) * 512],
                                                                              start=(k == 0), stop=(k == 7)), reads=["s5wglu", "c_hT"], writes=["ps%d" % bv], skip_self=True)
                        for k in range(8):
                            t.op('pe', lambda e, k=k, fo=fo, th=th, bg=bg: e.matmul(self.ps[bg][:], wg[:, k, D + fo * 128:D + (fo + 1) * 128], hT[:, k, th * 512:(th + 1) * 512],
                                                                              start=(k == 0), stop=(k == 7)), reads=["s5wglu", "c_hT"], writes=["ps%d" % bg], skip_self=True)
                        t.op('act', lambda e, bg=bg: e.activation(out=sg_[:], in_=self.ps[bg][:], func=AF.Sigmoid), reads=["ps%d" % bg], writes=["c_sg"])
                        t.op('dve', lambda e, bv=bv: e.tensor_tensor(out=sg_[:], in0=self.ps[bv][:], in1=sg_[:], op=ALU.mult), reads=["ps%d" % bv, "c_sg"], writes=["c_sg"])
                        t.op('dve', lambda e, fo=fo: e.scalar_tensor_tensor(out=z[:, fo, :], in0=xf[:, fo, :], scalar=ALPHA, in1=sg_[:], op0=ALU.mult, op1=ALU.add),
                             reads=["c_xf", "c_sg"], writes=["c_z"])
                    self.ln_block(z, "c_z", 0, 0, xo, "c_xo", 512, tmp, 0, 1)
                    t.dma(X1v[:, :, tok0:tok0 + 512], xo[:], reads=["c_xo"], writes=["X1"])
            t.barrier()

    def phase_peer(self, layer, xin, xout):
        sub = getattr(self, "peer_sub", ("p1", "p2", "p3"))
        if "p1" in sub:
            self.peer_p1(layer, xin)
        if "p2" in sub:
            self.peer_p2(layer)
        if "p3" in sub:
            self.peer_p3(layer, xin, xout)

    def peer_p1(self, layer, xin):
        nc, t = self.nc, self.t
        with ExitStack() as ph:
            sb = lambda name, shape, dt=F32: ph.enter_context(self.sbuf(name, list(shape), dt))
            wq = self.load_w_bf16(ph, "p1_wq", self.w_peer_q[layer], 8, 2 * D, eng_cast='mix')
            xf = sb("p1_xf", [128, 2, 8, 512])
            xb = sb("p1_xb", [128, 2, 8, 512], BF16)
            qb = sb("p1_qb", [128, 2, 16, 512], BF16)
            xv = xin.rearrange("(k p) n -> p k n", p=128)
            XBv = self.XB.rearrange("(k p) n -> p k n", p=128)
            QTv = self.QT.rearrange("(k p) n -> p k n", p=128)
            for blk in range(16):
                b = blk % 2
                tok0 = blk * 512
                t.dma(xf[:, b], xv[:, :, tok0:tok0 + 512], reads=["X1"], writes=["p1_xf%d" % b])
                t.op('pool', lambda e, b=b: e.tensor_copy(out=xb[:, b], in_=xf[:, b]), reads=["p1_xf%d" % b], writes=["p1_xb%d" % b])
                t.dma(XBv[:, :, tok0:tok0 + 512], xb[:, b], reads=["p1_xb%d" % b], writes=["XB_%d" % blk])
                for fo in range(16):
                    bank = fo % 4
                    for k in range(8):
                        t.op('pe', lambda e, k=k, fo=fo, b=b, bank=bank: e.matmul(self.ps[bank][:], wq[:, k, fo * 128:(fo + 1) * 128], xb[:, b, k, :],
                                                                             start=(k == 0), stop=(k == 7)),
                             reads=["p1_wq", "p1_xb%d" % b], writes=["ps%d" % bank], skip_self=True)
                    if fo % 2 == 0:
                        t.op('act', lambda e, fo=fo, b=b, bank=bank: e.copy(out=qb[:, b, fo, :], in_=self.ps[bank][:]), reads=["ps%d" % bank], writes=["p1_qb%d" % b])
                    else:
                        t.op('dve', lambda e, fo=fo, b=b, bank=bank: e.tensor_copy(out=qb[:, b, fo, :], in_=self.ps[bank][:]), reads=["ps%d" % bank], writes=["p1_qb%d" % b])
                t.dma(QTv[:, :, tok0:tok0 + 512], qb[:, b], reads=["p1_qb%d" % b], writes=["QT_%d" % blk])
            t.barrier()

    def peer_p2(self, layer):
        nc, t = self.nc, self.t
        NB = self.peer_nblk if hasattr(self, "peer_nblk") else 32
        with ExitStack() as ph:
            sb = lambda name, shape, dt=F32: ph.enter_context(self.sbuf(name, list(shape), dt))
            skf = sb("p2_skf", [128, 2, 128])
            skb = sb("p2_skb", [128, 2, 128], BF16)
            t.dma(skf[:], self.peer_skT[layer].rearrange("c d n -> d c n"), writes=["p2_skf"])
            t.op('dve', lambda e: e.tensor_copy(out=skb[:], in_=skf[:]), reads=["p2_skf"], writes=["p2_skb"])
            xb = sb("p2_xb", [128, 8, 256], BF16)
            qT = sb("p2_qT", [128, 16, 256], BF16)
            sc = sb("p2_sc", [128, 16, 128])
            sc2 = sb("p2_sc2", [128, 2, 128])
            cs = sb("p2_cs", [128, 8, 256])
            cs2 = sb("p2_cs2", [128, 2, 256])
            sv = sb("p2_sv", [128, 16, 16])
            ts_ = sb("p2_ts", [128, 8, 16])
            ex = sb("p2_ex", [128, 8, 16])
            st8 = sb("p2_st8", [128, 4, 8])
            TM = sb("p2_TM", [128, 3, 128])
            SM = sb("p2_SM", [128, 3, 256])
            QR = sb("p2_qr", [128, 2, 2, 16, 128], BF16)
            Pt = sb("p2_P", [128, 12, 128], BF16)
            Et = sb("p2_E", [128, 12, 128])
            Qt = sb("p2_Q", [128, 12, 128], BF16)
            Gs = sb("p2_Gs", [128, 256, 128], BF16)
            UTs = sb("p2_UT", [128, 2, 8, 512], BF16)
            Vs = sb("p2_V", [128, 2, 4, 1024], BF16)
            ga = sb("p2_ga", [128, 2, 256])
            Hh = sb("p2_H", [128, 2, 256], BF16)
            otok = sb("p2_otok", [128, 2, 1024])
            peT = sb("p2_peT", [128, 8, 256])
            XBv = self.XB.rearrange("(k p) n -> p k n", p=128)
            QTv = self.QT.rearrange("(k p) n -> p k n", p=128)
            PEv = self.PEo.rearrange("(k p) n -> p k n", p=128)
            Ubv = self.Ub[layer].rearrange("(k p) e -> p k e", p=128)
            Vbv = self.Vb[layer].rearrange("(i p) f -> p i f", p=128)
            NEG = -1.0e30
            import os as _os2
            TBK = int(_os2.environ.get("TBK", "7"))
            def topk(blk, restricted):
                tok0 = blk * 256
                b512 = tok0 // 512
                t.dma(qT[:], QTv[:, :, tok0:tok0 + 256], reads=["QT_%d" % b512], writes=["p2_qT"])
                for st in range(2):
                    tsl = slice(st * 128, (st + 1) * 128)
                    for r in range(2):
                        for q in (2 * r, 2 * r + 1):
                            bank = [4, 7][q % 2] if restricted else q
                            for h4 in range(4):
                                hc = q * 4 + h4
                                t.op('pe', lambda e, hc=hc, h4=h4, bank=bank, tsl=tsl: e.matmul(self.ps[bank][:, h4 * 128:(h4 + 1) * 128], qT[:, hc, tsl], skb[:, hc % 2, :],
                                                                                          start=True, stop=True),
                                     reads=["p2_qT", "p2_skb"], writes=["ps%d" % bank], skip_self=True)
                        for q in (2 * r, 2 * r + 1):
                            bank = [4, 7][q % 2] if restricted else q
                            t.op('act', lambda e, q=q, bank=bank: e.copy(out=sc[:, q * 4:(q + 1) * 4, :], in_=self.ps[bank][:].rearrange("p (a n) -> p a n", a=4)),
                                 reads=["ps%d" % bank], writes=["p2_sc%d" % q])
                    for hc in range(16):
                        kk = "p2_sc%d" % (hc // 4)
                        t.op('dve', lambda e, hc=hc: e.max(out=sv[:, hc, 0:8], in_=sc[:, hc, :]), reads=[kk], writes=["p2_sv%d" % hc])
                        t.op('dve', lambda e, hc=hc: e.match_replace(out=sc2[:, hc % 2, :], in_to_replace=sv[:, hc, 0:8], in_values=sc[:, hc, :], imm_value=NEG),
                             reads=[kk, "p2_sv%d" % hc], writes=["p2_sc2_%d" % (hc % 2)])
                        t.op('dve', lambda e, hc=hc: e.max(out=sv[:, hc, 8:16], in_=sc2[:, hc % 2, :]), reads=["p2_sc2_%d" % (hc % 2)], writes=["p2_sv%d" % hc])
                    svk = ["p2_sv%d" % hc for hc in range(16)]
                    sv4 = sv[:].rearrange("p (h c) a -> p h c a", c=2)
                    t.op('dve', lambda e: e.tensor_tensor(out=cs[:].rearrange("p h (a b) -> p h a b", a=16),
                                                          in0=bc(sv4[:, :, 0, :].unsqueeze(3), [128, 8, 16, 16]),
                                                          in1=bc(sv4[:, :, 1, :].unsqueeze(2), [128, 8, 16, 16]), op=ALU.add),
                         reads=svk, writes=["p2_cs"])
                    for h in range(8):
                        t.op('dve', lambda e, h=h: e.max(out=ts_[:, h, 0:8], in_=cs[:, h, :]), reads=["p2_cs"], writes=["p2_ts%d" % h])
                        t.op('dve', lambda e, h=h: e.match_replace(out=cs2[:, h % 2, :], in_to_replace=ts_[:, h, 0:8], in_values=cs[:, h, :], imm_value=NEG),
                             reads=["p2_cs", "p2_ts%d" % h], writes=["p2_cs2_%d" % (h % 2)])
                        t.op('dve', lambda e, h=h: e.max(out=ts_[:, h, 8:16], in_=cs2[:, h % 2, :]), reads=["p2_cs2_%d" % (h % 2)], writes=["p2_ts%d" % h])
                    tsk = ["p2_ts%d" % h for h in range(8)]
                    t.op('dve', lambda e: e.tensor_tensor(out=ex[:], in0=ts_[:], in1=bc(ts_[:, :, 0:1], [128, 8, 16]), op=ALU.subtract), reads=tsk, writes=["p2_ex"])
                    t.op('act', lambda e: e.activation(out=ex[:], in_=ex[:], func=AF.Exp), reads=["p2_ex"], writes=["p2_ex"])
                    t.op('dve', lambda e: e.tensor_reduce(out=st8[:, 0, :], in_=ex[:], axis=mybir.AxisListType.X, op=ALU.add), reads=["p2_ex"], writes=["p2_st8"])
                    t.op('act', lambda e: e.activation(out=st8[:, 1, :], in_=st8[:, 0, :], func=AF.Ln), reads=["p2_st8"], writes=["p2_st8"])
                    t.op('dve', lambda e: e.tensor_tensor(out=st8[:, 2, :], in0=st8[:, 1, :], in1=ts_[:, :, 0], op=ALU.add), reads=["p2_st8"] + tsk, writes=["p2_st8"])
                    TMv = TM[:].rearrange("p j (h a) -> p j h a", h=8)
                    t.op('pool', lambda e: e.tensor_copy(out=TMv[:, 0], in_=sv4[:, :, 0, :]), reads=svk, writes=["p2_TM0"])
                    t.op('dve', lambda e: e.scalar_tensor_tensor(out=st8[:, 3, :], in0=ts_[:, :, 15], scalar=-1.0e-5, in1=st8[:, 2, :], op0=ALU.add, op1=ALU.subtract),
                         reads=tsk + ["p2_st8"], writes=["p2_st8"])
                    t.op('act', lambda e: e.activation(out=st8[:, 3, :], in_=st8[:, 3, :], func=AF.Exp), reads=["p2_st8"], writes=["p2_st8"])
                    t.op('dve', lambda e: e.tensor_copy(out=TMv[:, 1], in_=bc(st8[:, 3, :].unsqueeze(2), [128, 8, 16])), reads=["p2_st8"], writes=["p2_TM1"])
                    t.op('dve', lambda e: e.tensor_tensor(out=TMv[:, 2], in0=sv4[:, :, 0, :], in1=bc(st8[:, 2, :].unsqueeze(2), [128, 8, 16]), op=ALU.subtract),
                         reads=svk + ["p2_st8"], writes=["p2_TM2"])
                    for j in range(3):
                        t.op('pe', lambda e, j=j: e.transpose(self.ps[TBK][:, j * 128:(j + 1) * 128], TM[:, j, :], self.ident[:]),
                             reads=["p2_TM%d" % j, "ident"], writes=["ps%d" % TBK], skip_self=True)
                    t.op('act', lambda e, tsl=tsl: e.copy(out=SM[:, :, tsl], in_=self.ps[TBK][:, 0:384].rearrange("p (j n) -> p j n", j=3)),
                         reads=["ps%d" % TBK], writes=["p2_SM%d" % st])
            for blk in range(NB):
                tok0 = blk * 256
                b512 = tok0 // 512
                t.dma(xb[:], XBv[:, :, tok0:tok0 + 256], reads=["XB_%d" % b512], writes=["p2_xb"])
                if blk == 0:
                    topk(0, False)
                import os as _os
                _old = _os.environ.get("BANKMAP") == "old"
                B0 = lambda pg: [4, 5, 2][pg]
                B1 = lambda pg: [6, 7, 3][pg]
                GB = lambda pg: pg % 2
                TB = 4 if _old else 7

                def fill_qr(c16):
                    qb_ = c16 % 2
                    for c in range(2):
                        src = bc(qT[:, c::2, c16 * 16:(c16 + 1) * 16].rearrange("p h t -> p t h").unsqueeze(3), [128, 16, 8, 16])
                        dst = QR[:, qb_, c].rearrange("p t (h a) -> p t h a", h=8)
                        t.op('act', lambda e, src=src, dst=dst: e.copy(out=dst, in_=src), reads=["p2_qT"], writes=["p2_qr%d_%d" % (qb_, c)])

                def stageA(g):
                    if g % 4 == 0:
                        fill_qr(g // 4)
                    qb_ = (g // 4) % 2
                    pg = g % 3
                    for s4 in range(4):
                        tl = (g % 4) * 4 + s4
                        t.op('pe', lambda e, tl=tl, s4=s4, qb_=qb_, pg=pg: e.matmul(self.ps[B0(pg)][:, s4 * 128:(s4 + 1) * 128], QR[:, qb_, 0, tl, :], skb[:, 0, :], start=True, stop=True),
                             reads=["p2_qr%d_0" % qb_, "p2_skb"], writes=["ps%d" % B0(pg)], skip_self=True)
                        t.op('pe', lambda e, tl=tl, s4=s4, qb_=qb_, pg=pg: e.matmul(self.ps[B1(pg)][:, s4 * 128:(s4 + 1) * 128], QR[:, qb_, 1, tl, :], skb[:, 1, :], start=True, stop=True),
                             reads=["p2_qr%d_1" % qb_, "p2_skb"], writes=["ps%d" % B1(pg)], skip_self=True)

                def stageB(g):
                    pg = g % 3
                    for s4 in range(4):
                        tt_ = g * 4 + s4
                        sl = pg * 4 + s4
                        smk = "p2_SM%d" % (tt_ // 128)
                        k0 = "ps%d" % B0(pg)
                        k1 = "ps%d" % B1(pg)
                        t.op('dve', lambda e, s4=s4, tt_=tt_, sl=sl, pg=pg: e.tensor_scalar(out=Pt[:, sl, :], in0=self.ps[B0(pg)][:, s4 * 128:(s4 + 1) * 128], scalar1=SM[:, 0, tt_:tt_ + 1], scalar2=None,
                                                                                       op0=ALU.is_equal, op1=ALU.bypass),
                             reads=[k0, smk], writes=["p2_P%d" % sl])
                        t.op('act', lambda e, s4=s4, tt_=tt_, sl=sl, pg=pg: e.activation(out=Et[:, sl, :], in_=self.ps[B1(pg)][:, s4 * 128:(s4 + 1) * 128], func=AF.Exp, bias=SM[:, 2, tt_:tt_ + 1], scale=1.0),
                             reads=[k1, smk], writes=["p2_E%d" % sl])
                        t.op('dve', lambda e, s4=s4, tt_=tt_, sl=sl, pg=pg: e.scalar_tensor_tensor(out=Qt[:, sl, :], in0=Et[:, sl, :], scalar=SM[:, 1, tt_:tt_ + 1],
                                                                                              in1=Et[:, sl, :], op0=ALU.is_ge, op1=ALU.mult),
                             reads=[smk, "p2_E%d" % sl], writes=["p2_Q%d" % sl])

                def stageC(g):
                    pg = g % 3
                    for s4 in range(4):
                        sl = pg * 4 + s4
                        t.op('pe', lambda e, s4=s4, sl=sl, pg=pg, g=g: e.matmul(self.ps[g % 2][:, s4 * 128:(s4 + 1) * 128], Qt[:, sl, :], Pt[:, sl, :], start=True, stop=True),
                             reads=["p2_Q%d" % sl, "p2_P%d" % sl], writes=["ps%d" % (g % 2)], skip_self=True)
                    t4 = g * 4
                    t.op('act', lambda e, t4=t4, pg=pg, g=g: e.copy(out=Gs[:, t4:t4 + 4, :], in_=self.ps[g % 2][:].rearrange("p (t i) -> p t i", t=4)),
                         reads=["ps%d" % (g % 2)], writes=["p2_Gs"])

                opt_tok = getattr(self, "opt_tok", True)
                opt_dense = getattr(self, "opt_dense", True)
                if opt_tok:
                    stageA(0)
                    stageA(1)
                for g in range(64):
                    if opt_tok:
                        if g + 2 < 64:
                            stageA(g + 2)
                    else:
                        stageA(g)
                    stageB(g)
                    stageC(g)
                def load_w(ib):
                    wbuf = ib % 2
                    e0 = ib * 512
                    ukeys = ["Ub%d_%d_%d" % (layer, r0, e0 // 2048) for r0 in range(8)]
                    vkeys = ["Vb%d_%d" % (layer, (ib * 512) // 256 + x) for x in range(2)]
                    t.dma(UTs[:, wbuf], Ubv[:, :, e0:e0 + 512], reads=ukeys, writes=["p2_UT%d" % wbuf])
                    t.dma(Vs[:, wbuf], Vbv[:, ib * 4:(ib + 1) * 4, :], reads=vkeys, writes=["p2_V%d" % wbuf])

                def act_mm(i):
                    ib, ii = i // 4, i % 4
                    wbuf = ib % 2
                    abank = 5 + i % 2
                    for k in range(8):
                        t.op('pe', lambda e, k=k, ii=ii, wbuf=wbuf, abank=abank: e.matmul(self.ps[abank][:, 0:256], UTs[:, wbuf, k, ii * 128:(ii + 1) * 128], xb[:, k, :],
                                                                                    start=(k == 0), stop=(k == 7)),
                             reads=["p2_UT%d" % wbuf, "p2_xb"], writes=["ps%d" % abank], skip_self=True)

                load_w(0)
                if opt_dense:
                    act_mm(0)
                for i in range(128):
                    ib, ii = i // 4, i % 4
                    wbuf = ib % 2
                    ab = i % 2
                    abank = 5 + ab
                    if ii == 0 and ib + 1 < 32:
                        load_w(ib + 1)
                    if i == 24 and blk + 1 < NB:
                        topk(blk + 1, True)
                    if opt_dense:
                        if i + 1 < 128:
                            act_mm(i + 1)
                    else:
                        act_mm(i)
                    t.op('act', lambda e, ab=ab, abank=abank: e.activation(out=ga[:, ab, :], in_=self.ps[abank][:, 0:256], func=AF.Gelu_apprx_tanh),
                         reads=["ps%d" % abank], writes=["p2_ga%d" % ab])
                    t.op('dve', lambda e, ab=ab, i=i: e.tensor_tensor(out=Hh[:, ab, :], in0=ga[:, ab, :], in1=Gs[:, :, i], op=ALU.mult),
                         reads=["p2_ga%d" % ab, "p2_Gs"], writes=["p2_H%d" % ab])
                    for st in range(2):
                        for fh in range(2):
                            ob = st * 2 + fh
                            t.op('pe', lambda e, st=st, fh=fh, ob=ob, ab=ab, ii=ii, wbuf=wbuf, i=i: e.matmul(
                                self.ps[ob][:], Hh[:, ab, st * 128:(st + 1) * 128], Vs[:, wbuf, ii, fh * 512:(fh + 1) * 512], start=(i == 0), stop=(i == 127)),
                                reads=["p2_H%d" % ab, "p2_V%d" % wbuf], writes=["ps%d" % ob], skip_self=True)
                for st in range(2):
                    for fh in range(2):
                        ob = st * 2 + fh
                        if ob % 2 == 0:
                            t.op('act', lambda e, st=st, fh=fh, ob=ob: e.copy(out=otok[:, st, fh * 512:(fh + 1) * 512], in_=self.ps[ob][:]), reads=["ps%d" % ob], writes=["p2_otok%d" % st])
                        else:
                            t.op('dve', lambda e, st=st, fh=fh, ob=ob: e.tensor_copy(out=otok[:, st, fh * 512:(fh + 1) * 512], in_=self.ps[ob][:]), reads=["ps%d" % ob], writes=["p2_otok%d" % st])
                for st in range(2):
                    for half in range(2):
                        tb = 5 + half
                        for kk in range(4):
                            fk = half * 4 + kk
                            t.op('pe', lambda e, st=st, fk=fk, kk=kk, tb=tb: e.transpose(self.ps[tb][:, kk * 128:(kk + 1) * 128], otok[:, st, fk * 128:(fk + 1) * 128], self.ident[:]),
                                 reads=["p2_otok%d" % st, "ident"], writes=["ps%d" % tb], skip_self=True)
                        if half == 0:
                            t.op('act', lambda e, st=st, half=half, tb=tb: e.copy(out=peT[:, half * 4:(half + 1) * 4, st * 128:(st + 1) * 128],
                                                                                in_=self.ps[tb][:].rearrange("p (k n) -> p k n", k=4)),
                                 reads=["ps%d" % tb], writes=["p2_peT"])
                        else:
                            t.op('dve', lambda e, st=st, half=half, tb=tb: e.tensor_copy(out=peT[:, half * 4:(half + 1) * 4, st * 128:(st + 1) * 128],
                                                                                       in_=self.ps[tb][:].rearrange("p (k n) -> p k n", k=4)),
                                 reads=["ps%d" % tb], writes=["p2_peT"])
                t.dma(PEv[:, :, tok0:tok0 + 256], peT[:], reads=["p2_peT"], writes=["PEo_%d" % blk])
            t.barrier()

    def peer_p3(self, layer, xin, xout):
        nc, t = self.nc, self.t
        NB = (self.peer_nblk + 1) // 2 if hasattr(self, "peer_nblk") else 16
        with ExitStack() as ph:
            sb = lambda name, shape, dt=F32: ph.enter_context(self.sbuf(name, list(shape), dt))
            self.mk_eps(ph)
            wpg = self.load_w_bf16(ph, "p3_wpg", self.w_ple_gate[layer], 8, D, eng_cast='mix')
            wpp = self.load_w_bf16(ph, "p3_wpp", self.w_ple_proj[layer], 2, D, eng_cast='mix')
            xf = sb("p3_xf", [128, 8, 512])
            pe = sb("p3_pe", [128, 8, 512])
            pf = sb("p3_pf", [128, 2, 512])
            pb = sb("p3_pb", [128, 2, 512], BF16)
            z = sb("p3_z", [128, 8, 512])
            tmp = {'zc': sb("p3_zc", [128, 8, 512]), 'sq': sb("p3_sq", [128, 8, 512]), 'sd': sb("p3_sd", [128, 512])}
            x2 = sb("p3_x2", [128, 8, 512])
            x2b = sb("p3_x2b", [128, 8, 512], BF16)
            sg_ = sb("p3_sg", [128, 2, 512])
            xo = sb("p3_xo", [128, 8, 512])
            xv = xin.rearrange("(k p) n -> p k n", p=128)
            PEv = self.PEo.rearrange("(k p) n -> p k n", p=128)
            pv = self.pT[layer].rearrange("(k p) n -> p k n", p=128)
            ov = xout.rearrange("(k p) n -> p k n", p=128)
            for blk in range(NB):
                tok0 = blk * 512
                t.dma(xf[:], xv[:, :, tok0:tok0 + 512], reads=["X1"], writes=["p3_xf"])
                t.dma(pe[:], PEv[:, :, tok0:tok0 + 512], reads=["PEo_%d" % (2 * blk), "PEo_%d" % (2 * blk + 1)], writes=["p3_pe"])
                t.dma(pf[:], pv[:, :, tok0:tok0 + 512], writes=["p3_pf"])
                t.op('pool', lambda e: e.tensor_copy(out=pb[:], in_=pf[:]), reads=["p3_pf"], writes=["p3_pb"])
                t.op('dve', lambda e: e.scalar_tensor_tensor(out=z[:], in0=xf[:], scalar=ALPHA, in1=pe[:], op0=ALU.mult, op1=ALU.add),
                     reads=["p3_xf", "p3_pe"], writes=["p3_z"])
                self.ln_block(z, "p3_z", layer, 1, x2, "p3_x2", 512, tmp, 0, 1)
                t.op('pool', lambda e: e.tensor_copy(out=x2b[:], in_=x2[:]), reads=["p3_x2"], writes=["p3_x2b"])
                for fo in range(8):
                    bg, bp = 2 + (fo % 2) * 2, 3 + (fo % 2) * 2
                    sgi = fo % 2
                    for k in range(8):
                        t.op('pe', lambda e, k=k, fo=fo, bg=bg: e.matmul(self.ps[bg][:], wpg[:, k, fo * 128:(fo + 1) * 128], x2b[:, k, :], start=(k == 0), stop=(k == 7)),
                             reads=["p3_wpg", "p3_x2b"], writes=["ps%d" % bg], skip_self=True)
                    for k in range(2):
                        t.op('pe', lambda e, k=k, fo=fo, bp=bp: e.matmul(self.ps[bp][:], wpp[:, k, fo * 128:(fo + 1) * 128], pb[:, k, :], start=(k == 0), stop=(k == 1)),
                             reads=["p3_wpp", "p3_pb"], writes=["ps%d" % bp], skip_self=True)
                    t.op('act', lambda e, bg=bg, sgi=sgi: e.activation(out=sg_[:, sgi, :], in_=self.ps[bg][:], func=AF.Sigmoid), reads=["ps%d" % bg], writes=["p3_sg%d" % sgi])
                    t.op('dve', lambda e, bp=bp, sgi=sgi: e.tensor_tensor(out=sg_[:, sgi, :], in0=self.ps[bp][:], in1=sg_[:, sgi, :], op=ALU.mult),
                         reads=["ps%d" % bp, "p3_sg%d" % sgi], writes=["p3_sg%d" % sgi])
                    t.op('pool', lambda e, fo=fo, sgi=sgi: e.tensor_tensor(out=xo[:, fo, :], in0=x2[:, fo, :], in1=sg_[:, sgi, :], op=ALU.add),
                         reads=["p3_x2", "p3_sg%d" % sgi], writes=["p3_xo"])
                t.dma(ov[:, :, tok0:tok0 + 512], xo[:], reads=["p3_xo"], writes=["XOUT%d_%d" % (layer, blk)])
            t.barrier()

    def phase_rg(self):
        self.rg_p1()
        self.rg_p2()
        self.rg_p3()

    def rg_p1(self):
        nc, t = self.nc, self.t
        with ExitStack() as ph:
            sb = lambda name, shape, dt=F32: ph.enter_context(self.sbuf(name, list(shape), dt))
            win = self.load_w_bf16(ph, "r1_win", self.w_rg_in, 8, 2 * D, eng_cast='mix')
            xf = sb("r1_xf", [128, 2, 8, 512])
            xb = sb("r1_xb", [128, 2, 8, 512], BF16)
            gg = sb("r1_gg", [128, 2, 8, 512], BF16)
            rr = sb("r1_rr", [128, 2, 8, 512])
            xv = self.XL1.rearrange("(k p) n -> p k n", p=128)
            ggv = self.RGg.rearrange("(k p) n -> p k n", p=128)
            rrv = self.RGr.rearrange("(k p) n -> p k n", p=128)
            for blk in range(16):
                b = blk % 2
                tok0 = blk * 512
                t.dma(xf[:, b], xv[:, :, tok0:tok0 + 512], writes=["r1_xf%d" % b])
                t.op('pool', lambda e, b=b: e.tensor_copy(out=xb[:, b], in_=xf[:, b]), reads=["r1_xf%d" % b], writes=["r1_xb%d" % b])
                for fo in range(16):
                    bank = fo % 4
                    for k in range(8):
                        t.op('pe', lambda e, k=k, fo=fo, b=b, bank=bank: e.matmul(self.ps[bank][:], win[:, k, fo * 128:(fo + 1) * 128], xb[:, b, k, :],
                                                                             start=(k == 0), stop=(k == 7)),
                             reads=["r1_win", "r1_xb%d" % b], writes=["ps%d" % bank], skip_self=True)
                    if fo < 8:
                        t.op('act', lambda e, fo=fo, b=b, bank=bank: e.activation(out=gg[:, b, fo, :], in_=self.ps[bank][:], func=AF.Gelu_apprx_tanh),
                             reads=["ps%d" % bank], writes=["r1_gg%d" % b])
                    else:
                        t.op('dve', lambda e, fo=fo, b=b, bank=bank: e.tensor_copy(out=rr[:, b, fo - 8, :], in_=self.ps[bank][:]), reads=["ps%d" % bank], writes=["r1_rr%d" % b])
                t.dma(ggv[:, :, tok0:tok0 + 512], gg[:, b], reads=["r1_gg%d" % b], writes=["RGg_%d" % blk])
                t.dma(rrv[:, :, tok0:tok0 + 512], rr[:, b], reads=["r1_rr%d" % b], writes=["RGr_%d" % blk])
            t.barrier()

    def rg_p2(self):
        nc, t = self.nc, self.t
        LM = 4096
        with ExitStack() as ph:
            sb = lambda name, shape, dt=F32: ph.enter_context(self.sbuf(name, list(shape), dt))
            cw = sb("r2_cw", [128, 8, 4]); cbias = sb("r2_cbias", [128, 8])
            bga = sb("r2_bga", [128, 2, 8]); bgx = sb("r2_bgx", [128, 2, 8]); lam = sb("r2_lam", [128, 2, 8])
            sp8 = sb("r2_sp8", [128, 2, 8]); sp16 = sb("r2_sp16", [128, 2, 8])
            for dst, src in [(cw, self.rg_convw), (cbias, self.rg_convb), (bga, self.rg_bga), (bgx, self.rg_bgx), (lam, self.rg_lam)]:
                t.dma(dst[:], src, writes=[dst.name])
            t.op('act', lambda e: e.activation(out=sp8[:], in_=lam[:], func=AF.Exp, scale=-1.0), reads=[lam.name], writes=["r2_sp8"])
            t.op('act', lambda e: e.activation(out=sp8[:], in_=sp8[:], func=AF.Ln, bias=1.0, scale=1.0), reads=["r2_sp8"], writes=["r2_sp8"])
            t.op('dve', lambda e: e.tensor_scalar(out=sp16[:], in0=sp8[:], scalar1=-16.0, scalar2=None, op0=ALU.mult, op1=ALU.bypass), reads=["r2_sp8"], writes=["r2_sp16"])
            t.op('dve', lambda e: e.tensor_scalar(out=sp8[:], in0=sp8[:], scalar1=-8.0, scalar2=None, op0=ALU.mult, op1=ALU.bypass), reads=["r2_sp8", "r2_sp16"], writes=["r2_sp8"])
            wst = sb("r2_wst", [128, 2, 256])
            wgt = sb("r2_wgt", [128, 2, 2, 4, 2, 256], BF16)
            for gi, src in enumerate([self.rg_wga, self.rg_wgx]):
                for d_ in range(2):
                    for h in range(4):
                        t.dma(wst[:], src[d_, h].rearrange("(i p) o -> p i o", p=128), writes=["r2_wst"])
                        t.op('pool', lambda e, gi=gi, d_=d_, h=h: e.tensor_copy(out=wgt[:, gi, d_, h], in_=wst[:]), reads=["r2_wst"], writes=["r2_wgt"])
            rp = sb("r2_rp", [128, 2, LM + 3])
            cc = sb("r2_cc", [128, 2, LM])
            cb = sb("r2_cb", [128, 2, LM], BF16)
            A = sb("r2_A", [128, LM]); B = sb("r2_B", [128, LM])
            HF = sb("r2_HF", [128, LM]); HB = sb("r2_HB", [128, LM])
            gg = sb("r2_gg", [128, LM], BF16); yy = sb("r2_yy", [128, LM], BF16)
            rg_ = sb("r2_rg", [128, 2, 512]); ig_ = sb("r2_ig", [128, 2, 512]); a2_ = sb("r2_a2", [128, 2, 512]); tm_ = sb("r2_tm", [128, 2, 512])
            for si, (s0, L) in enumerate(SEGS):
                for h in range(4):
                    for ct in range(2):
                        ch = 2 * h + ct
                        t.op('pool', lambda e, ct=ct, L=L: e.memset(rp[:, ct, 0:1], 0.0), writes=["r2_rp%d" % ct])
                        t.op('pool', lambda e, ct=ct, L=L: e.memset(rp[:, ct, L + 1:L + 3], 0.0), reads=["r2_rp%d" % ct], writes=["r2_rp%d" % ct])
                        t.dma(rp[:, ct, 1:L + 1], self.RGr[ch * 128:(ch + 1) * 128, s0:s0 + L], reads=["r2_rp%d" % ct], writes=["r2_rp%d" % ct])
                        t.op('dve', lambda e, ct=ct, ch=ch, L=L: e.tensor_scalar(out=cc[:, ct, 0:L], in0=rp[:, ct, 0:L], scalar1=cw[:, ch, 0:1], scalar2=cbias[:, ch:ch + 1],
                                                                            op0=ALU.mult, op1=ALU.add), reads=["r2_rp%d" % ct, cw.name, cbias.name], writes=["r2_cc%d" % ct])
                        for k in range(1, 4):
                            t.op('dve', lambda e, ct=ct, ch=ch, L=L, k=k: e.scalar_tensor_tensor(out=cc[:, ct, 0:L], in0=rp[:, ct, k:k + L], scalar=cw[:, ch, k:k + 1], in1=cc[:, ct, 0:L],
                                                                                           op0=ALU.mult, op1=ALU.add), reads=["r2_rp%d" % ct, cw.name, "r2_cc%d" % ct], writes=["r2_cc%d" % ct])
                        t.op('pool', lambda e, ct=ct, L=L: e.tensor_copy(out=cb[:, ct, 0:L], in_=cc[:, ct, 0:L]), reads=["r2_cc%d" % ct], writes=["r2_cb%d" % ct])
                    for oh in range(2):
                        ch = 2 * h + oh
                        for d_ in range(2):
                            Hd = HF if d_ == 0 else HB
                            for c0 in range(0, L, 512):
                                pi = (c0 // 512) % 2
                                ba, bx = 2 * pi, 2 * pi + 1
                                for ih in range(2):
                                    t.op('pe', lambda e, ih=ih, d_=d_, h=h, oh=oh, c0=c0, ba=ba: e.matmul(self.ps[ba][:], wgt[:, 0, d_, h, ih, oh * 128:(oh + 1) * 128], cb[:, ih, c0:c0 + 512],
                                                                                                    start=(ih == 0), stop=(ih == 1)),
                                         reads=["r2_wgt", "r2_cb0", "r2_cb1"], writes=["ps%d" % ba], skip_self=True)
                                for ih in range(2):
                                    t.op('pe', lambda e, ih=ih, d_=d_, h=h, oh=oh, c0=c0, bx=bx: e.matmul(self.ps[bx][:], wgt[:, 1, d_, h, ih, oh * 128:(oh + 1) * 128], cb[:, ih, c0:c0 + 512],
                                                                                                    start=(ih == 0), stop=(ih == 1)),
                                         reads=["r2_wgt", "r2_cb0", "r2_cb1"], writes=["ps%d" % bx], skip_self=True)
                                t.op('act', lambda e, pi=pi, ba=ba, d_=d_, ch=ch: e.activation(out=rg_[:, pi, :], in_=self.ps[ba][:], func=AF.Sigmoid, bias=bga[:, d_, ch:ch + 1], scale=1.0),
                                     reads=["ps%d" % ba, bga.name], writes=["r2_rg%d" % pi])
                                t.op('act', lambda e, pi=pi, bx=bx, d_=d_, ch=ch: e.activation(out=ig_[:, pi, :], in_=self.ps[bx][:], func=AF.Sigmoid, bias=bgx[:, d_, ch:ch + 1], scale=1.0),
                                     reads=["ps%d" % bx, bgx.name], writes=["r2_ig%d" % pi])
                                t.op('act', lambda e, pi=pi, c0=c0, d_=d_, ch=ch: e.activation(out=A[:, c0:c0 + 512], in_=rg_[:, pi, :], func=AF.Exp, scale=sp8[:, d_, ch:ch + 1]),
                                     reads=["r2_rg%d" % pi, "r2_sp8"], writes=["r2_A"])
                                t.op('act', lambda e, pi=pi, d_=d_, ch=ch: e.activation(out=a2_[:, pi, :], in_=rg_[:, pi, :], func=AF.Exp, scale=sp16[:, d_, ch:ch + 1]),
                                     reads=["r2_rg%d" % pi, "r2_sp16"], writes=["r2_a2%d" % pi])
                                t.op('dve', lambda e, pi=pi: e.tensor_scalar(out=a2_[:, pi, :], in0=a2_[:, pi, :], scalar1=-1.0, scalar2=1.0, op0=ALU.mult, op1=ALU.add),
                                     reads=["r2_a2%d" % pi], writes=["r2_a2%d" % pi])
                                t.op('act', lambda e, pi=pi: e.activation(out=a2_[:, pi, :], in_=a2_[:, pi, :], func=AF.Sqrt), reads=["r2_a2%d" % pi], writes=["r2_a2%d" % pi])
                                t.op('pool', lambda e, pi=pi, oh=oh, c0=c0: e.tensor_tensor(out=tm_[:, pi, :], in0=ig_[:, pi, :], in1=cc[:, oh, c0:c0 + 512], op=ALU.mult),
                                     reads=["r2_ig%d" % pi, "r2_cc%d" % oh], writes=["r2_tm%d" % pi])
                                t.op('dve', lambda e, pi=pi, c0=c0: e.tensor_tensor(out=B[:, c0:c0 + 512], in0=tm_[:, pi, :], in1=a2_[:, pi, :], op=ALU.mult),
                                     reads=["r2_tm%d" % pi, "r2_a2%d" % pi], writes=["r2_B"])
                            if d_ == 0:
                                t.op('dve', lambda e, L=L: e.tensor_tensor_scan(out=HF[:, 0:L], data0=A[:, 0:L], data1=B[:, 0:L], initial=0.0, op0=ALU.mult, op1=ALU.add),
                                     reads=["r2_A", "r2_B"], writes=["r2_HF"])
                            else:
                                t.op('dve', lambda e, L=L: e.tensor_tensor_scan(out=HB[:, 0:L][:, ::-1], data0=A[:, 0:L][:, ::-1], data1=B[:, 0:L][:, ::-1], initial=0.0, op0=ALU.mult, op1=ALU.add),
                                     reads=["r2_A", "r2_B"], writes=["r2_HB"])
                        t.dma(gg[:, 0:L], self.RGg[ch * 128:(ch + 1) * 128, s0:s0 + L], writes=["r2_gg"])
                        t.op('pool', lambda e, L=L: e.tensor_tensor(out=HF[:, 0:L], in0=HF[:, 0:L], in1=HB[:, 0:L], op=ALU.add), reads=["r2_HF", "r2_HB"], writes=["r2_HF"])
                        t.op('dve', lambda e, L=L: e.tensor_tensor(out=yy[:, 0:L], in0=HF[:, 0:L], in1=gg[:, 0:L], op=ALU.mult), reads=["r2_HF", "r2_gg"], writes=["r2_yy"])
                        t.dma(self.RGy[ch * 128:(ch + 1) * 128, s0:s0 + L], yy[:, 0:L], reads=["r2_yy"], writes=["RGy_%d_%d" % (si, ch)])
            t.barrier()

    def rg_p3(self):
        nc, t = self.nc, self.t
        with ExitStack() as ph:
            sb = lambda name, shape, dt=F32: ph.enter_context(self.sbuf(name, list(shape), dt))
            self.mk_eps(ph)
            wo = self.load_w_bf16(ph, "r3_wo", self.w_rg_out, 8, D, eng_cast='mix')
            yb = sb("r3_yb", [128, 8, 512], BF16)
            xf = sb("r3_xf", [128, 8, 512])
            z = sb("r3_z", [128, 8, 512])
            tmp = {'zc': sb("r3_zc", [128, 8, 512]), 'sq': sb("r3_sq", [128, 8, 512]), 'sd': sb("r3_sd", [128, 512])}
            xo = sb("r3_xo", [128, 8, 512])
            xv = self.XL1.rearrange("(k p) n -> p k n", p=128)
            yv = self.RGy.rearrange("(k p) n -> p k n", p=128)
            X1v = self.X1.rearrange("(k p) n -> p k n", p=128)
            for blk in range(16):
                tok0 = blk * 512
                t.dma(yb[:], yv[:, :, tok0:tok0 + 512], writes=["r3_yb"])
                t.dma(xf[:], xv[:, :, tok0:tok0 + 512], writes=["r3_xf"])
                for fo in range(8):
                    bank = 2 + fo % 4
                    for k in range(8):
                        t.op('pe', lambda e, k=k, fo=fo, bank=bank: e.matmul(self.ps[bank][:], wo[:, k, fo * 128:(fo + 1) * 128], yb[:, k, :], start=(k == 0), stop=(k == 7)),
                             reads=["r3_wo", "r3_yb"], writes=["ps%d" % bank], skip_self=True)
                    t.op('dve', lambda e, fo=fo, bank=bank: e.scalar_tensor_tensor(out=z[:, fo, :], in0=xf[:, fo, :], scalar=ALPHA, in1=self.ps[bank][:], op0=ALU.mult, op1=ALU.add),
                         reads=["r3_xf", "ps%d" % bank], writes=["r3_z"])
                self.ln_block(z, "r3_z", 1, 0, xo, "r3_xo", 512, tmp, 0, 1)
                t.dma(X1v[:, :, tok0:tok0 + 512], xo[:], reads=["r3_xo"], writes=["X1"])
            t.barrier()


def _consts():
    c = {}
    c["c_ident"] = np.eye(128, dtype=np.float32)
    c["c_onesm"] = np.full((128, 128), 1.0 / D, dtype=np.float32)
    s = np.arange(128) // 16
    c["c_maskf"] = (s[:, None] <= s[None, :]).astype(np.float32)
    c["c_maskb"] = (s[:, None] >= s[None, :]).astype(np.float32)
    m = np.ones((128, 1024), dtype=np.float32)
    m[0:64, [0, 512, 768]] = 0.0
    m[64:128, [511, 767, 1023]] = 0.0
    c["c_segmask"] = m
    return c


def _shared_weights(inp):
    f = lambda a: np.ascontiguousarray(np.asarray(a, dtype=np.float32))
    w = dict(_consts())
    w["s5_w_in"] = f(inp["s5_w_in"][0])
    w["s5_w_glu"] = f(inp["s5_w_glu"][0])
    w["s5_lamre"] = f(inp["s5_lam_re"][0].transpose(0, 2, 1).reshape(128, 64))
    w["s5_lamim"] = f(inp["s5_lam_im"][0].transpose(0, 2, 1).reshape(128, 64))
    w["s5_lstep"] = f(np.broadcast_to(inp["s5_log_step"][0][:, None, :], (2, 64, 64)).reshape(128, 64))
    w["s5_bre"] = f(inp["s5_b_re"][0].transpose(0, 2, 1, 3).reshape(128, 64, 16))
    w["s5_bim"] = f(inp["s5_b_im"][0].transpose(0, 2, 1, 3).reshape(128, 64, 16))
    w["s5_ctre"] = f(inp["s5_c_re"][0].transpose(0, 3, 1, 2).reshape(128, 64, 16))
    w["s5_ctim"] = f(inp["s5_c_im"][0].transpose(0, 3, 1, 2).reshape(128, 64, 16))
    d = np.asarray(inp["s5_d"][0]).reshape(64, 16)
    w["s5_dpk"] = f(np.broadcast_to(d.T[None, :, :], (8, 16, 64)).reshape(128, 64))
    w["rg_w_in"] = f(inp["rg_w_in"][0])
    w["rg_convw"] = f(inp["rg_conv_w"][0].reshape(4, 8, 128).transpose(2, 1, 0))
    w["rg_convb"] = f(inp["rg_conv_b"][0].reshape(8, 128).T)
    w["rg_wga"] = f(inp["rg_w_gate_a"][0])
    w["rg_wgx"] = f(inp["rg_w_gate_x"][0])
    w["rg_bga"] = f(inp["rg_b_gate_a"][0].reshape(2, 8, 128).transpose(2, 0, 1))
    w["rg_bgx"] = f(inp["rg_b_gate_x"][0].reshape(2, 8, 128).transpose(2, 0, 1))
    w["rg_lam"] = f(inp["rg_lambda"][0].reshape(2, 8, 128).transpose(2, 0, 1))
    w["rg_w_out"] = f(inp["rg_w_out"][0])
    ln = np.stack([np.asarray(inp[k]) for k in ("ln1_g", "ln1_b", "ln2_g", "ln2_b")], 0)
    w["ln_par"] = f(ln.reshape(4, 2, 8, 128).transpose(3, 0, 1, 2))
    w["peer_w_q"] = f(inp["peer_w_q"])
    w["peer_skT"] = f(np.asarray(inp["peer_subkeys"]).transpose(0, 1, 3, 2))
    w["peer_uT"] = f(np.asarray(inp["peer_u"]).transpose(0, 2, 1))
    w["peer_v"] = f(inp["peer_v"])
    w["ple_w_proj"] = f(inp["ple_w_proj"])
    w["ple_w_gate"] = f(inp["ple_w_gate"])
    return w


def _core_acts(inp, core):
    xp = np.asarray(inp["x_prompt"][core])
    xs = np.asarray(inp["x_sample"][2 * core:2 * core + 2]).reshape(4096, D)
    x = np.concatenate([xp, xs], 0)
    pp = np.asarray(inp["p_prompt"][:, core])
    ps = np.asarray(inp["p_sample"][:, 2 * core:2 * core + 2]).reshape(2, 4096, 256)
    p = np.concatenate([pp, ps], 1)
    return {"xT": np.ascontiguousarray(x.T.astype(np.float32)),
            "pT": np.ascontiguousarray(p.transpose(0, 2, 1).astype(np.float32))}


_CACHE = {}


def kernel(**inputs):
    if "nc" not in _CACHE:
        k = Ker()
        _CACHE["nc"] = k.build()
        _CACHE["in_names"] = list(k.in_names)
    nc = _CACHE["nc"]
    names = _CACHE["in_names"]
    w = _shared_weights(inputs)
    in_maps = []
    for core in range(8):
        m = dict(w)
        m.update(_core_acts(inputs, core))
        in_maps.append({n: m[n] for n in names})
    res = run_bass_kernel_spmd(nc, in_maps, core_ids=list(range(8)))
    yp = np.empty((8, 4096, D), dtype=np.float32)
    ys = np.empty((16, 2048, D), dtype=np.float32)
    for core in range(8):
        y = np.asarray(res.results[core]["yT"]).T
        yp[core] = y[0:4096]
        ys[2 * core] = y[4096:6144]
        ys[2 * core + 1] = y[6144:8192]
    return (yp, ys)
```

```python
from contextlib import ExitStack
import math
import numpy as np
import concourse.bass as bass
import concourse.mybir as mybir
from concourse.bass_utils import run_bass_kernel_spmd

F32 = mybir.dt.float32
BF16 = mybir.dt.bfloat16
I32 = mybir.dt.int32
AF = mybir.ActivationFunctionType
ALU = mybir.AluOpType

NTOK = 8192
D = 1024
ALPHA = 4.0 ** 0.25
LN_EPS = 1e-5
SEGS = [(0, 4096), (4096, 2048), (6144, 2048)]
TWO_PI = 2.0 * math.pi


class Trk:
    ENGS = ['pe', 'act', 'dve', 'pool', 'sp']

    def __init__(self, nc, stack, nslots=12):
        self.nc = nc
        self.e = {'pe': nc.tensor, 'act': nc.scalar, 'dve': nc.vector, 'pool': nc.gpsimd, 'sp': nc.sync}
        self.sem = {}
        self.cnt = {}
        for n in self.ENGS:
            self.sem[n] = stack.enter_context(nc.semaphore('s_' + n))
            self.cnt[n] = 0
        self.slots = ['d%d' % i for i in range(nslots)]
        for s in self.slots:
            self.sem[s] = stack.enter_context(nc.semaphore('s_' + s))
            self.cnt[s] = 0
        self.rr = 0
        self.seen = {n: {} for n in self.ENGS}
        self.lw = {}
        self.lr = {}
        self.ninst = 0

    def _deps(self, reads, writes):
        need = {}
        for k in reads:
            for e, c in self.lw.get(k, {}).items():
                if c > need.get(e, 0):
                    need[e] = c
        for k in writes:
            for e, c in self.lw.get(k, {}).items():
                if c > need.get(e, 0):
                    need[e] = c
            for e, c in self.lr.get(k, {}).items():
                if c > need.get(e, 0):
                    need[e] = c
        return need

    def _wait(self, eng, need, skip_self=False):
        for e, c in need.items():
            if skip_self and e == eng:
                continue
            if self.seen[eng].get(e, 0) >= c:
                continue
            self.e[eng].wait_ge(self.sem[e], c)
            self.seen[eng][e] = c

    def _record(self, who, c, reads, writes):
        for k in writes:
            self.lw[k] = {who: c}
            self.lr[k] = {}
        for k in reads:
            self.lr.setdefault(k, {})[who] = c

    def op(self, eng, fn, reads=(), writes=(), skip_self=False):
        need = self._deps(reads, writes)
        self._wait(eng, need, skip_self)
        ins = fn(self.e[eng])
        self.cnt[eng] += 1
        ins.then_inc(self.sem[eng], 1)
        self._record(eng, self.cnt[eng], reads, writes)
        self.ninst += 1

    def dma(self, out, in_, reads=(), writes=(), eng='sp'):
        need = self._deps(reads, writes)
        slot = self.slots[self.rr]
        self.rr = (self.rr + 1) % len(self.slots)
        if self.cnt[slot] > 0:
            need[slot] = max(need.get(slot, 0), self.cnt[slot])
        self._wait(eng, need)
        ins = self.e[eng].dma_start(out=out, in_=in_)
        self.cnt[slot] += 16
        ins.then_inc(self.sem[slot], 16)
        self._record(slot, self.cnt[slot], reads, writes)
        self.ninst += 1

    def barrier(self):
        self.min_rem = min(getattr(self, "min_rem", 1 << 30), self.nc.sbuf_bytes_remaining)
        allc = {k: v for k, v in self.cnt.items() if v > 0}
        for eng in self.ENGS:
            self._wait(eng, dict(allc), skip_self=True)

    def finish(self):
        allc = {k: v for k, v in self.cnt.items() if v > 0}
        self._wait('sp', dict(allc), skip_self=True)


def bc(ap, shape):
    return ap.to_broadcast(list(shape))


class Ker:
    def __init__(self, dbg_out=(), dbg_in=(), phases=None):
        self.dbg_out = set(dbg_out)
        self.dbg_in = set(dbg_in)
        self.phases = phases
        self.nc = bass.Bass("TRN2", target_bir_lowering=False)
        self.in_names = []
        self.out_names = []
        self.tmp_id = 0

    def sbuf(self, name, shape, dt):
        self.tmp_id += 1
        return self.nc.sbuf_tensor("%s_u%d" % (name, self.tmp_id), list(shape), dt)

    def din(self, name, shape, dt=F32):
        self.in_names.append(name)
        return self.nc.dram_tensor(name, list(shape), dt, kind="ExternalInput").ap()

    def dout(self, name, shape, dt=F32):
        self.out_names.append(name)
        return self.nc.dram_tensor(name, list(shape), dt, kind="ExternalOutput").ap()

    def scratch(self, name, shape, dt=F32):
        if name in self.dbg_in:
            return self.din(name, shape, dt)
        if name in self.dbg_out:
            return self.dout(name, shape, dt)
        return self.nc.dram_tensor(name, list(shape), dt, kind="Internal").ap()

    def __getattr__(self, attr):
        specs = self.__dict__.get("specs", {})
        if attr in specs:
            kind, name, shape = specs[attr]
            ap = self.din(name, shape)
            self.__dict__[attr] = ap
            return ap
        raise AttributeError(attr)

    def on(self, ph):
        return self.phases is None or ph in self.phases

    def build(self):
        nc = self.nc
        with ExitStack() as st:
            self.st = st
            self.t = Trk(nc, st)
            t = self.t
            self.specs = {
                "xT": ("din", "xT", [D, NTOK]),
                "pT": ("din", "pT", [2, 256, NTOK]),
                "yT": ("dout", "yT", [D, NTOK]),
                "c_ident": ("din", "c_ident", [128, 128]),
                "c_onesm": ("din", "c_onesm", [128, 128]),
                "c_maskf": ("din", "c_maskf", [128, 128]),
                "c_maskb": ("din", "c_maskb", [128, 128]),
                "c_segmask": ("din", "c_segmask", [128, 1024]),
                "w_s5_in": ("din", "s5_w_in", [D, D]),
                "w_s5_glu": ("din", "s5_w_glu", [D, 2 * D]),
                "s5_lamre": ("din", "s5_lamre", [128, 64]),
                "s5_lamim": ("din", "s5_lamim", [128, 64]),
                "s5_lstep": ("din", "s5_lstep", [128, 64]),
                "s5_bre": ("din", "s5_bre", [128, 64, 16]),
                "s5_bim": ("din", "s5_bim", [128, 64, 16]),
                "s5_ctre": ("din", "s5_ctre", [128, 64, 16]),
                "s5_ctim": ("din", "s5_ctim", [128, 64, 16]),
                "s5_dpk": ("din", "s5_dpk", [128, 64]),
                "w_rg_in": ("din", "rg_w_in", [D, 2 * D]),
                "rg_convw": ("din", "rg_convw", [128, 8, 4]),
                "rg_convb": ("din", "rg_convb", [128, 8]),
                "rg_wga": ("din", "rg_wga", [2, 4, 256, 256]),
                "rg_wgx": ("din", "rg_wgx", [2, 4, 256, 256]),
                "rg_bga": ("din", "rg_bga", [128, 2, 8]),
                "rg_bgx": ("din", "rg_bgx", [128, 2, 8]),
                "rg_lam": ("din", "rg_lam", [128, 2, 8]),
                "w_rg_out": ("din", "rg_w_out", [D, D]),
                "ln_par": ("din", "ln_par", [128, 4, 2, 8]),
                "w_peer_q": ("din", "peer_w_q", [2, D, 2 * D]),
                "peer_skT": ("din", "peer_skT", [2, 2, 128, 128]),
                "peer_uT": ("din", "peer_uT", [2, D, 16384]),
                "peer_v": ("din", "peer_v", [2, 16384, D]),
                "w_ple_proj": ("din", "ple_w_proj", [2, 256, D]),
                "w_ple_gate": ("din", "ple_w_gate", [2, D, D]),
            }
            self.yT = self.dout("yT", [D, NTOK]) if self.on("peer1") else None
            self.UpD = self.scratch("UpD", [64, 128, 1024], BF16)
            self.HpD = self.scratch("HpD", [64, 128, 1024], BF16)
            self.X1 = self.scratch("X1", [D, NTOK])
            self.XL1 = self.scratch("XL1", [D, NTOK])
            self.RGr = self.scratch("RGr", [D, NTOK])
            self.RGg = self.scratch("RGg", [D, NTOK], BF16)
            self.RGy = self.scratch("RGy", [D, NTOK], BF16)
            self.XB = self.scratch("XB", [D, NTOK], BF16)
            self.QT = self.scratch("QT", [2 * D, NTOK], BF16)
            self.PEo = self.scratch("PEo", [D, NTOK])
            self.Ub = self.scratch("Ub", [2, D, 16384], BF16)
            self.Vb = self.scratch("Vb", [2, 16384, D], BF16)
            self.ident = st.enter_context(self.sbuf("ident", [128, 128], F32))
            self.identb = st.enter_context(self.sbuf("identb", [128, 128], BF16))
            self.onesm = st.enter_context(self.sbuf("onesm", [128, 128], F32))
            self.lnp = st.enter_context(self.sbuf("lnp", [128, 4, 2, 8], F32))
            self.ps = [st.enter_context(nc.psum_tensor("ps%d" % i, [128, 512], F32)) for i in range(8)]
            t.dma(self.ident[:], self.c_ident, writes=["ident"])
            t.dma(self.onesm[:], self.c_onesm, writes=["onesm"])
            t.dma(self.lnp[:], self.ln_par, writes=["lnp"])
            t.op('dve', lambda e: e.tensor_copy(out=self.identb[:], in_=self.ident[:]), reads=["ident"], writes=["identb"])

            if self.on("tabcast"):
                self.phase_tabcast()
            if self.on("s5a"):
                self.phase_s5a()
            if self.on("s5b"):
                self.phase_s5b()
            if self.on("s5c"):
                self.phase_s5c()
            if self.on("peer0"):
                self.phase_peer(0, self.X1, self.XL1)
            if self.on("rg"):
                self.phase_rg()
            if self.on("peer1"):
                self.phase_peer(1, self.X1, self.yT)
            t.barrier()
            t.finish()
        return nc

    def load_w_bf16(self, ph, name, dram_ap, kt, ncols, eng_cast='pool'):
        nc, t = self.nc, self.t
        wb = ph.enter_context(self.sbuf(name, [128, kt, ncols], BF16))
        stg = ph.enter_context(self.sbuf(name + "_stg", [128, 2, 2048], F32))
        i = 0
        for k in range(kt):
            for c0 in range(0, ncols, 2048):
                cw = min(2048, ncols - c0)
                b = i % 2
                t.dma(stg[:, b, 0:cw], dram_ap[k * 128:(k + 1) * 128, c0:c0 + cw],
                      writes=[name + "_stg%d" % b])
                eng = ['pool', 'act'][i % 2] if eng_cast == 'mix' else eng_cast
                if eng == 'act':
                    t.op('act', lambda e, b=b, k=k, c0=c0, cw=cw: e.copy(out=wb[:, k, c0:c0 + cw], in_=stg[:, b, 0:cw]),
                         reads=[name + "_stg%d" % b], writes=[name])
                else:
                    t.op(eng, lambda e, b=b, k=k, c0=c0, cw=cw: e.tensor_copy(out=wb[:, k, c0:c0 + cw], in_=stg[:, b, 0:cw]),
                         reads=[name + "_stg%d" % b], writes=[name])
                i += 1
        return wb

    def ln_block(self, z, zkey, layer, which, out, outkey, N, tmp, pbank_a, pbank_b):
        t = self.t
        pm = self.ps[pbank_a]
        pv = self.ps[pbank_b]
        ka, kb = "ps%d" % pbank_a, "ps%d" % pbank_b
        for k in range(8):
            t.op('pe', lambda e, k=k: e.matmul(pm[:, 0:N], self.onesm[:], z[:, k, :], start=(k == 0), stop=(k == 7)),
                 reads=[zkey, "onesm"], writes=[ka], skip_self=True)
        zc, sq, sd = tmp['zc'], tmp['sq'], tmp['sd']
        t.op('dve', lambda e: e.tensor_tensor(out=zc[:], in0=z[:], in1=bc(pm[:, 0:N].unsqueeze(1), [128, 8, N]), op=ALU.subtract),
             reads=[zkey, ka], writes=[zc.name])
        t.op('act', lambda e: e.activation(out=sq[:], in_=zc[:], func=AF.Square), reads=[zc.name], writes=[sq.name])
        for k in range(8):
            t.op('pe', lambda e, k=k: e.matmul(pv[:, 0:N], self.onesm[:], sq[:, k, :], start=(k == 0), stop=(k == 7)),
                 reads=[sq.name, "onesm"], writes=[kb], skip_self=True)
        t.op('act', lambda e: e.activation(out=sd[:], in_=pv[:, 0:N], func=AF.Sqrt, bias=self.epsc[:, 0:1], scale=1.0),
             reads=[kb, "epsc"], writes=[sd.name])
        t.op('dve', lambda e: e.reciprocal(out=sd[:], in_=sd[:]), reads=[sd.name], writes=[sd.name])
        t.op('dve', lambda e: e.tensor_tensor(out=zc[:], in0=zc[:], in1=bc(sd[:].unsqueeze(1), [128, 8, N]), op=ALU.mult),
             reads=[zc.name, sd.name], writes=[zc.name])
        for k in range(8):
            t.op('act', lambda e, k=k: e.activation(out=out[:, k, :], in_=zc[:, k, :], func=AF.Identity,
                                                    bias=self.lnp[:, 2 * which + 1, layer, k:k + 1],
                                                    scale=self.lnp[:, 2 * which, layer, k:k + 1]),
                 reads=[zc.name, "lnp"], writes=[outkey])

    def mk_eps(self, ph):
        nc, t = self.nc, self.t
        self.epsc = ph.enter_context(self.sbuf("epsc", [128, 1], F32))
        t.op('dve', lambda e: e.memset(self.epsc[:], LN_EPS), writes=["epsc"])

    def phase_tabcast(self):
        t = self.t
        for l in range(2):
            for r0 in range(0, D, 128):
                for c0 in range(0, 16384, 2048):
                    t.dma(self.Ub[l, r0:r0 + 128, c0:c0 + 2048], self.peer_uT[l, r0:r0 + 128, c0:c0 + 2048],
                          writes=["Ub%d_%d_%d" % (l, r0 // 128, c0 // 2048)], eng='pool')
            for r0 in range(0, 16384, 256):
                t.dma(self.Vb[l, r0:r0 + 256, :], self.peer_v[l, r0:r0 + 256, :], writes=["Vb%d_%d" % (l, r0 // 256)], eng='pool')

    def phase_s5a(self):
        nc, t = self.nc, self.t
        with ExitStack() as ph:
            wb = self.load_w_bf16(ph, "s5win", self.w_s5_in, 8, D, eng_cast='mix')
            xf = ph.enter_context(self.sbuf("a_xf", [128, 2, 8, 512], F32))
            xb = ph.enter_context(self.sbuf("a_xb", [128, 8, 1024], BF16))
            U8 = ph.enter_context(self.sbuf("a_U8", [128, 64, 8, 16], BF16))
            Upt = ph.enter_context(self.sbuf("a_Upt", [128, 64, 128], BF16))
            xTv = self.xT.rearrange("(k p) n -> p k n", p=128)
            for tile in range(8):
                t0 = tile * 1024
                for h in range(2):
                    t.dma(xf[:, h, :, :], xTv[:, :, t0 + h * 512:t0 + (h + 1) * 512], writes=["a_xf%d" % h])
                    t.op('act' if h == 0 else 'pool',
                         (lambda e, h=h: e.copy(out=xb[:, :, h * 512:(h + 1) * 512], in_=xf[:, h, :, :])) if h == 0 else
                         (lambda e, h=h: e.tensor_copy(out=xb[:, :, h * 512:(h + 1) * 512], in_=xf[:, h, :, :])),
                         reads=["a_xf%d" % h], writes=["a_xb"])
                for s in range(8):
                    for fh in range(2):
                        bank = (s * 2 + fh) % 4
                        pk = "ps%d" % bank
                        for k in range(8):
                            t.op('pe', lambda e, k=k, s=s, fh=fh, bank=bank: e.matmul(
                                self.ps[bank][:], xb[:, k, s::8], wb[:, k, fh * 512:(fh + 1) * 512],
                                start=(k == 0), stop=(k == 7)),
                                reads=["a_xb", "s5win"], writes=[pk], skip_self=True)
                        if (s * 2 + fh) % 2 == 0:
                            t.op('act', lambda e, s=s, fh=fh, bank=bank: e.copy(out=U8[:, fh * 32:(fh + 1) * 32, s, :], in_=self.ps[bank][:].rearrange("p (g c) -> p g c", c=16)),
                                 reads=[pk], writes=["a_U8"])
                        else:
                            t.op('dve', lambda e, s=s, fh=fh, bank=bank: e.tensor_copy(out=U8[:, fh * 32:(fh + 1) * 32, s, :], in_=self.ps[bank][:].rearrange("p (g c) -> p g c", c=16)),
                                 reads=[pk], writes=["a_U8"])
                for gb in range(8):
                    bank = 4 + gb % 2
                    pk = "ps%d" % bank
                    pb = self.ps[bank][:].bitcast(BF16)
                    for gi in range(8):
                        g = gb * 8 + gi
                        t.op('pe', lambda e, g=g, gi=gi, pb=pb: e.transpose(pb[:, gi * 128:(gi + 1) * 128],
                                                                          U8[:, g, :, :].rearrange("p s c -> p (s c)"), self.identb[:]),
                             reads=["a_U8", "identb"], writes=[pk], skip_self=True)
                    if gb % 2 == 0:
                        t.op('dve', lambda e, gb=gb, pb=pb: e.tensor_copy(out=Upt[:, gb * 8:(gb + 1) * 8, :],
                                                                        in_=pb.rearrange("p (g c) -> p g c", g=8)),
                             reads=[pk], writes=["a_Upt"])
                    else:
                        t.op('act', lambda e, gb=gb, pb=pb: e.copy(out=Upt[:, gb * 8:(gb + 1) * 8, :],
                                                                 in_=pb.rearrange("p (g c) -> p g c", g=8)),
                             reads=[pk], writes=["a_Upt"])
                t.dma(self.UpD[:, :, tile * 128:(tile + 1) * 128].rearrange("g p c -> p g c"), Upt[:],
                      reads=["a_Upt"], writes=["UpD"])
            t.barrier()

    def cmul(self, eng, outr, outi, ar, ai, br, bi, tmp1, tmp2, rk, wk):
        t = self.t
        t.op(eng, lambda e: e.tensor_tensor(out=tmp1, in0=ar, in1=br, op=ALU.mult), reads=rk, writes=["cm_t1"])
        t.op(eng, lambda e: e.tensor_tensor(out=tmp2, in0=ai, in1=bi, op=ALU.mult), reads=rk, writes=["cm_t2"])
        t.op(eng, lambda e: e.tensor_tensor(out=outr, in0=tmp1, in1=tmp2, op=ALU.subtract), reads=["cm_t1", "cm_t2"] + rk, writes=wk)
        t.op(eng, lambda e: e.tensor_tensor(out=tmp1, in0=ar, in1=bi, op=ALU.mult), reads=rk + wk, writes=["cm_t1"])
        t.op(eng, lambda e: e.tensor_tensor(out=tmp2, in0=ai, in1=br, op=ALU.mult), reads=rk + wk, writes=["cm_t2"])
        t.op(eng, lambda e: e.tensor_tensor(out=outi, in0=tmp1, in1=tmp2, op=ALU.add), reads=["cm_t1", "cm_t2"] + rk, writes=wk)

    def phase_s5b(self):
        nc, t = self.nc, self.t
        with ExitStack() as ph:
            sb = lambda name, shape, dt=F32: ph.enter_context(self.sbuf(name, list(shape), dt))
            MATS = sb("b_mats", [128, 64, 5, 128], BF16)
            POW = sb("b_pow", [128, 2, 16, 64])
            PH = sb("b_ph", [128, 2, 10, 64])
            RHO = sb("b_rho", [128, 64])
            dpk = sb("b_dpk", [128, 64])
            segm = sb("b_segm", [128, 1024])
            t.dma(dpk[:], self.s5_dpk, writes=["b_dpk"])
            t.dma(segm[:], self.c_segmask, writes=["b_segm"])
            with ExitStack() as pg:
                sg = lambda name, shape, dt=F32: pg.enter_context(self.sbuf(name, list(shape), dt))
                lre = sg("g_lre", [128, 64]); lim = sg("g_lim", [128, 64]); lst = sg("g_lst", [128, 64])
                Bre = sg("g_bre", [128, 64, 16]); Bim = sg("g_bim", [128, 64, 16])
                Cre = sg("g_cre", [128, 64, 16]); Cim = sg("g_cim", [128, 64, 16])
                maskf = sg("g_maskf", [128, 128]); maskb = sg("g_maskb", [128, 128])
                for dst, src in [(lre, self.s5_lamre), (lim, self.s5_lamim), (lst, self.s5_lstep), (Bre, self.s5_bre),
                                 (Bim, self.s5_bim), (Cre, self.s5_ctre), (Cim, self.s5_ctim), (maskf, self.c_maskf),
                                 (maskb, self.c_maskb)]:
                    t.dma(dst[:], src, writes=[dst.name])
                S = {}
                for nm in ["step", "ang", "lrs", "mag", "magi", "kf", "r", "m1", "s1", "c1", "ar", "ai", "ari", "aii",
                           "den", "zr", "qr", "qi", "u1", "u2", "e8"]:
                    S[nm] = sg("g_" + nm, [128, 64])
                ki = sg("g_ki", [128, 64], I32)
                V = 'dve'

                def tt(out, a, b, op, rk, wk):
                    t.op(V, lambda e: e.tensor_tensor(out=out, in0=a, in1=b, op=op), reads=rk, writes=wk)

                def ts(out, a, s1, s2, op0, op1, rk, wk):
                    t.op(V, lambda e: e.tensor_scalar(out=out, in0=a, scalar1=s1, scalar2=s2, op0=op0, op1=op1), reads=rk, writes=wk)

                def act(out, a, func, rk, wk, scale=1.0):
                    t.op('act', lambda e: e.activation(out=out, in_=a, func=func, scale=scale), reads=rk, writes=wk)

                n = lambda k: S[k].name
                act(S["step"][:], lst[:], AF.Exp, [lst.name], [n("step")])
                tt(S["ang"][:], lim[:], S["step"][:], ALU.mult, [lim.name, n("step")], [n("ang")])
                tt(S["lrs"][:], lre[:], S["step"][:], ALU.mult, [lre.name, n("step")], [n("lrs")])
                act(S["mag"][:], S["lrs"][:], AF.Exp, [n("lrs")], [n("mag")])
                act(S["magi"][:], S["lrs"][:], AF.Exp, [n("lrs")], [n("magi")], scale=-1.0)
                act(S["e8"][:], S["lrs"][:], AF.Exp, [n("lrs")], [n("e8")], scale=-8.0)
                act(RHO[:], S["lrs"][:], AF.Exp, [n("lrs")], ["b_rho"], scale=8.0)

                def range_reduce(dst, src, shift):
                    ts(S["kf"][:], src, 1.0 / TWO_PI, shift / TWO_PI + 0.5, ALU.mult, ALU.add, [n("ang")], [n("kf")])
                    t.op(V, lambda e: e.tensor_copy(out=ki[:], in_=S["kf"][:]), reads=[n("kf")], writes=[ki.name])
                    t.op(V, lambda e: e.tensor_copy(out=S["kf"][:], in_=ki[:]), reads=[ki.name], writes=[n("kf")])
                    ts(S["kf"][:], S["kf"][:], -TWO_PI, shift, ALU.mult, ALU.add, [n("kf")], [n("kf")])
                    tt(dst, src, S["kf"][:], ALU.add, [n("ang"), n("kf")], [n("r")])
                    ts(S["m1"][:], dst, math.pi, -TWO_PI, ALU.is_gt, ALU.mult, [n("r")], [n("m1")])
                    tt(dst, dst, S["m1"][:], ALU.add, [n("r"), n("m1")], [n("r")])
                    ts(S["m1"][:], dst, -math.pi, TWO_PI, ALU.is_lt, ALU.mult, [n("r")], [n("m1")])
                    tt(dst, dst, S["m1"][:], ALU.add, [n("r"), n("m1")], [n("r")])
                    ts(dst, dst, math.pi, -math.pi, ALU.min, ALU.max, [n("r")], [n("r")])

                range_reduce(S["r"][:], S["ang"][:], 0.0)
                act(S["s1"][:], S["r"][:], AF.Sin, [n("r")], [n("s1")])
                range_reduce(S["r"][:], S["ang"][:], math.pi / 2)
                act(S["c1"][:], S["r"][:], AF.Sin, [n("r")], [n("c1")])
                tt(S["ar"][:], S["mag"][:], S["c1"][:], ALU.mult, [n("mag"), n("c1")], [n("ar")])
                tt(S["ai"][:], S["mag"][:], S["s1"][:], ALU.mult, [n("mag"), n("s1")], [n("ai")])
                tt(S["ari"][:], S["magi"][:], S["c1"][:], ALU.mult, [n("magi"), n("c1")], [n("ari")])
                tt(S["aii"][:], S["magi"][:], S["s1"][:], ALU.mult, [n("magi"), n("s1")], [n("aii")])
                ts(S["aii"][:], S["aii"][:], -1.0, None, ALU.mult, ALU.bypass, [n("aii")], [n("aii")])
                tt(S["den"][:], lre[:], lre[:], ALU.mult, [lre.name], [n("den")])
                tt(S["u1"][:], lim[:], lim[:], ALU.mult, [lim.name], [n("u1")])
                tt(S["den"][:], S["den"][:], S["u1"][:], ALU.add, [n("den"), n("u1")], [n("den")])
                t.op(V, lambda e: e.reciprocal(out=S["den"][:], in_=S["den"][:]), reads=[n("den")], writes=[n("den")])
                ts(S["zr"][:], S["ar"][:], -1.0, None, ALU.add, ALU.bypass, [n("ar")], [n("zr")])
                tt(S["u1"][:], S["zr"][:], lre[:], ALU.mult, [n("zr"), lre.name], [n("u1")])
                tt(S["u2"][:], S["ai"][:], lim[:], ALU.mult, [n("ai"), lim.name], [n("u2")])
                tt(S["u1"][:], S["u1"][:], S["u2"][:], ALU.add, [n("u1"), n("u2")], [n("u1")])
                tt(S["qr"][:], S["u1"][:], S["den"][:], ALU.mult, [n("u1"), n("den")], [n("qr")])
                tt(S["u1"][:], S["ai"][:], lre[:], ALU.mult, [n("ai"), lre.name], [n("u1")])
                tt(S["u2"][:], S["zr"][:], lim[:], ALU.mult, [n("zr"), lim.name], [n("u2")])
                tt(S["u1"][:], S["u1"][:], S["u2"][:], ALU.subtract, [n("u1"), n("u2")], [n("u1")])
                tt(S["qi"][:], S["u1"][:], S["den"][:], ALU.mult, [n("u1"), n("den")], [n("qi")])
                BBr = sg("g_bbr", [128, 64, 16]); BBi = sg("g_bbi", [128, 64, 16])
                T1 = sg("g_T1", [128, 1024]); T2 = sg("g_T2", [128, 1024])
                T1v = T1[:].rearrange("p (g c) -> p g c", c=16)
                T2v = T2[:].rearrange("p (g c) -> p g c", c=16)
                qrb = bc(S["qr"][:].unsqueeze(2), [128, 64, 16]); qib = bc(S["qi"][:].unsqueeze(2), [128, 64, 16])
                self.cmul(V, BBr[:], BBi[:], qrb, qib, Bre[:], Bim[:], T1v, T2v, [n("qr"), n("qi"), Bre.name, Bim.name], [BBr.name, BBi.name])
                t.op(V, lambda e: e.memset(POW[:, 0, 7, :], 1.0), writes=["b_pow"])
                t.op(V, lambda e: e.memset(POW[:, 1, 7, :], 0.0), reads=["b_pow"], writes=["b_pow"])
                for k in range(0, 8):
                    self.cmul(V, POW[:, 0, 8 + k, :], POW[:, 1, 8 + k, :], POW[:, 0, 7 + k, :], POW[:, 1, 7 + k, :],
                              S["ar"][:], S["ai"][:], T1[:, 0:64], T2[:, 0:64], ["b_pow", n("ar"), n("ai")], ["b_pow"])
                for k in range(0, 7):
                    self.cmul(V, POW[:, 0, 6 - k, :], POW[:, 1, 6 - k, :], POW[:, 0, 7 - k, :], POW[:, 1, 7 - k, :],
                              S["ari"][:], S["aii"][:], T1[:, 0:64], T2[:, 0:64], ["b_pow", n("ari"), n("aii")], ["b_pow"])
                tt(PH[:, 0, 0, :], POW[:, 0, 15, :], S["e8"][:], ALU.mult, ["b_pow", n("e8")], ["b_ph"])
                tt(PH[:, 1, 0, :], POW[:, 1, 15, :], S["e8"][:], ALU.mult, ["b_pow", n("e8"), "b_ph"], ["b_ph"])
                t.op(V, lambda e: e.tensor_scalar(out=PH[64:128, 1, 0, :], in0=PH[64:128, 1, 0, :], scalar1=-1.0, scalar2=None,
                                                  op0=ALU.mult, op1=ALU.bypass), reads=["b_ph"], writes=["b_ph"])
                for L in range(9):
                    self.cmul(V, PH[:, 0, L + 1, :], PH[:, 1, L + 1, :], PH[:, 0, L, :], PH[:, 1, L, :],
                              PH[:, 0, L, :], PH[:, 1, L, :], T1[:, 0:64], T2[:, 0:64], ["b_ph"], ["b_ph"])
                PA = sg("g_pa", [128, 4, 2, 8, 64])
                kmap = {0: (lambda j: 7 - j, lambda j: j), 1: (lambda j: j + 1, lambda j: 8 - j),
                        2: (lambda j: -j, lambda j: j), 3: (lambda j: j, lambda j: -j)}
                ci = 0
                for kind in range(4):
                    for j in range(8):
                        for half, (p0, p1) in enumerate([(0, 64), (64, 128)]):
                            kk = kmap[kind][half](j) + 7
                            eng = ['act', 'pool'][ci % 2]
                            ci += 1
                            if eng == 'act':
                                t.op('act', lambda e, kind=kind, j=j, p0=p0, p1=p1, kk=kk: e.copy(out=PA[p0:p1, kind, :, j, :], in_=POW[p0:p1, :, kk, :]),
                                     reads=["b_pow"], writes=["g_pa%d_%d_%d" % (kind, j, half)])
                            else:
                                t.op('pool', lambda e, kind=kind, j=j, p0=p0, p1=p1, kk=kk: e.tensor_copy(out=PA[p0:p1, kind, :, j, :], in_=POW[p0:p1, :, kk, :]),
                                     reads=["b_pow"], writes=["g_pa%d_%d_%d" % (kind, j, half)])
                pa_keys = ["g_pa%d_%d_%d" % (kind, j, half) for kind in range(4) for j in range(8) for half in range(2)]
                TB = sg("g_tb", [128, 4, 2, 1024])
                Ysn = sg("g_ysn", [128, 1024])
                for gb in range(8):
                    g0 = gb * 8
                    for kind in range(4):
                        src_r, src_i = (BBr, BBi) if kind in (0, 2) else (Cre, Cim)
                        par = bc(PA[:, kind, 0, :, g0:g0 + 8].rearrange("p j g -> p g j").unsqueeze(3), [128, 8, 8, 16])
                        pai = bc(PA[:, kind, 1, :, g0:g0 + 8].rearrange("p j g -> p g j").unsqueeze(3), [128, 8, 8, 16])
                        br = bc(src_r[:, g0:g0 + 8, :].unsqueeze(2), [128, 8, 8, 16])
                        bi = bc(src_i[:, g0:g0 + 8, :].unsqueeze(2), [128, 8, 8, 16])
                        outr = TB[:, kind, 0, :].rearrange("p (g j c) -> p g j c", g=8, j=8)
                        outi = TB[:, kind, 1, :].rearrange("p (g j c) -> p g j c", g=8, j=8)
                        t1 = T1[:].rearrange("p (g j c) -> p g j c", g=8, j=8)
                        t2 = T2[:].rearrange("p (g j c) -> p g j c", g=8, j=8)
                        self.cmul(V, outr, outi, par, pai, br, bi, t1, t2, pa_keys + [src_r.name, src_i.name], ["g_tb%d" % kind])
                    t.op(V, lambda e: e.tensor_scalar(out=Ysn[:], in0=TB[:, 2, 1, :], scalar1=-1.0, scalar2=None, op0=ALU.mult, op1=ALU.bypass),
                         reads=["g_tb2"], writes=["g_ysn"])
                    for gi in range(8):
                        g = g0 + gi
                        sl = slice(gi * 128, (gi + 1) * 128)
                        for ri in range(2):
                            bank = 4 + ri
                            t.op('pe', lambda e, ri=ri, sl=sl, bank=bank: e.transpose(self.ps[bank][:, 0:128], TB[:, 0, ri, sl], self.ident[:]),
                                 reads=["g_tb0", "ident"], writes=["ps%d" % bank], skip_self=True)
                            t.op('act', lambda e, ri=ri, g=g, bank=bank: e.copy(out=MATS[:, g, ri, :], in_=self.ps[bank][:, 0:128]),
                                 reads=["ps%d" % bank], writes=["b_mats"])
                        t.op('pool', lambda e, g=g, sl=sl: e.tensor_copy(out=MATS[:, g, 3, :], in_=TB[:, 1, 0, sl]), reads=["g_tb1"], writes=["b_mats"])
                        t.op('pool', lambda e, g=g, sl=sl: e.tensor_scalar(out=MATS[:, g, 4, :], in0=TB[:, 1, 1, sl], scalar1=-1.0, scalar2=None,
                                                                          op0=ALU.mult, op1=ALU.bypass), reads=["g_tb1"], writes=["b_mats"])
                        for half, (p0, p1) in enumerate([(0, 64), (64, 128)]):
                            bank = 6 + half
                            t.op('pe', lambda e, p0=p0, p1=p1, sl=sl, bank=bank: e.matmul(self.ps[bank][:, 0:128], TB[p0:p1, 2, 0, sl], TB[p0:p1, 3, 0, sl], start=True, stop=False),
                                 reads=["g_tb2", "g_tb3"], writes=["ps%d" % bank], skip_self=True)
                            t.op('pe', lambda e, p0=p0, p1=p1, sl=sl, bank=bank: e.matmul(self.ps[bank][:, 0:128], Ysn[p0:p1, sl], TB[p0:p1, 3, 1, sl], start=False, stop=True),
                                 reads=["g_ysn", "g_tb3"], writes=["ps%d" % bank], skip_self=True)
                        t.op(V, lambda e: e.tensor_tensor(out=T1[:, 0:128], in0=self.ps[6][:, 0:128], in1=maskf[:], op=ALU.mult),
                             reads=["ps6", maskf.name], writes=["cm_t1"])
                        t.op(V, lambda e: e.tensor_tensor(out=T2[:, 0:128], in0=self.ps[7][:, 0:128], in1=maskb[:], op=ALU.mult),
                             reads=["ps7", maskb.name], writes=["cm_t2"])
                        t.op(V, lambda e, g=g: e.tensor_tensor(out=MATS[:, g, 2, :], in0=T1[:, 0:128], in1=T2[:, 0:128], op=ALU.add),
                             reads=["cm_t1", "cm_t2"], writes=["b_mats"])
                t.barrier()
            Up = sb("b_up", [128, 2, 1024], BF16)
            TAB = sb("b_tab", [128, 2, 1024])
            RM = sb("b_rm", [128, 1024])
            W = [sb("b_w%d" % i, [128, 1024]) for i in range(6)]
            Gs = [sb("b_g%d" % i, [128, 1024]) for i in range(2)]
            Hu = [sb("b_h%d" % i, [128, 1024]) for i in range(2)]
            Hs = sb("b_hs", [128, 2, 1024], BF16)
            yd = sb("b_yd", [128, 1024])
            hp = sb("b_hp", [128, 2, 1024], BF16)
            t.op('pool', lambda e: e.memset(Hs[:], 0.0), writes=["b_hs"])
            V = 'dve'
            for g in range(64):
                ub = g % 2
                uk = "b_up%d" % ub
                t.dma(Up[:, ub, :], self.UpD[g, :, :], reads=["UpD"], writes=[uk])
                t.op('act', lambda e, g=g: e.activation(out=RM[:], in_=segm[:], func=AF.Copy, scale=RHO[:, g:g + 1]),
                     reads=["b_segm", "b_rho"], writes=["b_rm"])
                t.op(V, lambda e: e.memset(TAB[:, 0, 0:1], 1.0), writes=["b_tab"])
                t.op(V, lambda e: e.memset(TAB[:, 1, 0:1], 0.0), reads=["b_tab"], writes=["b_tab"])
                for L in range(9):
                    n0 = 1 << L
                    cr = PH[:, 0, L, g:g + 1]
                    ci_ = PH[:, 1, L, g:g + 1]
                    src_r = TAB[:, 0, 0:n0]; src_i = TAB[:, 1, 0:n0]
                    dst_r = TAB[:, 0, n0:2 * n0]; dst_i = TAB[:, 1, n0:2 * n0]
                    t.op(V, lambda e, src_i=src_i, ci_=ci_, n0=n0: e.tensor_scalar(out=W[0][:, 0:n0], in0=src_i, scalar1=ci_, scalar2=None, op0=ALU.mult, op1=ALU.bypass),
                         reads=["b_tab", "b_ph"], writes=["b_w0"])
                    t.op(V, lambda e, src_r=src_r, cr=cr, n0=n0, dst_r=dst_r: e.scalar_tensor_tensor(out=dst_r, in0=src_r, scalar=cr, in1=W[0][:, 0:n0], op0=ALU.mult, op1=ALU.subtract),
                         reads=["b_tab", "b_ph", "b_w0"], writes=["b_tab"])
                    t.op(V, lambda e, src_r=src_r, ci_=ci_, n0=n0: e.tensor_scalar(out=W[1][:, 0:n0], in0=src_r, scalar1=ci_, scalar2=None, op0=ALU.mult, op1=ALU.bypass),
                         reads=["b_tab", "b_ph"], writes=["b_w1"])
                    t.op(V, lambda e, src_i=src_i, cr=cr, n0=n0, dst_i=dst_i: e.scalar_tensor_tensor(out=dst_i, in0=src_i, scalar=cr, in1=W[1][:, 0:n0], op0=ALU.mult, op1=ALU.add),
                         reads=["b_tab", "b_ph", "b_w1"], writes=["b_tab"])
                for ri in range(2):
                    t.op('act', lambda e, ri=ri: e.copy(out=TAB[:, ri, 512:768], in_=TAB[:, ri, 0:256]), reads=["b_tab"], writes=["b_tab"])
                    t.op('act', lambda e, ri=ri: e.copy(out=TAB[:, ri, 768:1024], in_=TAB[:, ri, 0:256]), reads=["b_tab"], writes=["b_tab"])
                for ri in range(2):
                    for h in range(2):
                        bank = ri * 2 + h
                        t.op('pe', lambda e, ri=ri, h=h, bank=bank, g=g, ub=ub: e.matmul(self.ps[bank][:], MATS[:, g, ri, :], Up[:, ub, h * 512:(h + 1) * 512], start=True, stop=True),
                             reads=["b_mats", uk], writes=["ps%d" % bank], skip_self=True)
                cosT = TAB[:, 0, :]; sinT = TAB[:, 1, :]
                for h in range(2):
                    cs = slice(h * 512, (h + 1) * 512)
                    t.op(V, lambda e, h=h, cs=cs: e.tensor_tensor(out=W[0][:, cs], in0=self.ps[h][:], in1=cosT[:, cs], op=ALU.mult), reads=["ps%d" % h, "b_tab"], writes=["b_w0"])
                    t.op(V, lambda e, h=h, cs=cs: e.tensor_tensor(out=W[1][:, cs], in0=self.ps[2 + h][:], in1=sinT[:, cs], op=ALU.mult), reads=["ps%d" % (2 + h), "b_tab"], writes=["b_w1"])
                    t.op(V, lambda e, h=h, cs=cs: e.tensor_tensor(out=W[2][:, cs], in0=self.ps[2 + h][:], in1=cosT[:, cs], op=ALU.mult), reads=["ps%d" % (2 + h), "b_tab"], writes=["b_w2"])
                    t.op(V, lambda e, h=h, cs=cs: e.tensor_tensor(out=W[3][:, cs], in0=self.ps[h][:], in1=sinT[:, cs], op=ALU.mult), reads=["ps%d" % h, "b_tab"], writes=["b_w3"])
                t.op('pool', lambda e: e.tensor_tensor(out=W[4][:], in0=W[0][:], in1=W[1][:], op=ALU.add), reads=["b_w0", "b_w1"], writes=["b_w4"])
                t.op('pool', lambda e: e.tensor_tensor(out=W[5][:], in0=W[2][:], in1=W[3][:], op=ALU.subtract), reads=["b_w2", "b_w3"], writes=["b_w5"])
                for ri in range(2):
                    src = W[4 + ri]
                    t.op(V, lambda e, ri=ri, src=src: e.tensor_tensor_scan(out=Gs[ri][0:64, :], data0=RM[0:64, :], data1=src[0:64, :], initial=0.0, op0=ALU.mult, op1=ALU.add),
                         reads=["b_rm", src.name], writes=["b_g%d_f" % ri])
                    t.op(V, lambda e, ri=ri, src=src: e.tensor_tensor_scan(out=Gs[ri][64:128, ::-1], data0=RM[64:128, ::-1], data1=src[64:128, ::-1], initial=0.0, op0=ALU.mult, op1=ALU.add),
                         reads=["b_rm", src.name], writes=["b_g%d_b" % ri])
                gk = ["b_g0_f", "b_g0_b", "b_g1_f", "b_g1_b"]
                t.op(V, lambda e: e.tensor_tensor(out=W[0][:], in0=Gs[0][:], in1=cosT, op=ALU.mult), reads=gk + ["b_tab"], writes=["b_w0"])
                t.op('pool', lambda e: e.tensor_tensor(out=W[1][:], in0=Gs[1][:], in1=sinT, op=ALU.mult), reads=gk + ["b_tab"], writes=["b_w1"])
                t.op(V, lambda e: e.tensor_tensor(out=W[2][:], in0=Gs[1][:], in1=cosT, op=ALU.mult), reads=gk + ["b_tab"], writes=["b_w2"])
                t.op('pool', lambda e: e.tensor_tensor(out=W[3][:], in0=Gs[0][:], in1=sinT, op=ALU.mult), reads=gk + ["b_tab"], writes=["b_w3"])
                t.op(V, lambda e: e.tensor_tensor(out=Hu[0][:], in0=W[0][:], in1=W[1][:], op=ALU.subtract), reads=["b_w0", "b_w1"], writes=["b_h0"])
                t.op('pool', lambda e: e.tensor_tensor(out=Hu[1][:], in0=W[2][:], in1=W[3][:], op=ALU.add), reads=["b_w2", "b_w3"], writes=["b_h1"])
                for ri in range(2):
                    t.op(V, lambda e, ri=ri: e.tensor_tensor(out=Hs[0:64, ri, 1:1024], in0=Hu[ri][0:64, 0:1023], in1=segm[0:64, 1:1024], op=ALU.mult),
                         reads=["b_h%d" % ri, "b_segm"], writes=["b_hs"])
                    t.op('pool', lambda e, ri=ri: e.tensor_tensor(out=Hs[64:128, ri, 0:1023], in0=Hu[ri][64:128, 1:1024], in1=segm[64:128, 0:1023], op=ALU.mult),
                         reads=["b_h%d" % ri, "b_segm"], writes=["b_hs"])
                for h in range(2):
                    bank = 4 + h
                    cs = slice(h * 512, (h + 1) * 512)
                    t.op('pe', lambda e, g=g, ub=ub, cs=cs, bank=bank: e.matmul(self.ps[bank][:], MATS[:, g, 2, :], Up[:, ub, cs], start=True, stop=False),
                         reads=["b_mats", uk], writes=["ps%d" % bank], skip_self=True)
                    t.op('pe', lambda e, g=g, cs=cs, bank=bank: e.matmul(self.ps[bank][:], MATS[:, g, 3, :], Hs[:, 0, cs], start=False, stop=False),
                         reads=["b_mats", "b_hs"], writes=["ps%d" % bank], skip_self=True)
                    t.op('pe', lambda e, g=g, cs=cs, bank=bank: e.matmul(self.ps[bank][:], MATS[:, g, 4, :], Hs[:, 1, cs], start=False, stop=True),
                         reads=["b_mats", "b_hs"], writes=["ps%d" % bank], skip_self=True)
                    t.op(V, lambda e, g=g, ub=ub, cs=cs, bank=bank: e.scalar_tensor_tensor(out=yd[:, cs], in0=Up[:, ub, cs], scalar=dpk[:, g:g + 1], in1=self.ps[bank][:],
                                                                                         op0=ALU.mult, op1=ALU.add),
                         reads=[uk, "b_dpk", "ps%d" % bank], writes=["b_yd"])
                t.op('act', lambda e, ub=ub: e.activation(out=hp[:, ub, :], in_=yd[:], func=AF.Gelu_apprx_tanh), reads=["b_yd"], writes=["b_hp%d" % ub])
                t.dma(self.HpD[g, :, :], hp[:, ub, :], reads=["b_hp%d" % ub], writes=["HpD"])
            t.barrier()

    def phase_s5c(self):
        nc, t = self.nc, self.t
        with ExitStack() as ph:
            sb = lambda name, shape, dt=F32: ph.enter_context(self.sbuf(name, list(shape), dt))
            self.mk_eps(ph)
            wg = self.load_w_bf16(ph, "s5wglu", self.w_s5_glu, 8, 2 * D, eng_cast='mix')
            hpt = sb("c_hpt", [128, 64, 128], BF16)
            H8 = sb("c_H8", [128, 8, 1024], BF16)
            hT = sb("c_hT", [128, 8, 1024], BF16)
            xf = sb("c_xf", [128, 8, 512])
            z = sb("c_z", [128, 8, 512])
            sg_ = sb("c_sg", [128, 512])
            tmp = {'zc': sb("c_zc", [128, 8, 512]), 'sq': sb("c_sq", [128, 8, 512]), 'sd': sb("c_sd", [128, 512])}
            xo = sb("c_xo", [128, 8, 512])
            xTv = self.xT.rearrange("(k p) n -> p k n", p=128)
            X1v = self.X1.rearrange("(k p) n -> p k n", p=128)
            for tile in range(8):
                t.dma(hpt[:], self.HpD[:, :, tile * 128:(tile + 1) * 128].rearrange("g p c -> p g c"), reads=["HpD"], writes=["c_hpt"])
                for gb in range(8):
                    bank = gb % 2
                    pk = "ps%d" % bank
                    pb = self.ps[bank][:].bitcast(BF16)
                    for gi in range(8):
                        g = gb * 8 + gi
                        t.op('pe', lambda e, g=g, gi=gi, pb=pb: e.transpose(pb[:, gi * 128:(gi + 1) * 128], hpt[:, g, :], self.identb[:]),
                             reads=["c_hpt", "identb"], writes=[pk], skip_self=True)
                    src = pb.rearrange("p (g t c) -> p g t c", g=8, t=8)
                    dst = H8[:, :, gb * 128:(gb + 1) * 128].rearrange("p t (g c) -> p g t c", g=8)
                    if gb % 2 == 0:
                        t.op('dve', lambda e, src=src, dst=dst: e.tensor_copy(out=dst, in_=src), reads=[pk], writes=["c_H8"])
                    else:
                        t.op('act', lambda e, src=src, dst=dst: e.copy(out=dst, in_=src), reads=[pk], writes=["c_H8"])
                for k in range(8):
                    bank = 2 + k % 2
                    pk = "ps%d" % bank
                    pb = self.ps[bank][:].bitcast(BF16)
                    for tt_ in range(8):
                        t.op('pe', lambda e, k=k, tt_=tt_, pb=pb: e.transpose(pb[:, tt_ * 128:(tt_ + 1) * 128], H8[:, tt_, k * 128:(k + 1) * 128], self.identb[:]),
                             reads=["c_H8", "identb"], writes=[pk], skip_self=True)
                    src = pb.rearrange("p (t c) -> p t c", t=8)
                    dst = hT[:, k, :].rearrange("p (c t) -> p t c", t=8)
                    if k % 2 == 0:
                        t.op('dve', lambda e, src=src, dst=dst: e.tensor_copy(out=dst, in_=src), reads=[pk], writes=["c_hT"])
                    else:
                        t.op('act', lambda e, src=src, dst=dst: e.copy(out=dst, in_=src), reads=[pk], writes=["c_hT"])
                for th in range(2):
                    tok0 = tile * 1024 + th * 512
                    t.dma(xf[:], xTv[:, :, tok0:tok0 + 512], writes=["c_xf"])
                    for fo in range(8):
                        bv, bg = 4 + (fo % 2) * 2, 5 + (fo % 2) * 2
                        for k in range(8):
                            t.op('pe', lambda e, k=k, fo=fo, th=th, bv=bv: e.matmul(self.ps[bv][:], wg[:, k, fo * 128:(fo + 1) * 128], hT[:, k, th * 512:(th + 1) * 512],
                                                                              start=(k == 0), stop=(k == 7)), reads=["s5wglu", "c_hT"], writes=["ps%d" % bv], skip_self=True)
                        for k in range(8):
                            t.op('pe', lambda e, k=k, fo=fo, th=th, bg=bg: e.matmul(self.ps[bg][:], wg[:, k, D + fo * 128:D + (fo + 1) * 128], hT[:, k, th * 512:(th + 1) * 512],
                                                                              start=(k == 0), stop=(k == 7)), reads=["s5wglu", "c_hT"], writes=["ps%d" % bg], skip_self=True)
                        t.op('act', lambda e, bg=bg: e.activation(out=sg_[:], in_=self.ps[bg][:], func=AF.Sigmoid), reads=["ps%d" % bg], writes=["c_sg"])
                        t.op('dve', lambda e, bv=bv: e.tensor_tensor(out=sg_[:], in0=self.ps[bv][:], in1=sg_[:], op=ALU.mult), reads=["ps%d" % bv, "c_sg"], writes=["c_sg"])
                        t.op('dve', lambda e, fo=fo: e.scalar_tensor_tensor(out=z[:, fo, :], in0=xf[:, fo, :], scalar=ALPHA, in1=sg_[:], op0=ALU.mult, op1=ALU.add),
                             reads=["c_xf", "c_sg"], writes=["c_z"])
                    self.ln_block(z, "c_z", 0, 0, xo, "c_xo", 512, tmp, 0, 1)
                    t.dma(X1v[:, :, tok0:tok0 + 512], xo[:], reads=["c_xo"], writes=["X1"])
            t.barrier()

    def phase_peer(self, layer, xin, xout):
        sub = getattr(self, "peer_sub", ("p1", "p2", "p3"))
        if "p1" in sub:
            self.peer_p1(layer, xin)
        if "p2" in sub:
            self.peer_p2(layer)
        if "p3" in sub:
            self.peer_p3(layer, xin, xout)

    def peer_p1(self, layer, xin):
        nc, t = self.nc, self.t
        with ExitStack() as ph:
            sb = lambda name, shape, dt=F32: ph.enter_context(self.sbuf(name, list(shape), dt))
            wq = self.load_w_bf16(ph, "p1_wq", self.w_peer_q[layer], 8, 2 * D, eng_cast='mix')
            xf = sb("p1_xf", [128, 2, 8, 512])
            xb = sb("p1_xb", [128, 2, 8, 512], BF16)
            qb = sb("p1_qb", [128, 2, 16, 512], BF16)
            xv = xin.rearrange("(k p) n -> p k n", p=128)
            XBv = self.XB.rearrange("(k p) n -> p k n", p=128)
            QTv = self.QT.rearrange("(k p) n -> p k n", p=128)
            for blk in range(16):
                b = blk % 2
                tok0 = blk * 512
                t.dma(xf[:, b], xv[:, :, tok0:tok0 + 512], reads=["X1"], writes=["p1_xf%d" % b])
                t.op('pool', lambda e, b=b: e.tensor_copy(out=xb[:, b], in_=xf[:, b]), reads=["p1_xf%d" % b], writes=["p1_xb%d" % b])
                t.dma(XBv[:, :, tok0:tok0 + 512], xb[:, b], reads=["p1_xb%d" % b], writes=["XB_%d" % blk])
                for fo in range(16):
                    bank = fo % 4
                    for k in range(8):
                        t.op('pe', lambda e, k=k, fo=fo, b=b, bank=bank: e.matmul(self.ps[bank][:], wq[:, k, fo * 128:(fo + 1) * 128], xb[:, b, k, :],
                                                                             start=(k == 0), stop=(k == 7)),
                             reads=["p1_wq", "p1_xb%d" % b], writes=["ps%d" % bank], skip_self=True)
                    if fo % 2 == 0:
                        t.op('act', lambda e, fo=fo, b=b, bank=bank: e.copy(out=qb[:, b, fo, :], in_=self.ps[bank][:]), reads=["ps%d" % bank], writes=["p1_qb%d" % b])
                    else:
                        t.op('dve', lambda e, fo=fo, b=b, bank=bank: e.tensor_copy(out=qb[:, b, fo, :], in_=self.ps[bank][:]), reads=["ps%d" % bank], writes=["p1_qb%d" % b])
                t.dma(QTv[:, :, tok0:tok0 + 512], qb[:, b], reads=["p1_qb%d" % b], writes=["QT_%d" % blk])
            t.barrier()

    def peer_p2(self, layer):
        nc, t = self.nc, self.t
        NB = self.peer_nblk if hasattr(self, "peer_nblk") else 32
        with ExitStack() as ph:
            sb = lambda name, shape, dt=F32: ph.enter_context(self.sbuf(name, list(shape), dt))
            skf = sb("p2_skf", [128, 2, 128])
            skb = sb("p2_skb", [128, 2, 128], BF16)
            t.dma(skf[:], self.peer_skT[layer].rearrange("c d n -> d c n"), writes=["p2_skf"])
            t.op('dve', lambda e: e.tensor_copy(out=skb[:], in_=skf[:]), reads=["p2_skf"], writes=["p2_skb"])
            xb = sb("p2_xb", [128, 8, 256], BF16)
            qT = sb("p2_qT", [128, 16, 256], BF16)
            sc = sb("p2_sc", [128, 16, 128])
            sc2 = sb("p2_sc2", [128, 2, 128])
            cs = sb("p2_cs", [128, 8, 256])
            cs2 = sb("p2_cs2", [128, 2, 256])
            sv = sb("p2_sv", [128, 16, 16])
            ts_ = sb("p2_ts", [128, 8, 16])
            ex = sb("p2_ex", [128, 8, 16])
            st8 = sb("p2_st8", [128, 4, 8])
            TM = sb("p2_TM", [128, 3, 128])
            SM = sb("p2_SM", [128, 3, 256])
            QR = sb("p2_qr", [128, 2, 2, 16, 128], BF16)
            Pt = sb("p2_P", [128, 12, 128], BF16)
            Et = sb("p2_E", [128, 12, 128])
            Qt = sb("p2_Q", [128, 12, 128], BF16)
            Gs = sb("p2_Gs", [128, 256, 128], BF16)
            UTs = sb("p2_UT", [128, 2, 8, 512], BF16)
            Vs = sb("p2_V", [128, 2, 4, 1024], BF16)
            ga = sb("p2_ga", [128, 2, 256])
            Hh = sb("p2_H", [128, 2, 256], BF16)
            otok = sb("p2_otok", [128, 2, 1024])
            peT = sb("p2_peT", [128, 8, 256])
            XBv = self.XB.rearrange("(k p) n -> p k n", p=128)
            QTv = self.QT.rearrange("(k p) n -> p k n", p=128)
            PEv = self.PEo.rearrange("(k p) n -> p k n", p=128)
            Ubv = self.Ub[layer].rearrange("(k p) e -> p k e", p=128)
            Vbv = self.Vb[layer].rearrange("(i p) f -> p i f", p=128)
            NEG = -1.0e30
            import os as _os2
            TBK = int(_os2.environ.get("TBK", "7"))
            def topk(blk, restricted):
                tok0 = blk * 256
                b512 = tok0 // 512
                t.dma(qT[:], QTv[:, :, tok0:tok0 + 256], reads=["QT_%d" % b512], writes=["p2_qT"])
                for st in range(2):
                    tsl = slice(st * 128, (st + 1) * 128)
                    for r in range(2):
                        for q in (2 * r, 2 * r + 1):
                            bank = [4, 7][q % 2] if restricted else q
                            for h4 in range(4):
                                hc = q * 4 + h4
                                t.op('pe', lambda e, hc=hc, h4=h4, bank=bank, tsl=tsl: e.matmul(self.ps[bank][:, h4 * 128:(h4 + 1) * 128], qT[:, hc, tsl], skb[:, hc % 2, :],
                                                                                          start=True, stop=True),
                                     reads=["p2_qT", "p2_skb"], writes=["ps%d" % bank], skip_self=True)
                        for q in (2 * r, 2 * r + 1):
                            bank = [4, 7][q % 2] if restricted else q
                            t.op('act', lambda e, q=q, bank=bank: e.copy(out=sc[:, q * 4:(q + 1) * 4, :], in_=self.ps[bank][:].rearrange("p (a n) -> p a n", a=4)),
                                 reads=["ps%d" % bank], writes=["p2_sc%d" % q])
                    for hc in range(16):
                        kk = "p2_sc%d" % (hc // 4)
                        t.op('dve', lambda e, hc=hc: e.max(out=sv[:, hc, 0:8], in_=sc[:, hc, :]), reads=[kk], writes=["p2_sv%d" % hc])
                        t.op('dve', lambda e, hc=hc: e.match_replace(out=sc2[:, hc % 2, :], in_to_replace=sv[:, hc, 0:8], in_values=sc[:, hc, :], imm_value=NEG),
                             reads=[kk, "p2_sv%d" % hc], writes=["p2_sc2_%d" % (hc % 2)])
                        t.op('dve', lambda e, hc=hc: e.max(out=sv[:, hc, 8:16], in_=sc2[:, hc % 2, :]), reads=["p2_sc2_%d" % (hc % 2)], writes=["p2_sv%d" % hc])
                    svk = ["p2_sv%d" % hc for hc in range(16)]
                    sv4 = sv[:].rearrange("p (h c) a -> p h c a", c=2)
                    t.op('dve', lambda e: e.tensor_tensor(out=cs[:].rearrange("p h (a b) -> p h a b", a=16),
                                                          in0=bc(sv4[:, :, 0, :].unsqueeze(3), [128, 8, 16, 16]),
                                                          in1=bc(sv4[:, :, 1, :].unsqueeze(2), [128, 8, 16, 16]), op=ALU.add),
                         reads=svk, writes=["p2_cs"])
                    for h in range(8):
                        t.op('dve', lambda e, h=h: e.max(out=ts_[:, h, 0:8], in_=cs[:, h, :]), reads=["p2_cs"], writes=["p2_ts%d" % h])
                        t.op('dve', lambda e, h=h: e.match_replace(out=cs2[:, h % 2, :], in_to_replace=ts_[:, h, 0:8], in_values=cs[:, h, :], imm_value=NEG),
                             reads=["p2_cs", "p2_ts%d" % h], writes=["p2_cs2_%d" % (h % 2)])
                        t.op('dve', lambda e, h=h: e.max(out=ts_[:, h, 8:16], in_=cs2[:, h % 2, :]), reads=["p2_cs2_%d" % (h % 2)], writes=["p2_ts%d" % h])
                    tsk = ["p2_ts%d" % h for h in range(8)]
                    t.op('dve', lambda e: e.tensor_tensor(out=ex[:], in0=ts_[:], in1=bc(ts_[:, :, 0:1], [128, 8, 16]), op=ALU.subtract), reads=tsk, writes=["p2_ex"])
                    t.op('act', lambda e: e.activation(out=ex[:], in_=ex[:], func=AF.Exp), reads=["p2_ex"], writes=["p2_ex"])
                    t.op('dve', lambda e: e.tensor_reduce(out=st8[:, 0, :], in_=ex[:], axis=mybir.AxisListType.X, op=ALU.add), reads=["p2_ex"], writes=["p2_st8"])
                    t.op('act', lambda e: e.activation(out=st8[:, 1, :], in_=st8[:, 0, :], func=AF.Ln), reads=["p2_st8"], writes=["p2_st8"])
                    t.op('dve', lambda e: e.tensor_tensor(out=st8[:, 2, :], in0=st8[:, 1, :], in1=ts_[:, :, 0], op=ALU.add), reads=["p2_st8"] + tsk, writes=["p2_st8"])
                    TMv = TM[:].rearrange("p j (h a) -> p j h a", h=8)
                    t.op('pool', lambda e: e.tensor_copy(out=TMv[:, 0], in_=sv4[:, :, 0, :]), reads=svk, writes=["p2_TM0"])
                    t.op('dve', lambda e: e.scalar_tensor_tensor(out=st8[:, 3, :], in0=ts_[:, :, 15], scalar=-1.0e-5, in1=st8[:, 2, :], op0=ALU.add, op1=ALU.subtract),
                         reads=tsk + ["p2_st8"], writes=["p2_st8"])
                    t.op('act', lambda e: e.activation(out=st8[:, 3, :], in_=st8[:, 3, :], func=AF.Exp), reads=["p2_st8"], writes=["p2_st8"])
                    t.op('dve', lambda e: e.tensor_copy(out=TMv[:, 1], in_=bc(st8[:, 3, :].unsqueeze(2), [128, 8, 16])), reads=["p2_st8"], writes=["p2_TM1"])
                    t.op('dve', lambda e: e.tensor_tensor(out=TMv[:, 2], in0=sv4[:, :, 0, :], in1=bc(st8[:, 2, :].unsqueeze(2), [128, 8, 16]), op=ALU.subtract),
                         reads=svk + ["p2_st8"], writes=["p2_TM2"])
                    for j in range(3):
                        t.op('pe', lambda e, j=j: e.transpose(self.ps[TBK][:, j * 128:(j + 1) * 128], TM[:, j, :], self.ident[:]),
                             reads=["p2_TM%d" % j, "ident"], writes=["ps%d" % TBK], skip_self=True)
                    t.op('act', lambda e, tsl=tsl: e.copy(out=SM[:, :, tsl], in_=self.ps[TBK][:, 0:384].rearrange("p (j n) -> p j n", j=3)),
                         reads=["ps%d" % TBK], writes=["p2_SM%d" % st])
            for blk in range(NB):
                tok0 = blk * 256
                b512 = tok0 // 512
                t.dma(xb[:], XBv[:, :, tok0:tok0 + 256], reads=["XB_%d" % b512], writes=["p2_xb"])
                if blk == 0:
                    topk(0, False)
                import os as _os
                _old = _os.environ.get("BANKMAP") == "old"
                B0 = lambda pg: [4, 5, 2][pg]
                B1 = lambda pg: [6, 7, 3][pg]
                GB = lambda pg: pg % 2
                TB = 4 if _old else 7

                def fill_qr(c16):
                    qb_ = c16 % 2
                    for c in range(2):
                        src = bc(qT[:, c::2, c16 * 16:(c16 + 1) * 16].rearrange("p h t -> p t h").unsqueeze(3), [128, 16, 8, 16])
                        dst = QR[:, qb_, c].rearrange("p t (h a) -> p t h a", h=8)
                        t.op('act', lambda e, src=src, dst=dst: e.copy(out=dst, in_=src), reads=["p2_qT"], writes=["p2_qr%d_%d" % (qb_, c)])

                def stageA(g):
                    if g % 4 == 0:
                        fill_qr(g // 4)
                    qb_ = (g // 4) % 2
                    pg = g % 3
                    for s4 in range(4):
                        tl = (g % 4) * 4 + s4
                        t.op('pe', lambda e, tl=tl, s4=s4, qb_=qb_, pg=pg: e.matmul(self.ps[B0(pg)][:, s4 * 128:(s4 + 1) * 128], QR[:, qb_, 0, tl, :], skb[:, 0, :], start=True, stop=True),
                             reads=["p2_qr%d_0" % qb_, "p2_skb"], writes=["ps%d" % B0(pg)], skip_self=True)
                        t.op('pe', lambda e, tl=tl, s4=s4, qb_=qb_, pg=pg: e.matmul(self.ps[B1(pg)][:, s4 * 128:(s4 + 1) * 128], QR[:, qb_, 1, tl, :], skb[:, 1, :], start=True, stop=True),
                             reads=["p2_qr%d_1" % qb_, "p2_skb"], writes=["ps%d" % B1(pg)], skip_self=True)

                def stageB(g):
                    pg = g % 3
                    k0 = "ps%d" % B0(pg)
                    k1 = "ps%d" % B1(pg)
                    for s4 in range(4):
                        tt_ = g * 4 + s4
                        sl = pg * 4 + s4
                        smk = "p2_SM%d" % (tt_ // 128)
                        t.op('dve', lambda e, s4=s4, tt_=tt_, sl=sl, pg=pg: e.tensor_scalar(out=Pt[:, sl, :], in0=self.ps[B0(pg)][:, s4 * 128:(s4 + 1) * 128], scalar1=SM[:, 0, tt_:tt_ + 1], scalar2=None,
                                                                                       op0=ALU.is_equal, op1=ALU.bypass),
                             reads=[k0, smk], writes=["p2_P%d" % sl])
                    for s4 in range(4):
                        tt_ = g * 4 + s4
                        sl = pg * 4 + s4
                        smk = "p2_SM%d" % (tt_ // 128)
                        t.op('act', lambda e, s4=s4, tt_=tt_, sl=sl, pg=pg: e.activation(out=Et[:, sl, :], in_=self.ps[B1(pg)][:, s4 * 128:(s4 + 1) * 128], func=AF.Exp, bias=SM[:, 2, tt_:tt_ + 1], scale=1.0),
                             reads=[k1, smk], writes=["p2_E%d" % sl])
                    for s4 in range(4):
                        tt_ = g * 4 + s4
                        sl = pg * 4 + s4
                        smk = "p2_SM%d" % (tt_ // 128)
                        t.op('dve', lambda e, s4=s4, tt_=tt_, sl=sl, pg=pg: e.scalar_tensor_tensor(out=Qt[:, sl, :], in0=Et[:, sl, :], scalar=SM[:, 1, tt_:tt_ + 1],
                                                                                              in1=Et[:, sl, :], op0=ALU.is_ge, op1=ALU.mult),
                             reads=[smk, "p2_E%d" % sl], writes=["p2_Q%d" % sl])

                def stageC(g):
                    pg = g % 3
                    for s4 in range(4):
                        sl = pg * 4 + s4
                        t.op('pe', lambda e, s4=s4, sl=sl, pg=pg, g=g: e.matmul(self.ps[g % 2][:, s4 * 128:(s4 + 1) * 128], Qt[:, sl, :], Pt[:, sl, :], start=True, stop=True),
                             reads=["p2_Q%d" % sl, "p2_P%d" % sl], writes=["ps%d" % (g % 2)], skip_self=True)
                    t4 = g * 4
                    t.op('act', lambda e, t4=t4, pg=pg, g=g: e.copy(out=Gs[:, t4:t4 + 4, :], in_=self.ps[g % 2][:].rearrange("p (t i) -> p t i", t=4)),
                         reads=["ps%d" % (g % 2)], writes=["p2_Gs"])

                opt_tok = getattr(self, "opt_tok", True)
                opt_dense = getattr(self, "opt_dense", True)
                if opt_tok:
                    stageA(0)
                    stageA(1)
                for g in range(64):
                    if opt_tok:
                        if g + 2 < 64:
                            stageA(g + 2)
                    else:
                        stageA(g)
                    stageB(g)
                    if opt_tok:
                        if g >= 1:
                            stageC(g - 1)
                    else:
                        stageC(g)
                if opt_tok:
                    stageC(63)
                def load_w(ib):
                    wbuf = ib % 2
                    e0 = ib * 512
                    ukeys = ["Ub%d_%d_%d" % (layer, r0, e0 // 2048) for r0 in range(8)]
                    vkeys = ["Vb%d_%d" % (layer, (ib * 512) // 256 + x) for x in range(2)]
                    t.dma(UTs[:, wbuf], Ubv[:, :, e0:e0 + 512], reads=ukeys, writes=["p2_UT%d" % wbuf])
                    t.dma(Vs[:, wbuf], Vbv[:, ib * 4:(ib + 1) * 4, :], reads=vkeys, writes=["p2_V%d" % wbuf])

                def act_mm(i):
                    ib, ii = i // 4, i % 4
                    wbuf = ib % 2
                    abank = 5 + i % 2
                    for k in range(8):
                        t.op('pe', lambda e, k=k, ii=ii, wbuf=wbuf, abank=abank: e.matmul(self.ps[abank][:, 0:256], UTs[:, wbuf, k, ii * 128:(ii + 1) * 128], xb[:, k, :],
                                                                                    start=(k == 0), stop=(k == 7)),
                             reads=["p2_UT%d" % wbuf, "p2_xb"], writes=["ps%d" % abank], skip_self=True)

                load_w(0)
                if opt_dense:
                    act_mm(0)
                for i in range(128):
                    ib, ii = i // 4, i % 4
                    wbuf = ib % 2
                    ab = i % 2
                    abank = 5 + ab
                    if ii == 0 and ib + 1 < 32:
                        load_w(ib + 1)
                    if i == 24 and blk + 1 < NB:
                        topk(blk + 1, True)
                    if opt_dense:
                        if i + 1 < 128:
                            act_mm(i + 1)
                    else:
                        act_mm(i)
                    t.op('act', lambda e, ab=ab, abank=abank: e.activation(out=ga[:, ab, :], in_=self.ps[abank][:, 0:256], func=AF.Gelu_apprx_tanh),
                         reads=["ps%d" % abank], writes=["p2_ga%d" % ab])
                    t.op('dve', lambda e, ab=ab, i=i: e.tensor_tensor(out=Hh[:, ab, :], in0=ga[:, ab, :], in1=Gs[:, :, i], op=ALU.mult),
                         reads=["p2_ga%d" % ab, "p2_Gs"], writes=["p2_H%d" % ab])
                    for st in range(2):
                        for fh in range(2):
                            ob = st * 2 + fh
                            t.op('pe', lambda e, st=st, fh=fh, ob=ob, ab=ab, ii=ii, wbuf=wbuf, i=i: e.matmul(
                                self.ps[ob][:], Hh[:, ab, st * 128:(st + 1) * 128], Vs[:, wbuf, ii, fh * 512:(fh + 1) * 512], start=(i == 0), stop=(i == 127)),
                                reads=["p2_H%d" % ab, "p2_V%d" % wbuf], writes=["ps%d" % ob], skip_self=True)
                for st in range(2):
                    for fh in range(2):
                        ob = st * 2 + fh
                        if ob % 2 == 0:
                            t.op('act', lambda e, st=st, fh=fh, ob=ob: e.copy(out=otok[:, st, fh * 512:(fh + 1) * 512], in_=self.ps[ob][:]), reads=["ps%d" % ob], writes=["p2_otok%d" % st])
                        else:
                            t.op('dve', lambda e, st=st, fh=fh, ob=ob: e.tensor_copy(out=otok[:, st, fh * 512:(fh + 1) * 512], in_=self.ps[ob][:]), reads=["ps%d" % ob], writes=["p2_otok%d" % st])
                for st in range(2):
                    for half in range(2):
                        tb = 5 + half
                        for kk in range(4):
                            fk = half * 4 + kk
                            t.op('pe', lambda e, st=st, fk=fk, kk=kk, tb=tb: e.transpose(self.ps[tb][:, kk * 128:(kk + 1) * 128], otok[:, st, fk * 128:(fk + 1) * 128], self.ident[:]),
                                 reads=["p2_otok%d" % st, "ident"], writes=["ps%d" % tb], skip_self=True)
                        if half == 0:
                            t.op('act', lambda e, st=st, half=half, tb=tb: e.copy(out=peT[:, half * 4:(half + 1) * 4, st * 128:(st + 1) * 128],
                                                                                in_=self.ps[tb][:].rearrange("p (k n) -> p k n", k=4)),
                                 reads=["ps%d" % tb], writes=["p2_peT"])
                        else:
                            t.op('dve', lambda e, st=st, half=half, tb=tb: e.tensor_copy(out=peT[:, half * 4:(half + 1) * 4, st * 128:(st + 1) * 128],
                                                                                       in_=self.ps[tb][:].rearrange("p (k n) -> p k n", k=4)),
                                 reads=["ps%d" % tb], writes=["p2_peT"])
                t.dma(PEv[:, :, tok0:tok0 + 256], peT[:], reads=["p2_peT"], writes=["PEo_%d" % blk])
            t.barrier()

    def peer_p3(self, layer, xin, xout):
        nc, t = self.nc, self.t
        NB = (self.peer_nblk + 1) // 2 if hasattr(self, "peer_nblk") else 16
        with ExitStack() as ph:
            sb = lambda name, shape, dt=F32: ph.enter_context(self.sbuf(name, list(shape), dt))
            self.mk_eps(ph)
            wpg = self.load_w_bf16(ph, "p3_wpg", self.w_ple_gate[layer], 8, D, eng_cast='mix')
            wpp = self.load_w_bf16(ph, "p3_wpp", self.w_ple_proj[layer], 2, D, eng_cast='mix')
            xf = sb("p3_xf", [128, 8, 512])
            pe = sb("p3_pe", [128, 8, 512])
            pf = sb("p3_pf", [128, 2, 512])
            pb = sb("p3_pb", [128, 2, 512], BF16)
            z = sb("p3_z", [128, 8, 512])
            tmp = {'zc': sb("p3_zc", [128, 8, 512]), 'sq': sb("p3_sq", [128, 8, 512]), 'sd': sb("p3_sd", [128, 512])}
            x2 = sb("p3_x2", [128, 8, 512])
            x2b = sb("p3_x2b", [128, 8, 512], BF16)
            sg_ = sb("p3_sg", [128, 2, 512])
            xo = sb("p3_xo", [128, 8, 512])
            xv = xin.rearrange("(k p) n -> p k n", p=128)
            PEv = self.PEo.rearrange("(k p) n -> p k n", p=128)
            pv = self.pT[layer].rearrange("(k p) n -> p k n", p=128)
            ov = xout.rearrange("(k p) n -> p k n", p=128)
            for blk in range(NB):
                tok0 = blk * 512
                t.dma(xf[:], xv[:, :, tok0:tok0 + 512], reads=["X1"], writes=["p3_xf"])
                t.dma(pe[:], PEv[:, :, tok0:tok0 + 512], reads=["PEo_%d" % (2 * blk), "PEo_%d" % (2 * blk + 1)], writes=["p3_pe"])
                t.dma(pf[:], pv[:, :, tok0:tok0 + 512], writes=["p3_pf"])
                t.op('pool', lambda e: e.tensor_copy(out=pb[:], in_=pf[:]), reads=["p3_pf"], writes=["p3_pb"])
                t.op('dve', lambda e: e.scalar_tensor_tensor(out=z[:], in0=xf[:], scalar=ALPHA, in1=pe[:], op0=ALU.mult, op1=ALU.add),
                     reads=["p3_xf", "p3_pe"], writes=["p3_z"])
                self.ln_block(z, "p3_z", layer, 1, x2, "p3_x2", 512, tmp, 0, 1)
                t.op('pool', lambda e: e.tensor_copy(out=x2b[:], in_=x2[:]), reads=["p3_x2"], writes=["p3_x2b"])
                for fo in range(8):
                    bg, bp = 2 + (fo % 2) * 2, 3 + (fo % 2) * 2
                    sgi = fo % 2
                    for k in range(8):
                        t.op('pe', lambda e, k=k, fo=fo, bg=bg: e.matmul(self.ps[bg][:], wpg[:, k, fo * 128:(fo + 1) * 128], x2b[:, k, :], start=(k == 0), stop=(k == 7)),
                             reads=["p3_wpg", "p3_x2b"], writes=["ps%d" % bg], skip_self=True)
                    for k in range(2):
                        t.op('pe', lambda e, k=k, fo=fo, bp=bp: e.matmul(self.ps[bp][:], wpp[:, k, fo * 128:(fo + 1) * 128], pb[:, k, :], start=(k == 0), stop=(k == 1)),
                             reads=["p3_wpp", "p3_pb"], writes=["ps%d" % bp], skip_self=True)
                    t.op('act', lambda e, bg=bg, sgi=sgi: e.activation(out=sg_[:, sgi, :], in_=self.ps[bg][:], func=AF.Sigmoid), reads=["ps%d" % bg], writes=["p3_sg%d" % sgi])
                    t.op('dve', lambda e, bp=bp, sgi=sgi: e.tensor_tensor(out=sg_[:, sgi, :], in0=self.ps[bp][:], in1=sg_[:, sgi, :], op=ALU.mult),
                         reads=["ps%d" % bp, "p3_sg%d" % sgi], writes=["p3_sg%d" % sgi])
                    t.op('pool', lambda e, fo=fo, sgi=sgi: e.tensor_tensor(out=xo[:, fo, :], in0=x2[:, fo, :], in1=sg_[:, sgi, :], op=ALU.add),
                         reads=["p3_x2", "p3_sg%d" % sgi], writes=["p3_xo"])
                t.dma(ov[:, :, tok0:tok0 + 512], xo[:], reads=["p3_xo"], writes=["XOUT%d_%d" % (layer, blk)])
            t.barrier()

    def phase_rg(self):
        self.rg_p1()
        self.rg_p2()
        self.rg_p3()

    def rg_p1(self):
        nc, t = self.nc, self.t
        with ExitStack() as ph:
            sb = lambda name, shape, dt=F32: ph.enter_context(self.sbuf(name, list(shape), dt))
            win = self.load_w_bf16(ph, "r1_win", self.w_rg_in, 8, 2 * D, eng_cast='mix')
            xf = sb("r1_xf", [128, 2, 8, 512])
            xb = sb("r1_xb", [128, 2, 8, 512], BF16)
            gg = sb("r1_gg", [128, 2, 8, 512], BF16)
            rr = sb("r1_rr", [128, 2, 8, 512])
            xv = self.XL1.rearrange("(k p) n -> p k n", p=128)
            ggv = self.RGg.rearrange("(k p) n -> p k n", p=128)
            rrv = self.RGr.rearrange("(k p) n -> p k n", p=128)
            for blk in range(16):
                b = blk % 2
                tok0 = blk * 512
                t.dma(xf[:, b], xv[:, :, tok0:tok0 + 512], writes=["r1_xf%d" % b])
                t.op('pool', lambda e, b=b: e.tensor_copy(out=xb[:, b], in_=xf[:, b]), reads=["r1_xf%d" % b], writes=["r1_xb%d" % b])
                for fo in range(16):
                    bank = fo % 4
                    for k in range(8):
                        t.op('pe', lambda e, k=k, fo=fo, b=b, bank=bank: e.matmul(self.ps[bank][:], win[:, k, fo * 128:(fo + 1) * 128], xb[:, b, k, :],
                                                                             start=(k == 0), stop=(k == 7)),
                             reads=["r1_win", "r1_xb%d" % b], writes=["ps%d" % bank], skip_self=True)
                    if fo < 8:
                        t.op('act', lambda e, fo=fo, b=b, bank=bank: e.activation(out=gg[:, b, fo, :], in_=self.ps[bank][:], func=AF.Gelu_apprx_tanh),
                             reads=["ps%d" % bank], writes=["r1_gg%d" % b])
                    else:
                        t.op('dve', lambda e, fo=fo, b=b, bank=bank: e.tensor_copy(out=rr[:, b, fo - 8, :], in_=self.ps[bank][:]), reads=["ps%d" % bank], writes=["r1_rr%d" % b])
                t.dma(ggv[:, :, tok0:tok0 + 512], gg[:, b], reads=["r1_gg%d" % b], writes=["RGg_%d" % blk])
                t.dma(rrv[:, :, tok0:tok0 + 512], rr[:, b], reads=["r1_rr%d" % b], writes=["RGr_%d" % blk])
            t.barrier()

    def rg_p2(self):
        nc, t = self.nc, self.t
        LM = 4096
        with ExitStack() as ph:
            sb = lambda name, shape, dt=F32: ph.enter_context(self.sbuf(name, list(shape), dt))
            cw = sb("r2_cw", [128, 8, 4]); cbias = sb("r2_cbias", [128, 8])
            bga = sb("r2_bga", [128, 2, 8]); bgx = sb("r2_bgx", [128, 2, 8]); lam = sb("r2_lam", [128, 2, 8])
            sp8 = sb("r2_sp8", [128, 2, 8]); sp16 = sb("r2_sp16", [128, 2, 8])
            for dst, src in [(cw, self.rg_convw), (cbias, self.rg_convb), (bga, self.rg_bga), (bgx, self.rg_bgx), (lam, self.rg_lam)]:
                t.dma(dst[:], src, writes=[dst.name])
            t.op('act', lambda e: e.activation(out=sp8[:], in_=lam[:], func=AF.Exp, scale=-1.0), reads=[lam.name], writes=["r2_sp8"])
            t.op('act', lambda e: e.activation(out=sp8[:], in_=sp8[:], func=AF.Ln, bias=1.0, scale=1.0), reads=["r2_sp8"], writes=["r2_sp8"])
            t.op('dve', lambda e: e.tensor_scalar(out=sp16[:], in0=sp8[:], scalar1=-16.0, scalar2=None, op0=ALU.mult, op1=ALU.bypass), reads=["r2_sp8"], writes=["r2_sp16"])
            t.op('dve', lambda e: e.tensor_scalar(out=sp8[:], in0=sp8[:], scalar1=-8.0, scalar2=None, op0=ALU.mult, op1=ALU.bypass), reads=["r2_sp8", "r2_sp16"], writes=["r2_sp8"])
            wst = sb("r2_wst", [128, 2, 256])
            wgt = sb("r2_wgt", [128, 2, 2, 4, 2, 256], BF16)
            for gi, src in enumerate([self.rg_wga, self.rg_wgx]):
                for d_ in range(2):
                    for h in range(4):
                        t.dma(wst[:], src[d_, h].rearrange("(i p) o -> p i o", p=128), writes=["r2_wst"])
                        t.op('pool', lambda e, gi=gi, d_=d_, h=h: e.tensor_copy(out=wgt[:, gi, d_, h], in_=wst[:]), reads=["r2_wst"], writes=["r2_wgt"])
            rp = sb("r2_rp", [128, 2, LM + 3])
            cc = sb("r2_cc", [128, 2, LM])
            cb = sb("r2_cb", [128, 2, LM], BF16)
            A = sb("r2_A", [128, LM]); B = sb("r2_B", [128, LM])
            HF = sb("r2_HF", [128, LM]); HB = sb("r2_HB", [128, LM])
            gg = sb("r2_gg", [128, LM], BF16); yy = sb("r2_yy", [128, LM], BF16)
            rg_ = sb("r2_rg", [128, 2, 512]); ig_ = sb("r2_ig", [128, 2, 512]); a2_ = sb("r2_a2", [128, 2, 512]); tm_ = sb("r2_tm", [128, 2, 512])
            for si, (s0, L) in enumerate(SEGS):
                for h in range(4):
                    for ct in range(2):
                        ch = 2 * h + ct
                        t.op('pool', lambda e, ct=ct, L=L: e.memset(rp[:, ct, 0:1], 0.0), writes=["r2_rp%d" % ct])
                        t.op('pool', lambda e, ct=ct, L=L: e.memset(rp[:, ct, L + 1:L + 3], 0.0), reads=["r2_rp%d" % ct], writes=["r2_rp%d" % ct])
                        t.dma(rp[:, ct, 1:L + 1], self.RGr[ch * 128:(ch + 1) * 128, s0:s0 + L], reads=["r2_rp%d" % ct], writes=["r2_rp%d" % ct])
                        t.op('dve', lambda e, ct=ct, ch=ch, L=L: e.tensor_scalar(out=cc[:, ct, 0:L], in0=rp[:, ct, 0:L], scalar1=cw[:, ch, 0:1], scalar2=cbias[:, ch:ch + 1],
                                                                            op0=ALU.mult, op1=ALU.add), reads=["r2_rp%d" % ct, cw.name, cbias.name], writes=["r2_cc%d" % ct])
                        for k in range(1, 4):
                            t.op('dve', lambda e, ct=ct, ch=ch, L=L, k=k: e.scalar_tensor_tensor(out=cc[:, ct, 0:L], in0=rp[:, ct, k:k + L], scalar=cw[:, ch, k:k + 1], in1=cc[:, ct, 0:L],
                                                                                           op0=ALU.mult, op1=ALU.add), reads=["r2_rp%d" % ct, cw.name, "r2_cc%d" % ct], writes=["r2_cc%d" % ct])
                        t.op('pool', lambda e, ct=ct, L=L: e.tensor_copy(out=cb[:, ct, 0:L], in_=cc[:, ct, 0:L]), reads=["r2_cc%d" % ct], writes=["r2_cb%d" % ct])
                    for oh in range(2):
                        ch = 2 * h + oh
                        for d_ in range(2):
                            Hd = HF if d_ == 0 else HB
                            for c0 in range(0, L, 512):
                                pi = (c0 // 512) % 2
                                ba, bx = 2 * pi, 2 * pi + 1
                                for ih in range(2):
                                    t.op('pe', lambda e, ih=ih, d_=d_, h=h, oh=oh, c0=c0, ba=ba: e.matmul(self.ps[ba][:], wgt[:, 0, d_, h, ih, oh * 128:(oh + 1) * 128], cb[:, ih, c0:c0 + 512],
                                                                                                    start=(ih == 0), stop=(ih == 1)),
                                         reads=["r2_wgt", "r2_cb0", "r2_cb1"], writes=["ps%d" % ba], skip_self=True)
                                for ih in range(2):
                                    t.op('pe', lambda e, ih=ih, d_=d_, h=h, oh=oh, c0=c0, bx=bx: e.matmul(self.ps[bx][:], wgt[:, 1, d_, h, ih, oh * 128:(oh + 1) * 128], cb[:, ih, c0:c0 + 512],
                                                                                                    start=(ih == 0), stop=(ih == 1)),
                                         reads=["r2_wgt", "r2_cb0", "r2_cb1"], writes=["ps%d" % bx], skip_self=True)
                                t.op('act', lambda e, pi=pi, ba=ba, d_=d_, ch=ch: e.activation(out=rg_[:, pi, :], in_=self.ps[ba][:], func=AF.Sigmoid, bias=bga[:, d_, ch:ch + 1], scale=1.0),
                                     reads=["ps%d" % ba, bga.name], writes=["r2_rg%d" % pi])
                                t.op('act', lambda e, pi=pi, bx=bx, d_=d_, ch=ch: e.activation(out=ig_[:, pi, :], in_=self.ps[bx][:], func=AF.Sigmoid, bias=bgx[:, d_, ch:ch + 1], scale=1.0),
                                     reads=["ps%d" % bx, bgx.name], writes=["r2_ig%d" % pi])
                                t.op('act', lambda e, pi=pi, c0=c0, d_=d_, ch=ch: e.activation(out=A[:, c0:c0 + 512], in_=rg_[:, pi, :], func=AF.Exp, scale=sp8[:, d_, ch:ch + 1]),
                                     reads=["r2_rg%d" % pi, "r2_sp8"], writes=["r2_A"])
                                t.op('act', lambda e, pi=pi, d_=d_, ch=ch: e.activation(out=a2_[:, pi, :], in_=rg_[:, pi, :], func=AF.Exp, scale=sp16[:, d_, ch:ch + 1]),
                                     reads=["r2_rg%d" % pi, "r2_sp16"], writes=["r2_a2%d" % pi])
                                t.op('dve', lambda e, pi=pi: e.tensor_scalar(out=a2_[:, pi, :], in0=a2_[:, pi, :], scalar1=-1.0, scalar2=1.0, op0=ALU.mult, op1=ALU.add),
                                     reads=["r2_a2%d" % pi], writes=["r2_a2%d" % pi])
                                t.op('act', lambda e, pi=pi: e.activation(out=a2_[:, pi, :], in_=a2_[:, pi, :], func=AF.Sqrt), reads=["r2_a2%d" % pi], writes=["r2_a2%d" % pi])
                                t.op('pool', lambda e, pi=pi, oh=oh, c0=c0: e.tensor_tensor(out=tm_[:, pi, :], in0=ig_[:, pi, :], in1=cc[:, oh, c0:c0 + 512], op=ALU.mult),
                                     reads=["r2_ig%d" % pi, "r2_cc%d" % oh], writes=["r2_tm%d" % pi])
                                t.op('dve', lambda e, pi=pi, c0=c0: e.tensor_tensor(out=B[:, c0:c0 + 512], in0=tm_[:, pi, :], in1=a2_[:, pi, :], op=ALU.mult),
                                     reads=["r2_tm%d" % pi, "r2_a2%d" % pi], writes=["r2_B"])
                            if d_ == 0:
                                t.op('dve', lambda e, L=L: e.tensor_tensor_scan(out=HF[:, 0:L], data0=A[:, 0:L], data1=B[:, 0:L], initial=0.0, op0=ALU.mult, op1=ALU.add),
                                     reads=["r2_A", "r2_B"], writes=["r2_HF"])
                            else:
                                t.op('dve', lambda e, L=L: e.tensor_tensor_scan(out=HB[:, 0:L][:, ::-1], data0=A[:, 0:L][:, ::-1], data1=B[:, 0:L][:, ::-1], initial=0.0, op0=ALU.mult, op1=ALU.add),
                                     reads=["r2_A", "r2_B"], writes=["r2_HB"])
                        t.dma(gg[:, 0:L], self.RGg[ch * 128:(ch + 1) * 128, s0:s0 + L], writes=["r2_gg"])
                        t.op('pool', lambda e, L=L: e.tensor_tensor(out=HF[:, 0:L], in0=HF[:, 0:L], in1=HB[:, 0:L], op=ALU.add), reads=["r2_HF", "r2_HB"], writes=["r2_HF"])
                        t.op('dve', lambda e, L=L: e.tensor_tensor(out=yy[:, 0:L], in0=HF[:, 0:L], in1=gg[:, 0:L], op=ALU.mult), reads=["r2_HF", "r2_gg"], writes=["r2_yy"])
                        t.dma(self.RGy[ch * 128:(ch + 1) * 128, s0:s0 + L], yy[:, 0:L], reads=["r2_yy"], writes=["RGy_%d_%d" % (si, ch)])
            t.barrier()

    def rg_p3(self):
        nc, t = self.nc, self.t
        with ExitStack() as ph:
            sb = lambda name, shape, dt=F32: ph.enter_context(self.sbuf(name, list(shape), dt))
            self.mk_eps(ph)
            wo = self.load_w_bf16(ph, "r3_wo", self.w_rg_out, 8, D, eng_cast='mix')
            yb = sb("r3_yb", [128, 8, 512], BF16)
            xf = sb("r3_xf", [128, 8, 512])
            z = sb("r3_z", [128, 8, 512])
            tmp = {'zc': sb("r3_zc", [128, 8, 512]), 'sq': sb("r3_sq", [128, 8, 512]), 'sd': sb("r3_sd", [128, 512])}
            xo = sb("r3_xo", [128, 8, 512])
            xv = self.XL1.rearrange("(k p) n -> p k n", p=128)
            yv = self.RGy.rearrange("(k p) n -> p k n", p=128)
            X1v = self.X1.rearrange("(k p) n -> p k n", p=128)
            for blk in range(16):
                tok0 = blk * 512
                t.dma(yb[:], yv[:, :, tok0:tok0 + 512], writes=["r3_yb"])
                t.dma(xf[:], xv[:, :, tok0:tok0 + 512], writes=["r3_xf"])
                for fo in range(8):
                    bank = 2 + fo % 4
                    for k in range(8):
                        t.op('pe', lambda e, k=k, fo=fo, bank=bank: e.matmul(self.ps[bank][:], wo[:, k, fo * 128:(fo + 1) * 128], yb[:, k, :], start=(k == 0), stop=(k == 7)),
                             reads=["r3_wo", "r3_yb"], writes=["ps%d" % bank], skip_self=True)
                    t.op('dve', lambda e, fo=fo, bank=bank: e.scalar_tensor_tensor(out=z[:, fo, :], in0=xf[:, fo, :], scalar=ALPHA, in1=self.ps[bank][:], op0=ALU.mult, op1=ALU.add),
                         reads=["r3_xf", "ps%d" % bank], writes=["r3_z"])
                self.ln_block(z, "r3_z", 1, 0, xo, "r3_xo", 512, tmp, 0, 1)
                t.dma(X1v[:, :, tok0:tok0 + 512], xo[:], reads=["r3_xo"], writes=["X1"])
            t.barrier()


def _consts():
    c = {}
    c["c_ident"] = np.eye(128, dtype=np.float32)
    c["c_onesm"] = np.full((128, 128), 1.0 / D, dtype=np.float32)
    s = np.arange(128) // 16
    c["c_maskf"] = (s[:, None] <= s[None, :]).astype(np.float32)
    c["c_maskb"] = (s[:, None] >= s[None, :]).astype(np.float32)
    m = np.ones((128, 1024), dtype=np.float32)
    m[0:64, [0, 512, 768]] = 0.0
    m[64:128, [511, 767, 1023]] = 0.0
    c["c_segmask"] = m
    return c


def _shared_weights(inp):
    f = lambda a: np.ascontiguousarray(np.asarray(a, dtype=np.float32))
    w = dict(_consts())
    w["s5_w_in"] = f(inp["s5_w_in"][0])
    w["s5_w_glu"] = f(inp["s5_w_glu"][0])
    w["s5_lamre"] = f(inp["s5_lam_re"][0].transpose(0, 2, 1).reshape(128, 64))
    w["s5_lamim"] = f(inp["s5_lam_im"][0].transpose(0, 2, 1).reshape(128, 64))
    w["s5_lstep"] = f(np.broadcast_to(inp["s5_log_step"][0][:, None, :], (2, 64, 64)).reshape(128, 64))
    w["s5_bre"] = f(inp["s5_b_re"][0].transpose(0, 2, 1, 3).reshape(128, 64, 16))
    w["s5_bim"] = f(inp["s5_b_im"][0].transpose(0, 2, 1, 3).reshape(128, 64, 16))
    w["s5_ctre"] = f(inp["s5_c_re"][0].transpose(0, 3, 1, 2).reshape(128, 64, 16))
    w["s5_ctim"] = f(inp["s5_c_im"][0].transpose(0, 3, 1, 2).reshape(128, 64, 16))
    d = np.asarray(inp["s5_d"][0]).reshape(64, 16)
    w["s5_dpk"] = f(np.broadcast_to(d.T[None, :, :], (8, 16, 64)).reshape(128, 64))
    w["rg_w_in"] = f(inp["rg_w_in"][0])
    w["rg_convw"] = f(inp["rg_conv_w"][0].reshape(4, 8, 128).transpose(2, 1, 0))
    w["rg_convb"] = f(inp["rg_conv_b"][0].reshape(8, 128).T)
    w["rg_wga"] = f(inp["rg_w_gate_a"][0])
    w["rg_wgx"] = f(inp["rg_w_gate_x"][0])
    w["rg_bga"] = f(inp["rg_b_gate_a"][0].reshape(2, 8, 128).transpose(2, 0, 1))
    w["rg_bgx"] = f(inp["rg_b_gate_x"][0].reshape(2, 8, 128).transpose(2, 0, 1))
    w["rg_lam"] = f(inp["rg_lambda"][0].reshape(2, 8, 128).transpose(2, 0, 1))
    w["rg_w_out"] = f(inp["rg_w_out"][0])
    ln = np.stack([np.asarray(inp[k]) for k in ("ln1_g", "ln1_b", "ln2_g", "ln2_b")], 0)
    w["ln_par"] = f(ln.reshape(4, 2, 8, 128).transpose(3, 0, 1, 2))
    w["peer_w_q"] = f(inp["peer_w_q"])
    w["peer_skT"] = f(np.asarray(inp["peer_subkeys"]).transpose(0, 1, 3, 2))
    w["peer_uT"] = f(np.asarray(inp["peer_u"]).transpose(0, 2, 1))
    w["peer_v"] = f(inp["peer_v"])
    w["ple_w_proj"] = f(inp["ple_w_proj"])
    w["ple_w_gate"] = f(inp["ple_w_gate"])
    return w


def _core_acts(inp, core):
    xp = np.asarray(inp["x_prompt"][core])
    xs = np.asarray(inp["x_sample"][2 * core:2 * core + 2]).reshape(4096, D)
    x = np.concatenate([xp, xs], 0)
    pp = np.asarray(inp["p_prompt"][:, core])
    ps = np.asarray(inp["p_sample"][:, 2 * core:2 * core + 2]).reshape(2, 4096, 256)
    p = np.concatenate([pp, ps], 1)
    return {"xT": np.ascontiguousarray(x.T.astype(np.float32)),
            "pT": np.ascontiguousarray(p.transpose(0, 2, 1).astype(np.float32))}


_CACHE = {}


def kernel(**inputs):
    if "nc" not in _CACHE:
        k = Ker()
        _CACHE["nc"] = k.build()
        _CACHE["in_names"] = list(k.in_names)
    nc = _CACHE["nc"]
    names = _CACHE["in_names"]
    w = _shared_weights(inputs)
    in_maps = []
    for core in range(8):
        m = dict(w)
        m.update(_core_acts(inputs, core))
        in_maps.append({n: m[n] for n in names})
    res = run_bass_kernel_spmd(nc, in_maps, core_ids=list(range(8)))
    yp = np.empty((8, 4096, D), dtype=np.float32)
    ys = np.empty((16, 2048, D), dtype=np.float32)
    for core in range(8):
        y = np.asarray(res.results[core]["yT"]).T
        yp[core] = y[0:4096]
        ys[2 * core] = y[4096:6144]
        ys[2 * core + 1] = y[6144:8192]
    return (yp, ys)
```

```python
from contextlib import ExitStack
import math
import numpy as np
import concourse.bass as bass
import concourse.mybir as mybir
from concourse.bass_utils import run_bass_kernel_spmd

F32 = mybir.dt.float32
BF16 = mybir.dt.bfloat16
I32 = mybir.dt.int32
AF = mybir.ActivationFunctionType
ALU = mybir.AluOpType

NTOK = 8192
D = 1024
ALPHA = 4.0 ** 0.25
LN_EPS = 1e-5
SEGS = [(0, 4096), (4096, 2048), (6144, 2048)]
TWO_PI = 2.0 * math.pi


class Trk:
    ENGS = ['pe', 'act', 'dve', 'pool', 'sp']

    def __init__(self, nc, stack, nslots=12):
        self.nc = nc
        self.e = {'pe': nc.tensor, 'act': nc.scalar, 'dve': nc.vector, 'pool': nc.gpsimd, 'sp': nc.sync}
        self.sem = {}
        self.cnt = {}
        for n in self.ENGS:
            self.sem[n] = stack.enter_context(nc.semaphore('s_' + n))
            self.cnt[n] = 0
        self.slots = ['d%d' % i for i in range(nslots)]
        for s in self.slots:
            self.sem[s] = stack.enter_context(nc.semaphore('s_' + s))
            self.cnt[s] = 0
        self.rr = 0
        self.seen = {n: {} for n in self.ENGS}
        self.lw = {}
        self.lr = {}
        self.ninst = 0

    def _deps(self, reads, writes):
        need = {}
        for k in reads:
            for e, c in self.lw.get(k, {}).items():
                if c > need.get(e, 0):
                    need[e] = c
        for k in writes:
            for e, c in self.lw.get(k, {}).items():
                if c > need.get(e, 0):
                    need[e] = c
            for e, c in self.lr.get(k, {}).items():
                if c > need.get(e, 0):
                    need[e] = c
        return need

    def _wait(self, eng, need, skip_self=False):
        for e, c in need.items():
            if skip_self and e == eng:
                continue
            if self.seen[eng].get(e, 0) >= c:
                continue
            self.e[eng].wait_ge(self.sem[e], c)
            self.seen[eng][e] = c

    def _record(self, who, c, reads, writes):
        for k in writes:
            self.lw[k] = {who: c}
            self.lr[k] = {}
        for k in reads:
            self.lr.setdefault(k, {})[who] = c

    def op(self, eng, fn, reads=(), writes=(), skip_self=False):
        need = self._deps(reads, writes)
        self._wait(eng, need, skip_self)
        ins = fn(self.e[eng])
        self.cnt[eng] += 1
        ins.then_inc(self.sem[eng], 1)
        self._record(eng, self.cnt[eng], reads, writes)
        self.ninst += 1

    def dma(self, out, in_, reads=(), writes=(), eng='sp'):
        need = self._deps(reads, writes)
        slot = self.slots[self.rr]
        self.rr = (self.rr + 1) % len(self.slots)
        if self.cnt[slot] > 0:
            need[slot] = max(need.get(slot, 0), self.cnt[slot])
        self._wait(eng, need)
        ins = self.e[eng].dma_start(out=out, in_=in_)
        self.cnt[slot] += 16
        ins.then_inc(self.sem[slot], 16)
        self._record(slot, self.cnt[slot], reads, writes)
        self.ninst += 1

    def barrier(self):
        self.min_rem = min(getattr(self, "min_rem", 1 << 30), self.nc.sbuf_bytes_remaining)
        allc = {k: v for k, v in self.cnt.items() if v > 0}
        for eng in self.ENGS:
            self._wait(eng, dict(allc), skip_self=True)

    def finish(self):
        allc = {k: v for k, v in self.cnt.items() if v > 0}
        self._wait('sp', dict(allc), skip_self=True)


def bc(ap, shape):
    return ap.to_broadcast(list(shape))


class Ker:
    def __init__(self, dbg_out=(), dbg_in=(), phases=None):
        self.dbg_out = set(dbg_out)
        self.dbg_in = set(dbg_in)
        self.phases = phases
        self.nc = bass.Bass("TRN2", target_bir_lowering=False)
        self.in_names = []
        self.out_names = []
        self.tmp_id = 0

    def sbuf(self, name, shape, dt):
        self.tmp_id += 1
        return self.nc.sbuf_tensor("%s_u%d" % (name, self.tmp_id), list(shape), dt)

    def din(self, name, shape, dt=F32):
        self.in_names.append(name)
        return self.nc.dram_tensor(name, list(shape), dt, kind="ExternalInput").ap()

    def dout(self, name, shape, dt=F32):
        self.out_names.append(name)
        return self.nc.dram_tensor(name, list(shape), dt, kind="ExternalOutput").ap()

    def scratch(self, name, shape, dt=F32):
        if name in self.dbg_in:
            return self.din(name, shape, dt)
        if name in self.dbg_out:
            return self.dout(name, shape, dt)
        return self.nc.dram_tensor(name, list(shape), dt, kind="Internal").ap()

    def __getattr__(self, attr):
        specs = self.__dict__.get("specs", {})
        if attr in specs:
            kind, name, shape = specs[attr]
            ap = self.din(name, shape)
            self.__dict__[attr] = ap
            return ap
        raise AttributeError(attr)

    def on(self, ph):
        return self.phases is None or ph in self.phases

    def build(self):
        nc = self.nc
        with ExitStack() as st:
            self.st = st
            self.t = Trk(nc, st)
            t = self.t
            self.specs = {
                "xT": ("din", "xT", [D, NTOK]),
                "pT": ("din", "pT", [2, 256, NTOK]),
                "yT": ("dout", "yT", [D, NTOK]),
                "c_ident": ("din", "c_ident", [128, 128]),
                "c_onesm": ("din", "c_onesm", [128, 128]),
                "c_maskf": ("din", "c_maskf", [128, 128]),
                "c_maskb": ("din", "c_maskb", [128, 128]),
                "c_segmask": ("din", "c_segmask", [128, 1024]),
                "w_s5_in": ("din", "s5_w_in", [D, D]),
                "w_s5_glu": ("din", "s5_w_glu", [D, 2 * D]),
                "s5_lamre": ("din", "s5_lamre", [128, 64]),
                "s5_lamim": ("din", "s5_lamim", [128, 64]),
                "s5_lstep": ("din", "s5_lstep", [128, 64]),
                "s5_bre": ("din", "s5_bre", [128, 64, 16]),
                "s5_bim": ("din", "s5_bim", [128, 64, 16]),
                "s5_ctre": ("din", "s5_ctre", [128, 64, 16]),
                "s5_ctim": ("din", "s5_ctim", [128, 64, 16]),
                "s5_dpk": ("din", "s5_dpk", [128, 64]),
                "w_rg_in": ("din", "rg_w_in", [D, 2 * D]),
                "rg_convw": ("din", "rg_convw", [128, 8, 4]),
                "rg_convb": ("din", "rg_convb", [128, 8]),
                "rg_wga": ("din", "rg_wga", [2, 4, 256, 256]),
                "rg_wgx": ("din", "rg_wgx", [2, 4, 256, 256]),
                "rg_bga": ("din", "rg_bga", [128, 2, 8]),
                "rg_bgx": ("din", "rg_bgx", [128, 2, 8]),
                "rg_lam": ("din", "rg_lam", [128, 2, 8]),
                "w_rg_out": ("din", "rg_w_out", [D, D]),
                "ln_par": ("din", "ln_par", [128, 4, 2, 8]),
                "w_peer_q": ("din", "peer_w_q", [2, D, 2 * D]),
                "peer_skT": ("din", "peer_skT", [2, 2, 128, 128]),
                "peer_uT": ("din", "peer_uT", [2, D, 16384]),
                "peer_v": ("din", "peer_v", [2, 16384, D]),
                "w_ple_proj": ("din", "ple_w_proj", [2, 256, D]),
                "w_ple_gate": ("din", "ple_w_gate", [2, D, D]),
            }
            self.yT = self.dout("yT", [D, NTOK]) if self.on("peer1") else None
            self.UpD = self.scratch("UpD", [64, 128, 1024], BF16)
            self.HpD = self.scratch("HpD", [64, 128, 1024], BF16)
            self.X1 = self.scratch("X1", [D, NTOK])
            self.XL1 = self.scratch("XL1", [D, NTOK])
            self.RGr = self.scratch("RGr", [D, NTOK])
            self.RGg = self.scratch("RGg", [D, NTOK], BF16)
            self.RGy = self.scratch("RGy", [D, NTOK], BF16)
            self.XB = self.scratch("XB", [D, NTOK], BF16)
            self.QT = self.scratch("QT", [2 * D, NTOK], BF16)
            self.PEo = self.scratch("PEo", [D, NTOK])
            self.Ub = self.scratch("Ub", [2, D, 16384], BF16)
            self.Vb = self.scratch("Vb", [2, 16384, D], BF16)
            self.ident = st.enter_context(self.sbuf("ident", [128, 128], F32))
            self.identb = st.enter_context(self.sbuf("identb", [128, 128], BF16))
            self.onesm = st.enter_context(self.sbuf("onesm", [128, 128], F32))
            self.lnp = st.enter_context(self.sbuf("lnp", [128, 4, 2, 8], F32))
            self.ps = [st.enter_context(nc.psum_tensor("ps%d" % i, [128, 512], F32)) for i in range(8)]
            t.dma(self.ident[:], self.c_ident, writes=["ident"])
            t.dma(self.onesm[:], self.c_onesm, writes=["onesm"])
            t.dma(self.lnp[:], self.ln_par, writes=["lnp"])
            t.op('dve', lambda e: e.tensor_copy(out=self.identb[:], in_=self.ident[:]), reads=["ident"], writes=["identb"])

            if self.on("tabcast"):
                self.phase_tabcast()
            if self.on("s5a"):
                self.phase_s5a()
            if self.on("s5b"):
                self.phase_s5b()
            if self.on("s5c"):
                self.phase_s5c()
            if self.on("peer0"):
                self.phase_peer(0, self.X1, self.XL1)
            if self.on("rg"):
                self.phase_rg()
            if self.on("peer1"):
                self.phase_peer(1, self.X1, self.yT)
            t.barrier()
            t.finish()
        return nc

    def load_w_bf16(self, ph, name, dram_ap, kt, ncols, eng_cast='pool'):
        nc, t = self.nc, self.t
        wb = ph.enter_context(self.sbuf(name, [128, kt, ncols], BF16))
        stg = ph.enter_context(self.sbuf(name + "_stg", [128, 2, 2048], F32))
        i = 0
        for k in range(kt):
            for c0 in range(0, ncols, 2048):
                cw = min(2048, ncols - c0)
                b = i % 2
                t.dma(stg[:, b, 0:cw], dram_ap[k * 128:(k + 1) * 128, c0:c0 + cw],
                      writes=[name + "_stg%d" % b])
                eng = ['pool', 'act'][i % 2] if eng_cast == 'mix' else eng_cast
                if eng == 'act':
                    t.op('act', lambda e, b=b, k=k, c0=c0, cw=cw: e.copy(out=wb[:, k, c0:c0 + cw], in_=stg[:, b, 0:cw]),
                         reads=[name + "_stg%d" % b], writes=[name])
                else:
                    t.op(eng, lambda e, b=b, k=k, c0=c0, cw=cw: e.tensor_copy(out=wb[:, k, c0:c0 + cw], in_=stg[:, b, 0:cw]),
                         reads=[name + "_stg%d" % b], writes=[name])
                i += 1
        return wb

    def ln_block(self, z, zkey, layer, which, out, outkey, N, tmp, pbank_a, pbank_b):
        t = self.t
        pm = self.ps[pbank_a]
        pv = self.ps[pbank_b]
        ka, kb = "ps%d" % pbank_a, "ps%d" % pbank_b
        for k in range(8):
            t.op('pe', lambda e, k=k: e.matmul(pm[:, 0:N], self.onesm[:], z[:, k, :], start=(k == 0), stop=(k == 7)),
                 reads=[zkey, "onesm"], writes=[ka], skip_self=True)
        zc, sq, sd = tmp['zc'], tmp['sq'], tmp['sd']
        t.op('dve', lambda e: e.tensor_tensor(out=zc[:], in0=z[:], in1=bc(pm[:, 0:N].unsqueeze(1), [128, 8, N]), op=ALU.subtract),
             reads=[zkey, ka], writes=[zc.name])
        t.op('act', lambda e: e.activation(out=sq[:], in_=zc[:], func=AF.Square), reads=[zc.name], writes=[sq.name])
        for k in range(8):
            t.op('pe', lambda e, k=k: e.matmul(pv[:, 0:N], self.onesm[:], sq[:, k, :], start=(k == 0), stop=(k == 7)),
                 reads=[sq.name, "onesm"], writes=[kb], skip_self=True)
        t.op('act', lambda e: e.activation(out=sd[:], in_=pv[:, 0:N], func=AF.Sqrt, bias=self.epsc[:, 0:1], scale=1.0),
             reads=[kb, "epsc"], writes=[sd.name])
        t.op('dve', lambda e: e.reciprocal(out=sd[:], in_=sd[:]), reads=[sd.name], writes=[sd.name])
        t.op('dve', lambda e: e.tensor_tensor(out=zc[:], in0=zc[:], in1=bc(sd[:].unsqueeze(1), [128, 8, N]), op=ALU.mult),
             reads=[zc.name, sd.name], writes=[zc.name])
        for k in range(8):
            t.op('act', lambda e, k=k: e.activation(out=out[:, k, :], in_=zc[:, k, :], func=AF.Identity,
                                                    bias=self.lnp[:, 2 * which + 1, layer, k:k + 1],
                                                    scale=self.lnp[:, 2 * which, layer, k:k + 1]),
                 reads=[zc.name, "lnp"], writes=[outkey])

    def mk_eps(self, ph):
        nc, t = self.nc, self.t
        self.epsc = ph.enter_context(self.sbuf("epsc", [128, 1], F32))
        t.op('dve', lambda e: e.memset(self.epsc[:], LN_EPS), writes=["epsc"])

    def phase_tabcast(self):
        t = self.t
        for l in range(2):
            for r0 in range(0, D, 128):
                for c0 in range(0, 16384, 2048):
                    t.dma(self.Ub[l, r0:r0 + 128, c0:c0 + 2048], self.peer_uT[l, r0:r0 + 128, c0:c0 + 2048],
                          writes=["Ub%d_%d_%d" % (l, r0 // 128, c0 // 2048)], eng='pool')
            for r0 in range(0, 16384, 256):
                t.dma(self.Vb[l, r0:r0 + 256, :], self.peer_v[l, r0:r0 + 256, :], writes=["Vb%d_%d" % (l, r0 // 256)], eng='pool')

    def phase_s5a(self):
        nc, t = self.nc, self.t
        with ExitStack() as ph:
            wb = self.load_w_bf16(ph, "s5win", self.w_s5_in, 8, D, eng_cast='mix')
            xf = ph.enter_context(self.sbuf("a_xf", [128, 2, 8, 512], F32))
            xb = ph.enter_context(self.sbuf("a_xb", [128, 8, 1024], BF16))
            U8 = ph.enter_context(self.sbuf("a_U8", [128, 64, 8, 16], BF16))
            Upt = ph.enter_context(self.sbuf("a_Upt", [128, 64, 128], BF16))
            xTv = self.xT.rearrange("(k p) n -> p k n", p=128)
            for tile in range(8):
                t0 = tile * 1024
                for h in range(2):
                    t.dma(xf[:, h, :, :], xTv[:, :, t0 + h * 512:t0 + (h + 1) * 512], writes=["a_xf%d" % h])
                    t.op('act' if h == 0 else 'pool',
                         (lambda e, h=h: e.copy(out=xb[:, :, h * 512:(h + 1) * 512], in_=xf[:, h, :, :])) if h == 0 else
                         (lambda e, h=h: e.tensor_copy(out=xb[:, :, h * 512:(h + 1) * 512], in_=xf[:, h, :, :])),
                         reads=["a_xf%d" % h], writes=["a_xb"])
                for s in range(8):
                    for fh in range(2):
                        bank = (s * 2 + fh) % 4
                        pk = "ps%d" % bank
                        for k in range(8):
                            t.op('pe', lambda e, k=k, s=s, fh=fh, bank=bank: e.matmul(
                                self.ps[bank][:], xb[:, k, s::8], wb[:, k, fh * 512:(fh + 1) * 512],
                                start=(k == 0), stop=(k == 7)),
                                reads=["a_xb", "s5win"], writes=[pk], skip_self=True)
                        if (s * 2 + fh) % 2 == 0:
                            t.op('act', lambda e, s=s, fh=fh, bank=bank: e.copy(out=U8[:, fh * 32:(fh + 1) * 32, s, :], in_=self.ps[bank][:].rearrange("p (g c) -> p g c", c=16)),
                                 reads=[pk], writes=["a_U8"])
                        else:
                            t.op('dve', lambda e, s=s, fh=fh, bank=bank: e.tensor_copy(out=U8[:, fh * 32:(fh + 1) * 32, s, :], in_=self.ps[bank][:].rearrange("p (g c) -> p g c", c=16)),
                                 reads=[pk], writes=["a_U8"])
                for gb in range(8):
                    bank = 4 + gb % 2
                    pk = "ps%d" % bank
                    pb = self.ps[bank][:].bitcast(BF16)
                    for gi in range(8):
                        g = gb * 8 + gi
                        t.op('pe', lambda e, g=g, gi=gi, pb=pb: e.transpose(pb[:, gi * 128:(gi + 1) * 128],
                                                                          U8[:, g, :, :].rearrange("p s c -> p (s c)"), self.identb[:]),
                             reads=["a_U8", "identb"], writes=[pk], skip_self=True)
                    if gb % 2 == 0:
                        t.op('dve', lambda e, gb=gb, pb=pb: e.tensor_copy(out=Upt[:, gb * 8:(gb + 1) * 8, :],
                                                                        in_=pb.rearrange("p (g c) -> p g c", g=8)),
                             reads=[pk], writes=["a_Upt"])
                    else:
                        t.op('act', lambda e, gb=gb, pb=pb: e.copy(out=Upt[:, gb * 8:(gb + 1) * 8, :],
                                                                 in_=pb.rearrange("p (g c) -> p g c", g=8)),
                             reads=[pk], writes=["a_Upt"])
                t.dma(self.UpD[:, :, tile * 128:(tile + 1) * 128].rearrange("g p c -> p g c"), Upt[:],
                      reads=["a_Upt"], writes=["UpD"])
            t.barrier()

    def cmul(self, eng, outr, outi, ar, ai, br, bi, tmp1, tmp2, rk, wk):
        t = self.t
        t.op(eng, lambda e: e.tensor_tensor(out=tmp1, in0=ar, in1=br, op=ALU.mult), reads=rk, writes=["cm_t1"])
        t.op(eng, lambda e: e.tensor_tensor(out=tmp2, in0=ai, in1=bi, op=ALU.mult), reads=rk, writes=["cm_t2"])
        t.op(eng, lambda e: e.tensor_tensor(out=outr, in0=tmp1, in1=tmp2, op=ALU.subtract), reads=["cm_t1", "cm_t2"] + rk, writes=wk)
        t.op(eng, lambda e: e.tensor_tensor(out=tmp1, in0=ar, in1=bi, op=ALU.mult), reads=rk + wk, writes=["cm_t1"])
        t.op(eng, lambda e: e.tensor_tensor(out=tmp2, in0=ai, in1=br, op=ALU.mult), reads=rk + wk, writes=["cm_t2"])
        t.op(eng, lambda e: e.tensor_tensor(out=outi, in0=tmp1, in1=tmp2, op=ALU.add), reads=["cm_t1", "cm_t2"] + rk, writes=wk)

    def phase_s5b(self):
        nc, t = self.nc, self.t
        with ExitStack() as ph:
            sb = lambda name, shape, dt=F32: ph.enter_context(self.sbuf(name, list(shape), dt))
            MATS = sb("b_mats", [128, 64, 5, 128], BF16)
            POW = sb("b_pow", [128, 2, 16, 64])
            PH = sb("b_ph", [128, 2, 10, 64])
            RHO = sb("b_rho", [128, 64])
            dpk = sb("b_dpk", [128, 64])
            segm = sb("b_segm", [128, 1024])
            t.dma(dpk[:], self.s5_dpk, writes=["b_dpk"])
            t.dma(segm[:], self.c_segmask, writes=["b_segm"])
            with ExitStack() as pg:
                sg = lambda name, shape, dt=F32: pg.enter_context(self.sbuf(name, list(shape), dt))
                lre = sg("g_lre", [128, 64]); lim = sg("g_lim", [128, 64]); lst = sg("g_lst", [128, 64])
                Bre = sg("g_bre", [128, 64, 16]); Bim = sg("g_bim", [128, 64, 16])
                Cre = sg("g_cre", [128, 64, 16]); Cim = sg("g_cim", [128, 64, 16])
                maskf = sg("g_maskf", [128, 128]); maskb = sg("g_maskb", [128, 128])
                for dst, src in [(lre, self.s5_lamre), (lim, self.s5_lamim), (lst, self.s5_lstep), (Bre, self.s5_bre),
                                 (Bim, self.s5_bim), (Cre, self.s5_ctre), (Cim, self.s5_ctim), (maskf, self.c_maskf),
                                 (maskb, self.c_maskb)]:
                    t.dma(dst[:], src, writes=[dst.name])
                S = {}
                for nm in ["step", "ang", "lrs", "mag", "magi", "kf", "r", "m1", "s1", "c1", "ar", "ai", "ari", "aii",
                           "den", "zr", "qr", "qi", "u1", "u2", "e8"]:
                    S[nm] = sg("g_" + nm, [128, 64])
                ki = sg("g_ki", [128, 64], I32)
                V = 'dve'

                def tt(out, a, b, op, rk, wk):
                    t.op(V, lambda e: e.tensor_tensor(out=out, in0=a, in1=b, op=op), reads=rk, writes=wk)

                def ts(out, a, s1, s2, op0, op1, rk, wk):
                    t.op(V, lambda e: e.tensor_scalar(out=out, in0=a, scalar1=s1, scalar2=s2, op0=op0, op1=op1), reads=rk, writes=wk)

                def act(out, a, func, rk, wk, scale=1.0):
                    t.op('act', lambda e: e.activation(out=out, in_=a, func=func, scale=scale), reads=rk, writes=wk)

                n = lambda k: S[k].name
                act(S["step"][:], lst[:], AF.Exp, [lst.name], [n("step")])
                tt(S["ang"][:], lim[:], S["step"][:], ALU.mult, [lim.name, n("step")], [n("ang")])
                tt(S["lrs"][:], lre[:], S["step"][:], ALU.mult, [lre.name, n("step")], [n("lrs")])
                act(S["mag"][:], S["lrs"][:], AF.Exp, [n("lrs")], [n("mag")])
                act(S["magi"][:], S["lrs"][:], AF.Exp, [n("lrs")], [n("magi")], scale=-1.0)
                act(S["e8"][:], S["lrs"][:], AF.Exp, [n("lrs")], [n("e8")], scale=-8.0)
                act(RHO[:], S["lrs"][:], AF.Exp, [n("lrs")], ["b_rho"], scale=8.0)

                def range_reduce(dst, src, shift):
                    ts(S["kf"][:], src, 1.0 / TWO_PI, shift / TWO_PI + 0.5, ALU.mult, ALU.add, [n("ang")], [n("kf")])
                    t.op(V, lambda e: e.tensor_copy(out=ki[:], in_=S["kf"][:]), reads=[n("kf")], writes=[ki.name])
                    t.op(V, lambda e: e.tensor_copy(out=S["kf"][:], in_=ki[:]), reads=[ki.name], writes=[n("kf")])
                    ts(S["kf"][:], S["kf"][:], -TWO_PI, shift, ALU.mult, ALU.add, [n("kf")], [n("kf")])
                    tt(dst, src, S["kf"][:], ALU.add, [n("ang"), n("kf")], [n("r")])
                    ts(S["m1"][:], dst, math.pi, -TWO_PI, ALU.is_gt, ALU.mult, [n("r")], [n("m1")])
                    tt(dst, dst, S["m1"][:], ALU.add, [n("r"), n("m1")], [n("r")])
                    ts(S["m1"][:], dst, -math.pi, TWO_PI, ALU.is_lt, ALU.mult, [n("r")], [n("m1")])
                    tt(dst, dst, S["m1"][:], ALU.add, [n("r"), n("m1")], [n("r")])
                    ts(dst, dst, math.pi, -math.pi, ALU.min, ALU.max, [n("r")], [n("r")])

                range_reduce(S["r"][:], S["ang"][:], 0.0)
                act(S["s1"][:], S["r"][:], AF.Sin, [n("r")], [n("s1")])
                range_reduce(S["r"][:], S["ang"][:], math.pi / 2)
                act(S["c1"][:], S["r"][:], AF.Sin, [n("r")], [n("c1")])
                tt(S["ar"][:], S["mag"][:], S["c1"][:], ALU.mult, [n("mag"), n("c1")], [n("ar")])
                tt(S["ai"][:], S["mag"][:], S["s1"][:], ALU.mult, [n("mag"), n("s1")], [n("ai")])
                tt(S["ari"][:], S["magi"][:], S["c1"][:], ALU.mult, [n("magi"), n("c1")], [n("ari")])
                tt(S["aii"][:], S["magi"][:], S["s1"][:], ALU.mult, [n("magi"), n("s1")], [n("aii")])
                ts(S["aii"][:], S["aii"][:], -1.0, None, ALU.mult, ALU.bypass, [n("aii")], [n("aii")])
                tt(S["den"][:], lre[:], lre[:], ALU.mult, [lre.name], [n("den")])
                tt(S["u1"][:], lim[:], lim[:], ALU.mult, [lim.name], [n("u1")])
                tt(S["den"][:], S["den"][:], S["u1"][:], ALU.add, [n("den"), n("u1")], [n("den")])
                t.op(V, lambda e: e.reciprocal(out=S["den"][:], in_=S["den"][:]), reads=[n("den")], writes=[n("den")])
                ts(S["zr"][:], S["ar"][:], -1.0, None, ALU.add, ALU.bypass, [n("ar")], [n("zr")])
                tt(S["u1"][:], S["zr"][:], lre[:], ALU.mult, [n("zr"), lre.name], [n("u1")])
                tt(S["u2"][:], S["ai"][:], lim[:], ALU.mult, [n("ai"), lim.name], [n("u2")])
                tt(S["u1"][:], S["u1"][:], S["u2"][:], ALU.add, [n("u1"), n("u2")], [n("u1")])
                tt(S["qr"][:], S["u1"][:], S["den"][:], ALU.mult, [n("u1"), n("den")], [n("qr")])
                tt(S["u1"][:], S["ai"][:], lre[:], ALU.mult, [n("ai"), lre.name], [n("u1")])
                tt(S["u2"][:], S["zr"][:], lim[:], ALU.mult, [n("zr"), lim.name], [n("u2")])
                tt(S["u1"][:], S["u1"][:], S["u2"][:], ALU.subtract, [n("u1"), n("u2")], [n("u1")])
                tt(S["qi"][:], S["u1"][:], S["den"][:], ALU.mult, [n("u1"), n("den")], [n("qi")])
                BBr = sg("g_bbr", [128, 64, 16]); BBi = sg("g_bbi", [128, 64, 16])
                T1 = sg("g_T1", [128, 1024]); T2 = sg("g_T2", [128, 1024])
                T1v = T1[:].rearrange("p (g c) -> p g c", c=16)
                T2v = T2[:].rearrange("p (g c) -> p g c", c=16)
                qrb = bc(S["qr"][:].unsqueeze(2), [128, 64, 16]); qib = bc(S["qi"][:].unsqueeze(2), [128, 64, 16])
                self.cmul(V, BBr[:], BBi[:], qrb, qib, Bre[:], Bim[:], T1v, T2v, [n("qr"), n("qi"), Bre.name, Bim.name], [BBr.name, BBi.name])
                t.op(V, lambda e: e.memset(POW[:, 0, 7, :], 1.0), writes=["b_pow"])
                t.op(V, lambda e: e.memset(POW[:, 1, 7, :], 0.0), reads=["b_pow"], writes=["b_pow"])
                for k in range(0, 8):
                    self.cmul(V, POW[:, 0, 8 + k, :], POW[:, 1, 8 + k, :], POW[:, 0, 7 + k, :], POW[:, 1, 7 + k, :],
                              S["ar"][:], S["ai"][:], T1[:, 0:64], T2[:, 0:64], ["b_pow", n("ar"), n("ai")], ["b_pow"])
                for k in range(0, 7):
                    self.cmul(V, POW[:, 0, 6 - k, :], POW[:, 1, 6 - k, :], POW[:, 0, 7 - k, :], POW[:, 1, 7 - k, :],
                              S["ari"][:], S["aii"][:], T1[:, 0:64], T2[:, 0:64], ["b_pow", n("ari"), n("aii")], ["b_pow"])
                tt(PH[:, 0, 0, :], POW[:, 0, 15, :], S["e8"][:], ALU.mult, ["b_pow", n("e8")], ["b_ph"])
                tt(PH[:, 1, 0, :], POW[:, 1, 15, :], S["e8"][:], ALU.mult, ["b_pow", n("e8"), "b_ph"], ["b_ph"])
                t.op(V, lambda e: e.tensor_scalar(out=PH[64:128, 1, 0, :], in0=PH[64:128, 1, 0, :], scalar1=-1.0, scalar2=None,
                                                  op0=ALU.mult, op1=ALU.bypass), reads=["b_ph"], writes=["b_ph"])
                for L in range(9):
                    self.cmul(V, PH[:, 0, L + 1, :], PH[:, 1, L + 1, :], PH[:, 0, L, :], PH[:, 1, L, :],
                              PH[:, 0, L, :], PH[:, 1, L, :], T1[:, 0:64], T2[:, 0:64], ["b_ph"], ["b_ph"])
                PA = sg("g_pa", [128, 4, 2, 8, 64])
                kmap = {0: (lambda j: 7 - j, lambda j: j), 1: (lambda j: j + 1, lambda j: 8 - j),
                        2: (lambda j: -j, lambda j: j), 3: (lambda j: j, lambda j: -j)}
                ci = 0
                for kind in range(4):
                    for j in range(8):
                        for half, (p0, p1) in enumerate([(0, 64), (64, 128)]):
                            kk = kmap[kind][half](j) + 7
                            eng = ['act', 'pool'][ci % 2]
                            ci += 1
                            if eng == 'act':
                                t.op('act', lambda e, kind=kind, j=j, p0=p0, p1=p1, kk=kk: e.copy(out=PA[p0:p1, kind, :, j, :], in_=POW[p0:p1, :, kk, :]),
                                     reads=["b_pow"], writes=["g_pa%d_%d_%d" % (kind, j, half)])
                            else:
                                t.op('pool', lambda e, kind=kind, j=j, p0=p0, p1=p1, kk=kk: e.tensor_copy(out=PA[p0:p1, kind, :, j, :], in_=POW[p0:p1, :, kk, :]),
                                     reads=["b_pow"], writes=["g_pa%d_%d_%d" % (kind, j, half)])
                pa_keys = ["g_pa%d_%d_%d" % (kind, j, half) for kind in range(4) for j in range(8) for half in range(2)]
                TB = sg("g_tb", [128, 4, 2, 1024])
                Ysn = sg("g_ysn", [128, 1024])
                for gb in range(8):
                    g0 = gb * 8
                    for kind in range(4):
                        src_r, src_i = (BBr, BBi) if kind in (0, 2) else (Cre, Cim)
                        par = bc(PA[:, kind, 0, :, g0:g0 + 8].rearrange("p j g -> p g j").unsqueeze(3), [128, 8, 8, 16])
                        pai = bc(PA[:, kind, 1, :, g0:g0 + 8].rearrange("p j g -> p g j").unsqueeze(3), [128, 8, 8, 16])
                        br = bc(src_r[:, g0:g0 + 8, :].unsqueeze(2), [128, 8, 8, 16])
                        bi = bc(src_i[:, g0:g0 + 8, :].unsqueeze(2), [128, 8, 8, 16])
                        outr = TB[:, kind, 0, :].rearrange("p (g j c) -> p g j c", g=8, j=8)
                        outi = TB[:, kind, 1, :].rearrange("p (g j c) -> p g j c", g=8, j=8)
                        t1 = T1[:].rearrange("p (g j c) -> p g j c", g=8, j=8)
                        t2 = T2[:].rearrange("p (g j c) -> p g j c", g=8, j=8)
                        self.cmul(V, outr, outi, par, pai, br, bi, t1, t2, pa_keys + [src_r.name, src_i.name], ["g_tb%d" % kind])
                    t.op(V, lambda e: e.tensor_scalar(out=Ysn[:], in0=TB[:, 2, 1, :], scalar1=-1.0, scalar2=None, op0=ALU.mult, op1=ALU.bypass),
                         reads=["g_tb2"], writes=["g_ysn"])
                    for gi in range(8):
                        g = g0 + gi
                        sl = slice(gi * 128, (gi + 1) * 128)
                        for ri in range(2):
                            bank = 4 + ri
                            t.op('pe', lambda e, ri=ri, sl=sl, bank=bank: e.transpose(self.ps[bank][:, 0:128], TB[:, 0, ri, sl], self.ident[:]),
                                 reads=["g_tb0", "ident"], writes=["ps%d" % bank], skip_self=True)
                            t.op('act', lambda e, ri=ri, g=g, bank=bank: e.copy(out=MATS[:, g, ri, :], in_=self.ps[bank][:, 0:128]),
                                 reads=["ps%d" % bank], writes=["b_mats"])
                        t.op('pool', lambda e, g=g, sl=sl: e.tensor_copy(out=MATS[:, g, 3, :], in_=TB[:, 1, 0, sl]), reads=["g_tb1"], writes=["b_mats"])
                        t.op('pool', lambda e, g=g, sl=sl: e.tensor_scalar(out=MATS[:, g, 4, :], in0=TB[:, 1, 1, sl], scalar1=-1.0, scalar2=None,
                                                                          op0=ALU.mult, op1=ALU.bypass), reads=["g_tb1"], writes=["b_mats"])
                        for half, (p0, p1) in enumerate([(0, 64), (64, 128)]):
                            bank = 6 + half
                            t.op('pe', lambda e, p0=p0, p1=p1, sl=sl, bank=bank: e.matmul(self.ps[bank][:, 0:128], TB[p0:p1, 2, 0, sl], TB[p0:p1, 3, 0, sl], start=True, stop=False),
                                 reads=["g_tb2", "g_tb3"], writes=["ps%d" % bank], skip_self=True)
                            t.op('pe', lambda e, p0=p0, p1=p1, sl=sl, bank=bank: e.matmul(self.ps[bank][:, 0:128], Ysn[p0:p1, sl], TB[p0:p1, 3, 1, sl], start=False, stop=True),
                                 reads=["g_ysn", "g_tb3"], writes=["ps%d" % bank], skip_self=True)
                        t.op(V, lambda e: e.tensor_tensor(out=T1[:, 0:128], in0=self.ps[6][:, 0:128], in1=maskf[:], op=ALU.mult),
                             reads=["ps6", maskf.name], writes=["cm_t1"])
                        t.op(V, lambda e: e.tensor_tensor(out=T2[:, 0:128], in0=self.ps[7][:, 0:128], in1=maskb[:], op=ALU.mult),
                             reads=["ps7", maskb.name], writes=["cm_t2"])
                        t.op(V, lambda e, g=g: e.tensor_tensor(out=MATS[:, g, 2, :], in0=T1[:, 0:128], in1=T2[:, 0:128], op=ALU.add),
                             reads=["cm_t1", "cm_t2"], writes=["b_mats"])
                t.barrier()
            Up = sb("b_up", [128, 2, 1024], BF16)
            TAB = sb("b_tab", [128, 2, 1024])
            RM = sb("b_rm", [128, 1024])
            W = [sb("b_w%d" % i, [128, 1024]) for i in range(6)]
            Gs = [sb("b_g%d" % i, [128, 1024]) for i in range(2)]
            Hu = [sb("b_h%d" % i, [128, 1024]) for i in range(2)]
            Hs = sb("b_hs", [128, 2, 1024], BF16)
            yd = sb("b_yd", [128, 1024])
            hp = sb("b_hp", [128, 2, 1024], BF16)
            t.op('pool', lambda e: e.memset(Hs[:], 0.0), writes=["b_hs"])
            V = 'dve'
            for g in range(64):
                ub = g % 2
                uk = "b_up%d" % ub
                t.dma(Up[:, ub, :], self.UpD[g, :, :], reads=["UpD"], writes=[uk])
                t.op('act', lambda e, g=g: e.activation(out=RM[:], in_=segm[:], func=AF.Copy, scale=RHO[:, g:g + 1]),
                     reads=["b_segm", "b_rho"], writes=["b_rm"])
                t.op(V, lambda e: e.memset(TAB[:, 0, 0:1], 1.0), writes=["b_tab"])
                t.op(V, lambda e: e.memset(TAB[:, 1, 0:1], 0.0), reads=["b_tab"], writes=["b_tab"])
                for L in range(9):
                    n0 = 1 << L
                    cr = PH[:, 0, L, g:g + 1]
                    ci_ = PH[:, 1, L, g:g + 1]
                    src_r = TAB[:, 0, 0:n0]; src_i = TAB[:, 1, 0:n0]
                    dst_r = TAB[:, 0, n0:2 * n0]; dst_i = TAB[:, 1, n0:2 * n0]
                    t.op(V, lambda e, src_i=src_i, ci_=ci_, n0=n0: e.tensor_scalar(out=W[0][:, 0:n0], in0=src_i, scalar1=ci_, scalar2=None, op0=ALU.mult, op1=ALU.bypass),
                         reads=["b_tab", "b_ph"], writes=["b_w0"])
                    t.op(V, lambda e, src_r=src_r, cr=cr, n0=n0, dst_r=dst_r: e.scalar_tensor_tensor(out=dst_r, in0=src_r, scalar=cr, in1=W[0][:, 0:n0], op0=ALU.mult, op1=ALU.subtract),
                         reads=["b_tab", "b_ph", "b_w0"], writes=["b_tab"])
                    t.op(V, lambda e, src_r=src_r, ci_=ci_, n0=n0: e.tensor_scalar(out=W[1][:, 0:n0], in0=src_r, scalar1=ci_, scalar2=None, op0=ALU.mult, op1=ALU.bypass),
                         reads=["b_tab", "b_ph"], writes=["b_w1"])
                    t.op(V, lambda e, src_i=src_i, cr=cr, n0=n0, dst_i=dst_i: e.scalar_tensor_tensor(out=dst_i, in0=src_i, scalar=cr, in1=W[1][:, 0:n0], op0=ALU.mult, op1=ALU.add),
                         reads=["b_tab", "b_ph", "b_w1"], writes=["b_tab"])
                for ri in range(2):
                    t.op('act', lambda e, ri=ri: e.copy(out=TAB[:, ri, 512:768], in_=TAB[:, ri, 0:256]), reads=["b_tab"], writes=["b_tab"])
                    t.op('act', lambda e, ri=ri: e.copy(out=TAB[:, ri, 768:1024], in_=TAB[:, ri, 0:256]), reads=["b_tab"], writes=["b_tab"])
                for ri in range(2):
                    for h in range(2):
                        bank = ri * 2 + h
                        t.op('pe', lambda e, ri=ri, h=h, bank=bank, g=g, ub=ub: e.matmul(self.ps[bank][:], MATS[:, g, ri, :], Up[:, ub, h * 512:(h + 1) * 512], start=True, stop=True),
                             reads=["b_mats", uk], writes=["ps%d" % bank], skip_self=True)
                cosT = TAB[:, 0, :]; sinT = TAB[:, 1, :]
                for h in range(2):
                    cs = slice(h * 512, (h + 1) * 512)
                    t.op(V, lambda e, h=h, cs=cs: e.tensor_tensor(out=W[0][:, cs], in0=self.ps[h][:], in1=cosT[:, cs], op=ALU.mult), reads=["ps%d" % h, "b_tab"], writes=["b_w0"])
                    t.op(V, lambda e, h=h, cs=cs: e.tensor_tensor(out=W[1][:, cs], in0=self.ps[2 + h][:], in1=sinT[:, cs], op=ALU.mult), reads=["ps%d" % (2 + h), "b_tab"], writes=["b_w1"])
                    t.op(V, lambda e, h=h, cs=cs: e.tensor_tensor(out=W[2][:, cs], in0=self.ps[2 + h][:], in1=cosT[:, cs], op=ALU.mult), reads=["ps%d" % (2 + h), "b_tab"], writes=["b_w2"])
                    t.op(V, lambda e, h=h, cs=cs: e.tensor_tensor(out=W[3][:, cs], in0=self.ps[h][:], in1=sinT[:, cs], op=ALU.mult), reads=["ps%d" % h, "b_tab"], writes=["b_w3"])
                t.op('pool', lambda e: e.tensor_tensor(out=W[4][:], in0=W[0][:], in1=W[1][:], op=ALU.add), reads=["b_w0", "b_w1"], writes=["b_w4"])
                t.op('pool', lambda e: e.tensor_tensor(out=W[5][:], in0=W[2][:], in1=W[3][:], op=ALU.subtract), reads=["b_w2", "b_w3"], writes=["b_w5"])
                for ri in range(2):
                    src = W[4 + ri]
                    t.op(V, lambda e, ri=ri, src=src: e.tensor_tensor_scan(out=Gs[ri][0:64, :], data0=RM[0:64, :], data1=src[0:64, :], initial=0.0, op0=ALU.mult, op1=ALU.add),
                         reads=["b_rm", src.name], writes=["b_g%d_f" % ri])
                    t.op(V, lambda e, ri=ri, src=src: e.tensor_tensor_scan(out=Gs[ri][64:128, ::-1], data0=RM[64:128, ::-1], data1=src[64:128, ::-1], initial=0.0, op0=ALU.mult, op1=ALU.add),
                         reads=["b_rm", src.name], writes=["b_g%d_b" % ri])
                gk = ["b_g0_f", "b_g0_b", "b_g1_f", "b_g1_b"]
                t.op(V, lambda e: e.tensor_tensor(out=W[0][:], in0=Gs[0][:], in1=cosT, op=ALU.mult), reads=gk + ["b_tab"], writes=["b_w0"])
                t.op('pool', lambda e: e.tensor_tensor(out=W[1][:], in0=Gs[1][:], in1=sinT, op=ALU.mult), reads=gk + ["b_tab"], writes=["b_w1"])
                t.op(V, lambda e: e.tensor_tensor(out=W[2][:], in0=Gs[1][:], in1=cosT, op=ALU.mult), reads=gk + ["b_tab"], writes=["b_w2"])
                t.op('pool', lambda e: e.tensor_tensor(out=W[3][:], in0=Gs[0][:], in1=sinT, op=ALU.mult), reads=gk + ["b_tab"], writes=["b_w3"])
                t.op(V, lambda e: e.tensor_tensor(out=Hu[0][:], in0=W[0][:], in1=W[1][:], op=ALU.subtract), reads=["b_w0", "b_w1"], writes=["b_h0"])
                t.op('pool', lambda e: e.tensor_tensor(out=Hu[1][:], in0=W[2][:], in1=W[3][:], op=ALU.add), reads=["b_w2", "b_w3"], writes=["b_h1"])
                for ri in range(2):
                    t.op(V, lambda e, ri=ri: e.tensor_tensor(out=Hs[0:64, ri, 1:1024], in0=Hu[ri][0:64, 0:1023], in1=segm[0:64, 1:1024], op=ALU.mult),
                         reads=["b_h%d" % ri, "b_segm"], writes=["b_hs"])
                    t.op('pool', lambda e, ri=ri: e.tensor_tensor(out=Hs[64:128, ri, 0:1023], in0=Hu[ri][64:128, 1:1024], in1=segm[64:128, 0:1023], op=ALU.mult),
                         reads=["b_h%d" % ri, "b_segm"], writes=["b_hs"])
                for h in range(2):
                    bank = 4 + h
                    cs = slice(h * 512, (h + 1) * 512)
                    t.op('pe', lambda e, g=g, ub=ub, cs=cs, bank=bank: e.matmul(self.ps[bank][:], MATS[:, g, 2, :], Up[:, ub, cs], start=True, stop=False),
                         reads=["b_mats", uk], writes=["ps%d" % bank], skip_self=True)
                    t.op('pe', lambda e, g=g, cs=cs, bank=bank: e.matmul(self.ps[bank][:], MATS[:, g, 3, :], Hs[:, 0, cs], start=False, stop=False),
                         reads=["b_mats", "b_hs"], writes=["ps%d" % bank], skip_self=True)
                    t.op('pe', lambda e, g=g, cs=cs, bank=bank: e.matmul(self.ps[bank][:], MATS[:, g, 4, :], Hs[:, 1, cs], start=False, stop=True),
                         reads=["b_mats", "b_hs"], writes=["ps%d" % bank], skip_self=True)
                    t.op(V, lambda e, g=g, ub=ub, cs=cs, bank=bank: e.scalar_tensor_tensor(out=yd[:, cs], in0=Up[:, ub, cs], scalar=dpk[:, g:g + 1], in1=self.ps[bank][:],
                                                                                         op0=ALU.mult, op1=ALU.add),
                         reads=[uk, "b_dpk", "ps%d" % bank], writes=["b_yd"])
                t.op('act', lambda e, ub=ub: e.activation(out=hp[:, ub, :], in_=yd[:], func=AF.Gelu_apprx_tanh), reads=["b_yd"], writes=["b_hp%d" % ub])
                t.dma(self.HpD[g, :, :], hp[:, ub, :], reads=["b_hp%d" % ub], writes=["HpD"])
            t.barrier()

    def phase_s5c(self):
        nc, t = self.nc, self.t
        with ExitStack() as ph:
            sb = lambda name, shape, dt=F32: ph.enter_context(self.sbuf(name, list(shape), dt))
            self.mk_eps(ph)
            wg = self.load_w_bf16(ph, "s5wglu", self.w_s5_glu, 8, 2 * D, eng_cast='mix')
            hpt = sb("c_hpt", [128, 64, 128], BF16)
            H8 = sb("c_H8", [128, 8, 1024], BF16)
            hT = sb("c_hT", [128, 8, 1024], BF16)
            xf = sb("c_xf", [128, 8, 512])
            z = sb("c_z", [128, 8, 512])
            sg_ = sb("c_sg", [128, 512])
            tmp = {'zc': sb("c_zc", [128, 8, 512]), 'sq': sb("c_sq", [128, 8, 512]), 'sd': sb("c_sd", [128, 512])}
            xo = sb("c_xo", [128, 8, 512])
            xTv = self.xT.rearrange("(k p) n -> p k n", p=128)
            X1v = self.X1.rearrange("(k p) n -> p k n", p=128)
            for tile in range(8):
                t.dma(hpt[:], self.HpD[:, :, tile * 128:(tile + 1) * 128].rearrange("g p c -> p g c"), reads=["HpD"], writes=["c_hpt"])
                for gb in range(8):
                    bank = gb % 2
                    pk = "ps%d" % bank
                    pb = self.ps[bank][:].bitcast(BF16)
                    for gi in range(8):
                        g = gb * 8 + gi
                        t.op('pe', lambda e, g=g, gi=gi, pb=pb: e.transpose(pb[:, gi * 128:(gi + 1) * 128], hpt[:, g, :], self.identb[:]),
                             reads=["c_hpt", "identb"], writes=[pk], skip_self=True)
                    src = pb.rearrange("p (g t c) -> p g t c", g=8, t=8)
                    dst = H8[:, :, gb * 128:(gb + 1) * 128].rearrange("p t (g c) -> p g t c", g=8)
                    if gb % 2 == 0:
                        t.op('dve', lambda e, src=src, dst=dst: e.tensor_copy(out=dst, in_=src), reads=[pk], writes=["c_H8"])
                    else:
                        t.op('act', lambda e, src=src, dst=dst: e.copy(out=dst, in_=src), reads=[pk], writes=["c_H8"])
                for k in range(8):
                    bank = 2 + k % 2
                    pk = "ps%d" % bank
                    pb = self.ps[bank][:].bitcast(BF16)
                    for tt_ in range(8):
                        t.op('pe', lambda e, k=k, tt_=tt_, pb=pb: e.transpose(pb[:, tt_ * 128:(tt_ + 1) * 128], H8[:, tt_, k * 128:(k + 1) * 128], self.identb[:]),
                             reads=["c_H8", "identb"], writes=[pk], skip_self=True)
                    src = pb.rearrange("p (t c) -> p t c", t=8)
                    dst = hT[:, k, :].rearrange("p (c t) -> p t c", t=8)
                    if k % 2 == 0:
                        t.op('dve', lambda e, src=src, dst=dst: e.tensor_copy(out=dst, in_=src), reads=[pk], writes=["c_hT"])
                    else:
                        t.op('act', lambda e, src=src, dst=dst: e.copy(out=dst, in_=src), reads=[pk], writes=["c_hT"])
                for th in range(2):
                    tok0 = tile * 1024 + th * 512
                    t.dma(xf[:], xTv[:, :, tok0:tok0 + 512], writes=["c_xf"])
                    for fo in range(8):
                        bv, bg = 4 + (fo % 2) * 2, 5 + (fo % 2) * 2
                        for k in range(8):
                            t.op('pe', lambda e, k=k, fo=fo, th=th, bv=bv: e.matmul(self.ps[bv][:], wg[:, k, fo * 128:(fo + 1) * 128], hT[:, k, th * 512:(th + 1) * 512],
                                                                              start=(k == 0), stop=(k == 7)), reads=["s5wglu", "c_hT"], writes=["ps%d" % bv], skip_self=True)
                        for k in range(8):
                            t.op('pe', lambda e, k=k, fo=fo, th=th, bg=bg: e.matmul(self.ps[bg][:], wg[:, k, D + fo * 128:D + (fo + 1) * 128], hT[:, k, th * 512:(th + 1) * 512],
                                                                              start=(k == 0), stop=(k == 7)), reads=["s5wglu", "c_hT"], writes=["ps%d" % bg], skip_self=True)
                        t.op('act', lambda e, bg=bg: e.activation(out=sg_[:], in_=self.ps[bg][:], func=AF.Sigmoid), reads=["ps%d" % bg], writes=["c_sg"])
                        t.op('dve', lambda e, bv=bv: e.tensor_tensor(out=sg_[:], in0=self.ps[bv][:], in1=sg_[:], op=ALU.mult), reads=["ps%d" % bv, "c_sg"], writes=["c_sg"])
                        t.op('dve', lambda e, fo=fo: e.scalar_tensor_tensor(out=z[:, fo, :], in0=xf[:, fo, :], scalar=ALPHA, in1=sg_[:], op0=ALU.mult, op1=ALU.add),
                             reads=["c_xf", "c_sg"], writes=["c_z"])
                    self.ln_block(z, "c_z", 0, 0, xo, "c_xo", 512, tmp, 0, 1)
                    t.dma(X1v[:, :, tok0:tok0 + 512], xo[:], reads=["c_xo"], writes=["X1"])
            t.barrier()

    def phase_peer(self, layer, xin, xout):
        sub = getattr(self, "peer_sub", ("p1", "p2", "p3"))
        if "p1" in sub:
            self.peer_p1(layer, xin)
        if "p2" in sub:
            self.peer_p2(layer)
        if "p3" in sub:
            self.peer_p3(layer, xin, xout)

    def peer_p1(self, layer, xin):
        nc, t = self.nc, self.t
        with ExitStack() as ph:
            sb = lambda name, shape, dt=F32: ph.enter_context(self.sbuf(name, list(shape), dt))
            wq = self.load_w_bf16(ph, "p1_wq", self.w_peer_q[layer], 8, 2 * D, eng_cast='mix')
            xf = sb("p1_xf", [128, 2, 8, 512])
            xb = sb("p1_xb", [128, 2, 8, 512], BF16)
            qb = sb("p1_qb", [128, 2, 16, 512], BF16)
            xv = xin.rearrange("(k p) n -> p k n", p=128)
            XBv = self.XB.rearrange("(k p) n -> p k n", p=128)
            QTv = self.QT.rearrange("(k p) n -> p k n", p=128)
            for blk in range(16):
                b = blk % 2
                tok0 = blk * 512
                t.dma(xf[:, b], xv[:, :, tok0:tok0 + 512], reads=["X1"], writes=["p1_xf%d" % b])
                t.op('pool', lambda e, b=b: e.tensor_copy(out=xb[:, b], in_=xf[:, b]), reads=["p1_xf%d" % b], writes=["p1_xb%d" % b])
                t.dma(XBv[:, :, tok0:tok0 + 512], xb[:, b], reads=["p1_xb%d" % b], writes=["XB_%d" % blk])
                for fo in range(16):
                    bank = fo % 4
                    for k in range(8):
                        t.op('pe', lambda e, k=k, fo=fo, b=b, bank=bank: e.matmul(self.ps[bank][:], wq[:, k, fo * 128:(fo + 1) * 128], xb[:, b, k, :],
                                                                             start=(k == 0), stop=(k == 7)),
                             reads=["p1_wq", "p1_xb%d" % b], writes=["ps%d" % bank], skip_self=True)
                    if fo % 2 == 0:
                        t.op('act', lambda e, fo=fo, b=b, bank=bank: e.copy(out=qb[:, b, fo, :], in_=self.ps[bank][:]), reads=["ps%d" % bank], writes=["p1_qb%d" % b])
                    else:
                        t.op('dve', lambda e, fo=fo, b=b, bank=bank: e.tensor_copy(out=qb[:, b, fo, :], in_=self.ps[bank][:]), reads=["ps%d" % bank], writes=["p1_qb%d" % b])
                t.dma(QTv[:, :, tok0:tok0 + 512], qb[:, b], reads=["p1_qb%d" % b], writes=["QT_%d" % blk])
            t.barrier()

    def peer_p2(self, layer):
        nc, t = self.nc, self.t
        NB = self.peer_nblk if hasattr(self, "peer_nblk") else 32
        with ExitStack() as ph:
            sb = lambda name, shape, dt=F32: ph.enter_context(self.sbuf(name, list(shape), dt))
            skf = sb("p2_skf", [128, 2, 128])
            skb = sb("p2_skb", [128, 2, 128], BF16)
            t.dma(skf[:], self.peer_skT[layer].rearrange("c d n -> d c n"), writes=["p2_skf"])
            t.op('dve', lambda e: e.tensor_copy(out=skb[:], in_=skf[:]), reads=["p2_skf"], writes=["p2_skb"])
            xb = sb("p2_xb", [128, 8, 256], BF16)
            qT = sb("p2_qT", [128, 16, 256], BF16)
            sc = sb("p2_sc", [128, 16, 128])
            sc2 = sb("p2_sc2", [128, 2, 128])
            cs = sb("p2_cs", [128, 8, 256])
            cs2 = sb("p2_cs2", [128, 2, 256])
            sv = sb("p2_sv", [128, 16, 16])
            ts_ = sb("p2_ts", [128, 8, 16])
            ex = sb("p2_ex", [128, 8, 16])
            st8 = sb("p2_st8", [128, 4, 8])
            TM = sb("p2_TM", [128, 3, 128])
            SM = sb("p2_SM", [128, 3, 256])
            QR = sb("p2_qr", [128, 2, 2, 16, 128], BF16)
            Pt = sb("p2_P", [128, 12, 128], BF16)
            Et = sb("p2_E", [128, 12, 128])
            Qt = sb("p2_Q", [128, 12, 128], BF16)
            Gs = sb("p2_Gs", [128, 256, 128], BF16)
            UTs = sb("p2_UT", [128, 2, 8, 512], BF16)
            Vs = sb("p2_V", [128, 2, 4, 1024], BF16)
            ga = sb("p2_ga", [128, 2, 256])
            Hh = sb("p2_H", [128, 2, 256], BF16)
            otok = sb("p2_otok", [128, 2, 1024])
            peT = sb("p2_peT", [128, 8, 256])
            XBv = self.XB.rearrange("(k p) n -> p k n", p=128)
            QTv = self.QT.rearrange("(k p) n -> p k n", p=128)
            PEv = self.PEo.rearrange("(k p) n -> p k n", p=128)
            Ubv = self.Ub[layer].rearrange("(k p) e -> p k e", p=128)
            Vbv = self.Vb[layer].rearrange("(i p) f -> p i f", p=128)
            NEG = -1.0e30
            import os as _os2
            TBK = int(_os2.environ.get("TBK", "7"))
            def topk(blk, restricted):
                tok0 = blk * 256
                b512 = tok0 // 512
                t.dma(qT[:], QTv[:, :, tok0:tok0 + 256], reads=["QT_%d" % b512], writes=["p2_qT"])
                for st in range(2):
                    tsl = slice(st * 128, (st + 1) * 128)
                    for r in range(2):
                        for q in (2 * r, 2 * r + 1):
                            bank = [4, 7][q % 2] if restricted else q
                            for h4 in range(4):
                                hc = q * 4 + h4
                                t.op('pe', lambda e, hc=hc, h4=h4, bank=bank, tsl=tsl: e.matmul(self.ps[bank][:, h4 * 128:(h4 + 1) * 128], qT[:, hc, tsl], skb[:, hc % 2, :],
                                                                                          start=True, stop=True),
                                     reads=["p2_qT", "p2_skb"], writes=["ps%d" % bank], skip_self=True)
                        for q in (2 * r, 2 * r + 1):
                            bank = [4, 7][q % 2] if restricted else q
                            t.op('act', lambda e, q=q, bank=bank: e.copy(out=sc[:, q * 4:(q + 1) * 4, :], in_=self.ps[bank][:].rearrange("p (a n) -> p a n", a=4)),
                                 reads=["ps%d" % bank], writes=["p2_sc%d" % q])
                        yield
                    for hc in range(16):
                        kk = "p2_sc%d" % (hc // 4)
                        t.op('dve', lambda e, hc=hc: e.max(out=sv[:, hc, 0:8], in_=sc[:, hc, :]), reads=[kk], writes=["p2_sv%d" % hc])
                        t.op('dve', lambda e, hc=hc: e.match_replace(out=sc2[:, hc % 2, :], in_to_replace=sv[:, hc, 0:8], in_values=sc[:, hc, :], imm_value=NEG),
                             reads=[kk, "p2_sv%d" % hc], writes=["p2_sc2_%d" % (hc % 2)])
                        t.op('dve', lambda e, hc=hc: e.max(out=sv[:, hc, 8:16], in_=sc2[:, hc % 2, :]), reads=["p2_sc2_%d" % (hc % 2)], writes=["p2_sv%d" % hc])
                        yield
                    svk = ["p2_sv%d" % hc for hc in range(16)]
                    sv4 = sv[:].rearrange("p (h c) a -> p h c a", c=2)
                    t.op('dve', lambda e: e.tensor_tensor(out=cs[:].rearrange("p h (a b) -> p h a b", a=16),
                                                          in0=bc(sv4[:, :, 0, :].unsqueeze(3), [128, 8, 16, 16]),
                                                          in1=bc(sv4[:, :, 1, :].unsqueeze(2), [128, 8, 16, 16]), op=ALU.add),
                         reads=svk, writes=["p2_cs"])
                    for h in range(8):
                        t.op('dve', lambda e, h=h: e.max(out=ts_[:, h, 0:8], in_=cs[:, h, :]), reads=["p2_cs"], writes=["p2_ts%d" % h])
                        t.op('dve', lambda e, h=h: e.match_replace(out=cs2[:, h % 2, :], in_to_replace=ts_[:, h, 0:8], in_values=cs[:, h, :], imm_value=NEG),
                             reads=["p2_cs", "p2_ts%d" % h], writes=["p2_cs2_%d" % (h % 2)])
                        t.op('dve', lambda e, h=h: e.max(out=ts_[:, h, 8:16], in_=cs2[:, h % 2, :]), reads=["p2_cs2_%d" % (h % 2)], writes=["p2_ts%d" % h])
                        yield
                    tsk = ["p2_ts%d" % h for h in range(8)]
                    t.op('dve', lambda e: e.tensor_tensor(out=ex[:], in0=ts_[:], in1=bc(ts_[:, :, 0:1], [128, 8, 16]), op=ALU.subtract), reads=tsk, writes=["p2_ex"])
                    t.op('act', lambda e: e.activation(out=ex[:], in_=ex[:], func=AF.Exp), reads=["p2_ex"], writes=["p2_ex"])
                    t.op('dve', lambda e: e.tensor_reduce(out=st8[:, 0, :], in_=ex[:], axis=mybir.AxisListType.X, op=ALU.add), reads=["p2_ex"], writes=["p2_st8"])
                    t.op('act', lambda e: e.activation(out=st8[:, 1, :], in_=st8[:, 0, :], func=AF.Ln), reads=["p2_st8"], writes=["p2_st8"])
                    t.op('dve', lambda e: e.tensor_tensor(out=st8[:, 2, :], in0=st8[:, 1, :], in1=ts_[:, :, 0], op=ALU.add), reads=["p2_st8"] + tsk, writes=["p2_st8"])
                    TMv = TM[:].rearrange("p j (h a) -> p j h a", h=8)
                    t.op('pool', lambda e: e.tensor_copy(out=TMv[:, 0], in_=sv4[:, :, 0, :]), reads=svk, writes=["p2_TM0"])
                    t.op('dve', lambda e: e.scalar_tensor_tensor(out=st8[:, 3, :], in0=ts_[:, :, 15], scalar=-1.0e-5, in1=st8[:, 2, :], op0=ALU.add, op1=ALU.subtract),
                         reads=tsk + ["p2_st8"], writes=["p2_st8"])
                    t.op('act', lambda e: e.activation(out=st8[:, 3, :], in_=st8[:, 3, :], func=AF.Exp), reads=["p2_st8"], writes=["p2_st8"])
                    t.op('dve', lambda e: e.tensor_copy(out=TMv[:, 1], in_=bc(st8[:, 3, :].unsqueeze(2), [128, 8, 16])), reads=["p2_st8"], writes=["p2_TM1"])
                    t.op('dve', lambda e: e.tensor_tensor(out=TMv[:, 2], in0=sv4[:, :, 0, :], in1=bc(st8[:, 2, :].unsqueeze(2), [128, 8, 16]), op=ALU.subtract),
                         reads=svk + ["p2_st8"], writes=["p2_TM2"])
                    for j in range(3):
                        t.op('pe', lambda e, j=j: e.transpose(self.ps[TBK][:, j * 128:(j + 1) * 128], TM[:, j, :], self.ident[:]),
                             reads=["p2_TM%d" % j, "ident"], writes=["ps%d" % TBK], skip_self=True)
                    t.op('act', lambda e, tsl=tsl: e.copy(out=SM[:, :, tsl], in_=self.ps[TBK][:, 0:384].rearrange("p (j n) -> p j n", j=3)),
                         reads=["ps%d" % TBK], writes=["p2_SM%d" % st])
                    yield
            for blk in range(NB):
                tok0 = blk * 256
                b512 = tok0 // 512
                t.dma(xb[:], XBv[:, :, tok0:tok0 + 256], reads=["XB_%d" % b512], writes=["p2_xb"])
                if blk == 0:
                    for _ in topk(0, False):
                        pass
                tk_gen = None
                import os as _os
                _old = _os.environ.get("BANKMAP") == "old"
                B0 = lambda pg: [4, 5, 2][pg]
                B1 = lambda pg: [6, 7, 3][pg]
                GB = lambda pg: pg % 2
                TB = 4 if _old else 7

                def fill_qr(c16):
                    qb_ = c16 % 2
                    for c in range(2):
                        src = bc(qT[:, c::2, c16 * 16:(c16 + 1) * 16].rearrange("p h t -> p t h").unsqueeze(3), [128, 16, 8, 16])
                        dst = QR[:, qb_, c].rearrange("p t (h a) -> p t h a", h=8)
                        t.op('act', lambda e, src=src, dst=dst: e.copy(out=dst, in_=src), reads=["p2_qT"], writes=["p2_qr%d_%d" % (qb_, c)])

                def stageA(g):
                    if g % 4 == 0:
                        fill_qr(g // 4)
                    qb_ = (g // 4) % 2
                    pg = g % 3
                    for s4 in range(4):
                        tl = (g % 4) * 4 + s4
                        t.op('pe', lambda e, tl=tl, s4=s4, qb_=qb_, pg=pg: e.matmul(self.ps[B0(pg)][:, s4 * 128:(s4 + 1) * 128], QR[:, qb_, 0, tl, :], skb[:, 0, :], start=True, stop=True),
                             reads=["p2_qr%d_0" % qb_, "p2_skb"], writes=["ps%d" % B0(pg)], skip_self=True)
                        t.op('pe', lambda e, tl=tl, s4=s4, qb_=qb_, pg=pg: e.matmul(self.ps[B1(pg)][:, s4 * 128:(s4 + 1) * 128], QR[:, qb_, 1, tl, :], skb[:, 1, :], start=True, stop=True),
                             reads=["p2_qr%d_1" % qb_, "p2_skb"], writes=["ps%d" % B1(pg)], skip_self=True)

                def stageB(g):
                    pg = g % 3
                    k0 = "ps%d" % B0(pg)
                    k1 = "ps%d" % B1(pg)
                    for s4 in range(4):
                        tt_ = g * 4 + s4
                        sl = pg * 4 + s4
                        smk = "p2_SM%d" % (tt_ // 128)
                        t.op('dve', lambda e, s4=s4, tt_=tt_, sl=sl, pg=pg: e.tensor_scalar(out=Pt[:, sl, :], in0=self.ps[B0(pg)][:, s4 * 128:(s4 + 1) * 128], scalar1=SM[:, 0, tt_:tt_ + 1], scalar2=None,
                                                                                       op0=ALU.is_equal, op1=ALU.bypass),
                             reads=[k0, smk], writes=["p2_P%d" % sl])
                    for s4 in range(4):
                        tt_ = g * 4 + s4
                        sl = pg * 4 + s4
                        smk = "p2_SM%d" % (tt_ // 128)
                        t.op('act', lambda e, s4=s4, tt_=tt_, sl=sl, pg=pg: e.activation(out=Et[:, sl, :], in_=self.ps[B1(pg)][:, s4 * 128:(s4 + 1) * 128], func=AF.Exp, bias=SM[:, 2, tt_:tt_ + 1], scale=1.0),
                             reads=[k1, smk], writes=["p2_E%d" % sl])
                    for s4 in range(4):
                        tt_ = g * 4 + s4
                        sl = pg * 4 + s4
                        smk = "p2_SM%d" % (tt_ // 128)
                        t.op('dve', lambda e, s4=s4, tt_=tt_, sl=sl, pg=pg: e.scalar_tensor_tensor(out=Qt[:, sl, :], in0=Et[:, sl, :], scalar=SM[:, 1, tt_:tt_ + 1],
                                                                                              in1=Et[:, sl, :], op0=ALU.is_ge, op1=ALU.mult),
                             reads=[smk, "p2_E%d" % sl], writes=["p2_Q%d" % sl])

                def stageC(g):
                    pg = g % 3
                    for s4 in range(4):
                        sl = pg * 4 + s4
                        t.op('pe', lambda e, s4=s4, sl=sl, pg=pg, g=g: e.matmul(self.ps[g % 2][:, s4 * 128:(s4 + 1) * 128], Qt[:, sl, :], Pt[:, sl, :], start=True, stop=True),
                             reads=["p2_Q%d" % sl, "p2_P%d" % sl], writes=["ps%d" % (g % 2)], skip_self=True)
                    t4 = g * 4
                    t.op('act', lambda e, t4=t4, pg=pg, g=g: e.copy(out=Gs[:, t4:t4 + 4, :], in_=self.ps[g % 2][:].rearrange("p (t i) -> p t i", t=4)),
                         reads=["ps%d" % (g % 2)], writes=["p2_Gs"])

                opt_tok = getattr(self, "opt_tok", True)
                opt_dense = getattr(self, "opt_dense", True)
                if opt_tok:
                    stageA(0)
                    stageA(1)
                for g in range(64):
                    if opt_tok:
                        if g + 2 < 64:
                            stageA(g + 2)
                    else:
                        stageA(g)
                    stageB(g)
                    if opt_tok:
                        if g >= 1:
                            stageC(g - 1)
                    else:
                        stageC(g)
                if opt_tok:
                    stageC(63)
                def load_w(ib):
                    wbuf = ib % 2
                    e0 = ib * 512
                    ukeys = ["Ub%d_%d_%d" % (layer, r0, e0 // 2048) for r0 in range(8)]
                    vkeys = ["Vb%d_%d" % (layer, (ib * 512) // 256 + x) for x in range(2)]
                    t.dma(UTs[:, wbuf], Ubv[:, :, e0:e0 + 512], reads=ukeys, writes=["p2_UT%d" % wbuf])
                    t.dma(Vs[:, wbuf], Vbv[:, ib * 4:(ib + 1) * 4, :], reads=vkeys, writes=["p2_V%d" % wbuf])

                def act_mm(i):
                    ib, ii = i // 4, i % 4
                    wbuf = ib % 2
                    abank = 5 + i % 2
                    for k in range(8):
                        t.op('pe', lambda e, k=k, ii=ii, wbuf=wbuf, abank=abank: e.matmul(self.ps[abank][:, 0:256], UTs[:, wbuf, k, ii * 128:(ii + 1) * 128], xb[:, k, :],
                                                                                    start=(k == 0), stop=(k == 7)),
                             reads=["p2_UT%d" % wbuf, "p2_xb"], writes=["ps%d" % abank], skip_self=True)

                load_w(0)
                if opt_dense:
                    act_mm(0)
                for i in range(128):
                    ib, ii = i // 4, i % 4
                    wbuf = ib % 2
                    ab = i % 2
                    abank = 5 + ab
                    if ii == 0 and ib + 1 < 32:
                        load_w(ib + 1)
                    if i == 4 and blk + 1 < NB:
                        tk_gen = topk(blk + 1, True)
                    if tk_gen is not None:
                        next(tk_gen, None)
                    if opt_dense:
                        if i + 1 < 128:
                            act_mm(i + 1)
                    else:
                        act_mm(i)
                    t.op('act', lambda e, ab=ab, abank=abank: e.activation(out=ga[:, ab, :], in_=self.ps[abank][:, 0:256], func=AF.Gelu_apprx_tanh),
                         reads=["ps%d" % abank], writes=["p2_ga%d" % ab])
                    t.op('dve', lambda e, ab=ab, i=i: e.tensor_tensor(out=Hh[:, ab, :], in0=ga[:, ab, :], in1=Gs[:, :, i], op=ALU.mult),
                         reads=["p2_ga%d" % ab, "p2_Gs"], writes=["p2_H%d" % ab])
                    for st in range(2):
                        for fh in range(2):
                            ob = st * 2 + fh
                            t.op('pe', lambda e, st=st, fh=fh, ob=ob, ab=ab, ii=ii, wbuf=wbuf, i=i: e.matmul(
                                self.ps[ob][:], Hh[:, ab, st * 128:(st + 1) * 128], Vs[:, wbuf, ii, fh * 512:(fh + 1) * 512], start=(i == 0), stop=(i == 127)),
                                reads=["p2_H%d" % ab, "p2_V%d" % wbuf], writes=["ps%d" % ob], skip_self=True)
                if tk_gen is not None:
                    for _ in tk_gen:
                        pass
                for st in range(2):
                    for fh in range(2):
                        ob = st * 2 + fh
                        if ob % 2 == 0:
                            t.op('act', lambda e, st=st, fh=fh, ob=ob: e.copy(out=otok[:, st, fh * 512:(fh + 1) * 512], in_=self.ps[ob][:]), reads=["ps%d" % ob], writes=["p2_otok%d" % st])
                        else:
                            t.op('dve', lambda e, st=st, fh=fh, ob=ob: e.tensor_copy(out=otok[:, st, fh * 512:(fh + 1) * 512], in_=self.ps[ob][:]), reads=["ps%d" % ob], writes=["p2_otok%d" % st])
                for st in range(2):
                    for half in range(2):
                        tb = 5 + half
                        for kk in range(4):
                            fk = half * 4 + kk
                            t.op('pe', lambda e, st=st, fk=fk, kk=kk, tb=tb: e.transpose(self.ps[tb][:, kk * 128:(kk + 1) * 128], otok[:, st, fk * 128:(fk + 1) * 128], self.ident[:]),
                                 reads=["p2_otok%d" % st, "ident"], writes=["ps%d" % tb], skip_self=True)
                        if half == 0:
                            t.op('act', lambda e, st=st, half=half, tb=tb: e.copy(out=peT[:, half * 4:(half + 1) * 4, st * 128:(st + 1) * 128],
                                                                                in_=self.ps[tb][:].rearrange("p (k n) -> p k n", k=4)),
                                 reads=["ps%d" % tb], writes=["p2_peT"])
                        else:
                            t.op('dve', lambda e, st=st, half=half, tb=tb: e.tensor_copy(out=peT[:, half * 4:(half + 1) * 4, st * 128:(st + 1) * 128],
                                                                                       in_=self.ps[tb][:].rearrange("p (k n) -> p k n", k=4)),
                                 reads=["ps%d" % tb], writes=["p2_peT"])
                t.dma(PEv[:, :, tok0:tok0 + 256], peT[:], reads=["p2_peT"], writes=["PEo_%d" % blk])
            t.barrier()

    def peer_p3(self, layer, xin, xout):
        nc, t = self.nc, self.t
        NB = (self.peer_nblk + 1) // 2 if hasattr(self, "peer_nblk") else 16
        with ExitStack() as ph:
            sb = lambda name, shape, dt=F32: ph.enter_context(self.sbuf(name, list(shape), dt))
            self.mk_eps(ph)
            wpg = self.load_w_bf16(ph, "p3_wpg", self.w_ple_gate[layer], 8, D, eng_cast='mix')
            wpp = self.load_w_bf16(ph, "p3_wpp", self.w_ple_proj[layer], 2, D, eng_cast='mix')
            xf = sb("p3_xf", [128, 8, 512])
            pe = sb("p3_pe", [128, 8, 512])
            pf = sb("p3_pf", [128, 2, 512])
            pb = sb("p3_pb", [128, 2, 512], BF16)
            z = sb("p3_z", [128, 8, 512])
            tmp = {'zc': sb("p3_zc", [128, 8, 512]), 'sq': sb("p3_sq", [128, 8, 512]), 'sd': sb("p3_sd", [128, 512])}
            x2 = sb("p3_x2", [128, 8, 512])
            x2b = sb("p3_x2b", [128, 8, 512], BF16)
            sg_ = sb("p3_sg", [128, 2, 512])
            xo = sb("p3_xo", [128, 8, 512])
            xv = xin.rearrange("(k p) n -> p k n", p=128)
            PEv = self.PEo.rearrange("(k p) n -> p k n", p=128)
            pv = self.pT[layer].rearrange("(k p) n -> p k n", p=128)
            ov = xout.rearrange("(k p) n -> p k n", p=128)
            for blk in range(NB):
                tok0 = blk * 512
                t.dma(xf[:], xv[:, :, tok0:tok0 + 512], reads=["X1"], writes=["p3_xf"])
                t.dma(pe[:], PEv[:, :, tok0:tok0 + 512], reads=["PEo_%d" % (2 * blk), "PEo_%d" % (2 * blk + 1)], writes=["p3_pe"])
                t.dma(pf[:], pv[:, :, tok0:tok0 + 512], writes=["p3_pf"])
                t.op('pool', lambda e: e.tensor_copy(out=pb[:], in_=pf[:]), reads=["p3_pf"], writes=["p3_pb"])
                t.op('dve', lambda e: e.scalar_tensor_tensor(out=z[:], in0=xf[:], scalar=ALPHA, in1=pe[:], op0=ALU.mult, op1=ALU.add),
                     reads=["p3_xf", "p3_pe"], writes=["p3_z"])
                self.ln_block(z, "p3_z", layer, 1, x2, "p3_x2", 512, tmp, 0, 1)
                t.op('pool', lambda e: e.tensor_copy(out=x2b[:], in_=x2[:]), reads=["p3_x2"], writes=["p3_x2b"])
                for fo in range(8):
                    bg, bp = 2 + (fo % 2) * 2, 3 + (fo % 2) * 2
                    sgi = fo % 2
                    for k in range(8):
                        t.op('pe', lambda e, k=k, fo=fo, bg=bg: e.matmul(self.ps[bg][:], wpg[:, k, fo * 128:(fo + 1) * 128], x2b[:, k, :], start=(k == 0), stop=(k == 7)),
                             reads=["p3_wpg", "p3_x2b"], writes=["ps%d" % bg], skip_self=True)
                    for k in range(2):
                        t.op('pe', lambda e, k=k, fo=fo, bp=bp: e.matmul(self.ps[bp][:], wpp[:, k, fo * 128:(fo + 1) * 128], pb[:, k, :], start=(k == 0), stop=(k == 1)),
                             reads=["p3_wpp", "p3_pb"], writes=["ps%d" % bp], skip_self=True)
                    t.op('act', lambda e, bg=bg, sgi=sgi: e.activation(out=sg_[:, sgi, :], in_=self.ps[bg][:], func=AF.Sigmoid), reads=["ps%d" % bg], writes=["p3_sg%d" % sgi])
                    t.op('dve', lambda e, bp=bp, sgi=sgi: e.tensor_tensor(out=sg_[:, sgi, :], in0=self.ps[bp][:], in1=sg_[:, sgi, :], op=ALU.mult),
                         reads=["ps%d" % bp, "p3_sg%d" % sgi], writes=["p3_sg%d" % sgi])
                    t.op('pool', lambda e, fo=fo, sgi=sgi: e.tensor_tensor(out=xo[:, fo, :], in0=x2[:, fo, :], in1=sg_[:, sgi, :], op=ALU.add),
                         reads=["p3_x2", "p3_sg%d" % sgi], writes=["p3_xo"])
                t.dma(ov[:, :, tok0:tok0 + 512], xo[:], reads=["p3_xo"], writes=["XOUT%d_%d" % (layer, blk)])
            t.barrier()

    def phase_rg(self):
        self.rg_p1()
        self.rg_p2()
        self.rg_p3()

    def rg_p1(self):
        nc, t = self.nc, self.t
        with ExitStack() as ph:
            sb = lambda name, shape, dt=F32: ph.enter_context(self.sbuf(name, list(shape), dt))
            win = self.load_w_bf16(ph, "r1_win", self.w_rg_in, 8, 2 * D, eng_cast='mix')
            xf = sb("r1_xf", [128, 2, 8, 512])
            xb = sb("r1_xb", [128, 2, 8, 512], BF16)
            gg = sb("r1_gg", [128, 2, 8, 512], BF16)
            rr = sb("r1_rr", [128, 2, 8, 512])
            xv = self.XL1.rearrange("(k p) n -> p k n", p=128)
            ggv = self.RGg.rearrange("(k p) n -> p k n", p=128)
            rrv = self.RGr.rearrange("(k p) n -> p k n", p=128)
            for blk in range(16):
                b = blk % 2
                tok0 = blk * 512
                t.dma(xf[:, b], xv[:, :, tok0:tok0 + 512], writes=["r1_xf%d" % b])
                t.op('pool', lambda e, b=b: e.tensor_copy(out=xb[:, b], in_=xf[:, b]), reads=["r1_xf%d" % b], writes=["r1_xb%d" % b])
                for fo in range(16):
                    bank = fo % 4
                    for k in range(8):
                        t.op('pe', lambda e, k=k, fo=fo, b=b, bank=bank: e.matmul(self.ps[bank][:], win[:, k, fo * 128:(fo + 1) * 128], xb[:, b, k, :],
                                                                             start=(k == 0), stop=(k == 7)),
                             reads=["r1_win", "r1_xb%d" % b], writes=["ps%d" % bank], skip_self=True)
                    if fo < 8:
                        t.op('act', lambda e, fo=fo, b=b, bank=bank: e.activation(out=gg[:, b, fo, :], in_=self.ps[bank][:], func=AF.Gelu_apprx_tanh),
                             reads=["ps%d" % bank], writes=["r1_gg%d" % b])
                    else:
                        t.op('dve', lambda e, fo=fo, b=b, bank=bank: e.tensor_copy(out=rr[:, b, fo - 8, :], in_=self.ps[bank][:]), reads=["ps%d" % bank], writes=["r1_rr%d" % b])
                t.dma(ggv[:, :, tok0:tok0 + 512], gg[:, b], reads=["r1_gg%d" % b], writes=["RGg_%d" % blk])
                t.dma(rrv[:, :, tok0:tok0 + 512], rr[:, b], reads=["r1_rr%d" % b], writes=["RGr_%d" % blk])
            t.barrier()

    def rg_p2(self):
        nc, t = self.nc, self.t
        LM = 4096
        with ExitStack() as ph:
            sb = lambda name, shape, dt=F32: ph.enter_context(self.sbuf(name, list(shape), dt))
            cw = sb("r2_cw", [128, 8, 4]); cbias = sb("r2_cbias", [128, 8])
            bga = sb("r2_bga", [128, 2, 8]); bgx = sb("r2_bgx", [128, 2, 8]); lam = sb("r2_lam", [128, 2, 8])
            sp8 = sb("r2_sp8", [128, 2, 8]); sp16 = sb("r2_sp16", [128, 2, 8])
            for dst, src in [(cw, self.rg_convw), (cbias, self.rg_convb), (bga, self.rg_bga), (bgx, self.rg_bgx), (lam, self.rg_lam)]:
                t.dma(dst[:], src, writes=[dst.name])
            t.op('act', lambda e: e.activation(out=sp8[:], in_=lam[:], func=AF.Exp, scale=-1.0), reads=[lam.name], writes=["r2_sp8"])
            t.op('act', lambda e: e.activation(out=sp8[:], in_=sp8[:], func=AF.Ln, bias=1.0, scale=1.0), reads=["r2_sp8"], writes=["r2_sp8"])
            t.op('dve', lambda e: e.tensor_scalar(out=sp16[:], in0=sp8[:], scalar1=-16.0, scalar2=None, op0=ALU.mult, op1=ALU.bypass), reads=["r2_sp8"], writes=["r2_sp16"])
            t.op('dve', lambda e: e.tensor_scalar(out=sp8[:], in0=sp8[:], scalar1=-8.0, scalar2=None, op0=ALU.mult, op1=ALU.bypass), reads=["r2_sp8", "r2_sp16"], writes=["r2_sp8"])
            wst = sb("r2_wst", [128, 2, 256])
            wgt = sb("r2_wgt", [128, 2, 2, 4, 2, 256], BF16)
            for gi, src in enumerate([self.rg_wga, self.rg_wgx]):
                for d_ in range(2):
                    for h in range(4):
                        t.dma(wst[:], src[d_, h].rearrange("(i p) o -> p i o", p=128), writes=["r2_wst"])
                        t.op('pool', lambda e, gi=gi, d_=d_, h=h: e.tensor_copy(out=wgt[:, gi, d_, h], in_=wst[:]), reads=["r2_wst"], writes=["r2_wgt"])
            rp = sb("r2_rp", [128, 2, LM + 3])
            cc = sb("r2_cc", [128, 2, LM])
            cb = sb("r2_cb", [128, 2, LM], BF16)
            A = sb("r2_A", [128, LM]); B = sb("r2_B", [128, LM])
            HF = sb("r2_HF", [128, LM]); HB = sb("r2_HB", [128, LM])
            gg = sb("r2_gg", [128, LM], BF16); yy = sb("r2_yy", [128, LM], BF16)
            rg_ = sb("r2_rg", [128, 2, 512]); ig_ = sb("r2_ig", [128, 2, 512]); a2_ = sb("r2_a2", [128, 2, 512]); tm_ = sb("r2_tm", [128, 2, 512])
            for si, (s0, L) in enumerate(SEGS):
                for h in range(4):
                    for ct in range(2):
                        ch = 2 * h + ct
                        t.op('pool', lambda e, ct=ct, L=L: e.memset(rp[:, ct, 0:1], 0.0), writes=["r2_rp%d" % ct])
                        t.op('pool', lambda e, ct=ct, L=L: e.memset(rp[:, ct, L + 1:L + 3], 0.0), reads=["r2_rp%d" % ct], writes=["r2_rp%d" % ct])
                        t.dma(rp[:, ct, 1:L + 1], self.RGr[ch * 128:(ch + 1) * 128, s0:s0 + L], reads=["r2_rp%d" % ct], writes=["r2_rp%d" % ct])
                        t.op('dve', lambda e, ct=ct, ch=ch, L=L: e.tensor_scalar(out=cc[:, ct, 0:L], in0=rp[:, ct, 0:L], scalar1=cw[:, ch, 0:1], scalar2=cbias[:, ch:ch + 1],
                                                                            op0=ALU.mult, op1=ALU.add), reads=["r2_rp%d" % ct, cw.name, cbias.name], writes=["r2_cc%d" % ct])
                        for k in range(1, 4):
                            t.op('dve', lambda e, ct=ct, ch=ch, L=L, k=k: e.scalar_tensor_tensor(out=cc[:, ct, 0:L], in0=rp[:, ct, k:k + L], scalar=cw[:, ch, k:k + 1], in1=cc[:, ct, 0:L],
                                                                                           op0=ALU.mult, op1=ALU.add), reads=["r2_rp%d" % ct, cw.name, "r2_cc%d" % ct], writes=["r2_cc%d" % ct])
                        t.op('pool', lambda e, ct=ct, L=L: e.tensor_copy(out=cb[:, ct, 0:L], in_=cc[:, ct, 0:L]), reads=["r2_cc%d" % ct], writes=["r2_cb%d" % ct])
                    for oh in range(2):
                        ch = 2 * h + oh
                        for d_ in range(2):
                            Hd = HF if d_ == 0 else HB
                            for c0 in range(0, L, 512):
                                pi = (c0 // 512) % 2
                                ba, bx = 2 * pi, 2 * pi + 1
                                for ih in range(2):
                                    t.op('pe', lambda e, ih=ih, d_=d_, h=h, oh=oh, c0=c0, ba=ba: e.matmul(self.ps[ba][:], wgt[:, 0, d_, h, ih, oh * 128:(oh + 1) * 128], cb[:, ih, c0:c0 + 512],
                                                                                                    start=(ih == 0), stop=(ih == 1)),
                                         reads=["r2_wgt", "r2_cb0", "r2_cb1"], writes=["ps%d" % ba], skip_self=True)
                                for ih in range(2):
                                    t.op('pe', lambda e, ih=ih, d_=d_, h=h, oh=oh, c0=c0, bx=bx: e.matmul(self.ps[bx][:], wgt[:, 1, d_, h, ih, oh * 128:(oh + 1) * 128], cb[:, ih, c0:c0 + 512],
                                                                                                    start=(ih == 0), stop=(ih == 1)),
                                         reads=["r2_wgt", "r2_cb0", "r2_cb1"], writes=["ps%d" % bx], skip_self=True)
                                t.op('act', lambda e, pi=pi, ba=ba, d_=d_, ch=ch: e.activation(out=rg_[:, pi, :], in_=self.ps[ba][:], func=AF.Sigmoid, bias=bga[:, d_, ch:ch + 1], scale=1.0),
                                     reads=["ps%d" % ba, bga.name], writes=["r2_rg%d" % pi])
                                t.op('act', lambda e, pi=pi, bx=bx, d_=d_, ch=ch: e.activation(out=ig_[:, pi, :], in_=self.ps[bx][:], func=AF.Sigmoid, bias=bgx[:, d_, ch:ch + 1], scale=1.0),
                                     reads=["ps%d" % bx, bgx.name], writes=["r2_ig%d" % pi])
                                t.op('act', lambda e, pi=pi, c0=c0, d_=d_, ch=ch: e.activation(out=A[:, c0:c0 + 512], in_=rg_[:, pi, :], func=AF.Exp, scale=sp8[:, d_, ch:ch + 1]),
                                     reads=["r2_rg%d" % pi, "r2_sp8"], writes=["r2_A"])
                                t.op('act', lambda e, pi=pi, d_=d_, ch=ch: e.activation(out=a2_[:, pi, :], in_=rg_[:, pi, :], func=AF.Exp, scale=sp16[:, d_, ch:ch + 1]),
                                     reads=["r2_rg%d" % pi, "r2_sp16"], writes=["r2_a2%d" % pi])
                                t.op('dve', lambda e, pi=pi: e.tensor_scalar(out=a2_[:, pi, :], in0=a2_[:, pi, :], scalar1=-1.0, scalar2=1.0, op0=ALU.mult, op1=ALU.add),
                                     reads=["r2_a2%d" % pi], writes=["r2_a2%d" % pi])
                                t.op('act', lambda e, pi=pi: e.activation(out=a2_[:, pi, :], in_=a2_[:, pi, :], func=AF.Sqrt), reads=["r2_a2%d" % pi], writes=["r2_a2%d" % pi])
                                t.op('pool', lambda e, pi=pi, oh=oh, c0=c0: e.tensor_tensor(out=tm_[:, pi, :], in0=ig_[:, pi, :], in1=cc[:, oh, c0:c0 + 512], op=ALU.mult),
                                     reads=["r2_ig%d" % pi, "r2_cc%d" % oh], writes=["r2_tm%d" % pi])
                                t.op('dve', lambda e, pi=pi, c0=c0: e.tensor_tensor(out=B[:, c0:c0 + 512], in0=tm_[:, pi, :], in1=a2_[:, pi, :], op=ALU.mult),
                                     reads=["r2_tm%d" % pi, "r2_a2%d" % pi], writes=["r2_B"])
                            if d_ == 0:
                                t.op('dve', lambda e, L=L: e.tensor_tensor_scan(out=HF[:, 0:L], data0=A[:, 0:L], data1=B[:, 0:L], initial=0.0, op0=ALU.mult, op1=ALU.add),
                                     reads=["r2_A", "r2_B"], writes=["r2_HF"])
                            else:
                                t.op('dve', lambda e, L=L: e.tensor_tensor_scan(out=HB[:, 0:L][:, ::-1], data0=A[:, 0:L][:, ::-1], data1=B[:, 0:L][:, ::-1], initial=0.0, op0=ALU.mult, op1=ALU.add),
                                     reads=["r2_A", "r2_B"], writes=["r2_HB"])
                        t.dma(gg[:, 0:L], self.RGg[ch * 128:(ch + 1) * 128, s0:s0 + L], writes=["r2_gg"])
                        t.op('pool', lambda e, L=L: e.tensor_tensor(out=HF[:, 0:L], in0=HF[:, 0:L], in1=HB[:, 0:L], op=ALU.add), reads=["r2_HF", "r2_HB"], writes=["r2_HF"])
                        t.op('dve', lambda e, L=L: e.tensor_tensor(out=yy[:, 0:L], in0=HF[:, 0:L], in1=gg[:, 0:L], op=ALU.mult), reads=["r2_HF", "r2_gg"], writes=["r2_yy"])
                        t.dma(self.RGy[ch * 128:(ch + 1) * 128, s0:s0 + L], yy[:, 0:L], reads=["r2_yy"], writes=["RGy_%d_%d" % (si, ch)])
            t.barrier()

    def rg_p3(self):
        nc, t = self.nc, self.t
        with ExitStack() as ph:
            sb = lambda name, shape, dt=F32: ph.enter_context(self.sbuf(name, list(shape), dt))
            self.mk_eps(ph)
            wo = self.load_w_bf16(ph, "r3_wo", self.w_rg_out, 8, D, eng_cast='mix')
            yb = sb("r3_yb", [128, 8, 512], BF16)
            xf = sb("r3_xf", [128, 8, 512])
            z = sb("r3_z", [128, 8, 512])
            tmp = {'zc': sb("r3_zc", [128, 8, 512]), 'sq': sb("r3_sq", [128, 8, 512]), 'sd': sb("r3_sd", [128, 512])}
            xo = sb("r3_xo", [128, 8, 512])
            xv = self.XL1.rearrange("(k p) n -> p k n", p=128)
            yv = self.RGy.rearrange("(k p) n -> p k n", p=128)
            X1v = self.X1.rearrange("(k p) n -> p k n", p=128)
            for blk in range(16):
                tok0 = blk * 512
                t.dma(yb[:], yv[:, :, tok0:tok0 + 512], writes=["r3_yb"])
                t.dma(xf[:], xv[:, :, tok0:tok0 + 512], writes=["r3_xf"])
                for fo in range(8):
                    bank = 2 + fo % 4
                    for k in range(8):
                        t.op('pe', lambda e, k=k, fo=fo, bank=bank: e.matmul(self.ps[bank][:], wo[:, k, fo * 128:(fo + 1) * 128], yb[:, k, :], start=(k == 0), stop=(k == 7)),
                             reads=["r3_wo", "r3_yb"], writes=["ps%d" % bank], skip_self=True)
                    t.op('dve', lambda e, fo=fo, bank=bank: e.scalar_tensor_tensor(out=z[:, fo, :], in0=xf[:, fo, :], scalar=ALPHA, in1=self.ps[bank][:], op0=ALU.mult, op1=ALU.add),
                         reads=["r3_xf", "ps%d" % bank], writes=["r3_z"])
                self.ln_block(z, "r3_z", 1, 0, xo, "r3_xo", 512, tmp, 0, 1)
                t.dma(X1v[:, :, tok0:tok0 + 512], xo[:], reads=["r3_xo"], writes=["X1"])
            t.barrier()


def _consts():
    c = {}
    c["c_ident"] = np.eye(128, dtype=np.float32)
    c["c_onesm"] = np.full((128, 128), 1.0 / D, dtype=np.float32)
    s = np.arange(128) // 16
    c["c_maskf"] = (s[:, None] <= s[None, :]).astype(np.float32)
    c["c_maskb"] = (s[:, None] >= s[None, :]).astype(np.float32)
    m = np.ones((128, 1024), dtype=np.float32)
    m[0:64, [0, 512, 768]] = 0.0
    m[64:128, [511, 767, 1023]] = 0.0
    c["c_segmask"] = m
    return c


def _shared_weights(inp):
    f = lambda a: np.ascontiguousarray(np.asarray(a, dtype=np.float32))
    w = dict(_consts())
    w["s5_w_in"] = f(inp["s5_w_in"][0])
    w["s5_w_glu"] = f(inp["s5_w_glu"][0])
    w["s5_lamre"] = f(inp["s5_lam_re"][0].transpose(0, 2, 1).reshape(128, 64))
    w["s5_lamim"] = f(inp["s5_lam_im"][0].transpose(0, 2, 1).reshape(128, 64))
    w["s5_lstep"] = f(np.broadcast_to(inp["s5_log_step"][0][:, None, :], (2, 64, 64)).reshape(128, 64))
    w["s5_bre"] = f(inp["s5_b_re"][0].transpose(0, 2, 1, 3).reshape(128, 64, 16))
    w["s5_bim"] = f(inp["s5_b_im"][0].transpose(0, 2, 1, 3).reshape(128, 64, 16))
    w["s5_ctre"] = f(inp["s5_c_re"][0].transpose(0, 3, 1, 2).reshape(128, 64, 16))
    w["s5_ctim"] = f(inp["s5_c_im"][0].transpose(0, 3, 1, 2).reshape(128, 64, 16))
    d = np.asarray(inp["s5_d"][0]).reshape(64, 16)
    w["s5_dpk"] = f(np.broadcast_to(d.T[None, :, :], (8, 16, 64)).reshape(128, 64))
    w["rg_w_in"] = f(inp["rg_w_in"][0])
    w["rg_convw"] = f(inp["rg_conv_w"][0].reshape(4, 8, 128).transpose(2, 1, 0))
    w["rg_convb"] = f(inp["rg_conv_b"][0].reshape(8, 128).T)
    w["rg_wga"] = f(inp["rg_w_gate_a"][0])
    w["rg_wgx"] = f(inp["rg_w_gate_x"][0])
    w["rg_bga"] = f(inp["rg_b_gate_a"][0].reshape(2, 8, 128).transpose(2, 0, 1))
    w["rg_bgx"] = f(inp["rg_b_gate_x"][0].reshape(2, 8, 128).transpose(2, 0, 1))
    w["rg_lam"] = f(inp["rg_lambda"][0].reshape(2, 8, 128).transpose(2, 0, 1))
    w["rg_w_out"] = f(inp["rg_w_out"][0])
    ln = np.stack([np.asarray(inp[k]) for k in ("ln1_g", "ln1_b", "ln2_g", "ln2_b")], 0)
    w["ln_par"] = f(ln.reshape(4, 2, 8, 128).transpose(3, 0, 1, 2))
    w["peer_w_q"] = f(inp["peer_w_q"])
    w["peer_skT"] = f(np.asarray(inp["peer_subkeys"]).transpose(0, 1, 3, 2))
    w["peer_uT"] = f(np.asarray(inp["peer_u"]).transpose(0, 2, 1))
    w["peer_v"] = f(inp["peer_v"])
    w["ple_w_proj"] = f(inp["ple_w_proj"])
    w["ple_w_gate"] = f(inp["ple_w_gate"])
    return w


def _core_acts(inp, core):
    xp = np.asarray(inp["x_prompt"][core])
    xs = np.asarray(inp["x_sample"][2 * core:2 * core + 2]).reshape(4096, D)
    x = np.concatenate([xp, xs], 0)
    pp = np.asarray(inp["p_prompt"][:, core])
    ps = np.asarray(inp["p_sample"][:, 2 * core:2 * core + 2]).reshape(2, 4096, 256)
    p = np.concatenate([pp, ps], 1)
    return {"xT": np.ascontiguousarray(x.T.astype(np.float32)),
            "pT": np.ascontiguousarray(p.transpose(0, 2, 1).astype(np.float32))}


_CACHE = {}


def kernel(**inputs):
    if "nc" not in _CACHE:
        k = Ker()
        _CACHE["nc"] = k.build()
        _CACHE["in_names"] = list(k.in_names)
    nc = _CACHE["nc"]
    names = _CACHE["in_names"]
    w = _shared_weights(inputs)
    in_maps = []
    for core in range(8):
        m = dict(w)
        m.update(_core_acts(inputs, core))
        in_maps.append({n: m[n] for n in names})
    res = run_bass_kernel_spmd(nc, in_maps, core_ids=list(range(8)))
    yp = np.empty((8, 4096, D), dtype=np.float32)
    ys = np.empty((16, 2048, D), dtype=np.float32)
    for core in range(8):
        y = np.asarray(res.results[core]["yT"]).T
        yp[core] = y[0:4096]
        ys[2 * core] = y[4096:6144]
        ys[2 * core + 1] = y[6144:8192]
    return (yp, ys)
```
